# Optimizing a Trainium2 kernel written in Bass

```python
import jax, jax.numpy as jnp
from jax import lax
import numpy as np

D_MODEL = 1024
BATCH = 16
SEQ = 4096
DEPTH = 2
DEC_BATCH = 8
DEC_SEQ = 4096
PAST_LEN = 128

A_HEADS = 8
A_HEAD_DIM = 64
A_WIDTH = A_HEADS * A_HEAD_DIM
DECAY_LORA = 64
ICLR_LORA = 64
GATE_LORA = 160
GN_EPS = 64e-5
B_HEADS = 8
Q_RANK = 256
KV_RANK = 128
QK_NOPE = 64
QK_ROPE = 32
V_DIM = 64
B_WIDTH = B_HEADS * V_DIM
ROPE_THETA = 10000.0
Q_BLOCK = 128
D_FF = ((8 * D_MODEL // 3 + 255) // 256) * 256
RMS_EPS = 1e-6

SHIFT_WIDTH = 3 * A_WIDTH + 2 * DECAY_LORA + 2 * ICLR_LORA + GATE_LORA
IN_SPLITS = (2 * D_MODEL, SHIFT_WIDTH, Q_RANK, KV_RANK, QK_ROPE)
N_IN = sum(IN_SPLITS)
A_SPLITS = (A_WIDTH, A_WIDTH, A_WIDTH, DECAY_LORA, DECAY_LORA, ICLR_LORA, ICLR_LORA, GATE_LORA)

kernel_name = "hybrid_rwkv7_mla_gated_encoder"


def _split(t, sizes):
    offs = []
    acc = 0
    for s in sizes[:-1]:
        acc += s
        offs.append(acc)
    return jnp.split(t, offs, axis=-1)


def _rmsnorm(x, g):
    x32 = x.astype(jnp.float32)
    y = x32 * lax.rsqrt(jnp.mean(x32 * x32, axis=-1, keepdims=True) + RMS_EPS)
    return (y * g.astype(jnp.float32)).astype(x.dtype)


def _token_shift(p, mu):
    p_prev = jnp.pad(p[:, :-1], ((0, 0), (1, 0), (0, 0)))
    p_next = jnp.pad(p[:, 1:], ((0, 0), (0, 1), (0, 0)))
    return p + mu[0] * (p_prev - p) + mu[1] * (p_next - p)


def _rope_tables(s):
    pos = jnp.arange(s, dtype=jnp.float32)
    inv_freq = 1.0 / (ROPE_THETA ** (jnp.arange(0, QK_ROPE, 2, dtype=jnp.float32) / QK_ROPE))
    ang = pos[:, None] * inv_freq[None, :]
    ang = jnp.concatenate([ang, ang], axis=-1)
    return jnp.cos(ang), jnp.sin(ang)


def _rope(x, cos, sin):
    x1, x2 = jnp.split(x, 2, axis=-1)
    rot = jnp.concatenate([-x2, x1], axis=-1)
    return x * cos.astype(x.dtype) + rot * sin.astype(x.dtype)


def _rwkv7_scan(r, w, k, v, kk, a, reverse):
    bsz, _, h, n = r.shape

    def step(state, inp):
        r_t, w_t, k_t, v_t, kk_t, a_t = inp
        sa = jnp.einsum('bhvk,bhk->bhv', state, kk_t)
        state = (state * w_t[:, :, None, :]
                 - sa[..., None] * (kk_t * a_t)[:, :, None, :]
                 + v_t[..., None] * k_t[:, :, None, :])
        return state, jnp.einsum('bhvk,bhk->bhv', state, r_t)

    xs = tuple(jnp.swapaxes(t, 0, 1) for t in (r, w, k, v, kk, a))
    s0 = jnp.zeros((bsz, h, n, n), jnp.float32)
    _, o = lax.scan(step, s0, xs, reverse=reverse)
    return jnp.swapaxes(o, 0, 1)


def _rwkv7_branch(p, w2, w0, a2, a0, g2, k_k, k_a, r_k, gn_g, gn_b):
    bsz, s, _ = p.shape
    pr, pk, pv, pwf, pwb, paf, pab, pg = _split(p.astype(jnp.float32), A_SPLITS)

    def heads(t):
        return t.reshape(bsz, s, A_HEADS, A_HEAD_DIM)

    def decay(lora, up, base):
        wl = -jax.nn.softplus(-(base + jnp.tanh(lora) @ up)) - 0.5
        return jnp.exp(-jnp.exp(wl))

    def iclr(lora, up, base):
        return jax.nn.sigmoid(base + lora @ up)

    wf, wb = decay(pwf, w2[0], w0[0]), decay(pwb, w2[1], w0[1])
    af, ab = iclr(paf, a2[0], a0[0]), iclr(pab, a2[1], a0[1])
    g = jax.nn.sigmoid(pg) @ g2
    kk = heads(pk * k_k)
    kk = kk / jnp.maximum(jnp.sqrt(jnp.sum(kk * kk, axis=-1, keepdims=True)), 1e-12)
    kf = pk * (1.0 + (af - 1.0) * k_a)
    kb = pk * (1.0 + (ab - 1.0) * k_a)
    r, v = heads(pr), heads(pv)
    o = (_rwkv7_scan(r, heads(wf), heads(kf), v, kk, heads(af), reverse=False)
         + _rwkv7_scan(r, heads(wb), heads(kb), v, kk, heads(ab), reverse=True))
    mean = jnp.mean(o, axis=-1, keepdims=True)
    var = jnp.mean(jnp.square(o - mean), axis=-1, keepdims=True)
    o = ((o - mean) * lax.rsqrt(var + GN_EPS)).reshape(bsz, s, A_WIDTH) * gn_g + gn_b
    bonus = jnp.sum(r * heads(pk) * r_k, axis=-1, keepdims=True) * v
    return (o + bonus.reshape(bsz, s, A_WIDTH)) * g


def _mla_branch(pq, pkv, pkr, q_norm_g, w_uq, kv_norm_g, w_ukv, cos, sin):
    bsz, s, _ = pq.shape
    q = (_rmsnorm(pq, q_norm_g) @ w_uq).reshape(bsz, s, B_HEADS, QK_NOPE + QK_ROPE)
    q_nope = q[..., :QK_NOPE]
    q_rope = _rope(q[..., QK_NOPE:], cos[:, None, :], sin[:, None, :])
    kv = (_rmsnorm(pkv, kv_norm_g) @ w_ukv).reshape(bsz, s, B_HEADS, QK_NOPE + V_DIM)
    k_nope, v = kv[..., :QK_NOPE], kv[..., QK_NOPE:]
    k_rope = _rope(pkr, cos, sin)
    scale = (QK_NOPE + QK_ROPE) ** -0.5
    nb = s // Q_BLOCK

    def blk(qs):
        qn, qr = qs
        sc = (jnp.einsum('bqhd,bkhd->bhqk', qn, k_nope)
              + jnp.einsum('bqhd,bkd->bhqk', qr, k_rope))
        prob = jax.nn.softmax(sc.astype(jnp.float32) * scale, axis=-1)
        return jnp.einsum('bhqk,bkhd->bqhd', prob.astype(v.dtype), v)

    def qblocks(t):
        return jnp.moveaxis(t.reshape(bsz, nb, Q_BLOCK, B_HEADS, t.shape[-1]), 1, 0)

    o = lax.map(blk, (qblocks(q_nope), qblocks(q_rope)))
    return jnp.moveaxis(o, 0, 1).reshape(bsz, s, B_WIDTH)


def _trunk(x, norm_mix_g, w_in, shift_mu, decay_w2, decay_w0, iclr_a2, iclr_a0, gate_g2,
           k_k, k_a, r_k, gn_g, gn_b, w_oa, q_norm_g, w_uq, kv_norm_g, w_ukv, w_ob,
           w_out, norm_ffn_g, w_gu, w_down, final_norm_g):
    cos, sin = _rope_tables(x.shape[1])
    for l in range(DEPTH):
        h = _rmsnorm(x, norm_mix_g[l])
        gates, p_a, pq, pkv, pkr = _split(h @ w_in[l], IN_SPLITS)
        p_a = _token_shift(p_a, shift_mu[l])
        y_a = _rwkv7_branch(p_a, decay_w2[l], decay_w0[l], iclr_a2[l], iclr_a0[l], gate_g2[l],
                            k_k[l], k_a[l], r_k[l], gn_g[l], gn_b[l]).astype(x.dtype) @ w_oa[l]
        y_b = _mla_branch(pq, pkv, pkr, q_norm_g[l], w_uq[l], kv_norm_g[l], w_ukv[l], cos, sin) @ w_ob[l]
        g_a, g_b = jnp.split(jax.nn.sigmoid(gates), 2, axis=-1)
        x = x + (g_a * y_a + g_b * y_b) @ w_out[l]
        h = _rmsnorm(x, norm_ffn_g[l])
        gt, up = jnp.split(h @ w_gu[l], 2, axis=-1)
        x = x + (jax.nn.silu(gt) * up) @ w_down[l]
    return _rmsnorm(x, final_norm_g)


def setup_inputs(seed: int = 0) -> dict:
    key = jax.random.key(seed)
    ks = jax.random.split(key, 32)
    f32 = jnp.float32

    def nrm(k, shape, scale):
        return jax.random.normal(k, shape, f32) * scale

    return {
        "x_prompt": nrm(ks[0], (BATCH, SEQ, D_MODEL), 1.0),
        "x_sample": nrm(ks[1], (DEC_BATCH, DEC_SEQ, D_MODEL), 1.0),
        "norm_mix_g": 1.0 + nrm(ks[2], (DEPTH, D_MODEL), 0.02),
        "w_in": nrm(ks[3], (DEPTH, D_MODEL, N_IN), D_MODEL ** -0.5),
        "shift_mu": 0.3 + nrm(ks[4], (DEPTH, 2, SHIFT_WIDTH), 0.1),
        "decay_w2": nrm(ks[5], (DEPTH, 2, DECAY_LORA, A_WIDTH), 0.1 * DECAY_LORA ** -0.5),
        "decay_w0": nrm(ks[6], (DEPTH, 2, A_WIDTH), 1.0),
        "iclr_a2": nrm(ks[7], (DEPTH, 2, ICLR_LORA, A_WIDTH), 0.1 * ICLR_LORA ** -0.5),
        "iclr_a0": nrm(ks[8], (DEPTH, 2, A_WIDTH), 0.5),
        "gate_g2": nrm(ks[9], (DEPTH, GATE_LORA, A_WIDTH), GATE_LORA ** -0.5),
        "k_k": 0.85 + nrm(ks[10], (DEPTH, A_WIDTH), 0.05),
        "k_a": 1.0 + nrm(ks[11], (DEPTH, A_WIDTH), 0.05),
        "r_k": nrm(ks[12], (DEPTH, A_HEADS, A_HEAD_DIM), 0.1),
        "gn_g": 1.0 + nrm(ks[13], (DEPTH, A_WIDTH), 0.02),
        "gn_b": nrm(ks[14], (DEPTH, A_WIDTH), 0.02),
        "w_oa": nrm(ks[15], (DEPTH, A_WIDTH, D_MODEL), A_WIDTH ** -0.5),
        "q_norm_g": 1.0 + nrm(ks[16], (DEPTH, Q_RANK), 0.02),
        "w_uq": nrm(ks[17], (DEPTH, Q_RANK, B_HEADS * (QK_NOPE + QK_ROPE)), Q_RANK ** -0.5),
        "kv_norm_g": 1.0 + nrm(ks[18], (DEPTH, KV_RANK), 0.02),
        "w_ukv": nrm(ks[19], (DEPTH, KV_RANK, B_HEADS * (QK_NOPE + V_DIM)), KV_RANK ** -0.5),
        "w_ob": nrm(ks[20], (DEPTH, B_WIDTH, D_MODEL), B_WIDTH ** -0.5),
        "w_out": nrm(ks[21], (DEPTH, D_MODEL, D_MODEL), D_MODEL ** -0.5),
        "norm_ffn_g": 1.0 + nrm(ks[22], (DEPTH, D_MODEL), 0.02),
        "w_gu": nrm(ks[23], (DEPTH, D_MODEL, 2 * D_FF), D_MODEL ** -0.5),
        "w_down": nrm(ks[24], (DEPTH, D_FF, D_MODEL), D_FF ** -0.5),
        "final_norm_g": 1.0 + nrm(ks[25], (D_MODEL,), 0.02),
    }


def reference(x_prompt, x_sample, norm_mix_g, w_in, shift_mu, decay_w2, decay_w0, iclr_a2,
              iclr_a0, gate_g2, k_k, k_a, r_k, gn_g, gn_b, w_oa, q_norm_g, w_uq, kv_norm_g,
              w_ukv, w_ob, w_out, norm_ffn_g, w_gu, w_down, final_norm_g):
    y_prompt = _trunk(x_prompt, norm_mix_g, w_in, shift_mu, decay_w2, decay_w0, iclr_a2, iclr_a0,
                      gate_g2, k_k, k_a, r_k, gn_g, gn_b, w_oa, q_norm_g, w_uq, kv_norm_g, w_ukv,
                      w_ob, w_out, norm_ffn_g, w_gu, w_down, final_norm_g)
    y_sample = _trunk(x_sample, norm_mix_g, w_in, shift_mu, decay_w2, decay_w0, iclr_a2, iclr_a0,
                      gate_g2, k_k, k_a, r_k, gn_g, gn_b, w_oa, q_norm_g, w_uq, kv_norm_g, w_ukv,
                      w_ob, w_out, norm_ffn_g, w_gu, w_down, final_norm_g)
    return (y_prompt, y_sample)
```

```python
import contextlib
import os
import numpy as np
import concourse.bass as bass
import concourse.mybir as mybir
from concourse.alu_op_type import AluOpType as ALU
from concourse.bass_utils import run_bass_kernel_spmd

F32 = mybir.dt.float32
BF16 = mybir.dt.bfloat16
AF = mybir.ActivationFunctionType
AX = mybir.AxisListType

D = 1024
NIN = 4416
DFF = 2816
DEPTH = 2
NCORES = 8
SEQ_FULL = 4096
RMS_EPS = 1e-6
GN_EPS = 64e-5
CDEC = 0.6065306597126334
SCALE = 96.0 ** -0.5

ENGS = ("pe", "act", "dve", "pool", "sp")
N_DMA_SEMS = 8
SAME_ENGINE_SYNC = True


def _is_psum(r):
    n = r[0] if isinstance(r, tuple) else r
    return isinstance(n, str) and len(n) >= 2 and n[0] in "qp" and (n[1].isdigit() or n[1] in "TP")


class Sched:
    def __init__(self, nc):
        self.nc = nc
        self.q = {e: [] for e in ENGS}
        self.cnt = {e: 0 for e in ENGS}
        self.seen = {e: {} for e in ENGS}
        self.last_w = {}
        self.readers = {}
        self.dma_val = {}
        self.dma_rr = {e: 0 for e in ENGS}
        self.stack = contextlib.ExitStack()
        self.sems = {}
        self.nops = 0
        self.limit = int(os.environ.get("OPLIMIT", "1000000000"))
        self.marks = []

    def mark(self, label):
        self.marks.append((label, self.nops))

    def _deps(self, eng, reads, writes):
        deps = []
        for r in reads:
            ev = self.last_w.get(r)
            if ev is not None:
                deps.append(ev)
            if eng != "pe" and _is_psum(r):
                deps.extend(e2 for e2 in self.readers.get(r, ()) if e2[0] != eng)
        for w in writes:
            ev = self.last_w.get(w)
            if ev is not None:
                deps.append(ev)
            deps.extend(self.readers.get(w, ()))
        waits = {}
        seen = self.seen[eng]
        for sk, v in deps:
            if sk == eng and (eng == "pe" or not SAME_ENGINE_SYNC):
                continue
            if seen.get(sk, 0) >= v:
                continue
            if waits.get(sk, 0) < v:
                waits[sk] = v
        for sk, v in waits.items():
            seen[sk] = v
        return waits

    def _record(self, ev, reads, writes):
        for r in reads:
            self.readers.setdefault(r, []).append(ev)
        for w in writes:
            self.last_w[w] = ev
            self.readers[w] = []

    def op(self, eng, fn, reads=(), writes=()):
        self.nops += 1
        if self.nops > self.limit:
            return
        waits = self._deps(eng, reads, writes)
        self.cnt[eng] += 1
        ev = (eng, self.cnt[eng])
        self.q[eng].append((list(waits.items()), fn, (eng, 1)))
        self._record(ev, reads, writes)

    def dma(self, eng, fn, reads=(), writes=()):
        self.nops += 1
        if self.nops > self.limit:
            return
        k = self.dma_rr[eng]
        self.dma_rr[eng] = (k + 1) % N_DMA_SEMS
        sk = ("dma", eng, k)
        prev = self.dma_val.get(sk, 0)
        waits = self._deps(eng, reads, writes)
        if prev > 0 and self.seen[eng].get(sk, 0) < prev:
            waits[sk] = prev
            self.seen[eng][sk] = prev
        self.dma_val[sk] = prev + 16
        ev = (sk, prev + 16)
        self.q[eng].append((list(waits.items()), fn, (sk, 16)))
        self._record(ev, reads, writes)

    def barrier(self):
        tgt = {e: self.cnt[e] for e in ENGS if self.cnt[e] > 0}
        tgt.update(self.dma_val)
        for e in ENGS:
            waits = []
            for sk, v in tgt.items():
                if sk == e:
                    continue
                if self.seen[e].get(sk, 0) < v:
                    waits.append((sk, v))
                    self.seen[e][sk] = v
            if waits:
                self.q[e].append((waits, None, None))
        self.last_w = {}
        self.readers = {}

    def emit(self):
        nc = self.nc
        st = self.stack
        keys = list(ENGS) + list(self.dma_val)
        for sk in keys:
            nm = sk if isinstance(sk, str) else "d_%s_%d" % (sk[1], sk[2])
            self.sems[sk] = st.enter_context(nc.semaphore("s_" + nm))
        final = list(self.dma_val.items())
        block = st.enter_context(nc.Block())
        sems = self.sems

        def run(engname, final_waits=()):
            def body(e):
                for waits, fn, inc in self.q[engname]:
                    for sk, v in waits:
                        e.wait_ge(sems[sk], v)
                    if fn is not None:
                        fn(e).then_inc(sems[inc[0]], inc[1])
                for sk, v in final_waits:
                    e.wait_ge(sems[sk], v)
            return body

        block.tensor(run("pe"))
        block.scalar(run("act"))
        block.vector(run("dve"))
        block.gpsimd(run("pool"))
        block.sync(run("sp", final))


class Builder:
    def __init__(self, S_LEN, NSEQ, depth=DEPTH):
        self.S_LEN = S_LEN
        self.NSEQ = NSEQ
        self.depth = depth
        self.NT = S_LEN // 128
        nc = bass.Bass("TRN2", target_bir_lowering=False)
        self.nc = nc
        self.S = Sched(nc)
        self.ph = None
        self._uid = 0

        def inp(name, shape):
            return nc.dram_tensor(name, list(shape), F32, kind="ExternalInput").ap()

        L = depth
        self.x = inp("x", [NSEQ, S_LEN, D])
        self.w = dict(
            norm_mix_g=inp("norm_mix_g", [L, D]), w_in=inp("w_in", [L, D, NIN]),
            shift_mu=inp("shift_mu", [L, 2, 1952]), decay_w2=inp("decay_w2", [L, 2, 64, 512]),
            decay_w0=inp("decay_w0", [L, 2, 512]), iclr_a2=inp("iclr_a2", [L, 2, 64, 512]),
            iclr_a0=inp("iclr_a0", [L, 2, 512]), gate_g2=inp("gate_g2", [L, 160, 512]),
            k_k=inp("k_k", [L, 512]), k_a=inp("k_a", [L, 512]), r_k=inp("r_k", [L, 512]),
            gn_g=inp("gn_g", [L, 512]), gn_b=inp("gn_b", [L, 512]), w_oa=inp("w_oa", [L, 512, D]),
            q_norm_g=inp("q_norm_g", [L, 256]), w_uq=inp("w_uq", [L, 256, 768]),
            kv_norm_g=inp("kv_norm_g", [L, 128]), w_ukv=inp("w_ukv", [L, 128, 1024]),
            w_ob=inp("w_ob", [L, 512, D]), w_out=inp("w_out", [L, D, D]),
            norm_ffn_g=inp("norm_ffn_g", [L, D]), w_gu=inp("w_gu", [L, D, 2 * DFF]),
            w_down=inp("w_down", [L, DFF, D]), final_norm_g=inp("final_norm_g", [1, D]),
        )
        self.c_ident = inp("c_ident", [128, 128])
        self.c_tri = inp("c_tri", [2, 128, 128])
        self.c_m4 = inp("c_m4", [2, 128, 512])
        self.c_mn4 = inp("c_mn4", [2, 128, 512])
        self.c_ones = inp("c_ones", [128, 128])
        self.c_cos = inp("c_cos", [S_LEN, 32])
        self.c_sin = inp("c_sin", [S_LEN, 32])
        self.y = nc.dram_tensor("y", [NSEQ, S_LEN, D], F32, kind="ExternalOutput").ap()
        def scr(name, shape, dt=F32):
            return nc.dram_tensor(name, list(shape), dt).ap()
        self.P = scr("scr_P", [S_LEN, NIN])
        self.of = scr("scr_of", [S_LEN, 512])
        self.ya = scr("scr_ya", [S_LEN, 512])
        self.yb = scr("scr_yb", [S_LEN, 512])
        self.x1 = scr("scr_x1", [S_LEN, D])
        self.x2 = scr("scr_x2", [S_LEN, D])
        self.QT = scr("scr_QT", [8, 96, S_LEN], BF16)
        self.KT = scr("scr_KT", [8, 96, S_LEN], BF16)
        self.Vd = scr("scr_V", [S_LEN, 8 * 65], BF16)

    def begin_phase(self):
        self.ph = contextlib.ExitStack()

    def end_phase(self):
        self.S.barrier()
        self.ph.close()
        self.ph = None

    def sb(self, name, shape, dt):
        self._uid += 1
        return self.ph.enter_context(self.nc.sbuf_tensor("%s_%d" % (name, self._uid), list(shape), dt))

    def ps(self, name, shape, dt):
        self._uid += 1
        return self.ph.enter_context(self.nc.psum_tensor("%s_%d" % (name, self._uid), list(shape), dt))

    def load(self, out_ap, in_ap, writes, reads=(), cast=False):
        eng = "pool" if cast else "sp"
        self.S.dma(eng, lambda e: e.dma_start(out=out_ap, in_=in_ap), reads=reads, writes=writes)

    def store(self, out_ap, in_ap, reads, writes):
        self.S.dma("sp", lambda e: e.dma_start(out=out_ap, in_=in_ap), reads=reads, writes=writes)

    def bcast_load(self, tile, row_ap, width, name):
        self.load(tile[:], row_ap.broadcast_to([128, width]), writes=[name])

    def load_w_bf16(self, tile, w_ap, K, name):
        for k in range(K // 128):
            self.load(tile[:, k, :], w_ap[k * 128:(k + 1) * 128, :], writes=[name], cast=True)

    def mm(self, out, lhsT, rhs, start, stop, reads, writes):
        self.S.op("pe", lambda e: e.matmul(out=out, lhsT=lhsT, rhs=rhs, start=start, stop=stop),
                  reads=reads, writes=writes)

    def tr(self, out, in_, ident, reads, writes):
        self.S.op("pe", lambda e: e.transpose(out=out, in_=in_, identity=ident), reads=reads, writes=writes)

    def act(self, out, in_, func, reads, writes, scale=None, bias=None, accum_out=None):
        kw = {}
        if scale is not None:
            kw["scale"] = scale
        if bias is not None:
            kw["bias"] = bias
        if accum_out is not None:
            kw["accum_out"] = accum_out
        self.S.op("act", lambda e: e.activation(out=out, in_=in_, func=func, **kw), reads=reads, writes=writes)

    def tt(self, eng, out, in0, in1, op, reads, writes):
        self.S.op(eng, lambda e: e.tensor_tensor(out=out, in0=in0, in1=in1, op=op), reads=reads, writes=writes)

    def ts(self, out, in0, s1, s2, op0, op1, reads, writes, eng="dve"):
        self.S.op(eng, lambda e: e.tensor_scalar(out=out, in0=in0, scalar1=s1, scalar2=s2, op0=op0, op1=op1),
                  reads=reads, writes=writes)

    def stt(self, out, in0, scalar, in1, op0, op1, reads, writes):
        self.S.op("dve", lambda e: e.scalar_tensor_tensor(out=out, in0=in0, scalar=scalar, in1=in1, op0=op0, op1=op1),
                  reads=reads, writes=writes)

    def cp(self, eng, out, in_, reads, writes):
        if eng == "act":
            self.S.op("act", lambda e: e.activation(out=out, in_=in_, func=AF.Copy), reads=reads, writes=writes)
        else:
            self.S.op(eng, lambda e: e.tensor_copy(out=out, in_=in_), reads=reads, writes=writes)

    def red(self, out, in_, reads, writes):
        self.S.op("dve", lambda e: e.tensor_reduce(out=out, in_=in_, axis=AX.X, op=ALU.add), reads=reads, writes=writes)

    def recip(self, out, in_, reads, writes):
        self.S.op("dve", lambda e: e.reciprocal(out=out, in_=in_), reads=reads, writes=writes)

    def memset(self, eng, ap, val, writes):
        self.S.op(eng, lambda e: e.memset(ap, val), writes=writes)

    def rmsnorm(self, x_ap, xn, width, gbc_ap, gn, out_ap, outn, junk, ss, rstd, tag):
        jn, sn, rn = "junk" + tag, "ss" + tag, "rstd" + tag
        self.act(junk, x_ap, AF.Square, reads=[xn], writes=[jn, sn], accum_out=ss)
        self.ts(rstd, ss, 1.0 / width, RMS_EPS, ALU.mult, ALU.add, reads=[sn], writes=[rn])
        self.act(rstd, rstd, AF.Sqrt, reads=[rn], writes=[rn])
        self.recip(rstd, rstd, reads=[rn], writes=[rn])
        self.stt(out_ap, x_ap, rstd, gbc_ap, ALU.mult, ALU.mult, reads=[xn, rn, gn], writes=[outn])

    def phase_A(self, l, xin):
        NT = self.NT
        self.begin_phase()
        wA = self.sb("wA", [128, 8, NIN], BF16)
        gbc = self.sb("gA", [128, D], F32)
        idb = self.sb("idb", [128, 128], BF16)
        junk = self.sb("junk", [128, D], F32)
        ss = self.sb("ss", [128, 1], F32)
        rstd = self.sb("rstd", [128, 1], F32)
        xt = [self.sb("xt", [128, D], F32) for _ in range(2)]
        h = self.sb("h", [128, D], BF16)
        hT = self.sb("hT", [128, 8, 128], BF16)
        Pt = [self.sb("Pt", [128, NIN], F32) for _ in range(2)]
        pT = self.ps("pT", [128, 8, 128], BF16)
        pP = [self.ps("pP", [128, 512], F32) for _ in range(4)]
        self.load(idb[:], self.c_ident, writes=["idb"], cast=True)
        self.bcast_load(gbc, self.w["norm_mix_g"][l:l + 1, :], D, "gA")
        self.load_w_bf16(wA, self.w["w_in"][l], D, "wA")
        npieces = (NIN + 511) // 512
        for i in range(NT):
            b = i % 2
            xn = "xt%d" % b
            self.load(xt[b][:], xin[i * 128:(i + 1) * 128, :], writes=[xn])
            self.rmsnorm(xt[b][:], xn, D, gbc[:], "gA", h[:], "h", junk[:], ss[:], rstd[:], "A")
            for k in range(8):
                self.tr(pT[:, k, :], h[:, k * 128:(k + 1) * 128], idb[:], reads=["h", "idb"], writes=["pT"])
            self.cp("act", hT[:], pT[:], reads=["pT"], writes=["hT"])
            for j in range(npieces):
                n0 = j * 512
                n = min(512, NIN - n0)
                pp = pP[j % 4]
                pn = "pP%d" % (j % 4)
                for k in range(8):
                    self.mm(pp[:, 0:n], hT[:, k, :], wA[:, k, n0:n0 + n], k == 0, k == 7,
                            reads=["hT", "wA"], writes=[pn])
                self.cp("dve" if j % 2 == 0 else "act", Pt[b][:, n0:n0 + n], pp[:, 0:n],
                        reads=[pn], writes=[("Pt", b, j)])
            self.store(self.P[i * 128:(i + 1) * 128, :], Pt[b][:], reads=[("Pt", b, j) for j in range(npieces)],
                       writes=[("P", i)])
        self.end_phase()

    def phase_R(self, l, d):
        NT = self.NT
        S_LEN = self.S_LEN
        W = self.w
        self.begin_phase()
        sb, ps = self.sb, self.ps
        mu0 = sb("mu0", [128, 1952], F32)
        mu1 = sb("mu1", [128, 1952], F32)
        w0bc = sb("w0bc", [128, 512], F32)
        a0bc = sb("a0bc", [128, 512], F32)
        kkbc = sb("kkbc", [128, 512], F32)
        kabc = sb("kabc", [128, 512], F32)
        w2b = sb("w2b", [64, 512], BF16)
        a2b = sb("a2b", [64, 512], BF16)
        tri = sb("tri", [128, 128], F32)
        ones = sb("ones", [128, 128], F32)
        m4 = sb("m4", [128, 512], F32)
        mn4 = sb("mn4", [128, 512], F32)
        idb = sb("idb", [128, 128], BF16)
        pac = sb("pac", [128, 1952], F32)
        pap = sb("pap", [128, 1952], F32)
        pan = sb("pan", [128, 1952], F32)
        lo = sb("lo", [128, 128], BF16)
        loT = sb("loT", [64, 2, 128], BF16)
        sgm = sb("sgm", [128, 512], F32)
        av = sb("av", [128, 512], F32)
        kk = sb("kk", [128, 512], F32)
        tmp = sb("tmp", [128, 512], F32)
        kd = sb("kd", [128, 512], F32)
        ka = sb("ka", [128, 512], F32)
        Ls = sb("Ls", [128, 512], F32)
        Ld = sb("Ld", [128, 512], F32)
        E1 = sb("E1", [128, 512], F32)
        E2 = sb("E2", [128, 512], F32)
        E3 = sb("E3", [128, 512], F32)
        E4 = sb("E4", [128, 512], F32)
        ssq = sb("ssq", [128, 8], F32)
        gC = sb("gC", [64, 8], F32)
        Ab = sb("Ab", [128, 512], BF16)
        Rb = sb("Rb", [128, 512], BF16)
        Bb = sb("Bb", [128, 512], BF16)
        Kb = sb("Kb", [128, 512], BF16)
        Btb = sb("Btb", [128, 512], BF16)
        Ktb = sb("Ktb", [128, 512], BF16)
        Vb = sb("Vb", [128, 512], BF16)
        ART = sb("ART", [128, 8, 256], BF16)
        BT = sb("BT", [64, 8, 128], BF16)
        KTt = sb("KTt", [64, 8, 128], BF16)
        ATall = sb("ATall", [128, 8, 512], BF16)
        PP = [sb("PP", [128, 8, 256], BF16) for _ in range(2)]
        W32 = sb("W32", [128, 512], F32)
        Wb = sb("Wb", [128, 512], BF16)
        osb = sb("osb", [128, 512], F32)
        ST32 = sb("ST32", [64, 8, 64], F32)
        STb = sb("STb", [128, 8, 64], BF16)
        if d == 1:
            rkbc = sb("rkbc", [128, 512], F32)
            gngbc = sb("gngbc", [128, 512], F32)
            gnbbc = sb("gnbbc", [128, 512], F32)
            g2b = sb("g2b", [128, 2, 512], BF16)
            oft = sb("oft", [128, 512], F32)
            cen = sb("cen", [128, 512], F32)
            sq2 = sb("sq2", [128, 512], F32)
            bon = sb("bon", [128, 512], F32)
            st8 = sb("st8", [128, 8], F32)
            sv8 = sb("sv8", [128, 8], F32)
            sb8 = sb("sb8", [128, 8], F32)
            gs = sb("gs", [128, 160], BF16)
            gT = sb("gT", [128, 2, 128], BF16)
            yat = sb("yat", [128, 512], F32)
        q = [ps("q%d" % i, [128, 512], F32) for i in range(8)]
        q7 = q[7]
        qAT = q[3][:].bitcast(BF16)
        qRT = q[4][:].bitcast(BF16)
        qBT = q[5][:].bitcast(BF16)
        qKT = q[6][:].bitcast(BF16)
        q7b = q7[:].bitcast(BF16)

        self.load(idb[:], self.c_ident, writes=["idb"], cast=True)
        self.load(tri[:], self.c_tri[d], writes=["tri"])
        self.load(ones[:], self.c_ones, writes=["ones"])
        self.load(m4[:], self.c_m4[d], writes=["m4"])
        self.load(mn4[:], self.c_mn4[d], writes=["mn4"])
        self.bcast_load(mu0, W["shift_mu"][l, 0:1, :], 1952, "mu0")
        self.bcast_load(mu1, W["shift_mu"][l, 1:2, :], 1952, "mu1")
        self.bcast_load(w0bc, W["decay_w0"][l, d:d + 1, :], 512, "w0bc")
        self.bcast_load(a0bc, W["iclr_a0"][l, d:d + 1, :], 512, "a0bc")
        self.bcast_load(kkbc, W["k_k"][l:l + 1, :], 512, "kkbc")
        self.bcast_load(kabc, W["k_a"][l:l + 1, :], 512, "kabc")
        self.load(w2b[:], W["decay_w2"][l, d], writes=["w2b"], cast=True)
        self.load(a2b[:], W["iclr_a2"][l, d], writes=["a2b"], cast=True)
        if d == 1:
            self.bcast_load(rkbc, W["r_k"][l:l + 1, :], 512, "rkbc")
            self.bcast_load(gngbc, W["gn_g"][l:l + 1, :], 512, "gngbc")
            self.bcast_load(gnbbc, W["gn_b"][l:l + 1, :], 512, "gnbbc")
        self.memset("dve", ST32[:], 0.0, writes=["ST32"])
        self.memset("dve", STb[:], 0.0, writes=["STb"])
        self.memset("dve", ART[:], 0.0, writes=["ART"])
        if d == 1:
            self.memset("dve", gT[:], 0.0, writes=["gT"])
            self.memset("dve", g2b[:], 0.0, writes=["g2b"])
            self.load(g2b[:, 0, :], W["gate_g2"][l, 0:128, :], writes=["g2b"], cast=True)
            self.load(g2b[0:32, 1, :], W["gate_g2"][l, 128:160, :], writes=["g2b"], cast=True)

        def v3(t):
            return t[:].rearrange("p (h e) -> p h e", h=8)

        order = range(NT) if d == 0 else range(NT - 1, -1, -1)
        for i in order:
            t0 = i * 128
            self.S.mark("loads")
            self.load(pac[:], self.P[t0:t0 + 128, 2048:4000], writes=["pac"])
            if i == 0:
                self.memset("pool", pap[:], 0.0, writes=["pap"])
                self.load(pap[1:128, :], self.P[0:127, 2048:4000], writes=["pap"])
            else:
                self.load(pap[:], self.P[t0 - 1:t0 + 127, 2048:4000], writes=["pap"])
            if i == NT - 1:
                self.memset("pool", pan[:], 0.0, writes=["pan"])
                self.load(pan[0:127, :], self.P[t0 + 1:S_LEN, 2048:4000], writes=["pan"])
            else:
                self.load(pan[:], self.P[t0 + 1:t0 + 129, 2048:4000], writes=["pan"])
            self.tt("dve", pap[:], pap[:], pac[:], ALU.subtract, reads=["pap", "pac"], writes=["pap"])
            self.tt("pool", pap[:], pap[:], mu0[:], ALU.mult, reads=["pap", "mu0"], writes=["pap"])
            self.tt("dve", pan[:], pan[:], pac[:], ALU.subtract, reads=["pan", "pac"], writes=["pan"])
            self.tt("pool", pan[:], pan[:], mu1[:], ALU.mult, reads=["pan", "mu1"], writes=["pan"])
            self.tt("dve", pac[:], pac[:], pap[:], ALU.add, reads=["pac", "pap"], writes=["pac"])
            self.tt("dve", pac[:], pac[:], pan[:], ALU.add, reads=["pac", "pan"], writes=["pac"])
            r_ = pac[:, 0:512]
            k_ = pac[:, 512:1024]
            v_ = pac[:, 1024:1536]
            lw_ = pac[:, 1536 + 64 * d:1600 + 64 * d]
            la_ = pac[:, 1664 + 64 * d:1728 + 64 * d]
            lg_ = pac[:, 1792:1952]
            self.S.mark("lora")
            self.act(lo[:, 0:64], lw_, AF.Tanh, reads=["pac"], writes=["lo"])
            self.cp("dve", lo[:, 64:128], la_, reads=["pac"], writes=["lo"])
            self.tr(q7b[0:64, 0:128], lo[:, 0:64], idb[:], reads=["lo", "idb"], writes=["q7"])
            self.tr(q7b[0:64, 128:256], lo[:, 64:128], idb[:], reads=["lo", "idb"], writes=["q7"])
            self.cp("act", loT[:].rearrange("p a b -> p (a b)"), q7b[0:64, 0:256], reads=["q7"], writes=["loT"])
            self.mm(q[0][:], loT[:, 0, :], w2b[:], True, True, reads=["loT", "w2b"], writes=["q0"])
            self.mm(q[1][:], loT[:, 1, :], a2b[:], True, True, reads=["loT", "a2b"], writes=["q1"])
            self.tt("dve", sgm[:], q[0][:], w0bc[:], ALU.add, reads=["q0", "w0bc"], writes=["sgm"])
            self.act(sgm[:], sgm[:], AF.Sigmoid, reads=["sgm"], writes=["sgm"])
            self.tt("dve", av[:], q[1][:], a0bc[:], ALU.add, reads=["q1", "a0bc"], writes=["av"])
            self.act(av[:], av[:], AF.Sigmoid, reads=["av"], writes=["av"])
            self.S.mark("kk")
            self.tt("dve", kk[:], k_, kkbc[:], ALU.mult, reads=["pac", "kkbc"], writes=["kk"])
            self.tt("pool", tmp[:], kk[:], kk[:], ALU.mult, reads=["kk"], writes=["tmp"])
            self.red(ssq[:], v3(tmp), reads=["tmp"], writes=["ssq"])
            self.act(ssq[:], ssq[:], AF.Sqrt, reads=["ssq"], writes=["ssq"])
            self.ts(ssq[:], ssq[:], 1e-12, None, ALU.max, ALU.bypass, reads=["ssq"], writes=["ssq"])
            self.recip(ssq[:], ssq[:], reads=["ssq"], writes=["ssq"])
            self.tt("dve", v3(kk), v3(kk), ssq[:].unsqueeze(2).broadcast_to([128, 8, 64]), ALU.mult,
                    reads=["kk", "ssq"], writes=["kk"])
            self.stt(tmp[:], av[:], -1.0, kabc[:], ALU.add, ALU.mult, reads=["av", "kabc"], writes=["tmp"])
            self.stt(kd[:], tmp[:], 1.0, k_, ALU.add, ALU.mult, reads=["tmp", "pac"], writes=["kd"])
            self.tt("dve", ka[:], kk[:], av[:], ALU.mult, reads=["kk", "av"], writes=["ka"])
            self.S.mark("cum")
            self.mm(q[0][:], tri[:], sgm[:], True, True, reads=["tri", "sgm"], writes=["q0"])
            self.mm(q[1][:], ones[:], sgm[:], True, True, reads=["ones", "sgm"], writes=["q1"])
            for hh in range(8):
                self.mm(q[2][0:64, hh * 2:hh * 2 + 2], sgm[:, hh * 64:(hh + 1) * 64], ones[:, 0:2], True, True,
                        reads=["sgm", "ones"], writes=["q2"])
            self.act(gC[:], q[2][0:64, 0:16].rearrange("p (h t) -> p h t", t=2)[:, :, 0], AF.Exp,
                     reads=["q2"], writes=["gC"], scale=-CDEC)
            self.cp("act", Ls[:], q[0][:], reads=["q0"], writes=["Ls"])
            self.tt("dve", Ld[:], q[1][:], Ls[:], ALU.subtract, reads=["q1", "Ls"], writes=["Ld"])
            self.act(E2[:], q[0][:], AF.Exp, reads=["q0"], writes=["E2"], scale=-CDEC)
            self.act(E3[:], q[0][:], AF.Exp, reads=["q0"], writes=["E3"], scale=CDEC)
            self.tt("dve", Ls[:], Ls[:], sgm[:], ALU.subtract, reads=["Ls", "sgm"], writes=["Ls"])
            self.act(E1[:], Ls[:], AF.Exp, reads=["Ls"], writes=["E1"], scale=-CDEC)
            self.act(E4[:], Ld[:], AF.Exp, reads=["Ld"], writes=["E4"], scale=-CDEC)
            self.S.mark("scaled")
            self.stt(Ab[:], kk[:], -1.0, E1[:], ALU.mult, ALU.mult, reads=["kk", "E1"], writes=["Ab"])
            self.tt("dve", Rb[:], r_, E2[:], ALU.mult, reads=["pac", "E2"], writes=["Rb"])
            self.tt("pool", Bb[:], ka[:], E3[:], ALU.mult, reads=["ka", "E3"], writes=["Bb"])
            self.tt("pool", Kb[:], kd[:], E3[:], ALU.mult, reads=["kd", "E3"], writes=["Kb"])
            self.tt("pool", Btb[:], ka[:], E4[:], ALU.mult, reads=["ka", "E4"], writes=["Btb"])
            self.tt("dve", Ktb[:], kd[:], E4[:], ALU.mult, reads=["kd", "E4"], writes=["Ktb"])
            self.cp("pool", Vb[:], v_, reads=["pac"], writes=["Vb"])
            for hh in range(8):
                hs = slice(hh * 64, (hh + 1) * 64)
                ts_ = slice(hh * 128, (hh + 1) * 128)
                self.tr(qAT[0:64, ts_], Ab[:, hs], idb[:], reads=["Ab", "idb"], writes=["q3"])
                self.tr(qRT[0:64, ts_], Rb[:, hs], idb[:], reads=["Rb", "idb"], writes=["q4"])
                self.tr(qBT[0:64, ts_], Bb[:, hs], idb[:], reads=["Bb", "idb"], writes=["q5"])
                self.tr(qKT[0:64, ts_], Kb[:, hs], idb[:], reads=["Kb", "idb"], writes=["q6"])
            self.cp("act", ART[0:64, :, 0:128], qAT[0:64, :].rearrange("p (h t) -> p h t", h=8), reads=["q3"], writes=["ART"])
            self.cp("dve", ART[0:64, :, 128:256], qRT[0:64, :].rearrange("p (h t) -> p h t", h=8), reads=["q4"], writes=["ART"])
            self.cp("act", BT[:], qBT[0:64, :].rearrange("p (h t) -> p h t", h=8), reads=["q5"], writes=["BT"])
            self.cp("dve", KTt[:], qKT[0:64, :].rearrange("p (h t) -> p h t", h=8), reads=["q6"], writes=["KTt"])
            self.S.mark("A4")
            for hh in range(8):
                qq = q[hh % 2]
                qn = "q%d" % (hh % 2)
                self.mm(qq[:, 0:256], BT[:, hh, :], ART[0:64, hh, :], True, True, reads=["BT", "ART"], writes=[qn])
                self.mm(qq[:, 256:512], KTt[:, hh, :], ART[0:64, hh, :], True, True, reads=["KTt", "ART"], writes=[qn])
                self.tt("dve", ATall[:, hh, :], qq[:], m4[:], ALU.mult, reads=[qn, "m4"], writes=[("AT", hh)])
            self.S.mark("N")
            for g in range(2):
                for j in range(4):
                    hh = g * 4 + j
                    self.mm(q7[:, j * 128:(j + 1) * 128], ART[0:64, hh, 0:128], BT[:, hh, :], True, True,
                            reads=["ART", "BT"], writes=["q7"])
                self.tt("dve", PP[0][:, g * 4:(g + 1) * 4, 0:128], q7[:].rearrange("p (j s) -> p j s", j=4),
                        mn4[:].rearrange("p (j s) -> p j s", j=4), ALU.mult, reads=["q7", "mn4"],
                        writes=[("PP", 0, 2 * g), ("PP", 0, 2 * g + 1)])
            self.cp("pool", PP[0][:, :, 128:256], ATall[:, :, 0:128], reads=[("AT", hh) for hh in range(8)],
                    writes=[("PP", 0, pr) for pr in range(4)])
            self.S.mark("W")
            for hh in range(8):
                hs = slice(hh * 64, (hh + 1) * 64)
                self.mm(q[2][:, hs], ART[:, hh, 0:128], STb[:, hh, :], True, False, reads=["ART", "STb"], writes=["q2"])
                self.mm(q[2][:, hs], ATall[:, hh, 256:384], Vb[:, hs], False, True, reads=[("AT", hh), "Vb"], writes=["q2"])
            self.cp("act", W32[:], q[2][:], reads=["q2"], writes=["W32"])
            self.cp("dve", Wb[:], W32[:], reads=["W32"], writes=["Wb"])
            self.S.mark("neu")
            for j in range(7):
                cb = j % 2
                cur = PP[cb]
                for hh in range(8):
                    hs = slice(hh * 64, (hh + 1) * 64)
                    self.mm(q7[:, hs], cur[:, hh, 128:256], Wb[:, hs], True, True,
                            reads=[("PP", cb, hh // 2), "Wb"], writes=["q7"])
                self.tt("dve", W32[:], W32[:], q7[:], ALU.add, reads=["W32", "q7"], writes=["W32"])
                self.cp("act", Wb[:], W32[:], reads=["W32"], writes=["Wb"])
                if j < 6:
                    nxt = PP[1 - cb]
                    for pr in range(4):
                        bankt, bname = q[pr], "q%d" % pr
                        for u in range(2):
                            hh = pr * 2 + u
                            self.mm(bankt[:, u * 256:u * 256 + 128], cur[:, hh, 128:256], cur[:, hh, 0:128], True, True,
                                    reads=[("PP", cb, pr)], writes=[bname])
                            self.mm(bankt[:, u * 256 + 128:u * 256 + 256], cur[:, hh, 0:128], cur[:, hh, 128:256], True, True,
                                    reads=[("PP", cb, pr)], writes=[bname])
                        self.cp("dve" if pr % 2 == 0 else "act",
                                nxt[:, pr * 2:pr * 2 + 2, :].rearrange("p a b -> p (a b)"), bankt[:],
                                reads=[bname], writes=[("PP", 1 - cb, pr)])
            self.S.mark("O")
            for hh in range(8):
                hs = slice(hh * 64, (hh + 1) * 64)
                self.mm(q[4][:, hs], ART[:, hh, 128:256], STb[:, hh, :], True, False, reads=["ART", "STb"], writes=["q4"])
                self.mm(q[4][:, hs], ATall[:, hh, 128:256], Wb[:, hs], False, False, reads=[("AT", hh), "Wb"], writes=["q4"])
                self.mm(q[4][:, hs], ATall[:, hh, 384:512], Vb[:, hs], False, True, reads=[("AT", hh), "Vb"], writes=["q4"])
            self.cp("act", osb[:], q[4][:], reads=["q4"], writes=["osb"])
            self.S.mark("S")
            for hh in range(8):
                hs = slice(hh * 64, (hh + 1) * 64)
                self.mm(q[5][0:64, hs], Btb[:, hs], Wb[:, hs], True, False, reads=["Btb", "Wb"], writes=["q5"])
                self.mm(q[5][0:64, hs], Ktb[:, hs], Vb[:, hs], False, True, reads=["Ktb", "Vb"], writes=["q5"])
            self.tt("dve", ST32[:], ST32[:], gC[:].unsqueeze(2).broadcast_to([64, 8, 64]), ALU.mult,
                    reads=["ST32", "gC"], writes=["ST32"])
            self.tt("dve", ST32[:], ST32[:], q[5][0:64, :].rearrange("p (h e) -> p h e", h=8), ALU.add,
                    reads=["ST32", "q5"], writes=["ST32"])
            self.cp("act", STb[0:64, :, :], ST32[:], reads=["ST32"], writes=["STb"])
            if d == 0:
                self.store(self.of[t0:t0 + 128, :], osb[:], reads=["osb"], writes=[("of", i)])
                continue
            self.load(oft[:], self.of[t0:t0 + 128, :], writes=["oft"])
            self.tt("dve", oft[:], oft[:], osb[:], ALU.add, reads=["oft", "osb"], writes=["oft"])
            self.red(st8[:], v3(oft), reads=["oft"], writes=["st8"])
            self.ts(st8[:], st8[:], 1.0 / 64, None, ALU.mult, ALU.bypass, reads=["st8"], writes=["st8"])
            self.tt("dve", v3(cen), v3(oft), st8[:].unsqueeze(2).broadcast_to([128, 8, 64]), ALU.subtract,
                    reads=["oft", "st8"], writes=["cen"])
            self.tt("pool", sq2[:], cen[:], cen[:], ALU.mult, reads=["cen"], writes=["sq2"])
            self.red(sv8[:], v3(sq2), reads=["sq2"], writes=["sv8"])
            self.ts(sv8[:], sv8[:], 1.0 / 64, GN_EPS, ALU.mult, ALU.add, reads=["sv8"], writes=["sv8"])
            self.act(sv8[:], sv8[:], AF.Sqrt, reads=["sv8"], writes=["sv8"])
            self.recip(sv8[:], sv8[:], reads=["sv8"], writes=["sv8"])
            self.tt("dve", v3(cen), v3(cen), sv8[:].unsqueeze(2).broadcast_to([128, 8, 64]), ALU.mult,
                    reads=["cen", "sv8"], writes=["cen"])
            self.tt("pool", cen[:], cen[:], gngbc[:], ALU.mult, reads=["cen", "gngbc"], writes=["cen"])
            self.tt("pool", cen[:], cen[:], gnbbc[:], ALU.add, reads=["cen", "gnbbc"], writes=["cen"])
            self.tt("dve", sq2[:], r_, k_, ALU.mult, reads=["pac"], writes=["sq2"])
            self.tt("pool", sq2[:], sq2[:], rkbc[:], ALU.mult, reads=["sq2", "rkbc"], writes=["sq2"])
            self.red(sb8[:], v3(sq2), reads=["sq2"], writes=["sb8"])
            self.tt("dve", v3(bon), pac[:, 1024:1536].rearrange("p (h e) -> p h e", h=8),
                    sb8[:].unsqueeze(2).broadcast_to([128, 8, 64]), ALU.mult, reads=["pac", "sb8"], writes=["bon"])
            self.tt("dve", cen[:], cen[:], bon[:], ALU.add, reads=["cen", "bon"], writes=["cen"])
            self.act(gs[:], lg_, AF.Sigmoid, reads=["pac"], writes=["gs"])
            self.tr(q7b[:, 0:128], gs[:, 0:128], idb[:], reads=["gs", "idb"], writes=["q7"])
            self.tr(q7b[0:32, 128:256], gs[:, 128:160], idb[:], reads=["gs", "idb"], writes=["q7"])
            self.cp("act", gT[:, 0, :], q7b[:, 0:128], reads=["q7"], writes=["gT"])
            self.cp("act", gT[0:32, 1, :], q7b[0:32, 128:256], reads=["q7"], writes=["gT"])
            self.mm(q[2][:], gT[:, 0, :], g2b[:, 0, :], True, False, reads=["gT", "g2b"], writes=["q2"])
            self.mm(q[2][:], gT[:, 1, :], g2b[:, 1, :], False, True, reads=["gT", "g2b"], writes=["q2"])
            self.tt("dve", yat[:], cen[:], q[2][:], ALU.mult, reads=["cen", "q2"], writes=["yat"])
            self.store(self.ya[t0:t0 + 128, :], yat[:], reads=["yat"], writes=[("ya", i)])
        self.end_phase()

    def phase_MP(self, l):
        NT = self.NT
        W = self.w
        self.begin_phase()
        sb, ps = self.sb, self.ps
        qg = sb("qg", [128, 256], F32)
        kvg = sb("kvg", [128, 128], F32)
        wuq = sb("wuq", [128, 2, 768], BF16)
        wukv = sb("wukv", [128, 1, 1024], BF16)
        idb = sb("idb", [128, 128], BF16)
        pm = sb("pm", [128, 416], F32)
        cs = sb("cs", [128, 32], F32)
        sn = sb("sn", [128, 32], F32)
        junk = sb("junk", [128, 256], F32)
        ss = sb("ss", [128, 1], F32)
        rstd = sb("rstd", [128, 1], F32)
        nb = sb("nb", [128, 384], BF16)
        nT = sb("nT", [128, 3, 128], BF16)
        qf = sb("qf", [128, 768], F32)
        kvf = sb("kvf", [128, 1024], F32)
        t1 = sb("t1", [128, 8, 32], F32)
        t2 = sb("t2", [128, 8, 32], F32)
        kro = sb("kro", [128, 32], F32)
        kr2 = sb("kr2", [128, 32], F32)
        Qa = sb("Qa", [128, 8, 96], BF16)
        Ka = sb("Ka", [128, 8, 96], BF16)
        Va = sb("Va", [128, 8, 65], BF16)
        QTt = sb("QTt", [96, 8, 128], BF16)
        KTt = sb("KTt", [96, 8, 128], BF16)
        q = [ps("q%d" % i, [128, 512], F32) for i in range(7)]
        qb = [t[:].bitcast(BF16) for t in q]
        self.load(idb[:], self.c_ident, writes=["idb"], cast=True)
        self.bcast_load(qg, W["q_norm_g"][l:l + 1, :], 256, "qg")
        self.bcast_load(kvg, W["kv_norm_g"][l:l + 1, :], 128, "kvg")
        self.load_w_bf16(wuq, W["w_uq"][l], 256, "wuq")
        self.load_w_bf16(wukv, W["w_ukv"][l], 128, "wukv")
        self.memset("dve", Va[:], 1.0, writes=["Va"])
        qf3 = qf[:].rearrange("p (h e) -> p h e", h=8)
        kvf3 = kvf[:].rearrange("p (h e) -> p h e", h=8)
        for i in range(NT):
            t0 = i * 128
            self.load(pm[:], self.P[t0:t0 + 128, 4000:4416], writes=["pm"])
            self.load(cs[:], self.c_cos[t0:t0 + 128, :], writes=["cs"])
            self.load(sn[:], self.c_sin[t0:t0 + 128, :], writes=["sn"])
            self.rmsnorm(pm[:, 0:256], "pm", 256, qg[:], "qg", nb[:, 0:256], "nbq", junk[:, 0:256], ss[:], rstd[:], "M")
            self.rmsnorm(pm[:, 256:384], "pm", 128, kvg[:], "kvg", nb[:, 256:384], "nbk", junk[:, 0:128], ss[:], rstd[:], "M")
            for c in range(3):
                self.tr(qb[0][:, c * 128:(c + 1) * 128], nb[:, c * 128:(c + 1) * 128], idb[:],
                        reads=["nbq", "nbk", "idb"], writes=["q0"])
            self.cp("act", nT[:].rearrange("p a b -> p (a b)"), qb[0][:, 0:384], reads=["q0"], writes=["nT"])
            for c in range(2):
                self.mm(q[1][:], nT[:, c, :], wuq[:, c, 0:512], c == 0, c == 1, reads=["nT", "wuq"], writes=["q1"])
            for c in range(2):
                self.mm(q[2][:, 0:256], nT[:, c, :], wuq[:, c, 512:768], c == 0, c == 1, reads=["nT", "wuq"], writes=["q2"])
            self.mm(q[3][:], nT[:, 2, :], wukv[:, 0, 0:512], True, True, reads=["nT", "wukv"], writes=["q3"])
            self.mm(q[4][:], nT[:, 2, :], wukv[:, 0, 512:1024], True, True, reads=["nT", "wukv"], writes=["q4"])
            self.cp("act", qf[:, 0:512], q[1][:], reads=["q1"], writes=["qf"])
            self.cp("dve", qf[:, 512:768], q[2][:, 0:256], reads=["q2"], writes=["qf"])
            self.cp("act", kvf[:, 0:512], q[3][:], reads=["q3"], writes=["kvf"])
            self.cp("dve", kvf[:, 512:1024], q[4][:], reads=["q4"], writes=["kvf"])
            self.cp("pool", Qa[:, :, 0:64], qf3[:, :, 0:64], reads=["qf"], writes=["Qa"])
            csb = cs[:].unsqueeze(1).broadcast_to([128, 8, 32])
            self.tt("dve", t1[:], qf3[:, :, 64:96], csb, ALU.mult, reads=["qf", "cs"], writes=["t1"])
            self.tt("dve", t2[:, :, 0:16], qf3[:, :, 80:96], sn[:, 0:16].unsqueeze(1).broadcast_to([128, 8, 16]), ALU.mult,
                    reads=["qf", "sn"], writes=["t2"])
            self.tt("dve", t2[:, :, 16:32], qf3[:, :, 64:80], sn[:, 16:32].unsqueeze(1).broadcast_to([128, 8, 16]), ALU.mult,
                    reads=["qf", "sn"], writes=["t2"])
            self.tt("dve", Qa[:, :, 64:96], t1[:], t2[:], ALU.add, reads=["t1", "t2"], writes=["Qa"])
            self.tt("dve", kro[:], pm[:, 384:416], cs[:], ALU.mult, reads=["pm", "cs"], writes=["kro"])
            self.tt("dve", kr2[:, 0:16], pm[:, 400:416], sn[:, 0:16], ALU.mult, reads=["pm", "sn"], writes=["kr2"])
            self.tt("dve", kr2[:, 16:32], pm[:, 384:400], sn[:, 16:32], ALU.mult, reads=["pm", "sn"], writes=["kr2"])
            self.tt("dve", kro[:], kro[:], kr2[:], ALU.add, reads=["kro", "kr2"], writes=["kro"])
            self.cp("dve", Ka[:, :, 64:96], kro[:].unsqueeze(1).broadcast_to([128, 8, 32]), reads=["kro"], writes=["Ka"])
            self.cp("pool", Ka[:, :, 0:64], kvf3[:, :, 0:64], reads=["kvf"], writes=["Ka"])
            self.cp("pool", Va[:, :, 0:64], kvf3[:, :, 64:128], reads=["kvf"], writes=["Va"])
            for hh in range(8):
                self.tr(qb[5][0:96, hh * 128:(hh + 1) * 128], Qa[:, hh, :], idb[:], reads=["Qa", "idb"], writes=["q5"])
                self.tr(qb[6][0:96, hh * 128:(hh + 1) * 128], Ka[:, hh, :], idb[:], reads=["Ka", "idb"], writes=["q6"])
            self.cp("act", QTt[:].rearrange("p a b -> p (a b)"), qb[5][0:96, :], reads=["q5"], writes=["QTt"])
            self.cp("dve", KTt[:].rearrange("p a b -> p (a b)"), qb[6][0:96, :], reads=["q6"], writes=["KTt"])
            self.store(self.QT[:, :, t0:t0 + 128].rearrange("h p t -> p h t"), QTt[:], reads=["QTt"], writes=[("QT", i)])
            self.store(self.KT[:, :, t0:t0 + 128].rearrange("h p t -> p h t"), KTt[:], reads=["KTt"], writes=[("KT", i)])
            self.store(self.Vd[t0:t0 + 128, :], Va[:].rearrange("p a b -> p (a b)"), reads=["Va"], writes=[("Vd", i)])
        self.end_phase()

    def phase_MM(self):
        NT = self.NT
        S_LEN = self.S_LEN
        QB = min(512, S_LEN)
        nqb = S_LEN // QB
        nj = QB // 128
        self.begin_phase()
        sb, ps = self.sb, self.ps
        Vall = sb("Vall", [128, NT, 520], BF16)
        KTh = [sb("KTh", [96, S_LEN], BF16) for _ in range(2)]
        QTb = [sb("QTb", [96, QB], BF16) for _ in range(2)]
        PT = [sb("PT", [128, QB], BF16) for _ in range(2)]
        OT = sb("OT", [65, QB], F32)
        id32 = sb("id32", [128, 128], F32)
        osm = sb("osm", [128, nj, 64], F32)
        rec = sb("rec", [128, nj], F32)
        q = [ps("q%d" % i, [128, 512], F32) for i in range(4)]
        self.load(id32[:], self.c_ident, writes=["id32"])
        self.load(Vall[:], self.Vd.rearrange("(c p) f -> p c f", p=128), writes=["Vall"])
        it = 0
        for hh in range(8):
            kb = hh % 2
            self.load(KTh[kb][:], self.KT[hh], writes=["KTh%d" % kb])
            for qi in range(nqb):
                qb_ = it % 2
                it += 1
                self.load(QTb[qb_][:], self.QT[hh, :, qi * QB:(qi + 1) * QB], writes=["QTb%d" % qb_])
                for kc in range(NT):
                    pb = kc % 2
                    self.mm(q[pb][:, 0:QB], KTh[kb][:, kc * 128:(kc + 1) * 128], QTb[qb_][:], True, True,
                            reads=["KTh%d" % kb, "QTb%d" % qb_], writes=["q%d" % pb])
                    self.act(PT[pb][:], q[pb][:, 0:QB], AF.Exp, reads=["q%d" % pb], writes=["PT%d" % pb], scale=SCALE)
                    self.mm(q[2][0:65, 0:QB], Vall[:, kc, hh * 65:(hh + 1) * 65], PT[pb][:], kc == 0, kc == NT - 1,
                            reads=["Vall", "PT%d" % pb], writes=["q2"])
                self.cp("dve", OT[:], q[2][0:65, 0:QB], reads=["q2"], writes=["OT"])
                for j in range(nj):
                    self.tr(q[3][:, j * 65:(j + 1) * 65], OT[:, j * 128:(j + 1) * 128], id32[0:65, 0:65],
                            reads=["OT", "id32"], writes=["q3"])
                o3 = q[3][:, 0:nj * 65].rearrange("p (j e) -> p j e", j=nj)
                self.recip(rec[:], o3[:, :, 64], reads=["q3"], writes=["rec"])
                self.tt("dve", osm[:], o3[:, :, 0:64], rec[:].unsqueeze(2).broadcast_to([128, nj, 64]), ALU.mult,
                        reads=["q3", "rec"], writes=["osm"])
                self.store(self.yb[qi * QB:(qi + 1) * QB, hh * 64:(hh + 1) * 64].rearrange("(j p) e -> p j e", p=128),
                           osm[:], reads=["osm"], writes=[("yb", hh, qi)])
        self.end_phase()

    def phase_C1(self, l, xin):
        NT = self.NT
        W = self.w
        self.begin_phase()
        sb, ps = self.sb, self.ps
        woa = sb("woa", [128, 4, D], BF16)
        wob = sb("wob", [128, 4, D], BF16)
        wout = sb("wout", [128, 8, D], BF16)
        idb = sb("idb", [128, 128], BF16)
        xt = [sb("xt", [128, D], F32) for _ in range(2)]
        gt = [sb("gt", [128, 2048], F32) for _ in range(2)]
        yat = [sb("yat", [128, 512], F32) for _ in range(2)]
        ybt = [sb("ybt", [128, 512], F32) for _ in range(2)]
        yab = sb("yab", [128, D], BF16)
        yT = sb("yT", [128, 8, 128], BF16)
        m1 = sb("m1", [128, D], F32)
        m2 = sb("m2", [128, D], F32)
        mixb = sb("mixb", [128, D], BF16)
        mixT = sb("mixT", [128, 8, 128], BF16)
        x1t = sb("x1t", [128, D], F32)
        q = [ps("q%d" % i, [128, 512], F32) for i in range(7)]
        qTb = q[6][:].bitcast(BF16)
        self.load(idb[:], self.c_ident, writes=["idb"], cast=True)
        self.load_w_bf16(woa, W["w_oa"][l], 512, "woa")
        self.load_w_bf16(wob, W["w_ob"][l], 512, "wob")
        self.load_w_bf16(wout, W["w_out"][l], D, "wout")
        for i in range(NT):
            b = i % 2
            t0 = i * 128
            self.load(xt[b][:], xin[t0:t0 + 128, :], writes=["xt%d" % b])
            self.load(gt[b][:], self.P[t0:t0 + 128, 0:2048], writes=["gt%d" % b])
            self.load(yat[b][:], self.ya[t0:t0 + 128, :], writes=["yat%d" % b])
            self.load(ybt[b][:], self.yb[t0:t0 + 128, :], writes=["ybt%d" % b])
            self.cp("dve", yab[:, 0:512], yat[b][:], reads=["yat%d" % b], writes=["yab0"])
            self.cp("pool", yab[:, 512:1024], ybt[b][:], reads=["ybt%d" % b], writes=["yab1"])
            for k in range(8):
                self.tr(qTb[:, k * 128:(k + 1) * 128], yab[:, k * 128:(k + 1) * 128], idb[:],
                        reads=["yab0", "yab1", "idb"], writes=["q6"])
            self.cp("act", yT[:].rearrange("p a b -> p (a b)"), qTb[:], reads=["q6"], writes=["yT"])
            for hf in range(2):
                hs = slice(hf * 512, (hf + 1) * 512)
                for k in range(4):
                    self.mm(q[hf][:], yT[:, k, :], woa[:, k, hs], k == 0, k == 3, reads=["yT", "woa"], writes=["q%d" % hf])
                for k in range(4):
                    self.mm(q[2 + hf][:], yT[:, 4 + k, :], wob[:, k, hs], k == 0, k == 3, reads=["yT", "wob"],
                            writes=["q%d" % (2 + hf)])
            self.act(gt[b][:], gt[b][:], AF.Sigmoid, reads=["gt%d" % b], writes=["gt%d" % b])
            for hf in range(2):
                hs = slice(hf * 512, (hf + 1) * 512)
                hs2 = slice(1024 + hf * 512, 1024 + (hf + 1) * 512)
                self.tt("dve", m1[:, hs], gt[b][:, hs], q[hf][:], ALU.mult, reads=["gt%d" % b, "q%d" % hf], writes=[("m1", hf)])
                self.tt("dve", m2[:, hs], gt[b][:, hs2], q[2 + hf][:], ALU.mult, reads=["gt%d" % b, "q%d" % (2 + hf)],
                        writes=[("m2", hf)])
                self.tt("pool", mixb[:, hs], m1[:, hs], m2[:, hs], ALU.add, reads=[("m1", hf), ("m2", hf)], writes=[("mixb", hf)])
            for k in range(8):
                self.tr(qTb[:, k * 128:(k + 1) * 128], mixb[:, k * 128:(k + 1) * 128], idb[:],
                        reads=[("mixb", 0), ("mixb", 1), "idb"], writes=["q6"])
            self.cp("act", mixT[:].rearrange("p a b -> p (a b)"), qTb[:], reads=["q6"], writes=["mixT"])
            for hf in range(2):
                hs = slice(hf * 512, (hf + 1) * 512)
                for k in range(8):
                    self.mm(q[4 + hf][:], mixT[:, k, :], wout[:, k, hs], k == 0, k == 7, reads=["mixT", "wout"],
                            writes=["q%d" % (4 + hf)])
                self.tt("dve", x1t[:, hs], xt[b][:, hs], q[4 + hf][:], ALU.add, reads=["xt%d" % b, "q%d" % (4 + hf)],
                        writes=[("x1t", hf)])
            self.store(self.x1[t0:t0 + 128, :], x1t[:], reads=[("x1t", 0), ("x1t", 1)], writes=[("x1", i)])
        self.end_phase()

    def phase_C2(self, l, last, yout):
        NT = self.NT
        W = self.w
        self.begin_phase()
        sb, ps = self.sb, self.ps
        wgu = sb("wgu", [128, 8, 2 * DFF], BF16)
        wdn = sb("wdn", [128, 22, D], BF16)
        gbc = sb("gbc", [128, D], F32)
        idb = sb("idb", [128, 128], BF16)
        xt = [sb("xt", [128, D], F32) for _ in range(2)]
        junk = sb("junk", [128, D], F32)
        ss = sb("ss", [128, 1], F32)
        rstd = sb("rstd", [128, 1], F32)
        h = sb("h", [128, D], BF16)
        hT = sb("hT", [128, 8, 128], BF16)
        sl = [sb("sl", [128, 256], F32) for _ in range(2)]
        actb = sb("actb", [128, DFF], BF16)
        actT = sb("actT", [128, 22, 128], BF16)
        x2t = sb("x2t", [128, D], F32)
        if last:
            fbc = sb("fbc", [128, D], F32)
        q = [ps("q%d" % i, [128, 512], F32) for i in range(6)]
        qTb = q[4][:].bitcast(BF16)
        qT2 = q[5][:].bitcast(BF16)
        self.load(idb[:], self.c_ident, writes=["idb"], cast=True)
        self.bcast_load(gbc, W["norm_ffn_g"][l:l + 1, :], D, "gbc")
        if last:
            self.bcast_load(fbc, W["final_norm_g"][0:1, :], D, "fbc")
        self.load_w_bf16(wgu, W["w_gu"][l], D, "wgu")
        self.load_w_bf16(wdn, W["w_down"][l], DFF, "wdn")
        for i in range(NT):
            b = i % 2
            t0 = i * 128
            xn = "xt%d" % b
            self.load(xt[b][:], self.x1[t0:t0 + 128, :], writes=[xn])
            self.rmsnorm(xt[b][:], xn, D, gbc[:], "gbc", h[:], "h", junk[:], ss[:], rstd[:], "F")
            for k in range(8):
                self.tr(qTb[:, k * 128:(k + 1) * 128], h[:, k * 128:(k + 1) * 128], idb[:], reads=["h", "idb"], writes=["q4"])
            self.cp("act", hT[:].rearrange("p a b -> p (a b)"), qTb[:], reads=["q4"], writes=["hT"])
            for j in range(11):
                bk = q[j % 2]
                bn = "q%d" % (j % 2)
                for k in range(8):
                    self.mm(bk[:, 0:256], hT[:, k, :], wgu[:, k, j * 256:(j + 1) * 256], k == 0, k == 7,
                            reads=["hT", "wgu"], writes=[bn])
                for k in range(8):
                    self.mm(bk[:, 256:512], hT[:, k, :], wgu[:, k, DFF + j * 256:DFF + (j + 1) * 256], k == 0, k == 7,
                            reads=["hT", "wgu"], writes=[bn])
                self.act(sl[j % 2][:], bk[:, 0:256], AF.Silu, reads=[bn], writes=["sl%d" % (j % 2)])
                self.tt("dve", actb[:, j * 256:(j + 1) * 256], sl[j % 2][:], bk[:, 256:512], ALU.mult,
                        reads=["sl%d" % (j % 2), bn], writes=[("actb", j)])
                o = (j % 4) * 256
                for u in range(2):
                    self.tr(qT2[:, o + u * 128:o + (u + 1) * 128], actb[:, j * 256 + u * 128:j * 256 + (u + 1) * 128], idb[:],
                            reads=[("actb", j), "idb"], writes=["q5"])
                self.cp("dve", actT[:, 2 * j:2 * j + 2, :].rearrange("p a b -> p (a b)"),
                        qT2[:, o:o + 256], reads=["q5"], writes=[("actT", j)])
            for hf in range(2):
                hs = slice(hf * 512, (hf + 1) * 512)
                for c in range(22):
                    self.mm(q[2 + hf][:], actT[:, c, :], wdn[:, c, hs], c == 0, c == 21,
                            reads=[("actT", c // 2), "wdn"], writes=["q%d" % (2 + hf)])
                self.tt("dve", x2t[:, hs], xt[b][:, hs], q[2 + hf][:], ALU.add, reads=[xn, "q%d" % (2 + hf)],
                        writes=["x2t"])
            if last:
                self.rmsnorm(x2t[:], "x2t", D, fbc[:], "fbc", x2t[:], "x2t", junk[:], ss[:], rstd[:], "Y")
                self.store(yout[t0:t0 + 128, :], x2t[:], reads=["x2t"], writes=[("y", i)])
            else:
                self.store(self.x2[t0:t0 + 128, :], x2t[:], reads=["x2t"], writes=[("x2", i)])
        self.end_phase()

    def build(self, phases=None):
        def on(p):
            return phases is None or p in phases
        for s in range(self.NSEQ):
            for l in range(self.depth):
                last = l == self.depth - 1
                xin = self.x[s] if l == 0 else self.x2
                if on("A"):
                    self.phase_A(l, xin)
                if on("R0"):
                    self.phase_R(l, 0)
                if on("R1"):
                    self.phase_R(l, 1)
                if on("MP"):
                    self.phase_MP(l)
                if on("MM"):
                    self.phase_MM()
                if on("C1"):
                    self.phase_C1(l, xin)
                if on("C2"):
                    self.phase_C2(l, last, self.y[s])
        self.S.emit()
        self.S.stack.close()
        return self.nc


def make_consts(S_LEN):
    s = np.arange(128)[:, None]
    t = np.arange(128)[None, :]
    tri = np.stack([(s <= t), (s >= t)]).astype(np.float32)
    strict = [(s < t).astype(np.float32), (s > t).astype(np.float32)]
    incl = [(s <= t).astype(np.float32), (s >= t).astype(np.float32)]
    m4 = np.stack([np.concatenate([strict[d], incl[d], strict[d], incl[d]], axis=1) for d in range(2)])
    mn = [(t < s).astype(np.float32), (t > s).astype(np.float32)]
    mn4 = np.stack([np.concatenate([mn[d]] * 4, axis=1) for d in range(2)])
    pos = np.arange(S_LEN, dtype=np.float32)
    inv_freq = (1.0 / (np.float32(10000.0) ** (np.arange(0, 32, 2, dtype=np.float32) / np.float32(32)))).astype(np.float32)
    ang = pos[:, None] * inv_freq[None, :]
    ang = np.concatenate([ang, ang], axis=-1).astype(np.float32)
    cos = np.cos(ang).astype(np.float32)
    sin = np.sin(ang).astype(np.float32)
    sin_s = sin.copy()
    sin_s[:, 0:16] = -sin_s[:, 0:16]
    return dict(c_ident=np.eye(128, dtype=np.float32), c_tri=tri, c_m4=m4.astype(np.float32),
                c_mn4=mn4.astype(np.float32), c_ones=np.ones((128, 128), np.float32),
                c_cos=cos, c_sin=sin_s)


_WNAMES = ["norm_mix_g", "w_in", "shift_mu", "decay_w2", "decay_w0", "iclr_a2", "iclr_a0", "gate_g2", "k_k", "k_a",
           "r_k", "gn_g", "gn_b", "w_oa", "q_norm_g", "w_uq", "kv_norm_g", "w_ukv", "w_ob", "w_out", "norm_ffn_g",
           "w_gu", "w_down", "final_norm_g"]


def prep_weights(inputs, depth):
    out = {}
    for n in _WNAMES:
        a = np.ascontiguousarray(np.asarray(inputs[n], dtype=np.float32))
        if n == "r_k":
            a = a.reshape(a.shape[0], 512)
        if n == "final_norm_g":
            a = a.reshape(1, D)
        else:
            a = a[:depth]
        out[n] = np.ascontiguousarray(a)
    return out


def kernel(**inputs):
    xp = np.asarray(inputs["x_prompt"], dtype=np.float32)
    xs = np.asarray(inputs["x_sample"], dtype=np.float32)
    S_LEN = xp.shape[1]
    x_all = np.concatenate([xp, xs], axis=0)
    nseq = x_all.shape[0] // NCORES
    wts = prep_weights(inputs, DEPTH)
    consts = make_consts(S_LEN)
    nc = Builder(S_LEN, nseq, DEPTH).build()
    in_maps = []
    for c in range(NCORES):
        m = dict(x=np.ascontiguousarray(x_all[c * nseq:(c + 1) * nseq]))
        m.update(wts)
        m.update(consts)
        in_maps.append(m)
    res = run_bass_kernel_spmd(nc, in_maps, core_ids=list(range(NCORES)))
    y = np.concatenate([r["y"] for r in res.results], axis=0)
    return (np.ascontiguousarray(y[:xp.shape[0]]), np.ascontiguousarray(y[xp.shape[0]:]))
```

```python
import contextlib
import os
import numpy as np
import concourse.bass as bass
import concourse.mybir as mybir
from concourse.alu_op_type import AluOpType as ALU
from concourse.bass_utils import run_bass_kernel_spmd

F32 = mybir.dt.float32
BF16 = mybir.dt.bfloat16
AF = mybir.ActivationFunctionType
AX = mybir.AxisListType

D = 1024
NIN = 4416
DFF = 2816
DEPTH = 2
NCORES = 8
SEQ_FULL = 4096
RMS_EPS = 1e-6
GN_EPS = 64e-5
CDEC = 0.6065306597126334
SCALE = 96.0 ** -0.5

ENGS = ("pe", "act", "dve", "pool", "sp")
N_DMA_SEMS = 8
SAME_ENGINE_SYNC = True


def _is_psum(r):
    n = r[0] if isinstance(r, tuple) else r
    return isinstance(n, str) and len(n) >= 2 and n[0] in "qp" and (n[1].isdigit() or n[1] in "TP")


class Sched:
    def __init__(self, nc):
        self.nc = nc
        self.q = {e: [] for e in ENGS}
        self.cnt = {e: 0 for e in ENGS}
        self.seen = {e: {} for e in ENGS}
        self.last_w = {}
        self.readers = {}
        self.dma_val = {}
        self.dma_rr = {e: 0 for e in ENGS}
        self.stack = contextlib.ExitStack()
        self.sems = {}
        self.nops = 0
        self.limit = int(os.environ.get("OPLIMIT", "1000000000"))
        self.marks = []

    def mark(self, label):
        self.marks.append((label, self.nops))

    def _deps(self, eng, reads, writes):
        deps = []
        for r in reads:
            ev = self.last_w.get(r)
            if ev is not None:
                deps.append(ev)
            if eng != "pe" and _is_psum(r):
                deps.extend(e2 for e2 in self.readers.get(r, ()) if e2[0] != eng)
        for w in writes:
            ev = self.last_w.get(w)
            if ev is not None:
                deps.append(ev)
            deps.extend(self.readers.get(w, ()))
        waits = {}
        seen = self.seen[eng]
        for sk, v in deps:
            if sk == eng and (eng == "pe" or not SAME_ENGINE_SYNC):
                continue
            if seen.get(sk, 0) >= v:
                continue
            if waits.get(sk, 0) < v:
                waits[sk] = v
        for sk, v in waits.items():
            seen[sk] = v
        return waits

    def _record(self, ev, reads, writes):
        for r in reads:
            self.readers.setdefault(r, []).append(ev)
        for w in writes:
            self.last_w[w] = ev
            self.readers[w] = []

    def op(self, eng, fn, reads=(), writes=()):
        self.nops += 1
        if self.nops > self.limit:
            return
        waits = self._deps(eng, reads, writes)
        self.cnt[eng] += 1
        ev = (eng, self.cnt[eng])
        self.q[eng].append((list(waits.items()), fn, (eng, 1)))
        self._record(ev, reads, writes)

    def dma(self, eng, fn, reads=(), writes=()):
        self.nops += 1
        if self.nops > self.limit:
            return
        k = self.dma_rr[eng]
        self.dma_rr[eng] = (k + 1) % N_DMA_SEMS
        sk = ("dma", eng, k)
        prev = self.dma_val.get(sk, 0)
        waits = self._deps(eng, reads, writes)
        if prev > 0 and self.seen[eng].get(sk, 0) < prev:
            waits[sk] = prev
            self.seen[eng][sk] = prev
        self.dma_val[sk] = prev + 16
        ev = (sk, prev + 16)
        self.q[eng].append((list(waits.items()), fn, (sk, 16)))
        self._record(ev, reads, writes)

    def barrier(self):
        tgt = {e: self.cnt[e] for e in ENGS if self.cnt[e] > 0}
        tgt.update(self.dma_val)
        for e in ENGS:
            waits = []
            for sk, v in tgt.items():
                if sk == e:
                    continue
                if self.seen[e].get(sk, 0) < v:
                    waits.append((sk, v))
                    self.seen[e][sk] = v
            if waits:
                self.q[e].append((waits, None, None))
        self.last_w = {}
        self.readers = {}

    def emit(self):
        nc = self.nc
        st = self.stack
        keys = list(ENGS) + list(self.dma_val)
        for sk in keys:
            nm = sk if isinstance(sk, str) else "d_%s_%d" % (sk[1], sk[2])
            self.sems[sk] = st.enter_context(nc.semaphore("s_" + nm))
        final = list(self.dma_val.items())
        block = st.enter_context(nc.Block())
        sems = self.sems

        def run(engname, final_waits=()):
            def body(e):
                for waits, fn, inc in self.q[engname]:
                    for sk, v in waits:
                        e.wait_ge(sems[sk], v)
                    if fn is not None:
                        fn(e).then_inc(sems[inc[0]], inc[1])
                for sk, v in final_waits:
                    e.wait_ge(sems[sk], v)
            return body

        block.tensor(run("pe"))
        block.scalar(run("act"))
        block.vector(run("dve"))
        block.gpsimd(run("pool"))
        block.sync(run("sp", final))


class Builder:
    def __init__(self, S_LEN, NSEQ, depth=DEPTH):
        self.S_LEN = S_LEN
        self.NSEQ = NSEQ
        self.depth = depth
        self.NT = S_LEN // 128
        nc = bass.Bass("TRN2", target_bir_lowering=False)
        self.nc = nc
        self.S = Sched(nc)
        self.ph = None
        self._uid = 0

        def inp(name, shape):
            return nc.dram_tensor(name, list(shape), F32, kind="ExternalInput").ap()

        L = depth
        self.x = inp("x", [NSEQ, S_LEN, D])
        self.w = dict(
            norm_mix_g=inp("norm_mix_g", [L, D]), w_in=inp("w_in", [L, D, NIN]),
            shift_mu=inp("shift_mu", [L, 2, 1952]), decay_w2=inp("decay_w2", [L, 2, 64, 512]),
            decay_w0=inp("decay_w0", [L, 2, 512]), iclr_a2=inp("iclr_a2", [L, 2, 64, 512]),
            iclr_a0=inp("iclr_a0", [L, 2, 512]), gate_g2=inp("gate_g2", [L, 160, 512]),
            k_k=inp("k_k", [L, 512]), k_a=inp("k_a", [L, 512]), r_k=inp("r_k", [L, 512]),
            gn_g=inp("gn_g", [L, 512]), gn_b=inp("gn_b", [L, 512]), w_oa=inp("w_oa", [L, 512, D]),
            q_norm_g=inp("q_norm_g", [L, 256]), w_uq=inp("w_uq", [L, 256, 768]),
            kv_norm_g=inp("kv_norm_g", [L, 128]), w_ukv=inp("w_ukv", [L, 128, 1024]),
            w_ob=inp("w_ob", [L, 512, D]), w_out=inp("w_out", [L, D, D]),
            norm_ffn_g=inp("norm_ffn_g", [L, D]), w_gu=inp("w_gu", [L, D, 2 * DFF]),
            w_down=inp("w_down", [L, DFF, D]), final_norm_g=inp("final_norm_g", [1, D]),
        )
        self.c_ident = inp("c_ident", [128, 128])
        self.c_tri = inp("c_tri", [2, 128, 128])
        self.c_m4 = inp("c_m4", [2, 128, 512])
        self.c_mn4 = inp("c_mn4", [2, 128, 512])
        self.c_ones = inp("c_ones", [128, 128])
        self.c_cos = inp("c_cos", [S_LEN, 32])
        self.c_sin = inp("c_sin", [S_LEN, 32])
        self.y = nc.dram_tensor("y", [NSEQ, S_LEN, D], F32, kind="ExternalOutput").ap()
        def scr(name, shape, dt=F32):
            return nc.dram_tensor(name, list(shape), dt).ap()
        self.P = scr("scr_P", [S_LEN, NIN])
        self.of = scr("scr_of", [S_LEN, 512])
        self.ya = scr("scr_ya", [S_LEN, 512])
        self.yb = scr("scr_yb", [S_LEN, 512])
        self.x1 = scr("scr_x1", [S_LEN, D])
        self.x2 = scr("scr_x2", [S_LEN, D])
        self.QT = scr("scr_QT", [8, 96, S_LEN], BF16)
        self.KT = scr("scr_KT", [8, 96, S_LEN], BF16)
        self.Vd = scr("scr_V", [S_LEN, 8 * 65], BF16)

    def begin_phase(self):
        self.ph = contextlib.ExitStack()

    def end_phase(self):
        self.S.barrier()
        self.ph.close()
        self.ph = None

    def sb(self, name, shape, dt):
        self._uid += 1
        return self.ph.enter_context(self.nc.sbuf_tensor("%s_%d" % (name, self._uid), list(shape), dt))

    def ps(self, name, shape, dt):
        self._uid += 1
        return self.ph.enter_context(self.nc.psum_tensor("%s_%d" % (name, self._uid), list(shape), dt))

    def load(self, out_ap, in_ap, writes, reads=(), cast=False):
        eng = "pool" if cast else "sp"
        self.S.dma(eng, lambda e: e.dma_start(out=out_ap, in_=in_ap), reads=reads, writes=writes)

    def store(self, out_ap, in_ap, reads, writes):
        self.S.dma("sp", lambda e: e.dma_start(out=out_ap, in_=in_ap), reads=reads, writes=writes)

    def bcast_load(self, tile, row_ap, width, name):
        self.load(tile[:], row_ap.broadcast_to([128, width]), writes=[name])

    def load_w_bf16(self, tile, w_ap, K, name):
        for k in range(K // 128):
            self.load(tile[:, k, :], w_ap[k * 128:(k + 1) * 128, :], writes=[name], cast=True)

    def mm(self, out, lhsT, rhs, start, stop, reads, writes):
        self.S.op("pe", lambda e: e.matmul(out=out, lhsT=lhsT, rhs=rhs, start=start, stop=stop),
                  reads=reads, writes=writes)

    def tr(self, out, in_, ident, reads, writes):
        self.S.op("pe", lambda e: e.transpose(out=out, in_=in_, identity=ident), reads=reads, writes=writes)

    def act(self, out, in_, func, reads, writes, scale=None, bias=None, accum_out=None):
        kw = {}
        if scale is not None:
            kw["scale"] = scale
        if bias is not None:
            kw["bias"] = bias
        if accum_out is not None:
            kw["accum_out"] = accum_out
        self.S.op("act", lambda e: e.activation(out=out, in_=in_, func=func, **kw), reads=reads, writes=writes)

    def tt(self, eng, out, in0, in1, op, reads, writes):
        self.S.op(eng, lambda e: e.tensor_tensor(out=out, in0=in0, in1=in1, op=op), reads=reads, writes=writes)

    def ts(self, out, in0, s1, s2, op0, op1, reads, writes, eng="dve"):
        self.S.op(eng, lambda e: e.tensor_scalar(out=out, in0=in0, scalar1=s1, scalar2=s2, op0=op0, op1=op1),
                  reads=reads, writes=writes)

    def stt(self, out, in0, scalar, in1, op0, op1, reads, writes):
        self.S.op("dve", lambda e: e.scalar_tensor_tensor(out=out, in0=in0, scalar=scalar, in1=in1, op0=op0, op1=op1),
                  reads=reads, writes=writes)

    def cp(self, eng, out, in_, reads, writes):
        if eng == "act":
            self.S.op("act", lambda e: e.activation(out=out, in_=in_, func=AF.Copy), reads=reads, writes=writes)
        else:
            self.S.op(eng, lambda e: e.tensor_copy(out=out, in_=in_), reads=reads, writes=writes)

    def red(self, out, in_, reads, writes):
        self.S.op("dve", lambda e: e.tensor_reduce(out=out, in_=in_, axis=AX.X, op=ALU.add), reads=reads, writes=writes)

    def recip(self, out, in_, reads, writes):
        self.S.op("dve", lambda e: e.reciprocal(out=out, in_=in_), reads=reads, writes=writes)

    def memset(self, eng, ap, val, writes):
        self.S.op(eng, lambda e: e.memset(ap, val), writes=writes)

    def rmsnorm(self, x_ap, xn, width, gbc_ap, gn, out_ap, outn, junk, ss, rstd, tag):
        jn, sn, rn = "junk" + tag, "ss" + tag, "rstd" + tag
        self.act(junk, x_ap, AF.Square, reads=[xn], writes=[jn, sn], accum_out=ss)
        self.ts(rstd, ss, 1.0 / width, RMS_EPS, ALU.mult, ALU.add, reads=[sn], writes=[rn])
        self.act(rstd, rstd, AF.Sqrt, reads=[rn], writes=[rn])
        self.recip(rstd, rstd, reads=[rn], writes=[rn])
        self.stt(out_ap, x_ap, rstd, gbc_ap, ALU.mult, ALU.mult, reads=[xn, rn, gn], writes=[outn])

    def phase_A(self, l, xin):
        NT = self.NT
        self.begin_phase()
        wA = self.sb("wA", [128, 8, NIN], BF16)
        gbc = self.sb("gA", [128, D], F32)
        idb = self.sb("idb", [128, 128], BF16)
        junk = self.sb("junk", [128, D], F32)
        ss = self.sb("ss", [128, 1], F32)
        rstd = self.sb("rstd", [128, 1], F32)
        xt = [self.sb("xt", [128, D], F32) for _ in range(2)]
        h = self.sb("h", [128, D], BF16)
        hT = self.sb("hT", [128, 8, 128], BF16)
        Pt = [self.sb("Pt", [128, NIN], F32) for _ in range(2)]
        pT = self.ps("pT", [128, 8, 128], BF16)
        pP = [self.ps("pP", [128, 512], F32) for _ in range(4)]
        self.load(idb[:], self.c_ident, writes=["idb"], cast=True)
        self.bcast_load(gbc, self.w["norm_mix_g"][l:l + 1, :], D, "gA")
        self.load_w_bf16(wA, self.w["w_in"][l], D, "wA")
        npieces = (NIN + 511) // 512
        for i in range(NT):
            b = i % 2
            xn = "xt%d" % b
            self.load(xt[b][:], xin[i * 128:(i + 1) * 128, :], writes=[xn])
            self.rmsnorm(xt[b][:], xn, D, gbc[:], "gA", h[:], "h", junk[:], ss[:], rstd[:], "A")
            for k in range(8):
                self.tr(pT[:, k, :], h[:, k * 128:(k + 1) * 128], idb[:], reads=["h", "idb"], writes=["pT"])
            self.cp("act", hT[:], pT[:], reads=["pT"], writes=["hT"])
            for j in range(npieces):
                n0 = j * 512
                n = min(512, NIN - n0)
                pp = pP[j % 4]
                pn = "pP%d" % (j % 4)
                for k in range(8):
                    self.mm(pp[:, 0:n], hT[:, k, :], wA[:, k, n0:n0 + n], k == 0, k == 7,
                            reads=["hT", "wA"], writes=[pn])
                self.cp("dve" if j % 2 == 0 else "act", Pt[b][:, n0:n0 + n], pp[:, 0:n],
                        reads=[pn], writes=[("Pt", b, j)])
            self.store(self.P[i * 128:(i + 1) * 128, :], Pt[b][:], reads=[("Pt", b, j) for j in range(npieces)],
                       writes=[("P", i)])
        self.end_phase()

    def phase_R(self, l, d):
        NT = self.NT
        S_LEN = self.S_LEN
        W = self.w
        self.begin_phase()
        sb, ps = self.sb, self.ps
        mu0 = sb("mu0", [128, 1952], F32)
        mu1 = sb("mu1", [128, 1952], F32)
        w0bc = sb("w0bc", [128, 512], F32)
        a0bc = sb("a0bc", [128, 512], F32)
        kkbc = sb("kkbc", [128, 512], F32)
        kabc = sb("kabc", [128, 512], F32)
        w2b = sb("w2b", [64, 512], BF16)
        a2b = sb("a2b", [64, 512], BF16)
        tri = sb("tri", [128, 128], F32)
        ones = sb("ones", [128, 128], F32)
        m4 = sb("m4", [128, 512], F32)
        mn4 = sb("mn4", [128, 512], F32)
        idb = sb("idb", [128, 128], BF16)
        pac = sb("pac", [128, 1952], F32)
        pap = sb("pap", [128, 1952], F32)
        pan = sb("pan", [128, 1952], F32)
        lo = sb("lo", [128, 128], BF16)
        loT = sb("loT", [64, 2, 128], BF16)
        sgm = sb("sgm", [128, 512], F32)
        av = sb("av", [128, 512], F32)
        kk = sb("kk", [128, 512], F32)
        tmp = sb("tmp", [128, 512], F32)
        kd = sb("kd", [128, 512], F32)
        ka = sb("ka", [128, 512], F32)
        Ls = sb("Ls", [128, 512], F32)
        Ld = sb("Ld", [128, 512], F32)
        E1 = sb("E1", [128, 512], F32)
        E2 = sb("E2", [128, 512], F32)
        E3 = sb("E3", [128, 512], F32)
        E4 = sb("E4", [128, 512], F32)
        ssq = sb("ssq", [128, 8], F32)
        gC = sb("gC", [64, 8], F32)
        Ab = sb("Ab", [128, 512], BF16)
        Rb = sb("Rb", [128, 512], BF16)
        Bb = sb("Bb", [128, 512], BF16)
        Kb = sb("Kb", [128, 512], BF16)
        Btb = sb("Btb", [128, 512], BF16)
        Ktb = sb("Ktb", [128, 512], BF16)
        Vb = sb("Vb", [128, 512], BF16)
        ART = sb("ART", [128, 8, 256], BF16)
        BT = sb("BT", [64, 8, 128], BF16)
        KTt = sb("KTt", [64, 8, 128], BF16)
        ATall = sb("ATall", [128, 8, 512], BF16)
        PP = [sb("PP", [128, 8, 256], BF16) for _ in range(2)]
        W32 = sb("W32", [128, 512], F32)
        Wb = sb("Wb", [128, 512], BF16)
        osb = sb("osb", [128, 512], F32)
        ST32 = sb("ST32", [64, 8, 64], F32)
        STb = sb("STb", [128, 8, 64], BF16)
        if d == 1:
            rkbc = sb("rkbc", [128, 512], F32)
            gngbc = sb("gngbc", [128, 512], F32)
            gnbbc = sb("gnbbc", [128, 512], F32)
            g2b = sb("g2b", [128, 2, 512], BF16)
            oft = sb("oft", [128, 512], F32)
            cen = sb("cen", [128, 512], F32)
            sq2 = sb("sq2", [128, 512], F32)
            bon = sb("bon", [128, 512], F32)
            st8 = sb("st8", [128, 8], F32)
            sv8 = sb("sv8", [128, 8], F32)
            sb8 = sb("sb8", [128, 8], F32)
            gs = sb("gs", [128, 160], BF16)
            gT = sb("gT", [128, 2, 128], BF16)
            yat = sb("yat", [128, 512], F32)
        q = [ps("q%d" % i, [128, 512], F32) for i in range(8)]
        q7 = q[7]
        qAT = q[3][:].bitcast(BF16)
        qRT = q[4][:].bitcast(BF16)
        qBT = q[5][:].bitcast(BF16)
        qKT = q[6][:].bitcast(BF16)
        q7b = q7[:].bitcast(BF16)

        self.load(idb[:], self.c_ident, writes=["idb"], cast=True)
        self.load(tri[:], self.c_tri[d], writes=["tri"])
        self.load(ones[:], self.c_ones, writes=["ones"])
        self.load(m4[:], self.c_m4[d], writes=["m4"])
        self.load(mn4[:], self.c_mn4[d], writes=["mn4"])
        self.bcast_load(mu0, W["shift_mu"][l, 0:1, :], 1952, "mu0")
        self.bcast_load(mu1, W["shift_mu"][l, 1:2, :], 1952, "mu1")
        self.bcast_load(w0bc, W["decay_w0"][l, d:d + 1, :], 512, "w0bc")
        self.bcast_load(a0bc, W["iclr_a0"][l, d:d + 1, :], 512, "a0bc")
        self.bcast_load(kkbc, W["k_k"][l:l + 1, :], 512, "kkbc")
        self.bcast_load(kabc, W["k_a"][l:l + 1, :], 512, "kabc")
        self.load(w2b[:], W["decay_w2"][l, d], writes=["w2b"], cast=True)
        self.load(a2b[:], W["iclr_a2"][l, d], writes=["a2b"], cast=True)
        if d == 1:
            self.bcast_load(rkbc, W["r_k"][l:l + 1, :], 512, "rkbc")
            self.bcast_load(gngbc, W["gn_g"][l:l + 1, :], 512, "gngbc")
            self.bcast_load(gnbbc, W["gn_b"][l:l + 1, :], 512, "gnbbc")
        self.memset("dve", ST32[:], 0.0, writes=["ST32"])
        self.memset("dve", STb[:], 0.0, writes=["STb"])
        self.memset("dve", ART[:], 0.0, writes=["ART"])
        if d == 1:
            self.memset("dve", gT[:], 0.0, writes=["gT"])
            self.memset("dve", g2b[:], 0.0, writes=["g2b"])
            self.load(g2b[:, 0, :], W["gate_g2"][l, 0:128, :], writes=["g2b"], cast=True)
            self.load(g2b[0:32, 1, :], W["gate_g2"][l, 128:160, :], writes=["g2b"], cast=True)

        def v3(t):
            return t[:].rearrange("p (h e) -> p h e", h=8)

        order = range(NT) if d == 0 else range(NT - 1, -1, -1)
        for i in order:
            t0 = i * 128
            self.S.mark("loads")
            self.load(pac[:], self.P[t0:t0 + 128, 2048:4000], writes=["pac"])
            if i == 0:
                self.memset("pool", pap[:], 0.0, writes=["pap"])
                self.load(pap[1:128, :], self.P[0:127, 2048:4000], writes=["pap"])
            else:
                self.load(pap[:], self.P[t0 - 1:t0 + 127, 2048:4000], writes=["pap"])
            if i == NT - 1:
                self.memset("pool", pan[:], 0.0, writes=["pan"])
                self.load(pan[0:127, :], self.P[t0 + 1:S_LEN, 2048:4000], writes=["pan"])
            else:
                self.load(pan[:], self.P[t0 + 1:t0 + 129, 2048:4000], writes=["pan"])
            self.tt("dve", pap[:], pap[:], pac[:], ALU.subtract, reads=["pap", "pac"], writes=["pap"])
            self.tt("pool", pap[:], pap[:], mu0[:], ALU.mult, reads=["pap", "mu0"], writes=["pap"])
            self.tt("dve", pan[:], pan[:], pac[:], ALU.subtract, reads=["pan", "pac"], writes=["pan"])
            self.tt("pool", pan[:], pan[:], mu1[:], ALU.mult, reads=["pan", "mu1"], writes=["pan"])
            self.tt("dve", pac[:], pac[:], pap[:], ALU.add, reads=["pac", "pap"], writes=["pac"])
            self.tt("dve", pac[:], pac[:], pan[:], ALU.add, reads=["pac", "pan"], writes=["pac"])
            r_ = pac[:, 0:512]
            k_ = pac[:, 512:1024]
            v_ = pac[:, 1024:1536]
            lw_ = pac[:, 1536 + 64 * d:1600 + 64 * d]
            la_ = pac[:, 1664 + 64 * d:1728 + 64 * d]
            lg_ = pac[:, 1792:1952]
            self.S.mark("lora")
            self.act(lo[:, 0:64], lw_, AF.Tanh, reads=["pac"], writes=["lo"])
            self.cp("dve", lo[:, 64:128], la_, reads=["pac"], writes=["lo"])
            self.tr(q7b[0:64, 0:128], lo[:, 0:64], idb[:], reads=["lo", "idb"], writes=["q7"])
            self.tr(q7b[0:64, 128:256], lo[:, 64:128], idb[:], reads=["lo", "idb"], writes=["q7"])
            self.cp("act", loT[:].rearrange("p a b -> p (a b)"), q7b[0:64, 0:256], reads=["q7"], writes=["loT"])
            self.mm(q[0][:], loT[:, 0, :], w2b[:], True, True, reads=["loT", "w2b"], writes=["q0"])
            self.mm(q[1][:], loT[:, 1, :], a2b[:], True, True, reads=["loT", "a2b"], writes=["q1"])
            self.tt("dve", sgm[:], q[0][:], w0bc[:], ALU.add, reads=["q0", "w0bc"], writes=["sgm"])
            self.act(sgm[:], sgm[:], AF.Sigmoid, reads=["sgm"], writes=["sgm"])
            self.tt("dve", av[:], q[1][:], a0bc[:], ALU.add, reads=["q1", "a0bc"], writes=["av"])
            self.act(av[:], av[:], AF.Sigmoid, reads=["av"], writes=["av"])
            self.S.mark("kk")
            self.tt("dve", kk[:], k_, kkbc[:], ALU.mult, reads=["pac", "kkbc"], writes=["kk"])
            self.tt("pool", tmp[:], kk[:], kk[:], ALU.mult, reads=["kk"], writes=["tmp"])
            self.red(ssq[:], v3(tmp), reads=["tmp"], writes=["ssq"])
            self.act(ssq[:], ssq[:], AF.Sqrt, reads=["ssq"], writes=["ssq"])
            self.ts(ssq[:], ssq[:], 1e-12, None, ALU.max, ALU.bypass, reads=["ssq"], writes=["ssq"])
            self.recip(ssq[:], ssq[:], reads=["ssq"], writes=["ssq"])
            self.tt("dve", v3(kk), v3(kk), ssq[:].unsqueeze(2).broadcast_to([128, 8, 64]), ALU.mult,
                    reads=["kk", "ssq"], writes=["kk"])
            self.stt(tmp[:], av[:], -1.0, kabc[:], ALU.add, ALU.mult, reads=["av", "kabc"], writes=["tmp"])
            self.stt(kd[:], tmp[:], 1.0, k_, ALU.add, ALU.mult, reads=["tmp", "pac"], writes=["kd"])
            self.tt("dve", ka[:], kk[:], av[:], ALU.mult, reads=["kk", "av"], writes=["ka"])
            self.S.mark("cum")
            self.mm(q[0][:], tri[:], sgm[:], True, True, reads=["tri", "sgm"], writes=["q0"])
            self.mm(q[1][:], ones[:], sgm[:], True, True, reads=["ones", "sgm"], writes=["q1"])
            for hh in range(8):
                self.mm(q[2][0:64, hh * 2:hh * 2 + 2], sgm[:, hh * 64:(hh + 1) * 64], ones[:, 0:2], True, True,
                        reads=["sgm", "ones"], writes=["q2"])
            self.act(gC[:], q[2][0:64, 0:16].rearrange("p (h t) -> p h t", t=2)[:, :, 0], AF.Exp,
                     reads=["q2"], writes=["gC"], scale=-CDEC)
            self.cp("act", Ls[:], q[0][:], reads=["q0"], writes=["Ls"])
            self.tt("dve", Ld[:], q[1][:], Ls[:], ALU.subtract, reads=["q1", "Ls"], writes=["Ld"])
            self.act(E2[:], q[0][:], AF.Exp, reads=["q0"], writes=["E2"], scale=-CDEC)
            self.act(E3[:], q[0][:], AF.Exp, reads=["q0"], writes=["E3"], scale=CDEC)
            self.tt("dve", Ls[:], Ls[:], sgm[:], ALU.subtract, reads=["Ls", "sgm"], writes=["Ls"])
            self.act(E1[:], Ls[:], AF.Exp, reads=["Ls"], writes=["E1"], scale=-CDEC)
            self.act(E4[:], Ld[:], AF.Exp, reads=["Ld"], writes=["E4"], scale=-CDEC)
            self.S.mark("scaled")
            self.stt(Ab[:], kk[:], -1.0, E1[:], ALU.mult, ALU.mult, reads=["kk", "E1"], writes=["Ab"])
            self.tt("dve", Rb[:], r_, E2[:], ALU.mult, reads=["pac", "E2"], writes=["Rb"])
            self.tt("pool", Bb[:], ka[:], E3[:], ALU.mult, reads=["ka", "E3"], writes=["Bb"])
            self.tt("pool", Kb[:], kd[:], E3[:], ALU.mult, reads=["kd", "E3"], writes=["Kb"])
            self.tt("pool", Btb[:], ka[:], E4[:], ALU.mult, reads=["ka", "E4"], writes=["Btb"])
            self.tt("dve", Ktb[:], kd[:], E4[:], ALU.mult, reads=["kd", "E4"], writes=["Ktb"])
            self.cp("pool", Vb[:], v_, reads=["pac"], writes=["Vb"])
            for hh in range(8):
                hs = slice(hh * 64, (hh + 1) * 64)
                ts_ = slice(hh * 128, (hh + 1) * 128)
                self.tr(qAT[0:64, ts_], Ab[:, hs], idb[:], reads=["Ab", "idb"], writes=["q3"])
                self.tr(qRT[0:64, ts_], Rb[:, hs], idb[:], reads=["Rb", "idb"], writes=["q4"])
                self.tr(qBT[0:64, ts_], Bb[:, hs], idb[:], reads=["Bb", "idb"], writes=["q5"])
                self.tr(qKT[0:64, ts_], Kb[:, hs], idb[:], reads=["Kb", "idb"], writes=["q6"])
            self.cp("act", ART[0:64, :, 0:128], qAT[0:64, :].rearrange("p (h t) -> p h t", h=8), reads=["q3"], writes=["ART"])
            self.cp("dve", ART[0:64, :, 128:256], qRT[0:64, :].rearrange("p (h t) -> p h t", h=8), reads=["q4"], writes=["ART"])
            self.cp("act", BT[:], qBT[0:64, :].rearrange("p (h t) -> p h t", h=8), reads=["q5"], writes=["BT"])
            self.cp("dve", KTt[:], qKT[0:64, :].rearrange("p (h t) -> p h t", h=8), reads=["q6"], writes=["KTt"])
            self.S.mark("A4")
            for hh in range(8):
                qq = q[hh % 2]
                qn = "q%d" % (hh % 2)
                self.mm(qq[:, 0:256], BT[:, hh, :], ART[0:64, hh, :], True, True, reads=["BT", "ART"], writes=[qn])
                self.mm(qq[:, 256:512], KTt[:, hh, :], ART[0:64, hh, :], True, True, reads=["KTt", "ART"], writes=[qn])
                self.tt("dve", ATall[:, hh, :], qq[:], m4[:], ALU.mult, reads=[qn, "m4"], writes=[("AT", hh)])
            self.S.mark("N")
            for g in range(2):
                for j in range(4):
                    hh = g * 4 + j
                    self.mm(q7[:, j * 128:(j + 1) * 128], ART[0:64, hh, 0:128], BT[:, hh, :], True, True,
                            reads=["ART", "BT"], writes=["q7"])
                self.tt("dve", PP[0][:, g * 4:(g + 1) * 4, 0:128], q7[:].rearrange("p (j s) -> p j s", j=4),
                        mn4[:].rearrange("p (j s) -> p j s", j=4), ALU.mult, reads=["q7", "mn4"],
                        writes=[("PP", 0, 2 * g), ("PP", 0, 2 * g + 1)])
            self.cp("pool", PP[0][:, :, 128:256], ATall[:, :, 0:128], reads=[("AT", hh) for hh in range(8)],
                    writes=[("PP", 0, pr) for pr in range(4)])
            self.S.mark("W")
            for hh in range(8):
                hs = slice(hh * 64, (hh + 1) * 64)
                self.mm(q[2][:, hs], ART[:, hh, 0:128], STb[:, hh, :], True, False, reads=["ART", "STb"], writes=["q2"])
                self.mm(q[2][:, hs], ATall[:, hh, 256:384], Vb[:, hs], False, True, reads=[("AT", hh), "Vb"], writes=["q2"])
            self.cp("act", W32[:], q[2][:], reads=["q2"], writes=["W32"])
            self.cp("dve", Wb[:], W32[:], reads=["W32"], writes=["Wb"])
            self.S.mark("neu")
            for j in range(7):
                cb = j % 2
                cur = PP[cb]
                for hh in range(8):
                    hs = slice(hh * 64, (hh + 1) * 64)
                    self.mm(q7[:, hs], cur[:, hh, 128:256], Wb[:, hs], True, True,
                            reads=[("PP", cb, hh // 2), "Wb"], writes=["q7"])
                self.tt("dve", W32[:], W32[:], q7[:], ALU.add, reads=["W32", "q7"], writes=["W32"])
                self.cp("act", Wb[:], W32[:], reads=["W32"], writes=["Wb"])
                if j < 6:
                    nxt = PP[1 - cb]
                    for pr in range(4):
                        bankt, bname = q[pr], "q%d" % pr
                        for u in range(2):
                            hh = pr * 2 + u
                            self.mm(bankt[:, u * 256:u * 256 + 128], cur[:, hh, 128:256], cur[:, hh, 0:128], True, True,
                                    reads=[("PP", cb, pr)], writes=[bname])
                            self.mm(bankt[:, u * 256 + 128:u * 256 + 256], cur[:, hh, 0:128], cur[:, hh, 128:256], True, True,
                                    reads=[("PP", cb, pr)], writes=[bname])
                        self.cp("dve" if pr % 2 == 0 else "act",
                                nxt[:, pr * 2:pr * 2 + 2, :].rearrange("p a b -> p (a b)"), bankt[:],
                                reads=[bname], writes=[("PP", 1 - cb, pr)])
            self.S.mark("O")
            for hh in range(8):
                hs = slice(hh * 64, (hh + 1) * 64)
                self.mm(q[4][:, hs], ART[:, hh, 128:256], STb[:, hh, :], True, False, reads=["ART", "STb"], writes=["q4"])
                self.mm(q[4][:, hs], ATall[:, hh, 128:256], Wb[:, hs], False, False, reads=[("AT", hh), "Wb"], writes=["q4"])
                self.mm(q[4][:, hs], ATall[:, hh, 384:512], Vb[:, hs], False, True, reads=[("AT", hh), "Vb"], writes=["q4"])
            self.cp("act", osb[:], q[4][:], reads=["q4"], writes=["osb"])
            self.S.mark("S")
            for hh in range(8):
                hs = slice(hh * 64, (hh + 1) * 64)
                self.mm(q[5][0:64, hs], Btb[:, hs], Wb[:, hs], True, False, reads=["Btb", "Wb"], writes=["q5"])
                self.mm(q[5][0:64, hs], Ktb[:, hs], Vb[:, hs], False, True, reads=["Ktb", "Vb"], writes=["q5"])
            self.tt("dve", ST32[:], ST32[:], gC[:].unsqueeze(2).broadcast_to([64, 8, 64]), ALU.mult,
                    reads=["ST32", "gC"], writes=["ST32"])
            self.tt("dve", ST32[:], ST32[:], q[5][0:64, :].rearrange("p (h e) -> p h e", h=8), ALU.add,
                    reads=["ST32", "q5"], writes=["ST32"])
            self.cp("act", STb[0:64, :, :], ST32[:], reads=["ST32"], writes=["STb"])
            if d == 0:
                self.store(self.of[t0:t0 + 128, :], osb[:], reads=["osb"], writes=[("of", i)])
                continue
            self.load(oft[:], self.of[t0:t0 + 128, :], writes=["oft"])
            self.tt("dve", oft[:], oft[:], osb[:], ALU.add, reads=["oft", "osb"], writes=["oft"])
            self.red(st8[:], v3(oft), reads=["oft"], writes=["st8"])
            self.ts(st8[:], st8[:], 1.0 / 64, None, ALU.mult, ALU.bypass, reads=["st8"], writes=["st8"])
            self.tt("dve", v3(cen), v3(oft), st8[:].unsqueeze(2).broadcast_to([128, 8, 64]), ALU.subtract,
                    reads=["oft", "st8"], writes=["cen"])
            self.tt("pool", sq2[:], cen[:], cen[:], ALU.mult, reads=["cen"], writes=["sq2"])
            self.red(sv8[:], v3(sq2), reads=["sq2"], writes=["sv8"])
            self.ts(sv8[:], sv8[:], 1.0 / 64, GN_EPS, ALU.mult, ALU.add, reads=["sv8"], writes=["sv8"])
            self.act(sv8[:], sv8[:], AF.Sqrt, reads=["sv8"], writes=["sv8"])
            self.recip(sv8[:], sv8[:], reads=["sv8"], writes=["sv8"])
            self.tt("dve", v3(cen), v3(cen), sv8[:].unsqueeze(2).broadcast_to([128, 8, 64]), ALU.mult,
                    reads=["cen", "sv8"], writes=["cen"])
            self.tt("pool", cen[:], cen[:], gngbc[:], ALU.mult, reads=["cen", "gngbc"], writes=["cen"])
            self.tt("pool", cen[:], cen[:], gnbbc[:], ALU.add, reads=["cen", "gnbbc"], writes=["cen"])
            self.tt("dve", sq2[:], r_, k_, ALU.mult, reads=["pac"], writes=["sq2"])
            self.tt("pool", sq2[:], sq2[:], rkbc[:], ALU.mult, reads=["sq2", "rkbc"], writes=["sq2"])
            self.red(sb8[:], v3(sq2), reads=["sq2"], writes=["sb8"])
            self.tt("dve", v3(bon), pac[:, 1024:1536].rearrange("p (h e) -> p h e", h=8),
                    sb8[:].unsqueeze(2).broadcast_to([128, 8, 64]), ALU.mult, reads=["pac", "sb8"], writes=["bon"])
            self.tt("dve", cen[:], cen[:], bon[:], ALU.add, reads=["cen", "bon"], writes=["cen"])
            self.act(gs[:], lg_, AF.Sigmoid, reads=["pac"], writes=["gs"])
            self.tr(q7b[:, 0:128], gs[:, 0:128], idb[:], reads=["gs", "idb"], writes=["q7"])
            self.tr(q7b[0:32, 128:256], gs[:, 128:160], idb[:], reads=["gs", "idb"], writes=["q7"])
            self.cp("act", gT[:, 0, :], q7b[:, 0:128], reads=["q7"], writes=["gT"])
            self.cp("act", gT[0:32, 1, :], q7b[0:32, 128:256], reads=["q7"], writes=["gT"])
            self.mm(q[2][:], gT[:, 0, :], g2b[:, 0, :], True, False, reads=["gT", "g2b"], writes=["q2"])
            self.mm(q[2][:], gT[:, 1, :], g2b[:, 1, :], False, True, reads=["gT", "g2b"], writes=["q2"])
            self.tt("dve", yat[:], cen[:], q[2][:], ALU.mult, reads=["cen", "q2"], writes=["yat"])
            self.store(self.ya[t0:t0 + 128, :], yat[:], reads=["yat"], writes=[("ya", i)])
        self.end_phase()

    def phase_MP(self, l):
        NT = self.NT
        W = self.w
        self.begin_phase()
        sb, ps = self.sb, self.ps
        qg = sb("qg", [128, 256], F32)
        kvg = sb("kvg", [128, 128], F32)
        wuq = sb("wuq", [128, 2, 768], BF16)
        wukv = sb("wukv", [128, 1, 1024], BF16)
        idb = sb("idb", [128, 128], BF16)
        pm = sb("pm", [128, 416], F32)
        cs = sb("cs", [128, 32], F32)
        sn = sb("sn", [128, 32], F32)
        junk = sb("junk", [128, 256], F32)
        ss = sb("ss", [128, 1], F32)
        rstd = sb("rstd", [128, 1], F32)
        nb = sb("nb", [128, 384], BF16)
        nT = sb("nT", [128, 3, 128], BF16)
        qf = sb("qf", [128, 768], F32)
        kvf = sb("kvf", [128, 1024], F32)
        t1 = sb("t1", [128, 8, 32], F32)
        t2 = sb("t2", [128, 8, 32], F32)
        kro = sb("kro", [128, 32], F32)
        kr2 = sb("kr2", [128, 32], F32)
        Qa = sb("Qa", [128, 8, 96], BF16)
        Ka = sb("Ka", [128, 8, 96], BF16)
        Va = sb("Va", [128, 8, 65], BF16)
        QTt = sb("QTt", [96, 8, 128], BF16)
        KTt = sb("KTt", [96, 8, 128], BF16)
        q = [ps("q%d" % i, [128, 512], F32) for i in range(7)]
        qb = [t[:].bitcast(BF16) for t in q]
        self.load(idb[:], self.c_ident, writes=["idb"], cast=True)
        self.bcast_load(qg, W["q_norm_g"][l:l + 1, :], 256, "qg")
        self.bcast_load(kvg, W["kv_norm_g"][l:l + 1, :], 128, "kvg")
        self.load_w_bf16(wuq, W["w_uq"][l], 256, "wuq")
        self.load_w_bf16(wukv, W["w_ukv"][l], 128, "wukv")
        self.memset("dve", Va[:], 1.0, writes=["Va"])
        qf3 = qf[:].rearrange("p (h e) -> p h e", h=8)
        kvf3 = kvf[:].rearrange("p (h e) -> p h e", h=8)
        for i in range(NT):
            t0 = i * 128
            self.load(pm[:], self.P[t0:t0 + 128, 4000:4416], writes=["pm"])
            self.load(cs[:], self.c_cos[t0:t0 + 128, :], writes=["cs"])
            self.load(sn[:], self.c_sin[t0:t0 + 128, :], writes=["sn"])
            self.rmsnorm(pm[:, 0:256], "pm", 256, qg[:], "qg", nb[:, 0:256], "nbq", junk[:, 0:256], ss[:], rstd[:], "M")
            self.rmsnorm(pm[:, 256:384], "pm", 128, kvg[:], "kvg", nb[:, 256:384], "nbk", junk[:, 0:128], ss[:], rstd[:], "M")
            for c in range(3):
                self.tr(qb[0][:, c * 128:(c + 1) * 128], nb[:, c * 128:(c + 1) * 128], idb[:],
                        reads=["nbq", "nbk", "idb"], writes=["q0"])
            self.cp("act", nT[:].rearrange("p a b -> p (a b)"), qb[0][:, 0:384], reads=["q0"], writes=["nT"])
            for c in range(2):
                self.mm(q[1][:], nT[:, c, :], wuq[:, c, 0:512], c == 0, c == 1, reads=["nT", "wuq"], writes=["q1"])
            for c in range(2):
                self.mm(q[2][:, 0:256], nT[:, c, :], wuq[:, c, 512:768], c == 0, c == 1, reads=["nT", "wuq"], writes=["q2"])
            self.mm(q[3][:], nT[:, 2, :], wukv[:, 0, 0:512], True, True, reads=["nT", "wukv"], writes=["q3"])
            self.mm(q[4][:], nT[:, 2, :], wukv[:, 0, 512:1024], True, True, reads=["nT", "wukv"], writes=["q4"])
            self.cp("act", qf[:, 0:512], q[1][:], reads=["q1"], writes=["qf"])
            self.cp("dve", qf[:, 512:768], q[2][:, 0:256], reads=["q2"], writes=["qf"])
            self.cp("act", kvf[:, 0:512], q[3][:], reads=["q3"], writes=["kvf"])
            self.cp("dve", kvf[:, 512:1024], q[4][:], reads=["q4"], writes=["kvf"])
            self.cp("pool", Qa[:, :, 0:64], qf3[:, :, 0:64], reads=["qf"], writes=["Qa"])
            csb = cs[:].unsqueeze(1).broadcast_to([128, 8, 32])
            self.tt("dve", t1[:], qf3[:, :, 64:96], csb, ALU.mult, reads=["qf", "cs"], writes=["t1"])
            self.tt("dve", t2[:, :, 0:16], qf3[:, :, 80:96], sn[:, 0:16].unsqueeze(1).broadcast_to([128, 8, 16]), ALU.mult,
                    reads=["qf", "sn"], writes=["t2"])
            self.tt("dve", t2[:, :, 16:32], qf3[:, :, 64:80], sn[:, 16:32].unsqueeze(1).broadcast_to([128, 8, 16]), ALU.mult,
                    reads=["qf", "sn"], writes=["t2"])
            self.tt("dve", Qa[:, :, 64:96], t1[:], t2[:], ALU.add, reads=["t1", "t2"], writes=["Qa"])
            self.tt("dve", kro[:], pm[:, 384:416], cs[:], ALU.mult, reads=["pm", "cs"], writes=["kro"])
            self.tt("dve", kr2[:, 0:16], pm[:, 400:416], sn[:, 0:16], ALU.mult, reads=["pm", "sn"], writes=["kr2"])
            self.tt("dve", kr2[:, 16:32], pm[:, 384:400], sn[:, 16:32], ALU.mult, reads=["pm", "sn"], writes=["kr2"])
            self.tt("dve", kro[:], kro[:], kr2[:], ALU.add, reads=["kro", "kr2"], writes=["kro"])
            self.cp("dve", Ka[:, :, 64:96], kro[:].unsqueeze(1).broadcast_to([128, 8, 32]), reads=["kro"], writes=["Ka"])
            self.cp("pool", Ka[:, :, 0:64], kvf3[:, :, 0:64], reads=["kvf"], writes=["Ka"])
            self.cp("pool", Va[:, :, 0:64], kvf3[:, :, 64:128], reads=["kvf"], writes=["Va"])
            for hh in range(8):
                self.tr(qb[5][0:96, hh * 128:(hh + 1) * 128], Qa[:, hh, :], idb[:], reads=["Qa", "idb"], writes=["q5"])
                self.tr(qb[6][0:96, hh * 128:(hh + 1) * 128], Ka[:, hh, :], idb[:], reads=["Ka", "idb"], writes=["q6"])
            self.cp("act", QTt[:].rearrange("p a b -> p (a b)"), qb[5][0:96, :], reads=["q5"], writes=["QTt"])
            self.cp("dve", KTt[:].rearrange("p a b -> p (a b)"), qb[6][0:96, :], reads=["q6"], writes=["KTt"])
            self.store(self.QT[:, :, t0:t0 + 128].rearrange("h p t -> p h t"), QTt[:], reads=["QTt"], writes=[("QT", i)])
            self.store(self.KT[:, :, t0:t0 + 128].rearrange("h p t -> p h t"), KTt[:], reads=["KTt"], writes=[("KT", i)])
            self.store(self.Vd[t0:t0 + 128, :], Va[:].rearrange("p a b -> p (a b)"), reads=["Va"], writes=[("Vd", i)])
        self.end_phase()

    def phase_MM(self):
        NT = self.NT
        S_LEN = self.S_LEN
        QB = min(512, S_LEN)
        nqb = S_LEN // QB
        nj = QB // 128
        LOOK = 2
        self.begin_phase()
        sb, ps = self.sb, self.ps
        Vall = sb("Vall", [128, NT, 520], BF16)
        KTh = [sb("KTh", [96, S_LEN], BF16) for _ in range(2)]
        QTb = [sb("QTb", [96, QB], BF16) for _ in range(2)]
        PT = [sb("PT", [128, QB], BF16) for _ in range(4)]
        OT = sb("OT", [65, QB], F32)
        id32 = sb("id32", [128, 128], F32)
        osm = sb("osm", [128, nj, 64], F32)
        rec = sb("rec", [128, nj], F32)
        q = [ps("q%d" % i, [128, 512], F32) for i in range(7)]
        self.load(id32[:], self.c_ident, writes=["id32"])
        self.load(Vall[:], self.Vd.rearrange("(c p) f -> p c f", p=128), writes=["Vall"])
        blocks = [(hh, qi) for hh in range(8) for qi in range(nqb)]
        stream = [(bi, kc) for bi in range(len(blocks)) for kc in range(NT)]

        def load_k(hh):
            self.load(KTh[hh % 2][:], self.KT[hh], writes=["KTh%d" % (hh % 2)])

        def load_q(bi):
            hh, qi = blocks[bi]
            self.load(QTb[bi % 2][:], self.QT[hh, :, qi * QB:(qi + 1) * QB], writes=["QTb%d" % (bi % 2)])

        def emit_S(idx):
            bi, kc = stream[idx]
            hh, qi = blocks[bi]
            pb = idx % 4
            self.mm(q[pb][:, 0:QB], KTh[hh % 2][:, kc * 128:(kc + 1) * 128], QTb[bi % 2][:], True, True,
                    reads=["KTh%d" % (hh % 2), "QTb%d" % (bi % 2)], writes=["q%d" % pb])

        def epilogue_a(bi):
            ob = 4 + bi % 2
            self.cp("dve", OT[:], q[ob][0:65, 0:QB], reads=["q%d" % ob], writes=["OT"])

        def epilogue_b(bi):
            hh, qi = blocks[bi]
            for j in range(nj):
                self.tr(q[6][:, j * 65:(j + 1) * 65], OT[:, j * 128:(j + 1) * 128], id32[0:65, 0:65],
                        reads=["OT", "id32"], writes=["q6"])
            o3 = q[6][:, 0:nj * 65].rearrange("p (j e) -> p j e", j=nj)
            self.recip(rec[:], o3[:, :, 64], reads=["q6"], writes=["rec"])
            self.tt("dve", osm[:], o3[:, :, 0:64], rec[:].unsqueeze(2).broadcast_to([128, nj, 64]), ALU.mult,
                    reads=["q6", "rec"], writes=["osm"])
            self.store(self.yb[qi * QB:(qi + 1) * QB, hh * 64:(hh + 1) * 64].rearrange("(j p) e -> p j e", p=128),
                       osm[:], reads=["osm"], writes=[("yb", hh, qi)])

        load_k(0)
        load_q(0)
        if len(blocks) > 1:
            load_q(1)
        for idx in range(min(LOOK, len(stream))):
            emit_S(idx)
        pending = None
        for idx, (bi, kc) in enumerate(stream):
            hh, qi = blocks[bi]
            if kc == 0:
                if qi == 0 and hh + 1 < 8:
                    load_k(hh + 1)
            if idx + LOOK < len(stream):
                emit_S(idx + LOOK)
            pb = idx % 4
            ob = 4 + bi % 2
            self.act(PT[pb][:], q[pb][:, 0:QB], AF.Exp, reads=["q%d" % pb], writes=["PT%d" % pb], scale=SCALE)
            self.mm(q[ob][0:65, 0:QB], Vall[:, kc, hh * 65:(hh + 1) * 65], PT[pb][:], kc == 0, kc == NT - 1,
                    reads=["Vall", "PT%d" % pb], writes=["q%d" % ob])
            if pending is not None and kc == min(3, NT - 1):
                epilogue_b(pending)
                pending = None
            if kc == NT - 1:
                epilogue_a(bi)
                pending = bi
                if bi + 2 < len(blocks):
                    load_q(bi + 2)
        if pending is not None:
            epilogue_b(pending)
        self.end_phase()

    def phase_C1(self, l, xin):
        NT = self.NT
        W = self.w
        self.begin_phase()
        sb, ps = self.sb, self.ps
        woa = sb("woa", [128, 4, D], BF16)
        wob = sb("wob", [128, 4, D], BF16)
        wout = sb("wout", [128, 8, D], BF16)
        idb = sb("idb", [128, 128], BF16)
        xt = [sb("xt", [128, D], F32) for _ in range(2)]
        gt = [sb("gt", [128, 2048], F32) for _ in range(2)]
        yat = [sb("yat", [128, 512], F32) for _ in range(2)]
        ybt = [sb("ybt", [128, 512], F32) for _ in range(2)]
        yab = sb("yab", [128, D], BF16)
        yT = sb("yT", [128, 8, 128], BF16)
        m1 = sb("m1", [128, D], F32)
        m2 = sb("m2", [128, D], F32)
        mixb = sb("mixb", [128, D], BF16)
        mixT = sb("mixT", [128, 8, 128], BF16)
        x1t = sb("x1t", [128, D], F32)
        q = [ps("q%d" % i, [128, 512], F32) for i in range(7)]
        qTb = q[6][:].bitcast(BF16)
        self.load(idb[:], self.c_ident, writes=["idb"], cast=True)
        self.load_w_bf16(woa, W["w_oa"][l], 512, "woa")
        self.load_w_bf16(wob, W["w_ob"][l], 512, "wob")
        self.load_w_bf16(wout, W["w_out"][l], D, "wout")
        for i in range(NT):
            b = i % 2
            t0 = i * 128
            self.load(xt[b][:], xin[t0:t0 + 128, :], writes=["xt%d" % b])
            self.load(gt[b][:], self.P[t0:t0 + 128, 0:2048], writes=["gt%d" % b])
            self.load(yat[b][:], self.ya[t0:t0 + 128, :], writes=["yat%d" % b])
            self.load(ybt[b][:], self.yb[t0:t0 + 128, :], writes=["ybt%d" % b])
            self.cp("dve", yab[:, 0:512], yat[b][:], reads=["yat%d" % b], writes=["yab0"])
            self.cp("pool", yab[:, 512:1024], ybt[b][:], reads=["ybt%d" % b], writes=["yab1"])
            for k in range(8):
                self.tr(qTb[:, k * 128:(k + 1) * 128], yab[:, k * 128:(k + 1) * 128], idb[:],
                        reads=["yab0", "yab1", "idb"], writes=["q6"])
            self.cp("act", yT[:].rearrange("p a b -> p (a b)"), qTb[:], reads=["q6"], writes=["yT"])
            for hf in range(2):
                hs = slice(hf * 512, (hf + 1) * 512)
                for k in range(4):
                    self.mm(q[hf][:], yT[:, k, :], woa[:, k, hs], k == 0, k == 3, reads=["yT", "woa"], writes=["q%d" % hf])
                for k in range(4):
                    self.mm(q[2 + hf][:], yT[:, 4 + k, :], wob[:, k, hs], k == 0, k == 3, reads=["yT", "wob"],
                            writes=["q%d" % (2 + hf)])
            self.act(gt[b][:], gt[b][:], AF.Sigmoid, reads=["gt%d" % b], writes=["gt%d" % b])
            for hf in range(2):
                hs = slice(hf * 512, (hf + 1) * 512)
                hs2 = slice(1024 + hf * 512, 1024 + (hf + 1) * 512)
                self.tt("dve", m1[:, hs], gt[b][:, hs], q[hf][:], ALU.mult, reads=["gt%d" % b, "q%d" % hf], writes=[("m1", hf)])
                self.tt("dve", m2[:, hs], gt[b][:, hs2], q[2 + hf][:], ALU.mult, reads=["gt%d" % b, "q%d" % (2 + hf)],
                        writes=[("m2", hf)])
                self.tt("pool", mixb[:, hs], m1[:, hs], m2[:, hs], ALU.add, reads=[("m1", hf), ("m2", hf)], writes=[("mixb", hf)])
            for k in range(8):
                self.tr(qTb[:, k * 128:(k + 1) * 128], mixb[:, k * 128:(k + 1) * 128], idb[:],
                        reads=[("mixb", 0), ("mixb", 1), "idb"], writes=["q6"])
            self.cp("act", mixT[:].rearrange("p a b -> p (a b)"), qTb[:], reads=["q6"], writes=["mixT"])
            for hf in range(2):
                hs = slice(hf * 512, (hf + 1) * 512)
                for k in range(8):
                    self.mm(q[4 + hf][:], mixT[:, k, :], wout[:, k, hs], k == 0, k == 7, reads=["mixT", "wout"],
                            writes=["q%d" % (4 + hf)])
                self.tt("dve", x1t[:, hs], xt[b][:, hs], q[4 + hf][:], ALU.add, reads=["xt%d" % b, "q%d" % (4 + hf)],
                        writes=[("x1t", hf)])
            self.store(self.x1[t0:t0 + 128, :], x1t[:], reads=[("x1t", 0), ("x1t", 1)], writes=[("x1", i)])
        self.end_phase()

    def phase_C2(self, l, last, yout):
        NT = self.NT
        W = self.w
        self.begin_phase()
        sb, ps = self.sb, self.ps
        wgu = sb("wgu", [128, 8, 2 * DFF], BF16)
        wdn = sb("wdn", [128, 22, D], BF16)
        gbc = sb("gbc", [128, D], F32)
        idb = sb("idb", [128, 128], BF16)
        xt = [sb("xt", [128, D], F32) for _ in range(2)]
        junk = sb("junk", [128, D], F32)
        ss = sb("ss", [128, 1], F32)
        rstd = sb("rstd", [128, 1], F32)
        h = sb("h", [128, D], BF16)
        hT = sb("hT", [128, 8, 128], BF16)
        sl = [sb("sl", [128, 256], F32) for _ in range(2)]
        actb = sb("actb", [128, DFF], BF16)
        actT = sb("actT", [128, 22, 128], BF16)
        x2t = sb("x2t", [128, D], F32)
        if last:
            fbc = sb("fbc", [128, D], F32)
        q = [ps("q%d" % i, [128, 512], F32) for i in range(6)]
        qTb = q[4][:].bitcast(BF16)
        qT2 = q[5][:].bitcast(BF16)
        self.load(idb[:], self.c_ident, writes=["idb"], cast=True)
        self.bcast_load(gbc, W["norm_ffn_g"][l:l + 1, :], D, "gbc")
        if last:
            self.bcast_load(fbc, W["final_norm_g"][0:1, :], D, "fbc")
        self.load_w_bf16(wgu, W["w_gu"][l], D, "wgu")
        self.load_w_bf16(wdn, W["w_down"][l], DFF, "wdn")
        for i in range(NT):
            b = i % 2
            t0 = i * 128
            xn = "xt%d" % b
            self.load(xt[b][:], self.x1[t0:t0 + 128, :], writes=[xn])
            self.rmsnorm(xt[b][:], xn, D, gbc[:], "gbc", h[:], "h", junk[:], ss[:], rstd[:], "F")
            for k in range(8):
                self.tr(qTb[:, k * 128:(k + 1) * 128], h[:, k * 128:(k + 1) * 128], idb[:], reads=["h", "idb"], writes=["q4"])
            self.cp("act", hT[:].rearrange("p a b -> p (a b)"), qTb[:], reads=["q4"], writes=["hT"])
            for j in range(11):
                bk = q[j % 2]
                bn = "q%d" % (j % 2)
                for k in range(8):
                    self.mm(bk[:, 0:256], hT[:, k, :], wgu[:, k, j * 256:(j + 1) * 256], k == 0, k == 7,
                            reads=["hT", "wgu"], writes=[bn])
                for k in range(8):
                    self.mm(bk[:, 256:512], hT[:, k, :], wgu[:, k, DFF + j * 256:DFF + (j + 1) * 256], k == 0, k == 7,
                            reads=["hT", "wgu"], writes=[bn])
                self.act(sl[j % 2][:], bk[:, 0:256], AF.Silu, reads=[bn], writes=["sl%d" % (j % 2)])
                self.tt("dve", actb[:, j * 256:(j + 1) * 256], sl[j % 2][:], bk[:, 256:512], ALU.mult,
                        reads=["sl%d" % (j % 2), bn], writes=[("actb", j)])
                o = (j % 4) * 256
                for u in range(2):
                    self.tr(qT2[:, o + u * 128:o + (u + 1) * 128], actb[:, j * 256 + u * 128:j * 256 + (u + 1) * 128], idb[:],
                            reads=[("actb", j), "idb"], writes=["q5"])
                self.cp("dve", actT[:, 2 * j:2 * j + 2, :].rearrange("p a b -> p (a b)"),
                        qT2[:, o:o + 256], reads=["q5"], writes=[("actT", j)])
            for hf in range(2):
                hs = slice(hf * 512, (hf + 1) * 512)
                for c in range(22):
                    self.mm(q[2 + hf][:], actT[:, c, :], wdn[:, c, hs], c == 0, c == 21,
                            reads=[("actT", c // 2), "wdn"], writes=["q%d" % (2 + hf)])
                self.tt("dve", x2t[:, hs], xt[b][:, hs], q[2 + hf][:], ALU.add, reads=[xn, "q%d" % (2 + hf)],
                        writes=["x2t"])
            if last:
                self.rmsnorm(x2t[:], "x2t", D, fbc[:], "fbc", x2t[:], "x2t", junk[:], ss[:], rstd[:], "Y")
                self.store(yout[t0:t0 + 128, :], x2t[:], reads=["x2t"], writes=[("y", i)])
            else:
                self.store(self.x2[t0:t0 + 128, :], x2t[:], reads=["x2t"], writes=[("x2", i)])
        self.end_phase()

    def build(self, phases=None):
        def on(p):
            return phases is None or p in phases
        for s in range(self.NSEQ):
            for l in range(self.depth):
                last = l == self.depth - 1
                xin = self.x[s] if l == 0 else self.x2
                if on("A"):
                    self.phase_A(l, xin)
                if on("R0"):
                    self.phase_R(l, 0)
                if on("R1"):
                    self.phase_R(l, 1)
                if on("MP"):
                    self.phase_MP(l)
                if on("MM"):
                    self.phase_MM()
                if on("C1"):
                    self.phase_C1(l, xin)
                if on("C2"):
                    self.phase_C2(l, last, self.y[s])
        self.S.emit()
        self.S.stack.close()
        return self.nc


def make_consts(S_LEN):
    s = np.arange(128)[:, None]
    t = np.arange(128)[None, :]
    tri = np.stack([(s <= t), (s >= t)]).astype(np.float32)
    strict = [(s < t).astype(np.float32), (s > t).astype(np.float32)]
    incl = [(s <= t).astype(np.float32), (s >= t).astype(np.float32)]
    m4 = np.stack([np.concatenate([strict[d], incl[d], strict[d], incl[d]], axis=1) for d in range(2)])
    mn = [(t < s).astype(np.float32), (t > s).astype(np.float32)]
    mn4 = np.stack([np.concatenate([mn[d]] * 4, axis=1) for d in range(2)])
    pos = np.arange(S_LEN, dtype=np.float32)
    inv_freq = (1.0 / (np.float32(10000.0) ** (np.arange(0, 32, 2, dtype=np.float32) / np.float32(32)))).astype(np.float32)
    ang = pos[:, None] * inv_freq[None, :]
    ang = np.concatenate([ang, ang], axis=-1).astype(np.float32)
    cos = np.cos(ang).astype(np.float32)
    sin = np.sin(ang).astype(np.float32)
    sin_s = sin.copy()
    sin_s[:, 0:16] = -sin_s[:, 0:16]
    return dict(c_ident=np.eye(128, dtype=np.float32), c_tri=tri, c_m4=m4.astype(np.float32),
                c_mn4=mn4.astype(np.float32), c_ones=np.ones((128, 128), np.float32),
                c_cos=cos, c_sin=sin_s)


_WNAMES = ["norm_mix_g", "w_in", "shift_mu", "decay_w2", "decay_w0", "iclr_a2", "iclr_a0", "gate_g2", "k_k", "k_a",
           "r_k", "gn_g", "gn_b", "w_oa", "q_norm_g", "w_uq", "kv_norm_g", "w_ukv", "w_ob", "w_out", "norm_ffn_g",
           "w_gu", "w_down", "final_norm_g"]


def prep_weights(inputs, depth):
    out = {}
    for n in _WNAMES:
        a = np.ascontiguousarray(np.asarray(inputs[n], dtype=np.float32))
        if n == "r_k":
            a = a.reshape(a.shape[0], 512)
        if n == "final_norm_g":
            a = a.reshape(1, D)
        else:
            a = a[:depth]
        out[n] = np.ascontiguousarray(a)
    return out


def kernel(**inputs):
    xp = np.asarray(inputs["x_prompt"], dtype=np.float32)
    xs = np.asarray(inputs["x_sample"], dtype=np.float32)
    S_LEN = xp.shape[1]
    x_all = np.concatenate([xp, xs], axis=0)
    nseq = x_all.shape[0] // NCORES
    wts = prep_weights(inputs, DEPTH)
    consts = make_consts(S_LEN)
    nc = Builder(S_LEN, nseq, DEPTH).build()
    in_maps = []
    for c in range(NCORES):
        m = dict(x=np.ascontiguousarray(x_all[c * nseq:(c + 1) * nseq]))
        m.update(wts)
        m.update(consts)
        in_maps.append(m)
    res = run_bass_kernel_spmd(nc, in_maps, core_ids=list(range(NCORES)))
    y = np.concatenate([r["y"] for r in res.results], axis=0)
    return (np.ascontiguousarray(y[:xp.shape[0]]), np.ascontiguousarray(y[xp.shape[0]:]))
```

```python
import contextlib
import os
import numpy as np
import concourse.bass as bass
import concourse.mybir as mybir
from concourse.alu_op_type import AluOpType as ALU
from concourse.bass_utils import run_bass_kernel_spmd

F32 = mybir.dt.float32
BF16 = mybir.dt.bfloat16
AF = mybir.ActivationFunctionType
AX = mybir.AxisListType

D = 1024
NIN = 4416
DFF = 2816
DEPTH = 2
NCORES = 8
SEQ_FULL = 4096
RMS_EPS = 1e-6
GN_EPS = 64e-5
CDEC = 0.6065306597126334
SCALE = 96.0 ** -0.5

ENGS = ("pe", "act", "dve", "pool", "sp")
N_DMA_SEMS = 8
SAME_ENGINE_SYNC = True


def _is_psum(r):
    n = r[0] if isinstance(r, tuple) else r
    return isinstance(n, str) and len(n) >= 2 and n[0] in "qp" and (n[1].isdigit() or n[1] in "TP")


class Sched:
    def __init__(self, nc):
        self.nc = nc
        self.q = {e: [] for e in ENGS}
        self.cnt = {e: 0 for e in ENGS}
        self.seen = {e: {} for e in ENGS}
        self.last_w = {}
        self.readers = {}
        self.dma_val = {}
        self.dma_rr = {e: 0 for e in ENGS}
        self.stack = contextlib.ExitStack()
        self.sems = {}
        self.nops = 0
        self.limit = int(os.environ.get("OPLIMIT", "1000000000"))
        self.marks = []

    def mark(self, label):
        self.marks.append((label, self.nops))

    def _deps(self, eng, reads, writes):
        deps = []
        for r in reads:
            ev = self.last_w.get(r)
            if ev is not None:
                deps.append(ev)
            if eng != "pe" and _is_psum(r):
                deps.extend(e2 for e2 in self.readers.get(r, ()) if e2[0] != eng)
        for w in writes:
            ev = self.last_w.get(w)
            if ev is not None:
                deps.append(ev)
            deps.extend(self.readers.get(w, ()))
        waits = {}
        seen = self.seen[eng]
        for sk, v in deps:
            if sk == eng and (eng == "pe" or not SAME_ENGINE_SYNC):
                continue
            if seen.get(sk, 0) >= v:
                continue
            if waits.get(sk, 0) < v:
                waits[sk] = v
        for sk, v in waits.items():
            seen[sk] = v
        return waits

    def _record(self, ev, reads, writes):
        for r in reads:
            self.readers.setdefault(r, []).append(ev)
        for w in writes:
            self.last_w[w] = ev
            self.readers[w] = []

    def op(self, eng, fn, reads=(), writes=()):
        self.nops += 1
        if self.nops > self.limit:
            return
        waits = self._deps(eng, reads, writes)
        self.cnt[eng] += 1
        ev = (eng, self.cnt[eng])
        self.q[eng].append((list(waits.items()), fn, (eng, 1)))
        self._record(ev, reads, writes)

    def dma(self, eng, fn, reads=(), writes=()):
        self.nops += 1
        if self.nops > self.limit:
            return
        k = self.dma_rr[eng]
        self.dma_rr[eng] = (k + 1) % N_DMA_SEMS
        sk = ("dma", eng, k)
        prev = self.dma_val.get(sk, 0)
        waits = self._deps(eng, reads, writes)
        if prev > 0 and self.seen[eng].get(sk, 0) < prev:
            waits[sk] = prev
            self.seen[eng][sk] = prev
        self.dma_val[sk] = prev + 16
        ev = (sk, prev + 16)
        self.q[eng].append((list(waits.items()), fn, (sk, 16)))
        self._record(ev, reads, writes)

    def barrier(self):
        tgt = {e: self.cnt[e] for e in ENGS if self.cnt[e] > 0}
        tgt.update(self.dma_val)
        for e in ENGS:
            waits = []
            for sk, v in tgt.items():
                if sk == e:
                    continue
                if self.seen[e].get(sk, 0) < v:
                    waits.append((sk, v))
                    self.seen[e][sk] = v
            if waits:
                self.q[e].append((waits, None, None))
        self.last_w = {}
        self.readers = {}

    def emit(self):
        nc = self.nc
        st = self.stack
        keys = list(ENGS) + list(self.dma_val)
        for sk in keys:
            nm = sk if isinstance(sk, str) else "d_%s_%d" % (sk[1], sk[2])
            self.sems[sk] = st.enter_context(nc.semaphore("s_" + nm))
        final = list(self.dma_val.items())
        block = st.enter_context(nc.Block())
        sems = self.sems

        def run(engname, final_waits=()):
            def body(e):
                for waits, fn, inc in self.q[engname]:
                    for sk, v in waits:
                        e.wait_ge(sems[sk], v)
                    if fn is not None:
                        fn(e).then_inc(sems[inc[0]], inc[1])
                for sk, v in final_waits:
                    e.wait_ge(sems[sk], v)
            return body

        block.tensor(run("pe"))
        block.scalar(run("act"))
        block.vector(run("dve"))
        block.gpsimd(run("pool"))
        block.sync(run("sp", final))


class Builder:
    def __init__(self, S_LEN, NSEQ, depth=DEPTH):
        self.S_LEN = S_LEN
        self.NSEQ = NSEQ
        self.depth = depth
        self.NT = S_LEN // 128
        nc = bass.Bass("TRN2", target_bir_lowering=False)
        self.nc = nc
        self.S = Sched(nc)
        self.ph = None
        self._uid = 0

        def inp(name, shape):
            return nc.dram_tensor(name, list(shape), F32, kind="ExternalInput").ap()

        L = depth
        self.x = inp("x", [NSEQ, S_LEN, D])
        self.w = dict(
            norm_mix_g=inp("norm_mix_g", [L, D]), w_in=inp("w_in", [L, D, NIN]),
            shift_mu=inp("shift_mu", [L, 2, 1952]), decay_w2=inp("decay_w2", [L, 2, 64, 512]),
            decay_w0=inp("decay_w0", [L, 2, 512]), iclr_a2=inp("iclr_a2", [L, 2, 64, 512]),
            iclr_a0=inp("iclr_a0", [L, 2, 512]), gate_g2=inp("gate_g2", [L, 160, 512]),
            k_k=inp("k_k", [L, 512]), k_a=inp("k_a", [L, 512]), r_k=inp("r_k", [L, 512]),
            gn_g=inp("gn_g", [L, 512]), gn_b=inp("gn_b", [L, 512]), w_oa=inp("w_oa", [L, 512, D]),
            q_norm_g=inp("q_norm_g", [L, 256]), w_uq=inp("w_uq", [L, 256, 768]),
            kv_norm_g=inp("kv_norm_g", [L, 128]), w_ukv=inp("w_ukv", [L, 128, 1024]),
            w_ob=inp("w_ob", [L, 512, D]), w_out=inp("w_out", [L, D, D]),
            norm_ffn_g=inp("norm_ffn_g", [L, D]), w_gu=inp("w_gu", [L, D, 2 * DFF]),
            w_down=inp("w_down", [L, DFF, D]), final_norm_g=inp("final_norm_g", [1, D]),
        )
        self.c_ident = inp("c_ident", [128, 128])
        self.c_tri = inp("c_tri", [2, 128, 128])
        self.c_m4 = inp("c_m4", [2, 128, 512])
        self.c_mn4 = inp("c_mn4", [2, 128, 512])
        self.c_ones = inp("c_ones", [128, 128])
        self.c_cos = inp("c_cos", [S_LEN, 32])
        self.c_sin = inp("c_sin", [S_LEN, 32])
        self.y = nc.dram_tensor("y", [NSEQ, S_LEN, D], F32, kind="ExternalOutput").ap()
        def scr(name, shape, dt=F32):
            return nc.dram_tensor(name, list(shape), dt).ap()
        self.P = scr("scr_P", [S_LEN, NIN])
        self.of = scr("scr_of", [S_LEN, 512])
        self.ya = scr("scr_ya", [S_LEN, 512])
        self.yb = scr("scr_yb", [S_LEN, 512])
        self.x1 = scr("scr_x1", [S_LEN, D])
        self.x2 = scr("scr_x2", [S_LEN, D])
        self.QT = scr("scr_QT", [8, 96, S_LEN], BF16)
        self.KT = scr("scr_KT", [8, 96, S_LEN], BF16)
        self.Vd = scr("scr_V", [S_LEN, 8 * 65], BF16)

    def begin_phase(self):
        self.ph = contextlib.ExitStack()

    def end_phase(self):
        self.S.barrier()
        self.ph.close()
        self.ph = None

    def sb(self, name, shape, dt):
        self._uid += 1
        return self.ph.enter_context(self.nc.sbuf_tensor("%s_%d" % (name, self._uid), list(shape), dt))

    def ps(self, name, shape, dt):
        self._uid += 1
        return self.ph.enter_context(self.nc.psum_tensor("%s_%d" % (name, self._uid), list(shape), dt))

    def load(self, out_ap, in_ap, writes, reads=(), cast=False):
        eng = "pool" if cast else "sp"
        self.S.dma(eng, lambda e: e.dma_start(out=out_ap, in_=in_ap), reads=reads, writes=writes)

    def store(self, out_ap, in_ap, reads, writes):
        self.S.dma("sp", lambda e: e.dma_start(out=out_ap, in_=in_ap), reads=reads, writes=writes)

    def bcast_load(self, tile, row_ap, width, name):
        self.load(tile[:], row_ap.broadcast_to([128, width]), writes=[name])

    def load_w_bf16(self, tile, w_ap, K, name):
        for k in range(K // 128):
            self.load(tile[:, k, :], w_ap[k * 128:(k + 1) * 128, :], writes=[name], cast=True)

    def mm(self, out, lhsT, rhs, start, stop, reads, writes):
        self.S.op("pe", lambda e: e.matmul(out=out, lhsT=lhsT, rhs=rhs, start=start, stop=stop),
                  reads=reads, writes=writes)

    def tr(self, out, in_, ident, reads, writes):
        self.S.op("pe", lambda e: e.transpose(out=out, in_=in_, identity=ident), reads=reads, writes=writes)

    def act(self, out, in_, func, reads, writes, scale=None, bias=None, accum_out=None):
        kw = {}
        if scale is not None:
            kw["scale"] = scale
        if bias is not None:
            kw["bias"] = bias
        if accum_out is not None:
            kw["accum_out"] = accum_out
        self.S.op("act", lambda e: e.activation(out=out, in_=in_, func=func, **kw), reads=reads, writes=writes)

    def tt(self, eng, out, in0, in1, op, reads, writes):
        self.S.op(eng, lambda e: e.tensor_tensor(out=out, in0=in0, in1=in1, op=op), reads=reads, writes=writes)

    def ts(self, out, in0, s1, s2, op0, op1, reads, writes, eng="dve"):
        self.S.op(eng, lambda e: e.tensor_scalar(out=out, in0=in0, scalar1=s1, scalar2=s2, op0=op0, op1=op1),
                  reads=reads, writes=writes)

    def stt(self, out, in0, scalar, in1, op0, op1, reads, writes):
        self.S.op("dve", lambda e: e.scalar_tensor_tensor(out=out, in0=in0, scalar=scalar, in1=in1, op0=op0, op1=op1),
                  reads=reads, writes=writes)

    def cp(self, eng, out, in_, reads, writes):
        if eng == "act":
            self.S.op("act", lambda e: e.activation(out=out, in_=in_, func=AF.Copy), reads=reads, writes=writes)
        else:
            self.S.op(eng, lambda e: e.tensor_copy(out=out, in_=in_), reads=reads, writes=writes)

    def red(self, out, in_, reads, writes):
        self.S.op("dve", lambda e: e.tensor_reduce(out=out, in_=in_, axis=AX.X, op=ALU.add), reads=reads, writes=writes)

    def recip(self, out, in_, reads, writes):
        self.S.op("dve", lambda e: e.reciprocal(out=out, in_=in_), reads=reads, writes=writes)

    def memset(self, eng, ap, val, writes):
        self.S.op(eng, lambda e: e.memset(ap, val), writes=writes)

    def rmsnorm(self, x_ap, xn, width, gbc_ap, gn, out_ap, outn, junk, ss, rstd, tag):
        jn, sn, rn = "junk" + tag, "ss" + tag, "rstd" + tag
        self.act(junk, x_ap, AF.Square, reads=[xn], writes=[jn, sn], accum_out=ss)
        self.ts(rstd, ss, 1.0 / width, RMS_EPS, ALU.mult, ALU.add, reads=[sn], writes=[rn])
        self.act(rstd, rstd, AF.Sqrt, reads=[rn], writes=[rn])
        self.recip(rstd, rstd, reads=[rn], writes=[rn])
        self.stt(out_ap, x_ap, rstd, gbc_ap, ALU.mult, ALU.mult, reads=[xn, rn, gn], writes=[outn])

    def phase_A(self, l, xin):
        NT = self.NT
        self.begin_phase()
        wA = self.sb("wA", [128, 8, NIN], BF16)
        gbc = self.sb("gA", [128, D], F32)
        idb = self.sb("idb", [128, 128], BF16)
        junk = self.sb("junk", [128, D], F32)
        ss = self.sb("ss", [128, 1], F32)
        rstd = self.sb("rstd", [128, 1], F32)
        xt = [self.sb("xt", [128, D], F32) for _ in range(2)]
        h = [self.sb("h", [128, D], BF16) for _ in range(2)]
        hT = [self.sb("hT", [128, 8, 128], BF16) for _ in range(2)]
        Pt = [self.sb("Pt", [128, NIN], F32) for _ in range(2)]
        pT = [self.ps("pT", [128, 8, 128], BF16) for _ in range(2)]
        pP = [self.ps("pP", [128, 512], F32) for _ in range(4)]
        self.load(idb[:], self.c_ident, writes=["idb"], cast=True)
        self.bcast_load(gbc, self.w["norm_mix_g"][l:l + 1, :], D, "gA")
        self.load_w_bf16(wA, self.w["w_in"][l], D, "wA")
        npieces = (NIN + 511) // 512

        def norm(i):
            b = i % 2
            self.load(xt[b][:], xin[i * 128:(i + 1) * 128, :], writes=["xt%d" % b])
            self.rmsnorm(xt[b][:], "xt%d" % b, D, gbc[:], "gA", h[b][:], "h%d" % b, junk[:], ss[:], rstd[:], "A")

        def trans(i):
            b = i % 2
            for k in range(8):
                self.tr(pT[b][:, k, :], h[b][:, k * 128:(k + 1) * 128], idb[:], reads=["h%d" % b, "idb"], writes=["pT%d" % b])
            self.cp("act", hT[b][:], pT[b][:], reads=["pT%d" % b], writes=["hT%d" % b])

        norm(0)
        trans(0)
        for i in range(NT):
            b = i % 2
            if i + 1 < NT:
                norm(i + 1)
            for j in range(npieces):
                n0 = j * 512
                n = min(512, NIN - n0)
                pp = pP[j % 4]
                pn = "pP%d" % (j % 4)
                for k in range(8):
                    self.mm(pp[:, 0:n], hT[b][:, k, :], wA[:, k, n0:n0 + n], k == 0, k == 7,
                            reads=["hT%d" % b, "wA"], writes=[pn])
                self.cp("dve" if j % 2 == 0 else "act", Pt[b][:, n0:n0 + n], pp[:, 0:n],
                        reads=[pn], writes=[("Pt", b, j)])
                if j == 5 and i + 1 < NT:
                    trans(i + 1)
            self.store(self.P[i * 128:(i + 1) * 128, :], Pt[b][:], reads=[("Pt", b, j) for j in range(npieces)],
                       writes=[("P", i)])
        self.end_phase()

    def phase_R(self, l, d):
        NT = self.NT
        S_LEN = self.S_LEN
        W = self.w
        self.begin_phase()
        sb, ps = self.sb, self.ps
        mu0 = sb("mu0", [128, 1952], F32)
        mu1 = sb("mu1", [128, 1952], F32)
        w0bc = sb("w0bc", [128, 512], F32)
        a0bc = sb("a0bc", [128, 512], F32)
        kkbc = sb("kkbc", [128, 512], F32)
        kabc = sb("kabc", [128, 512], F32)
        w2b = sb("w2b", [64, 512], BF16)
        a2b = sb("a2b", [64, 512], BF16)
        tri = sb("tri", [128, 128], F32)
        ones = sb("ones", [128, 128], F32)
        m4 = sb("m4", [128, 512], F32)
        mn4 = sb("mn4", [128, 512], F32)
        idb = sb("idb", [128, 128], BF16)
        pac = sb("pac", [128, 1952], F32)
        pap = sb("pap", [128, 1952], F32)
        pan = sb("pan", [128, 1952], F32)
        lo = sb("lo", [128, 128], BF16)
        loT = sb("loT", [64, 2, 128], BF16)
        sgm = sb("sgm", [128, 512], F32)
        av = sb("av", [128, 512], F32)
        kk = sb("kk", [128, 512], F32)
        tmp = sb("tmp", [128, 512], F32)
        kd = sb("kd", [128, 512], F32)
        ka = sb("ka", [128, 512], F32)
        Ls = sb("Ls", [128, 512], F32)
        Ld = sb("Ld", [128, 512], F32)
        E1 = sb("E1", [128, 512], F32)
        E2 = sb("E2", [128, 512], F32)
        E3 = sb("E3", [128, 512], F32)
        E4 = sb("E4", [128, 512], F32)
        ssq = sb("ssq", [128, 8], F32)
        gC = sb("gC", [64, 8], F32)
        Ab = sb("Ab", [128, 512], BF16)
        Rb = sb("Rb", [128, 512], BF16)
        Bb = sb("Bb", [128, 512], BF16)
        Kb = sb("Kb", [128, 512], BF16)
        Btb = sb("Btb", [128, 512], BF16)
        Ktb = sb("Ktb", [128, 512], BF16)
        Vb = sb("Vb", [128, 512], BF16)
        ART = sb("ART", [128, 8, 256], BF16)
        BT = sb("BT", [64, 8, 128], BF16)
        KTt = sb("KTt", [64, 8, 128], BF16)
        ATall = sb("ATall", [128, 8, 512], BF16)
        PP = [sb("PP", [128, 8, 256], BF16) for _ in range(2)]
        W32 = sb("W32", [128, 512], F32)
        Wb = sb("Wb", [128, 512], BF16)
        osb = sb("osb", [128, 512], F32)
        ST32 = sb("ST32", [64, 8, 64], F32)
        STb = sb("STb", [128, 8, 64], BF16)
        if d == 1:
            rkbc = sb("rkbc", [128, 512], F32)
            gngbc = sb("gngbc", [128, 512], F32)
            gnbbc = sb("gnbbc", [128, 512], F32)
            g2b = sb("g2b", [128, 2, 512], BF16)
            oft = sb("oft", [128, 512], F32)
            cen = sb("cen", [128, 512], F32)
            sq2 = sb("sq2", [128, 512], F32)
            bon = sb("bon", [128, 512], F32)
            st8 = sb("st8", [128, 8], F32)
            sv8 = sb("sv8", [128, 8], F32)
            sb8 = sb("sb8", [128, 8], F32)
            gs = sb("gs", [128, 160], BF16)
            gT = sb("gT", [128, 2, 128], BF16)
            yat = sb("yat", [128, 512], F32)
        q = [ps("q%d" % i, [128, 512], F32) for i in range(8)]
        q7 = q[7]
        qAT = q[3][:].bitcast(BF16)
        qRT = q[4][:].bitcast(BF16)
        qBT = q[5][:].bitcast(BF16)
        qKT = q[6][:].bitcast(BF16)
        q7b = q7[:].bitcast(BF16)

        self.load(idb[:], self.c_ident, writes=["idb"], cast=True)
        self.load(tri[:], self.c_tri[d], writes=["tri"])
        self.load(ones[:], self.c_ones, writes=["ones"])
        self.load(m4[:], self.c_m4[d], writes=["m4"])
        self.load(mn4[:], self.c_mn4[d], writes=["mn4"])
        self.bcast_load(mu0, W["shift_mu"][l, 0:1, :], 1952, "mu0")
        self.bcast_load(mu1, W["shift_mu"][l, 1:2, :], 1952, "mu1")
        self.bcast_load(w0bc, W["decay_w0"][l, d:d + 1, :], 512, "w0bc")
        self.bcast_load(a0bc, W["iclr_a0"][l, d:d + 1, :], 512, "a0bc")
        self.bcast_load(kkbc, W["k_k"][l:l + 1, :], 512, "kkbc")
        self.bcast_load(kabc, W["k_a"][l:l + 1, :], 512, "kabc")
        self.load(w2b[:], W["decay_w2"][l, d], writes=["w2b"], cast=True)
        self.load(a2b[:], W["iclr_a2"][l, d], writes=["a2b"], cast=True)
        if d == 1:
            self.bcast_load(rkbc, W["r_k"][l:l + 1, :], 512, "rkbc")
            self.bcast_load(gngbc, W["gn_g"][l:l + 1, :], 512, "gngbc")
            self.bcast_load(gnbbc, W["gn_b"][l:l + 1, :], 512, "gnbbc")
        self.memset("dve", ST32[:], 0.0, writes=["ST32"])
        self.memset("dve", STb[:], 0.0, writes=["STb"])
        self.memset("dve", ART[:], 0.0, writes=["ART"])
        if d == 1:
            self.memset("dve", gT[:], 0.0, writes=["gT"])
            self.memset("dve", g2b[:], 0.0, writes=["g2b"])
            self.load(g2b[:, 0, :], W["gate_g2"][l, 0:128, :], writes=["g2b"], cast=True)
            self.load(g2b[0:32, 1, :], W["gate_g2"][l, 128:160, :], writes=["g2b"], cast=True)

        def v3(t):
            return t[:].rearrange("p (h e) -> p h e", h=8)

        order = range(NT) if d == 0 else range(NT - 1, -1, -1)
        for i in order:
            t0 = i * 128
            self.S.mark("loads")
            self.load(pac[:], self.P[t0:t0 + 128, 2048:4000], writes=["pac"])
            if i == 0:
                self.memset("pool", pap[:], 0.0, writes=["pap"])
                self.load(pap[1:128, :], self.P[0:127, 2048:4000], writes=["pap"])
            else:
                self.load(pap[:], self.P[t0 - 1:t0 + 127, 2048:4000], writes=["pap"])
            if i == NT - 1:
                self.memset("pool", pan[:], 0.0, writes=["pan"])
                self.load(pan[0:127, :], self.P[t0 + 1:S_LEN, 2048:4000], writes=["pan"])
            else:
                self.load(pan[:], self.P[t0 + 1:t0 + 129, 2048:4000], writes=["pan"])
            self.tt("dve", pap[:], pap[:], pac[:], ALU.subtract, reads=["pap", "pac"], writes=["pap"])
            self.tt("pool", pap[:], pap[:], mu0[:], ALU.mult, reads=["pap", "mu0"], writes=["pap"])
            self.tt("dve", pan[:], pan[:], pac[:], ALU.subtract, reads=["pan", "pac"], writes=["pan"])
            self.tt("pool", pan[:], pan[:], mu1[:], ALU.mult, reads=["pan", "mu1"], writes=["pan"])
            self.tt("dve", pac[:], pac[:], pap[:], ALU.add, reads=["pac", "pap"], writes=["pac"])
            self.tt("dve", pac[:], pac[:], pan[:], ALU.add, reads=["pac", "pan"], writes=["pac"])
            r_ = pac[:, 0:512]
            k_ = pac[:, 512:1024]
            v_ = pac[:, 1024:1536]
            lw_ = pac[:, 1536 + 64 * d:1600 + 64 * d]
            la_ = pac[:, 1664 + 64 * d:1728 + 64 * d]
            lg_ = pac[:, 1792:1952]
            self.S.mark("lora")
            self.act(lo[:, 0:64], lw_, AF.Tanh, reads=["pac"], writes=["lo"])
            self.cp("dve", lo[:, 64:128], la_, reads=["pac"], writes=["lo"])
            self.tr(q7b[0:64, 0:128], lo[:, 0:64], idb[:], reads=["lo", "idb"], writes=["q7"])
            self.tr(q7b[0:64, 128:256], lo[:, 64:128], idb[:], reads=["lo", "idb"], writes=["q7"])
            self.cp("act", loT[:].rearrange("p a b -> p (a b)"), q7b[0:64, 0:256], reads=["q7"], writes=["loT"])
            self.mm(q[0][:], loT[:, 0, :], w2b[:], True, True, reads=["loT", "w2b"], writes=["q0"])
            self.mm(q[1][:], loT[:, 1, :], a2b[:], True, True, reads=["loT", "a2b"], writes=["q1"])
            self.tt("dve", sgm[:], q[0][:], w0bc[:], ALU.add, reads=["q0", "w0bc"], writes=["sgm"])
            self.act(sgm[:], sgm[:], AF.Sigmoid, reads=["sgm"], writes=["sgm"])
            self.tt("dve", av[:], q[1][:], a0bc[:], ALU.add, reads=["q1", "a0bc"], writes=["av"])
            self.act(av[:], av[:], AF.Sigmoid, reads=["av"], writes=["av"])
            self.S.mark("kk")
            self.tt("dve", kk[:], k_, kkbc[:], ALU.mult, reads=["pac", "kkbc"], writes=["kk"])
            self.tt("pool", tmp[:], kk[:], kk[:], ALU.mult, reads=["kk"], writes=["tmp"])
            self.red(ssq[:], v3(tmp), reads=["tmp"], writes=["ssq"])
            self.act(ssq[:], ssq[:], AF.Sqrt, reads=["ssq"], writes=["ssq"])
            self.ts(ssq[:], ssq[:], 1e-12, None, ALU.max, ALU.bypass, reads=["ssq"], writes=["ssq"])
            self.recip(ssq[:], ssq[:], reads=["ssq"], writes=["ssq"])
            self.tt("dve", v3(kk), v3(kk), ssq[:].unsqueeze(2).broadcast_to([128, 8, 64]), ALU.mult,
                    reads=["kk", "ssq"], writes=["kk"])
            self.stt(tmp[:], av[:], -1.0, kabc[:], ALU.add, ALU.mult, reads=["av", "kabc"], writes=["tmp"])
            self.stt(kd[:], tmp[:], 1.0, k_, ALU.add, ALU.mult, reads=["tmp", "pac"], writes=["kd"])
            self.tt("dve", ka[:], kk[:], av[:], ALU.mult, reads=["kk", "av"], writes=["ka"])
            self.S.mark("cum")
            self.mm(q[0][:], tri[:], sgm[:], True, True, reads=["tri", "sgm"], writes=["q0"])
            self.mm(q[1][:], ones[:], sgm[:], True, True, reads=["ones", "sgm"], writes=["q1"])
            for hh in range(8):
                self.mm(q[2][0:64, hh * 2:hh * 2 + 2], sgm[:, hh * 64:(hh + 1) * 64], ones[:, 0:2], True, True,
                        reads=["sgm", "ones"], writes=["q2"])
            self.act(gC[:], q[2][0:64, 0:16].rearrange("p (h t) -> p h t", t=2)[:, :, 0], AF.Exp,
                     reads=["q2"], writes=["gC"], scale=-CDEC)
            self.cp("act", Ls[:], q[0][:], reads=["q0"], writes=["Ls"])
            self.tt("dve", Ld[:], q[1][:], Ls[:], ALU.subtract, reads=["q1", "Ls"], writes=["Ld"])
            self.act(E2[:], q[0][:], AF.Exp, reads=["q0"], writes=["E2"], scale=-CDEC)
            self.act(E3[:], q[0][:], AF.Exp, reads=["q0"], writes=["E3"], scale=CDEC)
            self.tt("dve", Ls[:], Ls[:], sgm[:], ALU.subtract, reads=["Ls", "sgm"], writes=["Ls"])
            self.act(E1[:], Ls[:], AF.Exp, reads=["Ls"], writes=["E1"], scale=-CDEC)
            self.act(E4[:], Ld[:], AF.Exp, reads=["Ld"], writes=["E4"], scale=-CDEC)
            self.S.mark("scaled")
            self.stt(Ab[:], kk[:], -1.0, E1[:], ALU.mult, ALU.mult, reads=["kk", "E1"], writes=["Ab"])
            self.tt("dve", Rb[:], r_, E2[:], ALU.mult, reads=["pac", "E2"], writes=["Rb"])
            self.tt("pool", Bb[:], ka[:], E3[:], ALU.mult, reads=["ka", "E3"], writes=["Bb"])
            self.tt("pool", Kb[:], kd[:], E3[:], ALU.mult, reads=["kd", "E3"], writes=["Kb"])
            self.tt("pool", Btb[:], ka[:], E4[:], ALU.mult, reads=["ka", "E4"], writes=["Btb"])
            self.tt("dve", Ktb[:], kd[:], E4[:], ALU.mult, reads=["kd", "E4"], writes=["Ktb"])
            self.cp("pool", Vb[:], v_, reads=["pac"], writes=["Vb"])
            for hh in range(8):
                hs = slice(hh * 64, (hh + 1) * 64)
                ts_ = slice(hh * 128, (hh + 1) * 128)
                self.tr(qAT[0:64, ts_], Ab[:, hs], idb[:], reads=["Ab", "idb"], writes=["q3"])
                self.tr(qRT[0:64, ts_], Rb[:, hs], idb[:], reads=["Rb", "idb"], writes=["q4"])
                self.tr(qBT[0:64, ts_], Bb[:, hs], idb[:], reads=["Bb", "idb"], writes=["q5"])
                self.tr(qKT[0:64, ts_], Kb[:, hs], idb[:], reads=["Kb", "idb"], writes=["q6"])
            self.cp("act", ART[0:64, :, 0:128], qAT[0:64, :].rearrange("p (h t) -> p h t", h=8), reads=["q3"], writes=["ART"])
            self.cp("dve", ART[0:64, :, 128:256], qRT[0:64, :].rearrange("p (h t) -> p h t", h=8), reads=["q4"], writes=["ART"])
            self.cp("act", BT[:], qBT[0:64, :].rearrange("p (h t) -> p h t", h=8), reads=["q5"], writes=["BT"])
            self.cp("dve", KTt[:], qKT[0:64, :].rearrange("p (h t) -> p h t", h=8), reads=["q6"], writes=["KTt"])
            self.S.mark("A4")
            for hh in range(8):
                qq = q[hh % 2]
                qn = "q%d" % (hh % 2)
                self.mm(qq[:, 0:256], BT[:, hh, :], ART[0:64, hh, :], True, True, reads=["BT", "ART"], writes=[qn])
                self.mm(qq[:, 256:512], KTt[:, hh, :], ART[0:64, hh, :], True, True, reads=["KTt", "ART"], writes=[qn])
                self.tt("dve", ATall[:, hh, :], qq[:], m4[:], ALU.mult, reads=[qn, "m4"], writes=[("AT", hh)])
            self.S.mark("N")
            for g in range(2):
                for j in range(4):
                    hh = g * 4 + j
                    self.mm(q7[:, j * 128:(j + 1) * 128], ART[0:64, hh, 0:128], BT[:, hh, :], True, True,
                            reads=["ART", "BT"], writes=["q7"])
                self.tt("dve", PP[0][:, g * 4:(g + 1) * 4, 0:128], q7[:].rearrange("p (j s) -> p j s", j=4),
                        mn4[:].rearrange("p (j s) -> p j s", j=4), ALU.mult, reads=["q7", "mn4"],
                        writes=[("PP", 0, 2 * g), ("PP", 0, 2 * g + 1)])
            self.cp("pool", PP[0][:, :, 128:256], ATall[:, :, 0:128], reads=[("AT", hh) for hh in range(8)],
                    writes=[("PP", 0, pr) for pr in range(4)])
            self.S.mark("W")
            for hh in range(8):
                hs = slice(hh * 64, (hh + 1) * 64)
                self.mm(q[2][:, hs], ART[:, hh, 0:128], STb[:, hh, :], True, False, reads=["ART", "STb"], writes=["q2"])
                self.mm(q[2][:, hs], ATall[:, hh, 256:384], Vb[:, hs], False, True, reads=[("AT", hh), "Vb"], writes=["q2"])
            self.cp("act", W32[:], q[2][:], reads=["q2"], writes=["W32"])
            self.cp("dve", Wb[:], W32[:], reads=["W32"], writes=["Wb"])
            self.S.mark("neu")
            for j in range(7):
                cb = j % 2
                cur = PP[cb]
                for hh in range(8):
                    hs = slice(hh * 64, (hh + 1) * 64)
                    self.mm(q7[:, hs], cur[:, hh, 128:256], Wb[:, hs], True, True,
                            reads=[("PP", cb, hh // 2), "Wb"], writes=["q7"])
                self.tt("dve", W32[:], W32[:], q7[:], ALU.add, reads=["W32", "q7"], writes=["W32"])
                self.cp("act", Wb[:], W32[:], reads=["W32"], writes=["Wb"])
                if j < 6:
                    nxt = PP[1 - cb]
                    for pr in range(4):
                        bankt, bname = q[pr], "q%d" % pr
                        for u in range(2):
                            hh = pr * 2 + u
                            self.mm(bankt[:, u * 256:u * 256 + 128], cur[:, hh, 128:256], cur[:, hh, 0:128], True, True,
                                    reads=[("PP", cb, pr)], writes=[bname])
                            self.mm(bankt[:, u * 256 + 128:u * 256 + 256], cur[:, hh, 0:128], cur[:, hh, 128:256], True, True,
                                    reads=[("PP", cb, pr)], writes=[bname])
                        self.cp("dve" if pr % 2 == 0 else "act",
                                nxt[:, pr * 2:pr * 2 + 2, :].rearrange("p a b -> p (a b)"), bankt[:],
                                reads=[bname], writes=[("PP", 1 - cb, pr)])
            self.S.mark("O")
            for hh in range(8):
                hs = slice(hh * 64, (hh + 1) * 64)
                self.mm(q[4][:, hs], ART[:, hh, 128:256], STb[:, hh, :], True, False, reads=["ART", "STb"], writes=["q4"])
                self.mm(q[4][:, hs], ATall[:, hh, 128:256], Wb[:, hs], False, False, reads=[("AT", hh), "Wb"], writes=["q4"])
                self.mm(q[4][:, hs], ATall[:, hh, 384:512], Vb[:, hs], False, True, reads=[("AT", hh), "Vb"], writes=["q4"])
            self.cp("act", osb[:], q[4][:], reads=["q4"], writes=["osb"])
            self.S.mark("S")
            for hh in range(8):
                hs = slice(hh * 64, (hh + 1) * 64)
                self.mm(q[5][0:64, hs], Btb[:, hs], Wb[:, hs], True, False, reads=["Btb", "Wb"], writes=["q5"])
                self.mm(q[5][0:64, hs], Ktb[:, hs], Vb[:, hs], False, True, reads=["Ktb", "Vb"], writes=["q5"])
            self.tt("dve", ST32[:], ST32[:], gC[:].unsqueeze(2).broadcast_to([64, 8, 64]), ALU.mult,
                    reads=["ST32", "gC"], writes=["ST32"])
            self.tt("dve", ST32[:], ST32[:], q[5][0:64, :].rearrange("p (h e) -> p h e", h=8), ALU.add,
                    reads=["ST32", "q5"], writes=["ST32"])
            self.cp("act", STb[0:64, :, :], ST32[:], reads=["ST32"], writes=["STb"])
            if d == 0:
                self.store(self.of[t0:t0 + 128, :], osb[:], reads=["osb"], writes=[("of", i)])
                continue
            self.load(oft[:], self.of[t0:t0 + 128, :], writes=["oft"])
            self.tt("dve", oft[:], oft[:], osb[:], ALU.add, reads=["oft", "osb"], writes=["oft"])
            self.red(st8[:], v3(oft), reads=["oft"], writes=["st8"])
            self.ts(st8[:], st8[:], 1.0 / 64, None, ALU.mult, ALU.bypass, reads=["st8"], writes=["st8"])
            self.tt("dve", v3(cen), v3(oft), st8[:].unsqueeze(2).broadcast_to([128, 8, 64]), ALU.subtract,
                    reads=["oft", "st8"], writes=["cen"])
            self.tt("pool", sq2[:], cen[:], cen[:], ALU.mult, reads=["cen"], writes=["sq2"])
            self.red(sv8[:], v3(sq2), reads=["sq2"], writes=["sv8"])
            self.ts(sv8[:], sv8[:], 1.0 / 64, GN_EPS, ALU.mult, ALU.add, reads=["sv8"], writes=["sv8"])
            self.act(sv8[:], sv8[:], AF.Sqrt, reads=["sv8"], writes=["sv8"])
            self.recip(sv8[:], sv8[:], reads=["sv8"], writes=["sv8"])
            self.tt("dve", v3(cen), v3(cen), sv8[:].unsqueeze(2).broadcast_to([128, 8, 64]), ALU.mult,
                    reads=["cen", "sv8"], writes=["cen"])
            self.tt("pool", cen[:], cen[:], gngbc[:], ALU.mult, reads=["cen", "gngbc"], writes=["cen"])
            self.tt("pool", cen[:], cen[:], gnbbc[:], ALU.add, reads=["cen", "gnbbc"], writes=["cen"])
            self.tt("dve", sq2[:], r_, k_, ALU.mult, reads=["pac"], writes=["sq2"])
            self.tt("pool", sq2[:], sq2[:], rkbc[:], ALU.mult, reads=["sq2", "rkbc"], writes=["sq2"])
            self.red(sb8[:], v3(sq2), reads=["sq2"], writes=["sb8"])
            self.tt("dve", v3(bon), pac[:, 1024:1536].rearrange("p (h e) -> p h e", h=8),
                    sb8[:].unsqueeze(2).broadcast_to([128, 8, 64]), ALU.mult, reads=["pac", "sb8"], writes=["bon"])
            self.tt("dve", cen[:], cen[:], bon[:], ALU.add, reads=["cen", "bon"], writes=["cen"])
            self.act(gs[:], lg_, AF.Sigmoid, reads=["pac"], writes=["gs"])
            self.tr(q7b[:, 0:128], gs[:, 0:128], idb[:], reads=["gs", "idb"], writes=["q7"])
            self.tr(q7b[0:32, 128:256], gs[:, 128:160], idb[:], reads=["gs", "idb"], writes=["q7"])
            self.cp("act", gT[:, 0, :], q7b[:, 0:128], reads=["q7"], writes=["gT"])
            self.cp("act", gT[0:32, 1, :], q7b[0:32, 128:256], reads=["q7"], writes=["gT"])
            self.mm(q[2][:], gT[:, 0, :], g2b[:, 0, :], True, False, reads=["gT", "g2b"], writes=["q2"])
            self.mm(q[2][:], gT[:, 1, :], g2b[:, 1, :], False, True, reads=["gT", "g2b"], writes=["q2"])
            self.tt("dve", yat[:], cen[:], q[2][:], ALU.mult, reads=["cen", "q2"], writes=["yat"])
            self.store(self.ya[t0:t0 + 128, :], yat[:], reads=["yat"], writes=[("ya", i)])
        self.end_phase()

    def phase_MP(self, l):
        NT = self.NT
        W = self.w
        self.begin_phase()
        sb, ps = self.sb, self.ps
        qg = sb("qg", [128, 256], F32)
        kvg = sb("kvg", [128, 128], F32)
        wuq = sb("wuq", [128, 2, 768], BF16)
        wukv = sb("wukv", [128, 1, 1024], BF16)
        idb = sb("idb", [128, 128], BF16)
        pm = sb("pm", [128, 416], F32)
        cs = sb("cs", [128, 32], F32)
        sn = sb("sn", [128, 32], F32)
        junk = sb("junk", [128, 256], F32)
        ss = sb("ss", [128, 1], F32)
        rstd = sb("rstd", [128, 1], F32)
        nb = sb("nb", [128, 384], BF16)
        nT = sb("nT", [128, 3, 128], BF16)
        qf = sb("qf", [128, 768], F32)
        kvf = sb("kvf", [128, 1024], F32)
        t1 = sb("t1", [128, 8, 32], F32)
        t2 = sb("t2", [128, 8, 32], F32)
        kro = sb("kro", [128, 32], F32)
        kr2 = sb("kr2", [128, 32], F32)
        Qa = sb("Qa", [128, 8, 96], BF16)
        Ka = sb("Ka", [128, 8, 96], BF16)
        Va = sb("Va", [128, 8, 65], BF16)
        QTt = sb("QTt", [96, 8, 128], BF16)
        KTt = sb("KTt", [96, 8, 128], BF16)
        q = [ps("q%d" % i, [128, 512], F32) for i in range(7)]
        qb = [t[:].bitcast(BF16) for t in q]
        self.load(idb[:], self.c_ident, writes=["idb"], cast=True)
        self.bcast_load(qg, W["q_norm_g"][l:l + 1, :], 256, "qg")
        self.bcast_load(kvg, W["kv_norm_g"][l:l + 1, :], 128, "kvg")
        self.load_w_bf16(wuq, W["w_uq"][l], 256, "wuq")
        self.load_w_bf16(wukv, W["w_ukv"][l], 128, "wukv")
        self.memset("dve", Va[:], 1.0, writes=["Va"])
        qf3 = qf[:].rearrange("p (h e) -> p h e", h=8)
        kvf3 = kvf[:].rearrange("p (h e) -> p h e", h=8)
        for i in range(NT):
            t0 = i * 128
            self.load(pm[:], self.P[t0:t0 + 128, 4000:4416], writes=["pm"])
            self.load(cs[:], self.c_cos[t0:t0 + 128, :], writes=["cs"])
            self.load(sn[:], self.c_sin[t0:t0 + 128, :], writes=["sn"])
            self.rmsnorm(pm[:, 0:256], "pm", 256, qg[:], "qg", nb[:, 0:256], "nbq", junk[:, 0:256], ss[:], rstd[:], "M")
            self.rmsnorm(pm[:, 256:384], "pm", 128, kvg[:], "kvg", nb[:, 256:384], "nbk", junk[:, 0:128], ss[:], rstd[:], "M")
            for c in range(3):
                self.tr(qb[0][:, c * 128:(c + 1) * 128], nb[:, c * 128:(c + 1) * 128], idb[:],
                        reads=["nbq", "nbk", "idb"], writes=["q0"])
            self.cp("act", nT[:].rearrange("p a b -> p (a b)"), qb[0][:, 0:384], reads=["q0"], writes=["nT"])
            for c in range(2):
                self.mm(q[1][:], nT[:, c, :], wuq[:, c, 0:512], c == 0, c == 1, reads=["nT", "wuq"], writes=["q1"])
            for c in range(2):
                self.mm(q[2][:, 0:256], nT[:, c, :], wuq[:, c, 512:768], c == 0, c == 1, reads=["nT", "wuq"], writes=["q2"])
            self.mm(q[3][:], nT[:, 2, :], wukv[:, 0, 0:512], True, True, reads=["nT", "wukv"], writes=["q3"])
            self.mm(q[4][:], nT[:, 2, :], wukv[:, 0, 512:1024], True, True, reads=["nT", "wukv"], writes=["q4"])
            self.cp("act", qf[:, 0:512], q[1][:], reads=["q1"], writes=["qf"])
            self.cp("dve", qf[:, 512:768], q[2][:, 0:256], reads=["q2"], writes=["qf"])
            self.cp("act", kvf[:, 0:512], q[3][:], reads=["q3"], writes=["kvf"])
            self.cp("dve", kvf[:, 512:1024], q[4][:], reads=["q4"], writes=["kvf"])
            self.cp("pool", Qa[:, :, 0:64], qf3[:, :, 0:64], reads=["qf"], writes=["Qa"])
            csb = cs[:].unsqueeze(1).broadcast_to([128, 8, 32])
            self.tt("dve", t1[:], qf3[:, :, 64:96], csb, ALU.mult, reads=["qf", "cs"], writes=["t1"])
            self.tt("dve", t2[:, :, 0:16], qf3[:, :, 80:96], sn[:, 0:16].unsqueeze(1).broadcast_to([128, 8, 16]), ALU.mult,
                    reads=["qf", "sn"], writes=["t2"])
            self.tt("dve", t2[:, :, 16:32], qf3[:, :, 64:80], sn[:, 16:32].unsqueeze(1).broadcast_to([128, 8, 16]), ALU.mult,
                    reads=["qf", "sn"], writes=["t2"])
            self.tt("dve", Qa[:, :, 64:96], t1[:], t2[:], ALU.add, reads=["t1", "t2"], writes=["Qa"])
            self.tt("dve", kro[:], pm[:, 384:416], cs[:], ALU.mult, reads=["pm", "cs"], writes=["kro"])
            self.tt("dve", kr2[:, 0:16], pm[:, 400:416], sn[:, 0:16], ALU.mult, reads=["pm", "sn"], writes=["kr2"])
            self.tt("dve", kr2[:, 16:32], pm[:, 384:400], sn[:, 16:32], ALU.mult, reads=["pm", "sn"], writes=["kr2"])
            self.tt("dve", kro[:], kro[:], kr2[:], ALU.add, reads=["kro", "kr2"], writes=["kro"])
            self.cp("dve", Ka[:, :, 64:96], kro[:].unsqueeze(1).broadcast_to([128, 8, 32]), reads=["kro"], writes=["Ka"])
            self.cp("pool", Ka[:, :, 0:64], kvf3[:, :, 0:64], reads=["kvf"], writes=["Ka"])
            self.cp("pool", Va[:, :, 0:64], kvf3[:, :, 64:128], reads=["kvf"], writes=["Va"])
            for hh in range(8):
                self.tr(qb[5][0:96, hh * 128:(hh + 1) * 128], Qa[:, hh, :], idb[:], reads=["Qa", "idb"], writes=["q5"])
                self.tr(qb[6][0:96, hh * 128:(hh + 1) * 128], Ka[:, hh, :], idb[:], reads=["Ka", "idb"], writes=["q6"])
            self.cp("act", QTt[:].rearrange("p a b -> p (a b)"), qb[5][0:96, :], reads=["q5"], writes=["QTt"])
            self.cp("dve", KTt[:].rearrange("p a b -> p (a b)"), qb[6][0:96, :], reads=["q6"], writes=["KTt"])
            self.store(self.QT[:, :, t0:t0 + 128].rearrange("h p t -> p h t"), QTt[:], reads=["QTt"], writes=[("QT", i)])
            self.store(self.KT[:, :, t0:t0 + 128].rearrange("h p t -> p h t"), KTt[:], reads=["KTt"], writes=[("KT", i)])
            self.store(self.Vd[t0:t0 + 128, :], Va[:].rearrange("p a b -> p (a b)"), reads=["Va"], writes=[("Vd", i)])
        self.end_phase()

    def phase_MM(self):
        NT = self.NT
        S_LEN = self.S_LEN
        QB = min(512, S_LEN)
        nqb = S_LEN // QB
        nj = QB // 128
        LOOK = 2
        self.begin_phase()
        sb, ps = self.sb, self.ps
        Vall = sb("Vall", [128, NT, 520], BF16)
        KTh = [sb("KTh", [96, S_LEN], BF16) for _ in range(2)]
        QTb = [sb("QTb", [96, QB], BF16) for _ in range(2)]
        PT = [sb("PT", [128, QB], BF16) for _ in range(4)]
        OT = sb("OT", [65, QB], F32)
        id32 = sb("id32", [128, 128], F32)
        osm = sb("osm", [128, nj, 64], F32)
        rec = sb("rec", [128, nj], F32)
        q = [ps("q%d" % i, [128, 512], F32) for i in range(7)]
        self.load(id32[:], self.c_ident, writes=["id32"])
        self.load(Vall[:], self.Vd.rearrange("(c p) f -> p c f", p=128), writes=["Vall"])
        blocks = [(hh, qi) for hh in range(8) for qi in range(nqb)]
        stream = [(bi, kc) for bi in range(len(blocks)) for kc in range(NT)]

        def load_k(hh):
            self.load(KTh[hh % 2][:], self.KT[hh], writes=["KTh%d" % (hh % 2)])

        def load_q(bi):
            hh, qi = blocks[bi]
            self.load(QTb[bi % 2][:], self.QT[hh, :, qi * QB:(qi + 1) * QB], writes=["QTb%d" % (bi % 2)])

        def emit_S(idx):
            bi, kc = stream[idx]
            hh, qi = blocks[bi]
            pb = idx % 4
            self.mm(q[pb][:, 0:QB], KTh[hh % 2][:, kc * 128:(kc + 1) * 128], QTb[bi % 2][:], True, True,
                    reads=["KTh%d" % (hh % 2), "QTb%d" % (bi % 2)], writes=["q%d" % pb])

        def epilogue_a(bi):
            ob = 4 + bi % 2
            self.cp("dve", OT[:], q[ob][0:65, 0:QB], reads=["q%d" % ob], writes=["OT"])

        def epilogue_b(bi):
            hh, qi = blocks[bi]
            for j in range(nj):
                self.tr(q[6][:, j * 65:(j + 1) * 65], OT[:, j * 128:(j + 1) * 128], id32[0:65, 0:65],
                        reads=["OT", "id32"], writes=["q6"])
            o3 = q[6][:, 0:nj * 65].rearrange("p (j e) -> p j e", j=nj)
            self.recip(rec[:], o3[:, :, 64], reads=["q6"], writes=["rec"])
            self.tt("dve", osm[:], o3[:, :, 0:64], rec[:].unsqueeze(2).broadcast_to([128, nj, 64]), ALU.mult,
                    reads=["q6", "rec"], writes=["osm"])
            self.store(self.yb[qi * QB:(qi + 1) * QB, hh * 64:(hh + 1) * 64].rearrange("(j p) e -> p j e", p=128),
                       osm[:], reads=["osm"], writes=[("yb", hh, qi)])

        load_k(0)
        load_q(0)
        if len(blocks) > 1:
            load_q(1)
        for idx in range(min(LOOK, len(stream))):
            emit_S(idx)
        pending = None
        for idx, (bi, kc) in enumerate(stream):
            hh, qi = blocks[bi]
            if kc == 0:
                if qi == 0 and hh + 1 < 8:
                    load_k(hh + 1)
            if idx + LOOK < len(stream):
                emit_S(idx + LOOK)
            pb = idx % 4
            ob = 4 + bi % 2
            self.act(PT[pb][:], q[pb][:, 0:QB], AF.Exp, reads=["q%d" % pb], writes=["PT%d" % pb], scale=SCALE)
            self.mm(q[ob][0:65, 0:QB], Vall[:, kc, hh * 65:(hh + 1) * 65], PT[pb][:], kc == 0, kc == NT - 1,
                    reads=["Vall", "PT%d" % pb], writes=["q%d" % ob])
            if pending is not None and kc == min(3, NT - 1):
                epilogue_b(pending)
                pending = None
            if kc == NT - 1:
                epilogue_a(bi)
                pending = bi
                if bi + 2 < len(blocks):
                    load_q(bi + 2)
        if pending is not None:
            epilogue_b(pending)
        self.end_phase()

    def phase_C1(self, l, xin):
        NT = self.NT
        W = self.w
        self.begin_phase()
        sb, ps = self.sb, self.ps
        woa = sb("woa", [128, 4, D], BF16)
        wob = sb("wob", [128, 4, D], BF16)
        wout = sb("wout", [128, 8, D], BF16)
        idb = sb("idb", [128, 128], BF16)
        xt = [sb("xt", [128, D], F32) for _ in range(2)]
        gt = [sb("gt", [128, 2048], F32) for _ in range(2)]
        yat = [sb("yat", [128, 512], F32) for _ in range(2)]
        ybt = [sb("ybt", [128, 512], F32) for _ in range(2)]
        yab = sb("yab", [128, D], BF16)
        yT = sb("yT", [128, 8, 128], BF16)
        m1 = sb("m1", [128, D], F32)
        m2 = sb("m2", [128, D], F32)
        mixb = sb("mixb", [128, D], BF16)
        mixT = sb("mixT", [128, 8, 128], BF16)
        x1t = sb("x1t", [128, D], F32)
        q = [ps("q%d" % i, [128, 512], F32) for i in range(7)]
        qTb = q[6][:].bitcast(BF16)
        self.load(idb[:], self.c_ident, writes=["idb"], cast=True)
        self.load_w_bf16(woa, W["w_oa"][l], 512, "woa")
        self.load_w_bf16(wob, W["w_ob"][l], 512, "wob")
        self.load_w_bf16(wout, W["w_out"][l], D, "wout")
        for i in range(NT):
            b = i % 2
            t0 = i * 128
            self.load(xt[b][:], xin[t0:t0 + 128, :], writes=["xt%d" % b])
            self.load(gt[b][:], self.P[t0:t0 + 128, 0:2048], writes=["gt%d" % b])
            self.load(yat[b][:], self.ya[t0:t0 + 128, :], writes=["yat%d" % b])
            self.load(ybt[b][:], self.yb[t0:t0 + 128, :], writes=["ybt%d" % b])
            self.cp("dve", yab[:, 0:512], yat[b][:], reads=["yat%d" % b], writes=["yab0"])
            self.cp("pool", yab[:, 512:1024], ybt[b][:], reads=["ybt%d" % b], writes=["yab1"])
            for k in range(8):
                self.tr(qTb[:, k * 128:(k + 1) * 128], yab[:, k * 128:(k + 1) * 128], idb[:],
                        reads=["yab0", "yab1", "idb"], writes=["q6"])
            self.cp("act", yT[:].rearrange("p a b -> p (a b)"), qTb[:], reads=["q6"], writes=["yT"])
            for hf in range(2):
                hs = slice(hf * 512, (hf + 1) * 512)
                for k in range(4):
                    self.mm(q[hf][:], yT[:, k, :], woa[:, k, hs], k == 0, k == 3, reads=["yT", "woa"], writes=["q%d" % hf])
                for k in range(4):
                    self.mm(q[2 + hf][:], yT[:, 4 + k, :], wob[:, k, hs], k == 0, k == 3, reads=["yT", "wob"],
                            writes=["q%d" % (2 + hf)])
            self.act(gt[b][:], gt[b][:], AF.Sigmoid, reads=["gt%d" % b], writes=["gt%d" % b])
            for hf in range(2):
                hs = slice(hf * 512, (hf + 1) * 512)
                hs2 = slice(1024 + hf * 512, 1024 + (hf + 1) * 512)
                self.tt("dve", m1[:, hs], gt[b][:, hs], q[hf][:], ALU.mult, reads=["gt%d" % b, "q%d" % hf], writes=[("m1", hf)])
                self.tt("dve", m2[:, hs], gt[b][:, hs2], q[2 + hf][:], ALU.mult, reads=["gt%d" % b, "q%d" % (2 + hf)],
                        writes=[("m2", hf)])
                self.tt("pool", mixb[:, hs], m1[:, hs], m2[:, hs], ALU.add, reads=[("m1", hf), ("m2", hf)], writes=[("mixb", hf)])
            for k in range(8):
                self.tr(qTb[:, k * 128:(k + 1) * 128], mixb[:, k * 128:(k + 1) * 128], idb[:],
                        reads=[("mixb", 0), ("mixb", 1), "idb"], writes=["q6"])
            self.cp("act", mixT[:].rearrange("p a b -> p (a b)"), qTb[:], reads=["q6"], writes=["mixT"])
            for hf in range(2):
                hs = slice(hf * 512, (hf + 1) * 512)
                for k in range(8):
                    self.mm(q[4 + hf][:], mixT[:, k, :], wout[:, k, hs], k == 0, k == 7, reads=["mixT", "wout"],
                            writes=["q%d" % (4 + hf)])
                self.tt("dve", x1t[:, hs], xt[b][:, hs], q[4 + hf][:], ALU.add, reads=["xt%d" % b, "q%d" % (4 + hf)],
                        writes=[("x1t", hf)])
            self.store(self.x1[t0:t0 + 128, :], x1t[:], reads=[("x1t", 0), ("x1t", 1)], writes=[("x1", i)])
        self.end_phase()

    def phase_C2(self, l, last, yout):
        NT = self.NT
        W = self.w
        self.begin_phase()
        sb, ps = self.sb, self.ps
        wgu = sb("wgu", [128, 8, 2 * DFF], BF16)
        wdn = sb("wdn", [128, 22, D], BF16)
        gbc = sb("gbc", [128, D], F32)
        idb = sb("idb", [128, 128], BF16)
        xt = [sb("xt", [128, D], F32) for _ in range(2)]
        junk = sb("junk", [128, D], F32)
        ss = sb("ss", [128, 1], F32)
        rstd = sb("rstd", [128, 1], F32)
        h = sb("h", [128, D], BF16)
        hT = [sb("hT", [128, 8, 128], BF16) for _ in range(2)]
        sl = [sb("sl", [128, 256], F32) for _ in range(2)]
        actb = sb("actb", [128, DFF], BF16)
        actT = sb("actT", [128, 22, 128], BF16)
        x2t = sb("x2t", [128, D], F32)
        if last:
            fbc = sb("fbc", [128, D], F32)
        q = [ps("q%d" % i, [128, 512], F32) for i in range(6)]
        qTb = q[4][:].bitcast(BF16)
        qT2 = q[5][:].bitcast(BF16)
        self.load(idb[:], self.c_ident, writes=["idb"], cast=True)
        self.bcast_load(gbc, W["norm_ffn_g"][l:l + 1, :], D, "gbc")
        if last:
            self.bcast_load(fbc, W["final_norm_g"][0:1, :], D, "fbc")
        self.load_w_bf16(wgu, W["w_gu"][l], D, "wgu")
        self.load_w_bf16(wdn, W["w_down"][l], DFF, "wdn")

        def norm(i):
            b = i % 2
            self.load(xt[b][:], self.x1[i * 128:(i + 1) * 128, :], writes=["xt%d" % b])
            self.rmsnorm(xt[b][:], "xt%d" % b, D, gbc[:], "gbc", h[:], "h", junk[:], ss[:], rstd[:], "F")

        def trans(i):
            b = i % 2
            for k in range(8):
                self.tr(qTb[:, k * 128:(k + 1) * 128], h[:, k * 128:(k + 1) * 128], idb[:], reads=["h", "idb"], writes=["q4"])
            self.cp("act", hT[b][:].rearrange("p a b -> p (a b)"), qTb[:], reads=["q4"], writes=["hT%d" % b])

        def tpose(j):
            o = (j % 4) * 256
            for u in range(2):
                self.tr(qT2[:, o + u * 128:o + (u + 1) * 128], actb[:, j * 256 + u * 128:j * 256 + (u + 1) * 128], idb[:],
                        reads=[("actb", j), "idb"], writes=[("q5", j % 4)])
            self.cp("dve", actT[:, 2 * j:2 * j + 2, :].rearrange("p a b -> p (a b)"),
                    qT2[:, o:o + 256], reads=[("q5", j % 4)], writes=[("actT", j)])

        norm(0)
        trans(0)
        for i in range(NT):
            b = i % 2
            t0 = i * 128
            xn = "xt%d" % b
            hn = "hT%d" % b
            if i + 1 < NT:
                norm(i + 1)
            for j in range(11):
                bk = q[j % 2]
                bn = "q%d" % (j % 2)
                for k in range(8):
                    self.mm(bk[:, 0:256], hT[b][:, k, :], wgu[:, k, j * 256:(j + 1) * 256], k == 0, k == 7,
                            reads=[hn, "wgu"], writes=[bn])
                for k in range(8):
                    self.mm(bk[:, 256:512], hT[b][:, k, :], wgu[:, k, DFF + j * 256:DFF + (j + 1) * 256], k == 0, k == 7,
                            reads=[hn, "wgu"], writes=[bn])
                self.act(sl[j % 2][:], bk[:, 0:256], AF.Silu, reads=[bn], writes=["sl%d" % (j % 2)])
                self.tt("dve", actb[:, j * 256:(j + 1) * 256], sl[j % 2][:], bk[:, 256:512], ALU.mult,
                        reads=["sl%d" % (j % 2), bn], writes=[("actb", j)])
                if j >= 1:
                    tpose(j - 1)
            tpose(10)
            if i + 1 < NT:
                trans(i + 1)
            for hf in range(2):
                hs = slice(hf * 512, (hf + 1) * 512)
                for c in range(22):
                    self.mm(q[2 + hf][:], actT[:, c, :], wdn[:, c, hs], c == 0, c == 21,
                            reads=[("actT", c // 2), "wdn"], writes=["q%d" % (2 + hf)])
                self.tt("dve", x2t[:, hs], xt[b][:, hs], q[2 + hf][:], ALU.add, reads=[xn, "q%d" % (2 + hf)],
                        writes=["x2t"])
            if last:
                self.rmsnorm(x2t[:], "x2t", D, fbc[:], "fbc", x2t[:], "x2t", junk[:], ss[:], rstd[:], "F")
                self.store(yout[t0:t0 + 128, :], x2t[:], reads=["x2t"], writes=[("y", i)])
            else:
                self.store(self.x2[t0:t0 + 128, :], x2t[:], reads=["x2t"], writes=[("x2", i)])
        self.end_phase()

    def build(self, phases=None):
        def on(p):
            return phases is None or p in phases
        for s in range(self.NSEQ):
            for l in range(self.depth):
                last = l == self.depth - 1
                xin = self.x[s] if l == 0 else self.x2
                if on("A"):
                    self.phase_A(l, xin)
                if on("R0"):
                    self.phase_R(l, 0)
                if on("R1"):
                    self.phase_R(l, 1)
                if on("MP"):
                    self.phase_MP(l)
                if on("MM"):
                    self.phase_MM()
                if on("C1"):
                    self.phase_C1(l, xin)
                if on("C2"):
                    self.phase_C2(l, last, self.y[s])
        self.S.emit()
        self.S.stack.close()
        return self.nc


def make_consts(S_LEN):
    s = np.arange(128)[:, None]
    t = np.arange(128)[None, :]
    tri = np.stack([(s <= t), (s >= t)]).astype(np.float32)
    strict = [(s < t).astype(np.float32), (s > t).astype(np.float32)]
    incl = [(s <= t).astype(np.float32), (s >= t).astype(np.float32)]
    m4 = np.stack([np.concatenate([strict[d], incl[d], strict[d], incl[d]], axis=1) for d in range(2)])
    mn = [(t < s).astype(np.float32), (t > s).astype(np.float32)]
    mn4 = np.stack([np.concatenate([mn[d]] * 4, axis=1) for d in range(2)])
    pos = np.arange(S_LEN, dtype=np.float32)
    inv_freq = (1.0 / (np.float32(10000.0) ** (np.arange(0, 32, 2, dtype=np.float32) / np.float32(32)))).astype(np.float32)
    ang = pos[:, None] * inv_freq[None, :]
    ang = np.concatenate([ang, ang], axis=-1).astype(np.float32)
    cos = np.cos(ang).astype(np.float32)
    sin = np.sin(ang).astype(np.float32)
    sin_s = sin.copy()
    sin_s[:, 0:16] = -sin_s[:, 0:16]
    return dict(c_ident=np.eye(128, dtype=np.float32), c_tri=tri, c_m4=m4.astype(np.float32),
                c_mn4=mn4.astype(np.float32), c_ones=np.ones((128, 128), np.float32),
                c_cos=cos, c_sin=sin_s)


_WNAMES = ["norm_mix_g", "w_in", "shift_mu", "decay_w2", "decay_w0", "iclr_a2", "iclr_a0", "gate_g2", "k_k", "k_a",
           "r_k", "gn_g", "gn_b", "w_oa", "q_norm_g", "w_uq", "kv_norm_g", "w_ukv", "w_ob", "w_out", "norm_ffn_g",
           "w_gu", "w_down", "final_norm_g"]


def prep_weights(inputs, depth):
    out = {}
    for n in _WNAMES:
        a = np.ascontiguousarray(np.asarray(inputs[n], dtype=np.float32))
        if n == "r_k":
            a = a.reshape(a.shape[0], 512)
        if n == "final_norm_g":
            a = a.reshape(1, D)
        else:
            a = a[:depth]
        out[n] = np.ascontiguousarray(a)
    return out


def kernel(**inputs):
    xp = np.asarray(inputs["x_prompt"], dtype=np.float32)
    xs = np.asarray(inputs["x_sample"], dtype=np.float32)
    S_LEN = xp.shape[1]
    x_all = np.concatenate([xp, xs], axis=0)
    nseq = x_all.shape[0] // NCORES
    wts = prep_weights(inputs, DEPTH)
    consts = make_consts(S_LEN)
    nc = Builder(S_LEN, nseq, DEPTH).build()
    in_maps = []
    for c in range(NCORES):
        m = dict(x=np.ascontiguousarray(x_all[c * nseq:(c + 1) * nseq]))
        m.update(wts)
        m.update(consts)
        in_maps.append(m)
    res = run_bass_kernel_spmd(nc, in_maps, core_ids=list(range(NCORES)))
    y = np.concatenate([r["y"] for r in res.results], axis=0)
    return (np.ascontiguousarray(y[:xp.shape[0]]), np.ascontiguousarray(y[xp.shape[0]:]))
```

```python
import contextlib
import os
import numpy as np
import concourse.bass as bass
import concourse.mybir as mybir
from concourse.alu_op_type import AluOpType as ALU
from concourse.bass_utils import run_bass_kernel_spmd

F32 = mybir.dt.float32
BF16 = mybir.dt.bfloat16
AF = mybir.ActivationFunctionType
AX = mybir.AxisListType

D = 1024
NIN = 4416
DFF = 2816
DEPTH = 2
NCORES = 8
SEQ_FULL = 4096
RMS_EPS = 1e-6
GN_EPS = 64e-5
CDEC = 0.6065306597126334
SCALE = 96.0 ** -0.5

ENGS = ("pe", "act", "dve", "pool", "sp")
N_DMA_SEMS = 8
SAME_ENGINE_SYNC = True


def _is_psum(r):
    n = r[0] if isinstance(r, tuple) else r
    return isinstance(n, str) and len(n) >= 2 and n[0] in "qp" and (n[1].isdigit() or n[1] in "TP")


class Sched:
    def __init__(self, nc):
        self.nc = nc
        self.q = {e: [] for e in ENGS}
        self.cnt = {e: 0 for e in ENGS}
        self.seen = {e: {} for e in ENGS}
        self.last_w = {}
        self.readers = {}
        self.dma_val = {}
        self.dma_rr = {e: 0 for e in ENGS}
        self.stack = contextlib.ExitStack()
        self.sems = {}
        self.nops = 0
        self.limit = int(os.environ.get("OPLIMIT", "1000000000"))
        self.marks = []

    def mark(self, label):
        self.marks.append((label, self.nops))

    def _deps(self, eng, reads, writes):
        deps = []
        for r in reads:
            ev = self.last_w.get(r)
            if ev is not None:
                deps.append(ev)
            if eng != "pe" and _is_psum(r):
                deps.extend(e2 for e2 in self.readers.get(r, ()) if e2[0] != eng)
        for w in writes:
            ev = self.last_w.get(w)
            if ev is not None:
                deps.append(ev)
            deps.extend(self.readers.get(w, ()))
        waits = {}
        seen = self.seen[eng]
        for sk, v in deps:
            if sk == eng and (eng == "pe" or not SAME_ENGINE_SYNC):
                continue
            if seen.get(sk, 0) >= v:
                continue
            if waits.get(sk, 0) < v:
                waits[sk] = v
        for sk, v in waits.items():
            seen[sk] = v
        return waits

    def _record(self, ev, reads, writes):
        for r in reads:
            self.readers.setdefault(r, []).append(ev)
        for w in writes:
            self.last_w[w] = ev
            self.readers[w] = []

    def op(self, eng, fn, reads=(), writes=()):
        self.nops += 1
        if self.nops > self.limit:
            return
        waits = self._deps(eng, reads, writes)
        self.cnt[eng] += 1
        ev = (eng, self.cnt[eng])
        self.q[eng].append((list(waits.items()), fn, (eng, 1)))
        self._record(ev, reads, writes)

    def dma(self, eng, fn, reads=(), writes=()):
        self.nops += 1
        if self.nops > self.limit:
            return
        k = self.dma_rr[eng]
        self.dma_rr[eng] = (k + 1) % N_DMA_SEMS
        sk = ("dma", eng, k)
        prev = self.dma_val.get(sk, 0)
        waits = self._deps(eng, reads, writes)
        if prev > 0 and self.seen[eng].get(sk, 0) < prev:
            waits[sk] = prev
            self.seen[eng][sk] = prev
        self.dma_val[sk] = prev + 16
        ev = (sk, prev + 16)
        self.q[eng].append((list(waits.items()), fn, (sk, 16)))
        self._record(ev, reads, writes)

    def barrier(self):
        tgt = {e: self.cnt[e] for e in ENGS if self.cnt[e] > 0}
        tgt.update(self.dma_val)
        for e in ENGS:
            waits = []
            for sk, v in tgt.items():
                if sk == e:
                    continue
                if self.seen[e].get(sk, 0) < v:
                    waits.append((sk, v))
                    self.seen[e][sk] = v
            if waits:
                self.q[e].append((waits, None, None))
        self.last_w = {}
        self.readers = {}

    def emit(self):
        nc = self.nc
        st = self.stack
        keys = list(ENGS) + list(self.dma_val)
        for sk in keys:
            nm = sk if isinstance(sk, str) else "d_%s_%d" % (sk[1], sk[2])
            self.sems[sk] = st.enter_context(nc.semaphore("s_" + nm))
        final = list(self.dma_val.items())
        block = st.enter_context(nc.Block())
        sems = self.sems

        def run(engname, final_waits=()):
            def body(e):
                for waits, fn, inc in self.q[engname]:
                    for sk, v in waits:
                        e.wait_ge(sems[sk], v)
                    if fn is not None:
                        fn(e).then_inc(sems[inc[0]], inc[1])
                for sk, v in final_waits:
                    e.wait_ge(sems[sk], v)
            return body

        block.tensor(run("pe"))
        block.scalar(run("act"))
        block.vector(run("dve"))
        block.gpsimd(run("pool"))
        block.sync(run("sp", final))


class Builder:
    def __init__(self, S_LEN, NSEQ, depth=DEPTH):
        self.S_LEN = S_LEN
        self.NSEQ = NSEQ
        self.depth = depth
        self.NT = S_LEN // 128
        nc = bass.Bass("TRN2", target_bir_lowering=False)
        self.nc = nc
        self.S = Sched(nc)
        self.ph = None
        self._uid = 0

        def inp(name, shape):
            return nc.dram_tensor(name, list(shape), F32, kind="ExternalInput").ap()

        L = depth
        self.x = inp("x", [NSEQ, S_LEN, D])
        self.w = dict(
            norm_mix_g=inp("norm_mix_g", [L, D]), w_in=inp("w_in", [L, D, NIN]),
            shift_mu=inp("shift_mu", [L, 2, 1952]), decay_w2=inp("decay_w2", [L, 2, 64, 512]),
            decay_w0=inp("decay_w0", [L, 2, 512]), iclr_a2=inp("iclr_a2", [L, 2, 64, 512]),
            iclr_a0=inp("iclr_a0", [L, 2, 512]), gate_g2=inp("gate_g2", [L, 160, 512]),
            k_k=inp("k_k", [L, 512]), k_a=inp("k_a", [L, 512]), r_k=inp("r_k", [L, 512]),
            gn_g=inp("gn_g", [L, 512]), gn_b=inp("gn_b", [L, 512]), w_oa=inp("w_oa", [L, 512, D]),
            q_norm_g=inp("q_norm_g", [L, 256]), w_uq=inp("w_uq", [L, 256, 768]),
            kv_norm_g=inp("kv_norm_g", [L, 128]), w_ukv=inp("w_ukv", [L, 128, 1024]),
            w_ob=inp("w_ob", [L, 512, D]), w_out=inp("w_out", [L, D, D]),
            norm_ffn_g=inp("norm_ffn_g", [L, D]), w_gu=inp("w_gu", [L, D, 2 * DFF]),
            w_down=inp("w_down", [L, DFF, D]), final_norm_g=inp("final_norm_g", [1, D]),
        )
        self.c_ident = inp("c_ident", [128, 128])
        self.c_tri = inp("c_tri", [2, 128, 128])
        self.c_m4 = inp("c_m4", [2, 128, 512])
        self.c_mn4 = inp("c_mn4", [2, 128, 512])
        self.c_ones = inp("c_ones", [128, 128])
        self.c_cos = inp("c_cos", [S_LEN, 32])
        self.c_sin = inp("c_sin", [S_LEN, 32])
        self.y = nc.dram_tensor("y", [NSEQ, S_LEN, D], F32, kind="ExternalOutput").ap()
        def scr(name, shape, dt=F32):
            return nc.dram_tensor(name, list(shape), dt).ap()
        self.P = scr("scr_P", [S_LEN, NIN])
        self.of = scr("scr_of", [S_LEN, 512])
        self.PSd = scr("scr_PS", [S_LEN, 1952])
        self.KKd = scr("scr_KK", [S_LEN, 512])
        self.ya = scr("scr_ya", [S_LEN, 512])
        self.yb = scr("scr_yb", [S_LEN, 512])
        self.x1 = scr("scr_x1", [S_LEN, D])
        self.x2 = scr("scr_x2", [S_LEN, D])
        self.QT = scr("scr_QT", [8, 96, S_LEN], BF16)
        self.KT = scr("scr_KT", [8, 96, S_LEN], BF16)
        self.Vd = scr("scr_V", [S_LEN, 8 * 65], BF16)

    def begin_phase(self):
        self.ph = contextlib.ExitStack()

    def end_phase(self):
        self.S.barrier()
        self.ph.close()
        self.ph = None

    def sb(self, name, shape, dt):
        self._uid += 1
        return self.ph.enter_context(self.nc.sbuf_tensor("%s_%d" % (name, self._uid), list(shape), dt))

    def ps(self, name, shape, dt):
        self._uid += 1
        return self.ph.enter_context(self.nc.psum_tensor("%s_%d" % (name, self._uid), list(shape), dt))

    def load(self, out_ap, in_ap, writes, reads=(), cast=False):
        eng = "pool" if cast else "sp"
        self.S.dma(eng, lambda e: e.dma_start(out=out_ap, in_=in_ap), reads=reads, writes=writes)

    def store(self, out_ap, in_ap, reads, writes):
        self.S.dma("sp", lambda e: e.dma_start(out=out_ap, in_=in_ap), reads=reads, writes=writes)

    def bcast_load(self, tile, row_ap, width, name):
        self.load(tile[:], row_ap.broadcast_to([128, width]), writes=[name])

    def load_w_bf16(self, tile, w_ap, K, name):
        for k in range(K // 128):
            self.load(tile[:, k, :], w_ap[k * 128:(k + 1) * 128, :], writes=[name], cast=True)

    def mm(self, out, lhsT, rhs, start, stop, reads, writes):
        self.S.op("pe", lambda e: e.matmul(out=out, lhsT=lhsT, rhs=rhs, start=start, stop=stop),
                  reads=reads, writes=writes)

    def tr(self, out, in_, ident, reads, writes):
        self.S.op("pe", lambda e: e.transpose(out=out, in_=in_, identity=ident), reads=reads, writes=writes)

    def act(self, out, in_, func, reads, writes, scale=None, bias=None, accum_out=None):
        kw = {}
        if scale is not None:
            kw["scale"] = scale
        if bias is not None:
            kw["bias"] = bias
        if accum_out is not None:
            kw["accum_out"] = accum_out
        self.S.op("act", lambda e: e.activation(out=out, in_=in_, func=func, **kw), reads=reads, writes=writes)

    def tt(self, eng, out, in0, in1, op, reads, writes):
        self.S.op(eng, lambda e: e.tensor_tensor(out=out, in0=in0, in1=in1, op=op), reads=reads, writes=writes)

    def ts(self, out, in0, s1, s2, op0, op1, reads, writes, eng="dve"):
        self.S.op(eng, lambda e: e.tensor_scalar(out=out, in0=in0, scalar1=s1, scalar2=s2, op0=op0, op1=op1),
                  reads=reads, writes=writes)

    def stt(self, out, in0, scalar, in1, op0, op1, reads, writes):
        self.S.op("dve", lambda e: e.scalar_tensor_tensor(out=out, in0=in0, scalar=scalar, in1=in1, op0=op0, op1=op1),
                  reads=reads, writes=writes)

    def cp(self, eng, out, in_, reads, writes):
        if eng == "act":
            self.S.op("act", lambda e: e.activation(out=out, in_=in_, func=AF.Copy), reads=reads, writes=writes)
        else:
            self.S.op(eng, lambda e: e.tensor_copy(out=out, in_=in_), reads=reads, writes=writes)

    def red(self, out, in_, reads, writes):
        self.S.op("dve", lambda e: e.tensor_reduce(out=out, in_=in_, axis=AX.X, op=ALU.add), reads=reads, writes=writes)

    def recip(self, out, in_, reads, writes):
        self.S.op("dve", lambda e: e.reciprocal(out=out, in_=in_), reads=reads, writes=writes)

    def memset(self, eng, ap, val, writes):
        self.S.op(eng, lambda e: e.memset(ap, val), writes=writes)

    def rmsnorm(self, x_ap, xn, width, gbc_ap, gn, out_ap, outn, junk, ss, rstd, tag):
        jn, sn, rn = "junk" + tag, "ss" + tag, "rstd" + tag
        self.act(junk, x_ap, AF.Square, reads=[xn], writes=[jn, sn], accum_out=ss)
        self.ts(rstd, ss, 1.0 / width, RMS_EPS, ALU.mult, ALU.add, reads=[sn], writes=[rn])
        self.act(rstd, rstd, AF.Sqrt, reads=[rn], writes=[rn])
        self.recip(rstd, rstd, reads=[rn], writes=[rn])
        self.stt(out_ap, x_ap, rstd, gbc_ap, ALU.mult, ALU.mult, reads=[xn, rn, gn], writes=[outn])

    def phase_A(self, l, xin):
        NT = self.NT
        self.begin_phase()
        wA = self.sb("wA", [128, 8, NIN], BF16)
        gbc = self.sb("gA", [128, D], F32)
        idb = self.sb("idb", [128, 128], BF16)
        junk = self.sb("junk", [128, D], F32)
        ss = self.sb("ss", [128, 1], F32)
        rstd = self.sb("rstd", [128, 1], F32)
        xt = [self.sb("xt", [128, D], F32) for _ in range(2)]
        h = [self.sb("h", [128, D], BF16) for _ in range(2)]
        hT = [self.sb("hT", [128, 8, 128], BF16) for _ in range(2)]
        Pt = [self.sb("Pt", [128, NIN], F32) for _ in range(2)]
        pT = [self.ps("pT", [128, 8, 128], BF16) for _ in range(2)]
        pP = [self.ps("pP", [128, 512], F32) for _ in range(4)]
        self.load(idb[:], self.c_ident, writes=["idb"], cast=True)
        self.bcast_load(gbc, self.w["norm_mix_g"][l:l + 1, :], D, "gA")
        self.load_w_bf16(wA, self.w["w_in"][l], D, "wA")
        npieces = (NIN + 511) // 512

        def norm(i):
            b = i % 2
            self.load(xt[b][:], xin[i * 128:(i + 1) * 128, :], writes=["xt%d" % b])
            self.rmsnorm(xt[b][:], "xt%d" % b, D, gbc[:], "gA", h[b][:], "h%d" % b, junk[:], ss[:], rstd[:], "A")

        def trans(i):
            b = i % 2
            for k in range(8):
                self.tr(pT[b][:, k, :], h[b][:, k * 128:(k + 1) * 128], idb[:], reads=["h%d" % b, "idb"], writes=["pT%d" % b])
            self.cp("act", hT[b][:], pT[b][:], reads=["pT%d" % b], writes=["hT%d" % b])

        norm(0)
        trans(0)
        for i in range(NT):
            b = i % 2
            if i + 1 < NT:
                norm(i + 1)
            for j in range(npieces):
                n0 = j * 512
                n = min(512, NIN - n0)
                pp = pP[j % 4]
                pn = "pP%d" % (j % 4)
                for k in range(8):
                    self.mm(pp[:, 0:n], hT[b][:, k, :], wA[:, k, n0:n0 + n], k == 0, k == 7,
                            reads=["hT%d" % b, "wA"], writes=[pn])
                self.cp("dve" if j % 2 == 0 else "act", Pt[b][:, n0:n0 + n], pp[:, 0:n],
                        reads=[pn], writes=[("Pt", b, j)])
                if j == 5 and i + 1 < NT:
                    trans(i + 1)
            self.store(self.P[i * 128:(i + 1) * 128, :], Pt[b][:], reads=[("Pt", b, j) for j in range(npieces)],
                       writes=[("P", i)])
        self.end_phase()

    def phase_R(self, l, d):
        NT = self.NT
        S_LEN = self.S_LEN
        W = self.w
        self.begin_phase()
        sb, ps = self.sb, self.ps
        if d == 0:
            mu0 = sb("mu0", [128, 1952], F32)
            mu1 = sb("mu1", [128, 1952], F32)
            c0 = sb("c0", [128, 1952], F32)
            pap = sb("pap", [128, 1952], F32)
            pan = sb("pan", [128, 1952], F32)
            kkbc = sb("kkbc", [128, 512], F32)
        w0bc = sb("w0bc", [128, 512], F32)
        a0bc = sb("a0bc", [128, 512], F32)
        kabc = sb("kabc", [128, 512], F32)
        w2b = sb("w2b", [64, 512], BF16)
        a2b = sb("a2b", [64, 512], BF16)
        tri = sb("tri", [128, 128], F32)
        ones = sb("ones", [128, 128], F32)
        m4 = sb("m4", [128, 512], F32)
        mn4 = sb("mn4", [128, 512], F32)
        idb = sb("idb", [128, 128], BF16)
        pac = [sb("pac", [128, 1952], F32) for _ in range(2)]
        lo = sb("lo", [128, 128], BF16)
        loT = sb("loT", [64, 2, 128], BF16)
        sgm = sb("sgm", [128, 512], F32)
        av = sb("av", [128, 512], F32)
        kk = sb("kk", [128, 512], F32)
        tmp = sb("tmp", [128, 512], F32)
        kd = sb("kd", [128, 512], F32)
        ka = sb("ka", [128, 512], F32)
        Ls = sb("Ls", [128, 512], F32)
        Ld = sb("Ld", [128, 512], F32)
        E1 = sb("E1", [128, 512], F32)
        E2 = sb("E2", [128, 512], F32)
        E3 = sb("E3", [128, 512], F32)
        E4 = sb("E4", [128, 512], F32)
        ssq = sb("ssq", [128, 8], F32)
        gC = [sb("gC", [64, 8], F32) for _ in range(2)]
        Ab = [sb("Ab", [128, 512], BF16) for _ in range(2)]
        Rb = [sb("Rb", [128, 512], BF16) for _ in range(2)]
        Bb = [sb("Bb", [128, 512], BF16) for _ in range(2)]
        Kb = [sb("Kb", [128, 512], BF16) for _ in range(2)]
        Btb = [sb("Btb", [128, 512], BF16) for _ in range(2)]
        Ktb = [sb("Ktb", [128, 512], BF16) for _ in range(2)]
        Vb = [sb("Vb", [128, 512], BF16) for _ in range(2)]
        ART = [sb("ART", [128, 8, 256], BF16) for _ in range(2)]
        BT = [sb("BT", [64, 8, 128], BF16) for _ in range(2)]
        KTt = [sb("KTt", [64, 8, 128], BF16) for _ in range(2)]
        ATall = sb("ATall", [128, 8, 512], BF16)
        PP = [sb("PP", [128, 8, 256], BF16) for _ in range(2)]
        Wb = sb("Wb", [128, 512], BF16)
        osb = sb("osb", [128, 512], F32)
        ST32 = sb("ST32", [64, 8, 64], F32)
        STb = sb("STb", [128, 8, 64], BF16)
        if d == 1:
            rkbc = sb("rkbc", [128, 512], F32)
            gngbc = sb("gngbc", [128, 512], F32)
            gnbbc = sb("gnbbc", [128, 512], F32)
            g2b = sb("g2b", [128, 2, 512], BF16)
            oft = sb("oft", [128, 512], F32)
            cen = sb("cen", [128, 512], F32)
            sq2 = sb("sq2", [128, 512], F32)
            bon = sb("bon", [128, 512], F32)
            st8 = sb("st8", [128, 8], F32)
            sv8 = sb("sv8", [128, 8], F32)
            sb8 = sb("sb8", [128, 8], F32)
            gs = sb("gs", [128, 160], BF16)
            gT = sb("gT", [128, 2, 128], BF16)
            yat = sb("yat", [128, 512], F32)
        pbk = [ps("pq%d" % i, [128, 512], F32) for i in range(3)]
        pbn = ["pq0", "pq1", "pq2"]
        pbb = [t[:].bitcast(BF16) for t in pbk]
        sbk = [ps("sq%d" % i, [128, 512], F32) for i in range(5)]
        sbn = ["sq0", "sq1", "sq2", "sq3", "sq4"]
        s3, s4 = sbk[3], sbk[4]
        s3b = s3[:].bitcast(BF16)

        self.load(idb[:], self.c_ident, writes=["idb"], cast=True)
        self.load(tri[:], self.c_tri[d], writes=["tri"])
        self.load(ones[:], self.c_ones, writes=["ones"])
        self.load(m4[:], self.c_m4[d], writes=["m4"])
        self.load(mn4[:], self.c_mn4[d], writes=["mn4"])
        if d == 0:
            self.bcast_load(mu0, W["shift_mu"][l, 0:1, :], 1952, "mu0")
            self.bcast_load(mu1, W["shift_mu"][l, 1:2, :], 1952, "mu1")
            self.bcast_load(kkbc, W["k_k"][l:l + 1, :], 512, "kkbc")
            self.tt("dve", c0[:], mu0[:], mu1[:], ALU.add, reads=["mu0", "mu1"], writes=["c0"])
            self.ts(c0[:], c0[:], -1.0, 1.0, ALU.mult, ALU.add, reads=["c0"], writes=["c0"])
        self.bcast_load(w0bc, W["decay_w0"][l, d:d + 1, :], 512, "w0bc")
        self.bcast_load(a0bc, W["iclr_a0"][l, d:d + 1, :], 512, "a0bc")
        self.bcast_load(kabc, W["k_a"][l:l + 1, :], 512, "kabc")
        self.load(w2b[:], W["decay_w2"][l, d], writes=["w2b"], cast=True)
        self.load(a2b[:], W["iclr_a2"][l, d], writes=["a2b"], cast=True)
        self.memset("dve", ST32[:], 0.0, writes=["ST32"])
        self.memset("dve", STb[:], 0.0, writes=["STb"])
        for b in range(2):
            self.memset("dve", ART[b][:], 0.0, writes=["ART%d" % b])
        if d == 1:
            self.bcast_load(rkbc, W["r_k"][l:l + 1, :], 512, "rkbc")
            self.bcast_load(gngbc, W["gn_g"][l:l + 1, :], 512, "gngbc")
            self.bcast_load(gnbbc, W["gn_b"][l:l + 1, :], 512, "gnbbc")
            self.memset("dve", gT[:], 0.0, writes=["gT"])
            self.memset("dve", g2b[:], 0.0, writes=["g2b"])
            self.load(g2b[:, 0, :], W["gate_g2"][l, 0:128, :], writes=["g2b"], cast=True)
            self.load(g2b[0:32, 1, :], W["gate_g2"][l, 128:160, :], writes=["g2b"], cast=True)

        def v3(ap):
            return ap.rearrange("p (h e) -> p h e", h=8)

        def prep(i, pb):
            t0 = i * 128
            pc = pac[pb]
            pn_ = "pac%d" % pb
            if d == 0:
                self.load(pc[:], self.P[t0:t0 + 128, 2048:4000], writes=[pn_])
                if i == 0:
                    self.memset("pool", pap[:], 0.0, writes=["pap"])
                    self.load(pap[1:128, :], self.P[0:127, 2048:4000], writes=["pap"])
                else:
                    self.load(pap[:], self.P[t0 - 1:t0 + 127, 2048:4000], writes=["pap"])
                if i == NT - 1:
                    self.memset("pool", pan[:], 0.0, writes=["pan"])
                    self.load(pan[0:127, :], self.P[t0 + 1:S_LEN, 2048:4000], writes=["pan"])
                else:
                    self.load(pan[:], self.P[t0 + 1:t0 + 129, 2048:4000], writes=["pan"])
                self.tt("pool", pap[:], pap[:], mu0[:], ALU.mult, reads=["pap", "mu0"], writes=["pap"])
                self.tt("dve", pan[:], pan[:], mu1[:], ALU.mult, reads=["pan", "mu1"], writes=["pan"])
                self.tt("pool", pc[:], pc[:], c0[:], ALU.mult, reads=[pn_, "c0"], writes=[pn_])
                yield
                self.tt("dve", pc[:], pc[:], pan[:], ALU.add, reads=[pn_, "pan"], writes=[pn_])
                self.tt("dve", pc[:], pc[:], pap[:], ALU.add, reads=[pn_, "pap"], writes=[pn_])
                self.store(self.PSd[t0:t0 + 128, :], pc[:], reads=[pn_], writes=[("PS", i)])
            else:
                self.load(pc[:], self.PSd[t0:t0 + 128, :], writes=[pn_])
                self.load(kk[:], self.KKd[t0:t0 + 128, :], writes=["kk"])
            yield
            r_ = pc[:, 0:512]
            k_ = pc[:, 512:1024]
            v_ = pc[:, 1024:1536]
            lw_ = pc[:, 1536 + 64 * d:1600 + 64 * d]
            la_ = pc[:, 1664 + 64 * d:1728 + 64 * d]
            self.act(lo[:, 0:64], lw_, AF.Tanh, reads=[pn_], writes=["lo"])
            self.cp("dve", lo[:, 64:128], la_, reads=[pn_], writes=["lo"])
            self.tr(pbb[0][0:64, 0:128], lo[:, 0:64], idb[:], reads=["lo", "idb"], writes=[pbn[0]])
            self.tr(pbb[0][0:64, 128:256], lo[:, 64:128], idb[:], reads=["lo", "idb"], writes=[pbn[0]])
            self.cp("act", loT[:].rearrange("p a b -> p (a b)"), pbb[0][0:64, 0:256], reads=[pbn[0]], writes=["loT"])
            self.mm(pbk[1][:], loT[:, 0, :], w2b[:], True, True, reads=["loT", "w2b"], writes=[pbn[1]])
            self.mm(pbk[2][:], loT[:, 1, :], a2b[:], True, True, reads=["loT", "a2b"], writes=[pbn[2]])
            self.tt("dve", sgm[:], pbk[1][:], w0bc[:], ALU.add, reads=[pbn[1], "w0bc"], writes=["sgm"])
            self.act(sgm[:], sgm[:], AF.Sigmoid, reads=["sgm"], writes=["sgm"])
            self.tt("dve", av[:], pbk[2][:], a0bc[:], ALU.add, reads=[pbn[2], "a0bc"], writes=["av"])
            self.act(av[:], av[:], AF.Sigmoid, reads=["av"], writes=["av"])
            yield
            if d == 0:
                self.tt("dve", kk[:], k_, kkbc[:], ALU.mult, reads=[pn_, "kkbc"], writes=["kk"])
                self.tt("pool", tmp[:], kk[:], kk[:], ALU.mult, reads=["kk"], writes=["tmp"])
                self.red(ssq[:], v3(tmp[:]), reads=["tmp"], writes=["ssq"])
                self.act(ssq[:], ssq[:], AF.Sqrt, reads=["ssq"], writes=["ssq"])
                self.ts(ssq[:], ssq[:], 1e-12, None, ALU.max, ALU.bypass, reads=["ssq"], writes=["ssq"])
                self.recip(ssq[:], ssq[:], reads=["ssq"], writes=["ssq"])
                self.tt("dve", v3(kk[:]), v3(kk[:]), ssq[:].unsqueeze(2).broadcast_to([128, 8, 64]), ALU.mult,
                        reads=["kk", "ssq"], writes=["kk"])
                self.store(self.KKd[t0:t0 + 128, :], kk[:], reads=["kk"], writes=[("KK", i)])
            self.stt(tmp[:], av[:], -1.0, kabc[:], ALU.add, ALU.mult, reads=["av", "kabc"], writes=["tmp"])
            self.stt(kd[:], tmp[:], 1.0, k_, ALU.add, ALU.mult, reads=["tmp", pn_], writes=["kd"])
            self.tt("pool", ka[:], kk[:], av[:], ALU.mult, reads=["kk", "av"], writes=["ka"])
            yield
            self.mm(pbk[0][:], tri[:], sgm[:], True, True, reads=["tri", "sgm"], writes=[pbn[0]])
            self.mm(pbk[1][:], ones[:], sgm[:], True, True, reads=["ones", "sgm"], writes=[pbn[1]])
            for hh in range(8):
                self.mm(pbk[2][0:64, hh * 2:hh * 2 + 2], sgm[:, hh * 64:(hh + 1) * 64], ones[:, 0:2], True, True,
                        reads=["sgm", "ones"], writes=[pbn[2]])
            self.act(gC[pb][:], pbk[2][0:64, 0:16].rearrange("p (h t) -> p h t", t=2)[:, :, 0], AF.Exp,
                     reads=[pbn[2]], writes=["gC%d" % pb], scale=-CDEC)
            self.cp("act", Ls[:], pbk[0][:], reads=[pbn[0]], writes=["Ls"])
            self.tt("dve", Ld[:], pbk[1][:], Ls[:], ALU.subtract, reads=[pbn[1], "Ls"], writes=["Ld"])
            self.act(E2[:], pbk[0][:], AF.Exp, reads=[pbn[0]], writes=["E2"], scale=-CDEC)
            self.act(E3[:], pbk[0][:], AF.Exp, reads=[pbn[0]], writes=["E3"], scale=CDEC)
            yield
            self.tt("dve", Ls[:], Ls[:], sgm[:], ALU.subtract, reads=["Ls", "sgm"], writes=["Ls"])
            self.act(E1[:], Ls[:], AF.Exp, reads=["Ls"], writes=["E1"], scale=-CDEC)
            self.act(E4[:], Ld[:], AF.Exp, reads=["Ld"], writes=["E4"], scale=-CDEC)
            self.tt("dve", Rb[pb][:], r_, E2[:], ALU.mult, reads=[pn_, "E2"], writes=["Rb%d" % pb])
            self.tt("pool", Bb[pb][:], ka[:], E3[:], ALU.mult, reads=["ka", "E3"], writes=["Bb%d" % pb])
            self.tt("pool", Kb[pb][:], kd[:], E3[:], ALU.mult, reads=["kd", "E3"], writes=["Kb%d" % pb])
            self.stt(Ab[pb][:], kk[:], -1.0, E1[:], ALU.mult, ALU.mult, reads=["kk", "E1"], writes=["Ab%d" % pb])
            yield
            self.tt("pool", Btb[pb][:], ka[:], E4[:], ALU.mult, reads=["ka", "E4"], writes=["Btb%d" % pb])
            self.tt("dve", Ktb[pb][:], kd[:], E4[:], ALU.mult, reads=["kd", "E4"], writes=["Ktb%d" % pb])
            self.cp("pool", Vb[pb][:], v_, reads=[pn_], writes=["Vb%d" % pb])
            for hh in range(8):
                hs = slice(hh * 64, (hh + 1) * 64)
                ts_ = slice(hh * 128, (hh + 1) * 128)
                self.tr(pbb[0][0:64, ts_], Ab[pb][:, hs], idb[:], reads=["Ab%d" % pb, "idb"], writes=[pbn[0]])
                self.tr(pbb[1][0:64, ts_], Rb[pb][:, hs], idb[:], reads=["Rb%d" % pb, "idb"], writes=[pbn[1]])
                self.tr(pbb[2][0:64, ts_], Bb[pb][:, hs], idb[:], reads=["Bb%d" % pb, "idb"], writes=[pbn[2]])
            self.cp("act", ART[pb][0:64, :, 0:128], pbb[0][0:64, :].rearrange("p (h t) -> p h t", h=8),
                    reads=[pbn[0]], writes=["ART%d" % pb])
            self.cp("dve", ART[pb][0:64, :, 128:256], pbb[1][0:64, :].rearrange("p (h t) -> p h t", h=8),
                    reads=[pbn[1]], writes=["ART%d" % pb])
            self.cp("act", BT[pb][:], pbb[2][0:64, :].rearrange("p (h t) -> p h t", h=8), reads=[pbn[2]], writes=["BT%d" % pb])
            yield
            for hh in range(8):
                hs = slice(hh * 64, (hh + 1) * 64)
                ts_ = slice(hh * 128, (hh + 1) * 128)
                self.tr(pbb[0][0:64, ts_], Kb[pb][:, hs], idb[:], reads=["Kb%d" % pb, "idb"], writes=[pbn[0]])
            self.cp("dve", KTt[pb][:], pbb[0][0:64, :].rearrange("p (h t) -> p h t", h=8), reads=[pbn[0]], writes=["KTt%d" % pb])
            yield

        def solve(i, pb):
            t0 = i * 128
            pc = pac[pb]
            pn_ = "pac%d" % pb
            art, bt, ktt = ART[pb], BT[pb], KTt[pb]
            an, bn_, kn = "ART%d" % pb, "BT%d" % pb, "KTt%d" % pb
            vb, vn = Vb[pb], "Vb%d" % pb
            for hh in range(8):
                qq, qn = sbk[hh % 3], sbn[hh % 3]
                self.mm(qq[:, 0:256], bt[:, hh, :], art[0:64, hh, :], True, True, reads=[bn_, an], writes=[qn])
                self.mm(qq[:, 256:512], ktt[:, hh, :], art[0:64, hh, :], True, True, reads=[kn, an], writes=[qn])
                self.tt("dve", ATall[:, hh, :], qq[:], m4[:], ALU.mult, reads=[qn, "m4"], writes=[("AT", hh)])
                if hh == 3:
                    yield
            yield
            for g in range(2):
                for j in range(4):
                    hh = g * 4 + j
                    self.mm(s3[:, j * 128:(j + 1) * 128], art[0:64, hh, 0:128], bt[:, hh, :], True, True,
                            reads=[an, bn_], writes=["sq3"])
                self.tt("dve", PP[0][:, g * 4:(g + 1) * 4, 0:128], s3[:].rearrange("p (j s) -> p j s", j=4),
                        mn4[:].rearrange("p (j s) -> p j s", j=4), ALU.mult, reads=["sq3", "mn4"],
                        writes=[("PP", 0, 2 * g), ("PP", 0, 2 * g + 1)])
            self.cp("pool", PP[0][:, :, 128:256], ATall[:, :, 0:128], reads=[("AT", hh) for hh in range(8)],
                    writes=[("PP", 0, pr) for pr in range(4)])
            for hh in range(8):
                hs = slice(hh * 64, (hh + 1) * 64)
                self.mm(s4[:, hs], art[:, hh, 0:128], STb[:, hh, :], True, False, reads=[an, "STb"], writes=["sq4"])
                self.mm(s4[:, hs], ATall[:, hh, 256:384], vb[:, hs], False, True, reads=[("AT", hh), vn], writes=["sq4"])
            self.cp("act", Wb[:], s4[:], reads=["sq4"], writes=["Wb"])
            yield
            rot = 0
            for j in range(7):
                cb = j % 2
                cur = PP[cb]
                for hh in range(8):
                    hs = slice(hh * 64, (hh + 1) * 64)
                    self.mm(s3[:, hs], cur[:, hh, 128:256], Wb[:, hs], True, False,
                            reads=[("PP", cb, hh // 2), "Wb"], writes=["sq3"])
                    self.mm(s3[:, hs], idb[:], Wb[:, hs], False, True, reads=["idb", "Wb"], writes=["sq3"])
                self.cp("act", Wb[:], s3[:], reads=["sq3"], writes=["Wb"])
                if j < 6:
                    nxt = PP[1 - cb]
                    for pr in range(4):
                        bankt, bname = sbk[rot % 3], sbn[rot % 3]
                        rot += 1
                        for u in range(2):
                            hh = pr * 2 + u
                            self.mm(bankt[:, u * 256:u * 256 + 128], cur[:, hh, 128:256], cur[:, hh, 0:128], True, True,
                                    reads=[("PP", cb, pr)], writes=[bname])
                            self.mm(bankt[:, u * 256 + 128:u * 256 + 256], cur[:, hh, 0:128], cur[:, hh, 128:256], True, True,
                                    reads=[("PP", cb, pr)], writes=[bname])
                        self.cp("dve" if pr == 0 else "act",
                                nxt[:, pr * 2:pr * 2 + 2, :].rearrange("p a b -> p (a b)"), bankt[:],
                                reads=[bname], writes=[("PP", 1 - cb, pr)])
                yield
            for hh in range(8):
                hs = slice(hh * 64, (hh + 1) * 64)
                self.mm(s4[:, hs], art[:, hh, 128:256], STb[:, hh, :], True, False, reads=[an, "STb"], writes=["sq4"])
                self.mm(s4[:, hs], ATall[:, hh, 128:256], Wb[:, hs], False, False, reads=[("AT", hh), "Wb"], writes=["sq4"])
                self.mm(s4[:, hs], ATall[:, hh, 384:512], vb[:, hs], False, True, reads=[("AT", hh), vn], writes=["sq4"])
            self.cp("act", osb[:], s4[:], reads=["sq4"], writes=["osb"])
            for hh in range(8):
                hs = slice(hh * 64, (hh + 1) * 64)
                self.mm(s3[0:64, hs], Btb[pb][:, hs], Wb[:, hs], True, False, reads=["Btb%d" % pb, "Wb"], writes=["sq3"])
                self.mm(s3[0:64, hs], Ktb[pb][:, hs], vb[:, hs], False, True, reads=["Ktb%d" % pb, vn], writes=["sq3"])
            self.tt("dve", ST32[:], ST32[:], gC[pb][:].unsqueeze(2).broadcast_to([64, 8, 64]), ALU.mult,
                    reads=["ST32", "gC%d" % pb], writes=["ST32"])
            self.tt("dve", ST32[:], ST32[:], s3[0:64, :].rearrange("p (h e) -> p h e", h=8), ALU.add,
                    reads=["ST32", "sq3"], writes=["ST32"])
            self.cp("act", STb[0:64, :, :], ST32[:], reads=["ST32"], writes=["STb"])
            yield
            if d == 0:
                self.store(self.of[t0:t0 + 128, :], osb[:], reads=["osb"], writes=[("of", i)])
                return
            r_ = pc[:, 0:512]
            k_ = pc[:, 512:1024]
            lg_ = pc[:, 1792:1952]
            self.load(oft[:], self.of[t0:t0 + 128, :], writes=["oft"])
            self.tt("dve", oft[:], oft[:], osb[:], ALU.add, reads=["oft", "osb"], writes=["oft"])
            self.red(st8[:], v3(oft[:]), reads=["oft"], writes=["st8"])
            self.ts(st8[:], st8[:], 1.0 / 64, None, ALU.mult, ALU.bypass, reads=["st8"], writes=["st8"])
            self.tt("dve", v3(cen[:]), v3(oft[:]), st8[:].unsqueeze(2).broadcast_to([128, 8, 64]), ALU.subtract,
                    reads=["oft", "st8"], writes=["cen"])
            self.tt("pool", sq2[:], cen[:], cen[:], ALU.mult, reads=["cen"], writes=["sq2"])
            self.red(sv8[:], v3(sq2[:]), reads=["sq2"], writes=["sv8"])
            self.ts(sv8[:], sv8[:], 1.0 / 64, GN_EPS, ALU.mult, ALU.add, reads=["sv8"], writes=["sv8"])
            self.act(sv8[:], sv8[:], AF.Sqrt, reads=["sv8"], writes=["sv8"])
            self.recip(sv8[:], sv8[:], reads=["sv8"], writes=["sv8"])
            yield
            self.tt("dve", v3(cen[:]), v3(cen[:]), sv8[:].unsqueeze(2).broadcast_to([128, 8, 64]), ALU.mult,
                    reads=["cen", "sv8"], writes=["cen"])
            self.tt("pool", cen[:], cen[:], gngbc[:], ALU.mult, reads=["cen", "gngbc"], writes=["cen"])
            self.tt("pool", cen[:], cen[:], gnbbc[:], ALU.add, reads=["cen", "gnbbc"], writes=["cen"])
            self.tt("dve", sq2[:], r_, k_, ALU.mult, reads=[pn_], writes=["sq2"])
            self.tt("pool", sq2[:], sq2[:], rkbc[:], ALU.mult, reads=["sq2", "rkbc"], writes=["sq2"])
            self.red(sb8[:], v3(sq2[:]), reads=["sq2"], writes=["sb8"])
            self.tt("dve", v3(bon[:]), v3(pc[:, 1024:1536]),
                    sb8[:].unsqueeze(2).broadcast_to([128, 8, 64]), ALU.mult, reads=[pn_, "sb8"], writes=["bon"])
            self.tt("dve", cen[:], cen[:], bon[:], ALU.add, reads=["cen", "bon"], writes=["cen"])
            self.act(gs[:], lg_, AF.Sigmoid, reads=[pn_], writes=["gs"])
            self.tr(s3b[:, 0:128], gs[:, 0:128], idb[:], reads=["gs", "idb"], writes=["sq3"])
            self.tr(s3b[0:32, 128:256], gs[:, 128:160], idb[:], reads=["gs", "idb"], writes=["sq3"])
            self.cp("act", gT[:, 0, :], s3b[:, 0:128], reads=["sq3"], writes=["gT"])
            self.cp("act", gT[0:32, 1, :], s3b[0:32, 128:256], reads=["sq3"], writes=["gT"])
            self.mm(s4[:], gT[:, 0, :], g2b[:, 0, :], True, False, reads=["gT", "g2b"], writes=["sq4"])
            self.mm(s4[:], gT[:, 1, :], g2b[:, 1, :], False, True, reads=["gT", "g2b"], writes=["sq4"])
            self.tt("dve", yat[:], cen[:], s4[:], ALU.mult, reads=["cen", "sq4"], writes=["yat"])
            self.store(self.ya[t0:t0 + 128, :], yat[:], reads=["yat"], writes=[("ya", i)])
            yield

        order = list(range(NT)) if d == 0 else list(range(NT - 1, -1, -1))
        for _ in prep(order[0], 0):
            pass
        for n, i in enumerate(order):
            gs_ = solve(i, n % 2)
            gp_ = prep(order[n + 1], (n + 1) % 2) if n + 1 < NT else None
            while gs_ is not None or gp_ is not None:
                if gs_ is not None:
                    try:
                        next(gs_)
                    except StopIteration:
                        gs_ = None
                if gp_ is not None:
                    try:
                        next(gp_)
                    except StopIteration:
                        gp_ = None
        self.end_phase()

    def phase_MP(self, l):
        NT = self.NT
        W = self.w
        self.begin_phase()
        sb, ps = self.sb, self.ps
        qg = sb("qg", [128, 256], F32)
        kvg = sb("kvg", [128, 128], F32)
        wuq = sb("wuq", [128, 2, 768], BF16)
        wukv = sb("wukv", [128, 1, 1024], BF16)
        idb = sb("idb", [128, 128], BF16)
        pm = sb("pm", [128, 416], F32)
        cs = sb("cs", [128, 32], F32)
        sn = sb("sn", [128, 32], F32)
        junk = sb("junk", [128, 256], F32)
        ss = sb("ss", [128, 1], F32)
        rstd = sb("rstd", [128, 1], F32)
        nb = sb("nb", [128, 384], BF16)
        nT = sb("nT", [128, 3, 128], BF16)
        qf = sb("qf", [128, 768], F32)
        kvf = sb("kvf", [128, 1024], F32)
        t1 = sb("t1", [128, 8, 32], F32)
        t2 = sb("t2", [128, 8, 32], F32)
        kro = sb("kro", [128, 32], F32)
        kr2 = sb("kr2", [128, 32], F32)
        Qa = sb("Qa", [128, 8, 96], BF16)
        Ka = sb("Ka", [128, 8, 96], BF16)
        Va = sb("Va", [128, 8, 65], BF16)
        QTt = sb("QTt", [96, 8, 128], BF16)
        KTt = sb("KTt", [96, 8, 128], BF16)
        q = [ps("q%d" % i, [128, 512], F32) for i in range(7)]
        qb = [t[:].bitcast(BF16) for t in q]
        self.load(idb[:], self.c_ident, writes=["idb"], cast=True)
        self.bcast_load(qg, W["q_norm_g"][l:l + 1, :], 256, "qg")
        self.bcast_load(kvg, W["kv_norm_g"][l:l + 1, :], 128, "kvg")
        self.load_w_bf16(wuq, W["w_uq"][l], 256, "wuq")
        self.load_w_bf16(wukv, W["w_ukv"][l], 128, "wukv")
        self.memset("dve", Va[:], 1.0, writes=["Va"])
        qf3 = qf[:].rearrange("p (h e) -> p h e", h=8)
        kvf3 = kvf[:].rearrange("p (h e) -> p h e", h=8)
        for i in range(NT):
            t0 = i * 128
            self.load(pm[:], self.P[t0:t0 + 128, 4000:4416], writes=["pm"])
            self.load(cs[:], self.c_cos[t0:t0 + 128, :], writes=["cs"])
            self.load(sn[:], self.c_sin[t0:t0 + 128, :], writes=["sn"])
            self.rmsnorm(pm[:, 0:256], "pm", 256, qg[:], "qg", nb[:, 0:256], "nbq", junk[:, 0:256], ss[:], rstd[:], "M")
            self.rmsnorm(pm[:, 256:384], "pm", 128, kvg[:], "kvg", nb[:, 256:384], "nbk", junk[:, 0:128], ss[:], rstd[:], "M")
            for c in range(3):
                self.tr(qb[0][:, c * 128:(c + 1) * 128], nb[:, c * 128:(c + 1) * 128], idb[:],
                        reads=["nbq", "nbk", "idb"], writes=["q0"])
            self.cp("act", nT[:].rearrange("p a b -> p (a b)"), qb[0][:, 0:384], reads=["q0"], writes=["nT"])
            for c in range(2):
                self.mm(q[1][:], nT[:, c, :], wuq[:, c, 0:512], c == 0, c == 1, reads=["nT", "wuq"], writes=["q1"])
            for c in range(2):
                self.mm(q[2][:, 0:256], nT[:, c, :], wuq[:, c, 512:768], c == 0, c == 1, reads=["nT", "wuq"], writes=["q2"])
            self.mm(q[3][:], nT[:, 2, :], wukv[:, 0, 0:512], True, True, reads=["nT", "wukv"], writes=["q3"])
            self.mm(q[4][:], nT[:, 2, :], wukv[:, 0, 512:1024], True, True, reads=["nT", "wukv"], writes=["q4"])
            self.cp("act", qf[:, 0:512], q[1][:], reads=["q1"], writes=["qf"])
            self.cp("dve", qf[:, 512:768], q[2][:, 0:256], reads=["q2"], writes=["qf"])
            self.cp("act", kvf[:, 0:512], q[3][:], reads=["q3"], writes=["kvf"])
            self.cp("dve", kvf[:, 512:1024], q[4][:], reads=["q4"], writes=["kvf"])
            self.cp("pool", Qa[:, :, 0:64], qf3[:, :, 0:64], reads=["qf"], writes=["Qa"])
            csb = cs[:].unsqueeze(1).broadcast_to([128, 8, 32])
            self.tt("dve", t1[:], qf3[:, :, 64:96], csb, ALU.mult, reads=["qf", "cs"], writes=["t1"])
            self.tt("dve", t2[:, :, 0:16], qf3[:, :, 80:96], sn[:, 0:16].unsqueeze(1).broadcast_to([128, 8, 16]), ALU.mult,
                    reads=["qf", "sn"], writes=["t2"])
            self.tt("dve", t2[:, :, 16:32], qf3[:, :, 64:80], sn[:, 16:32].unsqueeze(1).broadcast_to([128, 8, 16]), ALU.mult,
                    reads=["qf", "sn"], writes=["t2"])
            self.tt("dve", Qa[:, :, 64:96], t1[:], t2[:], ALU.add, reads=["t1", "t2"], writes=["Qa"])
            self.tt("dve", kro[:], pm[:, 384:416], cs[:], ALU.mult, reads=["pm", "cs"], writes=["kro"])
            self.tt("dve", kr2[:, 0:16], pm[:, 400:416], sn[:, 0:16], ALU.mult, reads=["pm", "sn"], writes=["kr2"])
            self.tt("dve", kr2[:, 16:32], pm[:, 384:400], sn[:, 16:32], ALU.mult, reads=["pm", "sn"], writes=["kr2"])
            self.tt("dve", kro[:], kro[:], kr2[:], ALU.add, reads=["kro", "kr2"], writes=["kro"])
            self.cp("dve", Ka[:, :, 64:96], kro[:].unsqueeze(1).broadcast_to([128, 8, 32]), reads=["kro"], writes=["Ka"])
            self.cp("pool", Ka[:, :, 0:64], kvf3[:, :, 0:64], reads=["kvf"], writes=["Ka"])
            self.cp("pool", Va[:, :, 0:64], kvf3[:, :, 64:128], reads=["kvf"], writes=["Va"])
            for hh in range(8):
                self.tr(qb[5][0:96, hh * 128:(hh + 1) * 128], Qa[:, hh, :], idb[:], reads=["Qa", "idb"], writes=["q5"])
                self.tr(qb[6][0:96, hh * 128:(hh + 1) * 128], Ka[:, hh, :], idb[:], reads=["Ka", "idb"], writes=["q6"])
            self.cp("act", QTt[:].rearrange("p a b -> p (a b)"), qb[5][0:96, :], reads=["q5"], writes=["QTt"])
            self.cp("dve", KTt[:].rearrange("p a b -> p (a b)"), qb[6][0:96, :], reads=["q6"], writes=["KTt"])
            self.store(self.QT[:, :, t0:t0 + 128].rearrange("h p t -> p h t"), QTt[:], reads=["QTt"], writes=[("QT", i)])
            self.store(self.KT[:, :, t0:t0 + 128].rearrange("h p t -> p h t"), KTt[:], reads=["KTt"], writes=[("KT", i)])
            self.store(self.Vd[t0:t0 + 128, :], Va[:].rearrange("p a b -> p (a b)"), reads=["Va"], writes=[("Vd", i)])
        self.end_phase()

    def phase_MM(self):
        NT = self.NT
        S_LEN = self.S_LEN
        QB = min(512, S_LEN)
        nqb = S_LEN // QB
        nj = QB // 128
        LOOK = 2
        self.begin_phase()
        sb, ps = self.sb, self.ps
        Vall = sb("Vall", [128, NT, 520], BF16)
        KTh = [sb("KTh", [96, S_LEN], BF16) for _ in range(2)]
        QTb = [sb("QTb", [96, QB], BF16) for _ in range(2)]
        PT = [sb("PT", [128, QB], BF16) for _ in range(4)]
        OT = sb("OT", [65, QB], F32)
        id32 = sb("id32", [128, 128], F32)
        osm = sb("osm", [128, nj, 64], F32)
        rec = sb("rec", [128, nj], F32)
        q = [ps("q%d" % i, [128, 512], F32) for i in range(7)]
        self.load(id32[:], self.c_ident, writes=["id32"])
        self.load(Vall[:], self.Vd.rearrange("(c p) f -> p c f", p=128), writes=["Vall"])
        blocks = [(hh, qi) for hh in range(8) for qi in range(nqb)]
        stream = [(bi, kc) for bi in range(len(blocks)) for kc in range(NT)]

        def load_k(hh):
            self.load(KTh[hh % 2][:], self.KT[hh], writes=["KTh%d" % (hh % 2)])

        def load_q(bi):
            hh, qi = blocks[bi]
            self.load(QTb[bi % 2][:], self.QT[hh, :, qi * QB:(qi + 1) * QB], writes=["QTb%d" % (bi % 2)])

        def emit_S(idx):
            bi, kc = stream[idx]
            hh, qi = blocks[bi]
            pb = idx % 4
            self.mm(q[pb][:, 0:QB], KTh[hh % 2][:, kc * 128:(kc + 1) * 128], QTb[bi % 2][:], True, True,
                    reads=["KTh%d" % (hh % 2), "QTb%d" % (bi % 2)], writes=["q%d" % pb])

        def epilogue_a(bi):
            ob = 4 + bi % 2
            self.cp("dve", OT[:], q[ob][0:65, 0:QB], reads=["q%d" % ob], writes=["OT"])

        def epilogue_b(bi):
            hh, qi = blocks[bi]
            for j in range(nj):
                self.tr(q[6][:, j * 65:(j + 1) * 65], OT[:, j * 128:(j + 1) * 128], id32[0:65, 0:65],
                        reads=["OT", "id32"], writes=["q6"])
            o3 = q[6][:, 0:nj * 65].rearrange("p (j e) -> p j e", j=nj)
            self.recip(rec[:], o3[:, :, 64], reads=["q6"], writes=["rec"])
            self.tt("dve", osm[:], o3[:, :, 0:64], rec[:].unsqueeze(2).broadcast_to([128, nj, 64]), ALU.mult,
                    reads=["q6", "rec"], writes=["osm"])
            self.store(self.yb[qi * QB:(qi + 1) * QB, hh * 64:(hh + 1) * 64].rearrange("(j p) e -> p j e", p=128),
                       osm[:], reads=["osm"], writes=[("yb", hh, qi)])

        load_k(0)
        load_q(0)
        if len(blocks) > 1:
            load_q(1)
        for idx in range(min(LOOK, len(stream))):
            emit_S(idx)
        pending = None
        for idx, (bi, kc) in enumerate(stream):
            hh, qi = blocks[bi]
            if kc == 0:
                if qi == 0 and hh + 1 < 8:
                    load_k(hh + 1)
            if idx + LOOK < len(stream):
                emit_S(idx + LOOK)
            pb = idx % 4
            ob = 4 + bi % 2
            self.act(PT[pb][:], q[pb][:, 0:QB], AF.Exp, reads=["q%d" % pb], writes=["PT%d" % pb], scale=SCALE)
            self.mm(q[ob][0:65, 0:QB], Vall[:, kc, hh * 65:(hh + 1) * 65], PT[pb][:], kc == 0, kc == NT - 1,
                    reads=["Vall", "PT%d" % pb], writes=["q%d" % ob])
            if pending is not None and kc == min(3, NT - 1):
                epilogue_b(pending)
                pending = None
            if kc == NT - 1:
                epilogue_a(bi)
                pending = bi
                if bi + 2 < len(blocks):
                    load_q(bi + 2)
        if pending is not None:
            epilogue_b(pending)
        self.end_phase()

    def phase_C1(self, l, xin):
        NT = self.NT
        W = self.w
        self.begin_phase()
        sb, ps = self.sb, self.ps
        woa = sb("woa", [128, 4, D], BF16)
        wob = sb("wob", [128, 4, D], BF16)
        wout = sb("wout", [128, 8, D], BF16)
        idb = sb("idb", [128, 128], BF16)
        xt = [sb("xt", [128, D], F32) for _ in range(2)]
        gt = [sb("gt", [128, 2048], F32) for _ in range(2)]
        yat = [sb("yat", [128, 512], F32) for _ in range(2)]
        ybt = [sb("ybt", [128, 512], F32) for _ in range(2)]
        yab = sb("yab", [128, D], BF16)
        yT = sb("yT", [128, 8, 128], BF16)
        m1 = sb("m1", [128, D], F32)
        m2 = sb("m2", [128, D], F32)
        mixb = sb("mixb", [128, D], BF16)
        mixT = sb("mixT", [128, 8, 128], BF16)
        x1t = sb("x1t", [128, D], F32)
        q = [ps("q%d" % i, [128, 512], F32) for i in range(7)]
        qTb = q[6][:].bitcast(BF16)
        self.load(idb[:], self.c_ident, writes=["idb"], cast=True)
        self.load_w_bf16(woa, W["w_oa"][l], 512, "woa")
        self.load_w_bf16(wob, W["w_ob"][l], 512, "wob")
        self.load_w_bf16(wout, W["w_out"][l], D, "wout")
        for i in range(NT):
            b = i % 2
            t0 = i * 128
            self.load(xt[b][:], xin[t0:t0 + 128, :], writes=["xt%d" % b])
            self.load(gt[b][:], self.P[t0:t0 + 128, 0:2048], writes=["gt%d" % b])
            self.load(yat[b][:], self.ya[t0:t0 + 128, :], writes=["yat%d" % b])
            self.load(ybt[b][:], self.yb[t0:t0 + 128, :], writes=["ybt%d" % b])
            self.cp("dve", yab[:, 0:512], yat[b][:], reads=["yat%d" % b], writes=["yab0"])
            self.cp("pool", yab[:, 512:1024], ybt[b][:], reads=["ybt%d" % b], writes=["yab1"])
            for k in range(8):
                self.tr(qTb[:, k * 128:(k + 1) * 128], yab[:, k * 128:(k + 1) * 128], idb[:],
                        reads=["yab0", "yab1", "idb"], writes=["q6"])
            self.cp("act", yT[:].rearrange("p a b -> p (a b)"), qTb[:], reads=["q6"], writes=["yT"])
            for hf in range(2):
                hs = slice(hf * 512, (hf + 1) * 512)
                for k in range(4):
                    self.mm(q[hf][:], yT[:, k, :], woa[:, k, hs], k == 0, k == 3, reads=["yT", "woa"], writes=["q%d" % hf])
                for k in range(4):
                    self.mm(q[2 + hf][:], yT[:, 4 + k, :], wob[:, k, hs], k == 0, k == 3, reads=["yT", "wob"],
                            writes=["q%d" % (2 + hf)])
            self.act(gt[b][:], gt[b][:], AF.Sigmoid, reads=["gt%d" % b], writes=["gt%d" % b])
            for hf in range(2):
                hs = slice(hf * 512, (hf + 1) * 512)
                hs2 = slice(1024 + hf * 512, 1024 + (hf + 1) * 512)
                self.tt("dve", m1[:, hs], gt[b][:, hs], q[hf][:], ALU.mult, reads=["gt%d" % b, "q%d" % hf], writes=[("m1", hf)])
                self.tt("dve", m2[:, hs], gt[b][:, hs2], q[2 + hf][:], ALU.mult, reads=["gt%d" % b, "q%d" % (2 + hf)],
                        writes=[("m2", hf)])
                self.tt("pool", mixb[:, hs], m1[:, hs], m2[:, hs], ALU.add, reads=[("m1", hf), ("m2", hf)], writes=[("mixb", hf)])
            for k in range(8):
                self.tr(qTb[:, k * 128:(k + 1) * 128], mixb[:, k * 128:(k + 1) * 128], idb[:],
                        reads=[("mixb", 0), ("mixb", 1), "idb"], writes=["q6"])
            self.cp("act", mixT[:].rearrange("p a b -> p (a b)"), qTb[:], reads=["q6"], writes=["mixT"])
            for hf in range(2):
                hs = slice(hf * 512, (hf + 1) * 512)
                for k in range(8):
                    self.mm(q[4 + hf][:], mixT[:, k, :], wout[:, k, hs], k == 0, k == 7, reads=["mixT", "wout"],
                            writes=["q%d" % (4 + hf)])
                self.tt("dve", x1t[:, hs], xt[b][:, hs], q[4 + hf][:], ALU.add, reads=["xt%d" % b, "q%d" % (4 + hf)],
                        writes=[("x1t", hf)])
            self.store(self.x1[t0:t0 + 128, :], x1t[:], reads=[("x1t", 0), ("x1t", 1)], writes=[("x1", i)])
        self.end_phase()

    def phase_C2(self, l, last, yout):
        NT = self.NT
        W = self.w
        self.begin_phase()
        sb, ps = self.sb, self.ps
        wgu = sb("wgu", [128, 8, 2 * DFF], BF16)
        wdn = sb("wdn", [128, 22, D], BF16)
        gbc = sb("gbc", [128, D], F32)
        idb = sb("idb", [128, 128], BF16)
        xt = [sb("xt", [128, D], F32) for _ in range(2)]
        junk = sb("junk", [128, D], F32)
        ss = sb("ss", [128, 1], F32)
        rstd = sb("rstd", [128, 1], F32)
        h = sb("h", [128, D], BF16)
        hT = [sb("hT", [128, 8, 128], BF16) for _ in range(2)]
        sl = [sb("sl", [128, 256], F32) for _ in range(2)]
        actb = sb("actb", [128, DFF], BF16)
        actT = sb("actT", [128, 22, 128], BF16)
        x2t = sb("x2t", [128, D], F32)
        if last:
            fbc = sb("fbc", [128, D], F32)
        q = [ps("q%d" % i, [128, 512], F32) for i in range(6)]
        qTb = q[4][:].bitcast(BF16)
        qT2 = q[5][:].bitcast(BF16)
        self.load(idb[:], self.c_ident, writes=["idb"], cast=True)
        self.bcast_load(gbc, W["norm_ffn_g"][l:l + 1, :], D, "gbc")
        if last:
            self.bcast_load(fbc, W["final_norm_g"][0:1, :], D, "fbc")
        self.load_w_bf16(wgu, W["w_gu"][l], D, "wgu")
        self.load_w_bf16(wdn, W["w_down"][l], DFF, "wdn")

        def norm(i):
            b = i % 2
            self.load(xt[b][:], self.x1[i * 128:(i + 1) * 128, :], writes=["xt%d" % b])
            self.rmsnorm(xt[b][:], "xt%d" % b, D, gbc[:], "gbc", h[:], "h", junk[:], ss[:], rstd[:], "F")

        def trans(i):
            b = i % 2
            for k in range(8):
                self.tr(qTb[:, k * 128:(k + 1) * 128], h[:, k * 128:(k + 1) * 128], idb[:], reads=["h", "idb"], writes=["q4"])
            self.cp("act", hT[b][:].rearrange("p a b -> p (a b)"), qTb[:], reads=["q4"], writes=["hT%d" % b])

        def tpose(j):
            o = (j % 4) * 256
            for u in range(2):
                self.tr(qT2[:, o + u * 128:o + (u + 1) * 128], actb[:, j * 256 + u * 128:j * 256 + (u + 1) * 128], idb[:],
                        reads=[("actb", j), "idb"], writes=["q5"])
            self.cp("dve", actT[:, 2 * j:2 * j + 2, :].rearrange("p a b -> p (a b)"),
                    qT2[:, o:o + 256], reads=["q5"], writes=[("actT", j)])

        norm(0)
        trans(0)
        for i in range(NT):
            b = i % 2
            t0 = i * 128
            xn = "xt%d" % b
            hn = "hT%d" % b
            if i + 1 < NT:
                norm(i + 1)
            for j in range(11):
                bk = q[j % 2]
                bn = "q%d" % (j % 2)
                for k in range(8):
                    self.mm(bk[:, 0:256], hT[b][:, k, :], wgu[:, k, j * 256:(j + 1) * 256], k == 0, k == 7,
                            reads=[hn, "wgu"], writes=[bn])
                for k in range(8):
                    self.mm(bk[:, 256:512], hT[b][:, k, :], wgu[:, k, DFF + j * 256:DFF + (j + 1) * 256], k == 0, k == 7,
                            reads=[hn, "wgu"], writes=[bn])
                self.act(sl[j % 2][:], bk[:, 0:256], AF.Silu, reads=[bn], writes=["sl%d" % (j % 2)])
                self.tt("dve", actb[:, j * 256:(j + 1) * 256], sl[j % 2][:], bk[:, 256:512], ALU.mult,
                        reads=["sl%d" % (j % 2), bn], writes=[("actb", j)])
                if j >= 1:
                    tpose(j - 1)
            tpose(10)
            if i + 1 < NT:
                trans(i + 1)
            for hf in range(2):
                hs = slice(hf * 512, (hf + 1) * 512)
                for c in range(22):
                    self.mm(q[2 + hf][:], actT[:, c, :], wdn[:, c, hs], c == 0, c == 21,
                            reads=[("actT", c // 2), "wdn"], writes=["q%d" % (2 + hf)])
                self.tt("dve", x2t[:, hs], xt[b][:, hs], q[2 + hf][:], ALU.add, reads=[xn, "q%d" % (2 + hf)],
                        writes=["x2t"])
            if last:
                self.rmsnorm(x2t[:], "x2t", D, fbc[:], "fbc", x2t[:], "x2t", junk[:], ss[:], rstd[:], "F")
                self.store(yout[t0:t0 + 128, :], x2t[:], reads=["x2t"], writes=[("y", i)])
            else:
                self.store(self.x2[t0:t0 + 128, :], x2t[:], reads=["x2t"], writes=[("x2", i)])
        self.end_phase()

    def build(self, phases=None):
        def on(p):
            return phases is None or p in phases
        for s in range(self.NSEQ):
            for l in range(self.depth):
                last = l == self.depth - 1
                xin = self.x[s] if l == 0 else self.x2
                if on("A"):
                    self.phase_A(l, xin)
                if on("R0"):
                    self.phase_R(l, 0)
                if on("R1"):
                    self.phase_R(l, 1)
                if on("MP"):
                    self.phase_MP(l)
                if on("MM"):
                    self.phase_MM()
                if on("C1"):
                    self.phase_C1(l, xin)
                if on("C2"):
                    self.phase_C2(l, last, self.y[s])
        self.S.emit()
        self.S.stack.close()
        return self.nc


def make_consts(S_LEN):
    s = np.arange(128)[:, None]
    t = np.arange(128)[None, :]
    tri = np.stack([(s <= t), (s >= t)]).astype(np.float32)
    strict = [(s < t).astype(np.float32), (s > t).astype(np.float32)]
    incl = [(s <= t).astype(np.float32), (s >= t).astype(np.float32)]
    m4 = np.stack([np.concatenate([strict[d], incl[d], strict[d], incl[d]], axis=1) for d in range(2)])
    mn = [(t < s).astype(np.float32), (t > s).astype(np.float32)]
    mn4 = np.stack([np.concatenate([mn[d]] * 4, axis=1) for d in range(2)])
    pos = np.arange(S_LEN, dtype=np.float32)
    inv_freq = (1.0 / (np.float32(10000.0) ** (np.arange(0, 32, 2, dtype=np.float32) / np.float32(32)))).astype(np.float32)
    ang = pos[:, None] * inv_freq[None, :]
    ang = np.concatenate([ang, ang], axis=-1).astype(np.float32)
    cos = np.cos(ang).astype(np.float32)
    sin = np.sin(ang).astype(np.float32)
    sin_s = sin.copy()
    sin_s[:, 0:16] = -sin_s[:, 0:16]
    return dict(c_ident=np.eye(128, dtype=np.float32), c_tri=tri, c_m4=m4.astype(np.float32),
                c_mn4=mn4.astype(np.float32), c_ones=np.ones((128, 128), np.float32),
                c_cos=cos, c_sin=sin_s)


_WNAMES = ["norm_mix_g", "w_in", "shift_mu", "decay_w2", "decay_w0", "iclr_a2", "iclr_a0", "gate_g2", "k_k", "k_a",
           "r_k", "gn_g", "gn_b", "w_oa", "q_norm_g", "w_uq", "kv_norm_g", "w_ukv", "w_ob", "w_out", "norm_ffn_g",
           "w_gu", "w_down", "final_norm_g"]


def prep_weights(inputs, depth):
    out = {}
    for n in _WNAMES:
        a = np.ascontiguousarray(np.asarray(inputs[n], dtype=np.float32))
        if n == "r_k":
            a = a.reshape(a.shape[0], 512)
        if n == "final_norm_g":
            a = a.reshape(1, D)
        else:
            a = a[:depth]
        out[n] = np.ascontiguousarray(a)
    return out


def kernel(**inputs):
    xp = np.asarray(inputs["x_prompt"], dtype=np.float32)
    xs = np.asarray(inputs["x_sample"], dtype=np.float32)
    S_LEN = xp.shape[1]
    x_all = np.concatenate([xp, xs], axis=0)
    nseq = x_all.shape[0] // NCORES
    wts = prep_weights(inputs, DEPTH)
    consts = make_consts(S_LEN)
    nc = Builder(S_LEN, nseq, DEPTH).build()
    in_maps = []
    for c in range(NCORES):
        m = dict(x=np.ascontiguousarray(x_all[c * nseq:(c + 1) * nseq]))
        m.update(wts)
        m.update(consts)
        in_maps.append(m)
    res = run_bass_kernel_spmd(nc, in_maps, core_ids=list(range(NCORES)))
    y = np.concatenate([r["y"] for r in res.results], axis=0)
    return (np.ascontiguousarray(y[:xp.shape[0]]), np.ascontiguousarray(y[xp.shape[0]:]))
```

```python
import contextlib
import os
import numpy as np
import concourse.bass as bass
import concourse.mybir as mybir
from concourse.alu_op_type import AluOpType as ALU
from concourse.bass_utils import run_bass_kernel_spmd

F32 = mybir.dt.float32
BF16 = mybir.dt.bfloat16
AF = mybir.ActivationFunctionType
AX = mybir.AxisListType

D = 1024
NIN = 4416
DFF = 2816
DEPTH = 2
NCORES = 8
SEQ_FULL = 4096
RMS_EPS = 1e-6
GN_EPS = 64e-5
CDEC = 0.6065306597126334
SCALE = 96.0 ** -0.5

ENGS = ("pe", "act", "dve", "pool", "sp")
N_DMA_SEMS = 8
SAME_ENGINE_SYNC = True


def _is_psum(r):
    n = r[0] if isinstance(r, tuple) else r
    return isinstance(n, str) and len(n) >= 2 and n[0] in "qp" and (n[1].isdigit() or n[1] in "TP")


class Sched:
    def __init__(self, nc):
        self.nc = nc
        self.q = {e: [] for e in ENGS}
        self.cnt = {e: 0 for e in ENGS}
        self.seen = {e: {} for e in ENGS}
        self.last_w = {}
        self.readers = {}
        self.dma_val = {}
        self.dma_rr = {e: 0 for e in ENGS}
        self.stack = contextlib.ExitStack()
        self.sems = {}
        self.nops = 0
        self.limit = int(os.environ.get("OPLIMIT", "1000000000"))
        self.marks = []

    def mark(self, label):
        self.marks.append((label, self.nops))

    def _deps(self, eng, reads, writes):
        deps = []
        for r in reads:
            ev = self.last_w.get(r)
            if ev is not None:
                deps.append(ev)
            if eng != "pe" and _is_psum(r):
                deps.extend(e2 for e2 in self.readers.get(r, ()) if e2[0] != eng)
        for w in writes:
            ev = self.last_w.get(w)
            if ev is not None:
                deps.append(ev)
            deps.extend(self.readers.get(w, ()))
        waits = {}
        seen = self.seen[eng]
        for sk, v in deps:
            if sk == eng and (eng == "pe" or not SAME_ENGINE_SYNC):
                continue
            if seen.get(sk, 0) >= v:
                continue
            if waits.get(sk, 0) < v:
                waits[sk] = v
        for sk, v in waits.items():
            seen[sk] = v
        return waits

    def _record(self, ev, reads, writes):
        for r in reads:
            self.readers.setdefault(r, []).append(ev)
        for w in writes:
            self.last_w[w] = ev
            self.readers[w] = []

    def op(self, eng, fn, reads=(), writes=()):
        self.nops += 1
        if self.nops > self.limit:
            return
        waits = self._deps(eng, reads, writes)
        self.cnt[eng] += 1
        ev = (eng, self.cnt[eng])
        self.q[eng].append((list(waits.items()), fn, (eng, 1)))
        self._record(ev, reads, writes)

    def dma(self, eng, fn, reads=(), writes=()):
        self.nops += 1
        if self.nops > self.limit:
            return
        k = self.dma_rr[eng]
        self.dma_rr[eng] = (k + 1) % N_DMA_SEMS
        sk = ("dma", eng, k)
        prev = self.dma_val.get(sk, 0)
        waits = self._deps(eng, reads, writes)
        if prev > 0 and self.seen[eng].get(sk, 0) < prev:
            waits[sk] = prev
            self.seen[eng][sk] = prev
        self.dma_val[sk] = prev + 16
        ev = (sk, prev + 16)
        self.q[eng].append((list(waits.items()), fn, (sk, 16)))
        self._record(ev, reads, writes)

    def barrier(self):
        tgt = {e: self.cnt[e] for e in ENGS if self.cnt[e] > 0}
        tgt.update(self.dma_val)
        for e in ENGS:
            waits = []
            for sk, v in tgt.items():
                if sk == e:
                    continue
                if self.seen[e].get(sk, 0) < v:
                    waits.append((sk, v))
                    self.seen[e][sk] = v
            if waits:
                self.q[e].append((waits, None, None))
        self.last_w = {}
        self.readers = {}

    def emit(self):
        nc = self.nc
        st = self.stack
        keys = list(ENGS) + list(self.dma_val)
        for sk in keys:
            nm = sk if isinstance(sk, str) else "d_%s_%d" % (sk[1], sk[2])
            self.sems[sk] = st.enter_context(nc.semaphore("s_" + nm))
        final = list(self.dma_val.items())
        block = st.enter_context(nc.Block())
        sems = self.sems

        def run(engname, final_waits=()):
            def body(e):
                for waits, fn, inc in self.q[engname]:
                    for sk, v in waits:
                        e.wait_ge(sems[sk], v)
                    if fn is not None:
                        fn(e).then_inc(sems[inc[0]], inc[1])
                for sk, v in final_waits:
                    e.wait_ge(sems[sk], v)
            return body

        block.tensor(run("pe"))
        block.scalar(run("act"))
        block.vector(run("dve"))
        block.gpsimd(run("pool"))
        block.sync(run("sp", final))


class Builder:
    def __init__(self, S_LEN, NSEQ, depth=DEPTH):
        self.S_LEN = S_LEN
        self.NSEQ = NSEQ
        self.depth = depth
        self.NT = S_LEN // 128
        nc = bass.Bass("TRN2", target_bir_lowering=False)
        self.nc = nc
        self.S = Sched(nc)
        self.ph = None
        self._uid = 0

        def inp(name, shape):
            return nc.dram_tensor(name, list(shape), F32, kind="ExternalInput").ap()

        L = depth
        self.x = inp("x", [NSEQ, S_LEN, D])
        self.w = dict(
            norm_mix_g=inp("norm_mix_g", [L, D]), w_in=inp("w_in", [L, D, NIN]),
            shift_mu=inp("shift_mu", [L, 2, 1952]), decay_w2=inp("decay_w2", [L, 2, 64, 512]),
            decay_w0=inp("decay_w0", [L, 2, 512]), iclr_a2=inp("iclr_a2", [L, 2, 64, 512]),
            iclr_a0=inp("iclr_a0", [L, 2, 512]), gate_g2=inp("gate_g2", [L, 160, 512]),
            k_k=inp("k_k", [L, 512]), k_a=inp("k_a", [L, 512]), r_k=inp("r_k", [L, 512]),
            gn_g=inp("gn_g", [L, 512]), gn_b=inp("gn_b", [L, 512]), w_oa=inp("w_oa", [L, 512, D]),
            q_norm_g=inp("q_norm_g", [L, 256]), w_uq=inp("w_uq", [L, 256, 768]),
            kv_norm_g=inp("kv_norm_g", [L, 128]), w_ukv=inp("w_ukv", [L, 128, 1024]),
            w_ob=inp("w_ob", [L, 512, D]), w_out=inp("w_out", [L, D, D]),
            norm_ffn_g=inp("norm_ffn_g", [L, D]), w_gu=inp("w_gu", [L, D, 2 * DFF]),
            w_down=inp("w_down", [L, DFF, D]), final_norm_g=inp("final_norm_g", [1, D]),
        )
        self.c_ident = inp("c_ident", [128, 128])
        self.c_tri = inp("c_tri", [2, 128, 128])
        self.c_m4 = inp("c_m4", [2, 128, 512])
        self.c_mn4 = inp("c_mn4", [2, 128, 512])
        self.c_ones = inp("c_ones", [128, 128])
        self.c_cos = inp("c_cos", [S_LEN, 32])
        self.c_sin = inp("c_sin", [S_LEN, 32])
        self.y = nc.dram_tensor("y", [NSEQ, S_LEN, D], F32, kind="ExternalOutput").ap()
        def scr(name, shape, dt=F32):
            return nc.dram_tensor(name, list(shape), dt).ap()
        self.P = scr("scr_P", [S_LEN, NIN])
        self.of = scr("scr_of", [S_LEN, 512])
        self.PSd = scr("scr_PS", [S_LEN, 1952])
        self.KKd = scr("scr_KK", [S_LEN, 512])
        self.ya = scr("scr_ya", [S_LEN, 512])
        self.yb = scr("scr_yb", [S_LEN, 512])
        self.x1 = scr("scr_x1", [S_LEN, D])
        self.x2 = scr("scr_x2", [S_LEN, D])
        self.QT = scr("scr_QT", [8, 96, S_LEN], BF16)
        self.KT = scr("scr_KT", [8, 96, S_LEN], BF16)
        self.Vd = scr("scr_V", [S_LEN, 8 * 65], BF16)

    def begin_phase(self):
        self.ph = contextlib.ExitStack()

    def end_phase(self):
        self.S.barrier()
        self.ph.close()
        self.ph = None

    def sb(self, name, shape, dt):
        self._uid += 1
        return self.ph.enter_context(self.nc.sbuf_tensor("%s_%d" % (name, self._uid), list(shape), dt))

    def ps(self, name, shape, dt):
        self._uid += 1
        return self.ph.enter_context(self.nc.psum_tensor("%s_%d" % (name, self._uid), list(shape), dt))

    def load(self, out_ap, in_ap, writes, reads=(), cast=False):
        eng = "pool" if cast else "sp"
        self.S.dma(eng, lambda e: e.dma_start(out=out_ap, in_=in_ap), reads=reads, writes=writes)

    def store(self, out_ap, in_ap, reads, writes):
        self.S.dma("sp", lambda e: e.dma_start(out=out_ap, in_=in_ap), reads=reads, writes=writes)

    def bcast_load(self, tile, row_ap, width, name):
        self.load(tile[:], row_ap.broadcast_to([128, width]), writes=[name])

    def load_w_bf16(self, tile, w_ap, K, name):
        for k in range(K // 128):
            self.load(tile[:, k, :], w_ap[k * 128:(k + 1) * 128, :], writes=[name], cast=True)

    def mm(self, out, lhsT, rhs, start, stop, reads, writes):
        self.S.op("pe", lambda e: e.matmul(out=out, lhsT=lhsT, rhs=rhs, start=start, stop=stop),
                  reads=reads, writes=writes)

    def tr(self, out, in_, ident, reads, writes):
        self.S.op("pe", lambda e: e.transpose(out=out, in_=in_, identity=ident), reads=reads, writes=writes)

    def act(self, out, in_, func, reads, writes, scale=None, bias=None, accum_out=None):
        kw = {}
        if scale is not None:
            kw["scale"] = scale
        if bias is not None:
            kw["bias"] = bias
        if accum_out is not None:
            kw["accum_out"] = accum_out
        self.S.op("act", lambda e: e.activation(out=out, in_=in_, func=func, **kw), reads=reads, writes=writes)

    def tt(self, eng, out, in0, in1, op, reads, writes):
        self.S.op(eng, lambda e: e.tensor_tensor(out=out, in0=in0, in1=in1, op=op), reads=reads, writes=writes)

    def ts(self, out, in0, s1, s2, op0, op1, reads, writes, eng="dve"):
        self.S.op(eng, lambda e: e.tensor_scalar(out=out, in0=in0, scalar1=s1, scalar2=s2, op0=op0, op1=op1),
                  reads=reads, writes=writes)

    def stt(self, out, in0, scalar, in1, op0, op1, reads, writes):
        self.S.op("dve", lambda e: e.scalar_tensor_tensor(out=out, in0=in0, scalar=scalar, in1=in1, op0=op0, op1=op1),
                  reads=reads, writes=writes)

    def cp(self, eng, out, in_, reads, writes):
        if eng == "act":
            self.S.op("act", lambda e: e.activation(out=out, in_=in_, func=AF.Copy), reads=reads, writes=writes)
        else:
            self.S.op(eng, lambda e: e.tensor_copy(out=out, in_=in_), reads=reads, writes=writes)

    def red(self, out, in_, reads, writes):
        self.S.op("dve", lambda e: e.tensor_reduce(out=out, in_=in_, axis=AX.X, op=ALU.add), reads=reads, writes=writes)

    def recip(self, out, in_, reads, writes):
        self.S.op("dve", lambda e: e.reciprocal(out=out, in_=in_), reads=reads, writes=writes)

    def memset(self, eng, ap, val, writes):
        self.S.op(eng, lambda e: e.memset(ap, val), writes=writes)

    def rmsnorm(self, x_ap, xn, width, gbc_ap, gn, out_ap, outn, junk, ss, rstd, tag):
        jn, sn, rn = "junk" + tag, "ss" + tag, "rstd" + tag
        self.act(junk, x_ap, AF.Square, reads=[xn], writes=[jn, sn], accum_out=ss)
        self.ts(rstd, ss, 1.0 / width, RMS_EPS, ALU.mult, ALU.add, reads=[sn], writes=[rn])
        self.act(rstd, rstd, AF.Sqrt, reads=[rn], writes=[rn])
        self.recip(rstd, rstd, reads=[rn], writes=[rn])
        self.stt(out_ap, x_ap, rstd, gbc_ap, ALU.mult, ALU.mult, reads=[xn, rn, gn], writes=[outn])

    def phase_A(self, l, xin):
        NT = self.NT
        self.begin_phase()
        wA = self.sb("wA", [128, 8, NIN], BF16)
        gbc = self.sb("gA", [128, D], F32)
        idb = self.sb("idb", [128, 128], BF16)
        junk = self.sb("junk", [128, D], F32)
        ss = self.sb("ss", [128, 1], F32)
        rstd = self.sb("rstd", [128, 1], F32)
        xt = [self.sb("xt", [128, D], F32) for _ in range(2)]
        h = [self.sb("h", [128, D], BF16) for _ in range(2)]
        hT = [self.sb("hT", [128, 8, 128], BF16) for _ in range(2)]
        Pt = [self.sb("Pt", [128, NIN], F32) for _ in range(2)]
        pT = [self.ps("pT", [128, 8, 128], BF16) for _ in range(2)]
        pP = [self.ps("pP", [128, 512], F32) for _ in range(4)]
        self.load(idb[:], self.c_ident, writes=["idb"], cast=True)
        self.bcast_load(gbc, self.w["norm_mix_g"][l:l + 1, :], D, "gA")
        self.load_w_bf16(wA, self.w["w_in"][l], D, "wA")
        npieces = (NIN + 511) // 512

        def norm(i):
            b = i % 2
            self.load(xt[b][:], xin[i * 128:(i + 1) * 128, :], writes=["xt%d" % b])
            self.rmsnorm(xt[b][:], "xt%d" % b, D, gbc[:], "gA", h[b][:], "h%d" % b, junk[:], ss[:], rstd[:], "A")

        def trans(i):
            b = i % 2
            for k in range(8):
                self.tr(pT[b][:, k, :], h[b][:, k * 128:(k + 1) * 128], idb[:], reads=["h%d" % b, "idb"], writes=["pT%d" % b])
            self.cp("act", hT[b][:], pT[b][:], reads=["pT%d" % b], writes=["hT%d" % b])

        norm(0)
        trans(0)
        for i in range(NT):
            b = i % 2
            if i + 1 < NT:
                norm(i + 1)
            for j in range(npieces):
                n0 = j * 512
                n = min(512, NIN - n0)
                pp = pP[j % 4]
                pn = "pP%d" % (j % 4)
                for k in range(8):
                    self.mm(pp[:, 0:n], hT[b][:, k, :], wA[:, k, n0:n0 + n], k == 0, k == 7,
                            reads=["hT%d" % b, "wA"], writes=[pn])
                self.cp("dve" if j % 2 == 0 else "act", Pt[b][:, n0:n0 + n], pp[:, 0:n],
                        reads=[pn], writes=[("Pt", b, j)])
                if j == 5 and i + 1 < NT:
                    trans(i + 1)
            self.store(self.P[i * 128:(i + 1) * 128, :], Pt[b][:], reads=[("Pt", b, j) for j in range(npieces)],
                       writes=[("P", i)])
        self.end_phase()

    def phase_R(self, l, d):
        NT = self.NT
        S_LEN = self.S_LEN
        W = self.w
        self.begin_phase()
        sb, ps = self.sb, self.ps
        if d == 0:
            mu0 = sb("mu0", [128, 1952], F32)
            mu1 = sb("mu1", [128, 1952], F32)
            c0 = sb("c0", [128, 1952], F32)
            pap = sb("pap", [128, 1952], F32)
            pan = sb("pan", [128, 1952], F32)
            kkbc = sb("kkbc", [128, 512], F32)
        w0bc = sb("w0bc", [128, 512], F32)
        a0bc = sb("a0bc", [128, 512], F32)
        kabc = sb("kabc", [128, 512], F32)
        w2b = sb("w2b", [64, 512], BF16)
        a2b = sb("a2b", [64, 512], BF16)
        tri = sb("tri", [128, 128], F32)
        ones = sb("ones", [128, 128], F32)
        m4 = sb("m4", [128, 512], F32)
        mn4 = sb("mn4", [128, 512], F32)
        idb = sb("idb", [128, 128], BF16)
        pac = [sb("pac", [128, 1952], F32) for _ in range(2)]
        lo = sb("lo", [128, 128], BF16)
        loT = sb("loT", [64, 2, 128], BF16)
        sgm = sb("sgm", [128, 512], F32)
        av = sb("av", [128, 512], F32)
        kk = sb("kk", [128, 512], F32)
        tmp = sb("tmp", [128, 512], F32)
        kd = sb("kd", [128, 512], F32)
        ka = sb("ka", [128, 512], F32)
        Ls = sb("Ls", [128, 512], F32)
        Ld = sb("Ld", [128, 512], F32)
        E1 = sb("E1", [128, 512], F32)
        E2 = sb("E2", [128, 512], F32)
        E3 = sb("E3", [128, 512], F32)
        E4 = sb("E4", [128, 512], F32)
        ssq = sb("ssq", [128, 8], F32)
        gC = [sb("gC", [64, 8], F32) for _ in range(2)]
        Ab = [sb("Ab", [128, 512], BF16) for _ in range(2)]
        Rb = [sb("Rb", [128, 512], BF16) for _ in range(2)]
        Bb = [sb("Bb", [128, 512], BF16) for _ in range(2)]
        Kb = [sb("Kb", [128, 512], BF16) for _ in range(2)]
        Btb = [sb("Btb", [128, 512], BF16) for _ in range(2)]
        Ktb = [sb("Ktb", [128, 512], BF16) for _ in range(2)]
        Vb = [sb("Vb", [128, 512], BF16) for _ in range(2)]
        ART = [sb("ART", [128, 8, 256], BF16) for _ in range(2)]
        BT = [sb("BT", [64, 8, 128], BF16) for _ in range(2)]
        KTt = [sb("KTt", [64, 8, 128], BF16) for _ in range(2)]
        ATall = sb("ATall", [128, 8, 512], BF16)
        PP = [sb("PP", [128, 8, 256], BF16) for _ in range(2)]
        Wb = sb("Wb", [128, 512], BF16)
        osb = sb("osb", [128, 512], F32)
        ST32 = sb("ST32", [64, 8, 64], F32)
        STb = sb("STb", [128, 8, 64], BF16)
        if d == 1:
            rkbc = sb("rkbc", [128, 512], F32)
            gngbc = sb("gngbc", [128, 512], F32)
            gnbbc = sb("gnbbc", [128, 512], F32)
            g2b = sb("g2b", [128, 2, 512], BF16)
            oft = sb("oft", [128, 512], F32)
            cen = sb("cen", [128, 512], F32)
            sq2 = sb("sq2", [128, 512], F32)
            bon = sb("bon", [128, 512], F32)
            st8 = sb("st8", [128, 8], F32)
            sv8 = sb("sv8", [128, 8], F32)
            sb8 = sb("sb8", [128, 8], F32)
            gs = sb("gs", [128, 160], BF16)
            gT = sb("gT", [128, 2, 128], BF16)
            yat = sb("yat", [128, 512], F32)
        pbk = [ps("pq%d" % i, [128, 512], F32) for i in range(3)]
        pbn = ["pq0", "pq1", "pq2"]
        pbb = [t[:].bitcast(BF16) for t in pbk]
        sbk = [ps("sq%d" % i, [128, 512], F32) for i in range(5)]
        sbn = ["sq0", "sq1", "sq2", "sq3", "sq4"]
        s3, s4 = sbk[3], sbk[4]
        s3b = s3[:].bitcast(BF16)

        self.load(idb[:], self.c_ident, writes=["idb"], cast=True)
        self.load(tri[:], self.c_tri[d], writes=["tri"])
        self.load(ones[:], self.c_ones, writes=["ones"])
        self.load(m4[:], self.c_m4[d], writes=["m4"])
        self.load(mn4[:], self.c_mn4[d], writes=["mn4"])
        if d == 0:
            self.bcast_load(mu0, W["shift_mu"][l, 0:1, :], 1952, "mu0")
            self.bcast_load(mu1, W["shift_mu"][l, 1:2, :], 1952, "mu1")
            self.bcast_load(kkbc, W["k_k"][l:l + 1, :], 512, "kkbc")
            self.tt("dve", c0[:], mu0[:], mu1[:], ALU.add, reads=["mu0", "mu1"], writes=["c0"])
            self.ts(c0[:], c0[:], -1.0, 1.0, ALU.mult, ALU.add, reads=["c0"], writes=["c0"])
        self.bcast_load(w0bc, W["decay_w0"][l, d:d + 1, :], 512, "w0bc")
        self.bcast_load(a0bc, W["iclr_a0"][l, d:d + 1, :], 512, "a0bc")
        self.bcast_load(kabc, W["k_a"][l:l + 1, :], 512, "kabc")
        self.load(w2b[:], W["decay_w2"][l, d], writes=["w2b"], cast=True)
        self.load(a2b[:], W["iclr_a2"][l, d], writes=["a2b"], cast=True)
        self.memset("dve", ST32[:], 0.0, writes=["ST32"])
        self.memset("dve", STb[:], 0.0, writes=["STb"])
        for b in range(2):
            self.memset("dve", ART[b][:], 0.0, writes=["ART%d" % b])
        if d == 1:
            self.bcast_load(rkbc, W["r_k"][l:l + 1, :], 512, "rkbc")
            self.bcast_load(gngbc, W["gn_g"][l:l + 1, :], 512, "gngbc")
            self.bcast_load(gnbbc, W["gn_b"][l:l + 1, :], 512, "gnbbc")
            self.memset("dve", gT[:], 0.0, writes=["gT"])
            self.memset("dve", g2b[:], 0.0, writes=["g2b"])
            self.load(g2b[:, 0, :], W["gate_g2"][l, 0:128, :], writes=["g2b"], cast=True)
            self.load(g2b[0:32, 1, :], W["gate_g2"][l, 128:160, :], writes=["g2b"], cast=True)

        def v3(ap):
            return ap.rearrange("p (h e) -> p h e", h=8)

        def prep(i, pb):
            t0 = i * 128
            pc = pac[pb]
            pn_ = "pac%d" % pb
            if d == 0:
                self.load(pc[:], self.P[t0:t0 + 128, 2048:4000], writes=[pn_])
                if i == 0:
                    self.memset("pool", pap[:], 0.0, writes=["pap"])
                    self.load(pap[1:128, :], self.P[0:127, 2048:4000], writes=["pap"])
                else:
                    self.load(pap[:], self.P[t0 - 1:t0 + 127, 2048:4000], writes=["pap"])
                if i == NT - 1:
                    self.memset("pool", pan[:], 0.0, writes=["pan"])
                    self.load(pan[0:127, :], self.P[t0 + 1:S_LEN, 2048:4000], writes=["pan"])
                else:
                    self.load(pan[:], self.P[t0 + 1:t0 + 129, 2048:4000], writes=["pan"])
                self.tt("pool", pap[:], pap[:], mu0[:], ALU.mult, reads=["pap", "mu0"], writes=["pap"])
                self.tt("dve", pan[:], pan[:], mu1[:], ALU.mult, reads=["pan", "mu1"], writes=["pan"])
                self.tt("pool", pc[:], pc[:], c0[:], ALU.mult, reads=[pn_, "c0"], writes=[pn_])
                yield
                self.tt("dve", pc[:], pc[:], pan[:], ALU.add, reads=[pn_, "pan"], writes=[pn_])
                self.tt("dve", pc[:], pc[:], pap[:], ALU.add, reads=[pn_, "pap"], writes=[pn_])
                self.store(self.PSd[t0:t0 + 128, :], pc[:], reads=[pn_], writes=[("PS", i)])
            else:
                self.load(pc[:], self.PSd[t0:t0 + 128, :], writes=[pn_])
                self.load(kk[:], self.KKd[t0:t0 + 128, :], writes=["kk"])
            yield
            r_ = pc[:, 0:512]
            k_ = pc[:, 512:1024]
            v_ = pc[:, 1024:1536]
            lw_ = pc[:, 1536 + 64 * d:1600 + 64 * d]
            la_ = pc[:, 1664 + 64 * d:1728 + 64 * d]
            self.act(lo[:, 0:64], lw_, AF.Tanh, reads=[pn_], writes=["lo"])
            self.cp("dve", lo[:, 64:128], la_, reads=[pn_], writes=["lo"])
            self.tr(pbb[0][0:64, 0:128], lo[:, 0:64], idb[:], reads=["lo", "idb"], writes=[pbn[0]])
            self.tr(pbb[0][0:64, 128:256], lo[:, 64:128], idb[:], reads=["lo", "idb"], writes=[pbn[0]])
            self.cp("act", loT[:].rearrange("p a b -> p (a b)"), pbb[0][0:64, 0:256], reads=[pbn[0]], writes=["loT"])
            self.mm(pbk[1][:], loT[:, 0, :], w2b[:], True, True, reads=["loT", "w2b"], writes=[pbn[1]])
            self.mm(pbk[2][:], loT[:, 1, :], a2b[:], True, True, reads=["loT", "a2b"], writes=[pbn[2]])
            self.tt("dve", sgm[:], pbk[1][:], w0bc[:], ALU.add, reads=[pbn[1], "w0bc"], writes=["sgm"])
            self.act(sgm[:], sgm[:], AF.Sigmoid, reads=["sgm"], writes=["sgm"])
            self.tt("dve", av[:], pbk[2][:], a0bc[:], ALU.add, reads=[pbn[2], "a0bc"], writes=["av"])
            self.act(av[:], av[:], AF.Sigmoid, reads=["av"], writes=["av"])
            yield
            if d == 0:
                self.tt("dve", kk[:], k_, kkbc[:], ALU.mult, reads=[pn_, "kkbc"], writes=["kk"])
                self.tt("pool", tmp[:], kk[:], kk[:], ALU.mult, reads=["kk"], writes=["tmp"])
                self.red(ssq[:], v3(tmp[:]), reads=["tmp"], writes=["ssq"])
                self.act(ssq[:], ssq[:], AF.Sqrt, reads=["ssq"], writes=["ssq"])
                self.ts(ssq[:], ssq[:], 1e-12, None, ALU.max, ALU.bypass, reads=["ssq"], writes=["ssq"])
                self.recip(ssq[:], ssq[:], reads=["ssq"], writes=["ssq"])
                self.tt("dve", v3(kk[:]), v3(kk[:]), ssq[:].unsqueeze(2).broadcast_to([128, 8, 64]), ALU.mult,
                        reads=["kk", "ssq"], writes=["kk"])
                self.store(self.KKd[t0:t0 + 128, :], kk[:], reads=["kk"], writes=[("KK", i)])
            self.stt(tmp[:], av[:], -1.0, kabc[:], ALU.add, ALU.mult, reads=["av", "kabc"], writes=["tmp"])
            self.stt(kd[:], tmp[:], 1.0, k_, ALU.add, ALU.mult, reads=["tmp", pn_], writes=["kd"])
            self.tt("pool", ka[:], kk[:], av[:], ALU.mult, reads=["kk", "av"], writes=["ka"])
            yield
            self.mm(pbk[0][:], tri[:], sgm[:], True, True, reads=["tri", "sgm"], writes=[pbn[0]])
            self.mm(pbk[1][:], ones[:], sgm[:], True, True, reads=["ones", "sgm"], writes=[pbn[1]])
            for hh in range(8):
                self.mm(pbk[2][0:64, hh * 2:hh * 2 + 2], sgm[:, hh * 64:(hh + 1) * 64], ones[:, 0:2], True, True,
                        reads=["sgm", "ones"], writes=[pbn[2]])
            self.act(gC[pb][:], pbk[2][0:64, 0:16].rearrange("p (h t) -> p h t", t=2)[:, :, 0], AF.Exp,
                     reads=[pbn[2]], writes=["gC%d" % pb], scale=-CDEC)
            self.cp("act", Ls[:], pbk[0][:], reads=[pbn[0]], writes=["Ls"])
            self.tt("dve", Ld[:], pbk[1][:], Ls[:], ALU.subtract, reads=[pbn[1], "Ls"], writes=["Ld"])
            self.act(E2[:], pbk[0][:], AF.Exp, reads=[pbn[0]], writes=["E2"], scale=-CDEC)
            self.act(E3[:], pbk[0][:], AF.Exp, reads=[pbn[0]], writes=["E3"], scale=CDEC)
            yield
            self.tt("dve", Ls[:], Ls[:], sgm[:], ALU.subtract, reads=["Ls", "sgm"], writes=["Ls"])
            self.act(E1[:], Ls[:], AF.Exp, reads=["Ls"], writes=["E1"], scale=-CDEC)
            self.act(E4[:], Ld[:], AF.Exp, reads=["Ld"], writes=["E4"], scale=-CDEC)
            self.tt("dve", Rb[pb][:], r_, E2[:], ALU.mult, reads=[pn_, "E2"], writes=["Rb%d" % pb])
            self.tt("pool", Bb[pb][:], ka[:], E3[:], ALU.mult, reads=["ka", "E3"], writes=["Bb%d" % pb])
            self.tt("pool", Kb[pb][:], kd[:], E3[:], ALU.mult, reads=["kd", "E3"], writes=["Kb%d" % pb])
            self.stt(Ab[pb][:], kk[:], -1.0, E1[:], ALU.mult, ALU.mult, reads=["kk", "E1"], writes=["Ab%d" % pb])
            yield
            self.tt("pool", Btb[pb][:], ka[:], E4[:], ALU.mult, reads=["ka", "E4"], writes=["Btb%d" % pb])
            self.tt("dve", Ktb[pb][:], kd[:], E4[:], ALU.mult, reads=["kd", "E4"], writes=["Ktb%d" % pb])
            self.cp("pool", Vb[pb][:], v_, reads=[pn_], writes=["Vb%d" % pb])
            for hh in range(8):
                hs = slice(hh * 64, (hh + 1) * 64)
                ts_ = slice(hh * 128, (hh + 1) * 128)
                self.tr(pbb[0][0:64, ts_], Ab[pb][:, hs], idb[:], reads=["Ab%d" % pb, "idb"], writes=[pbn[0]])
                self.tr(pbb[1][0:64, ts_], Rb[pb][:, hs], idb[:], reads=["Rb%d" % pb, "idb"], writes=[pbn[1]])
                self.tr(pbb[2][0:64, ts_], Bb[pb][:, hs], idb[:], reads=["Bb%d" % pb, "idb"], writes=[pbn[2]])
            self.cp("act", ART[pb][0:64, :, 0:128], pbb[0][0:64, :].rearrange("p (h t) -> p h t", h=8),
                    reads=[pbn[0]], writes=["ART%d" % pb])
            self.cp("dve", ART[pb][0:64, :, 128:256], pbb[1][0:64, :].rearrange("p (h t) -> p h t", h=8),
                    reads=[pbn[1]], writes=["ART%d" % pb])
            self.cp("act", BT[pb][:], pbb[2][0:64, :].rearrange("p (h t) -> p h t", h=8), reads=[pbn[2]], writes=["BT%d" % pb])
            yield
            for hh in range(8):
                hs = slice(hh * 64, (hh + 1) * 64)
                ts_ = slice(hh * 128, (hh + 1) * 128)
                self.tr(pbb[0][0:64, ts_], Kb[pb][:, hs], idb[:], reads=["Kb%d" % pb, "idb"], writes=[pbn[0]])
            self.cp("dve", KTt[pb][:], pbb[0][0:64, :].rearrange("p (h t) -> p h t", h=8), reads=[pbn[0]], writes=["KTt%d" % pb])
            yield

        def solve(i, pb):
            t0 = i * 128
            pc = pac[pb]
            pn_ = "pac%d" % pb
            art, bt, ktt = ART[pb], BT[pb], KTt[pb]
            an, bn_, kn = "ART%d" % pb, "BT%d" % pb, "KTt%d" % pb
            vb, vn = Vb[pb], "Vb%d" % pb
            for hh in range(8):
                qq, qn = sbk[hh % 3], sbn[hh % 3]
                self.mm(qq[:, 0:256], bt[:, hh, :], art[0:64, hh, :], True, True, reads=[bn_, an], writes=[qn])
                self.mm(qq[:, 256:512], ktt[:, hh, :], art[0:64, hh, :], True, True, reads=[kn, an], writes=[qn])
                self.tt("dve", ATall[:, hh, :], qq[:], m4[:], ALU.mult, reads=[qn, "m4"], writes=[("AT", hh)])
                if hh == 3:
                    yield
            yield
            for g in range(2):
                for j in range(4):
                    hh = g * 4 + j
                    self.mm(s3[:, j * 128:(j + 1) * 128], art[0:64, hh, 0:128], bt[:, hh, :], True, True,
                            reads=[an, bn_], writes=["sq3"])
                self.tt("dve", PP[0][:, g * 4:(g + 1) * 4, 0:128], s3[:].rearrange("p (j s) -> p j s", j=4),
                        mn4[:].rearrange("p (j s) -> p j s", j=4), ALU.mult, reads=["sq3", "mn4"],
                        writes=[("PP", 0, 2 * g), ("PP", 0, 2 * g + 1)])
            self.cp("pool", PP[0][:, :, 128:256], ATall[:, :, 0:128], reads=[("AT", hh) for hh in range(8)],
                    writes=[("PP", 0, pr) for pr in range(4)])
            for hh in range(8):
                hs = slice(hh * 64, (hh + 1) * 64)
                self.mm(s4[:, hs], art[:, hh, 0:128], STb[:, hh, :], True, False, reads=[an, "STb"], writes=["sq4"])
                self.mm(s4[:, hs], ATall[:, hh, 256:384], vb[:, hs], False, True, reads=[("AT", hh), vn], writes=["sq4"])
            self.cp("act", Wb[:], s4[:], reads=["sq4"], writes=["Wb"])
            yield
            rot = 0
            for j in range(7):
                cb = j % 2
                cur = PP[cb]
                for hh in range(8):
                    hs = slice(hh * 64, (hh + 1) * 64)
                    self.mm(s3[:, hs], cur[:, hh, 128:256], Wb[:, hs], True, False,
                            reads=[("PP", cb, hh // 2), "Wb"], writes=["sq3"])
                    self.mm(s3[:, hs], idb[:], Wb[:, hs], False, True, reads=["idb", "Wb"], writes=["sq3"])
                self.cp("act", Wb[:], s3[:], reads=["sq3"], writes=["Wb"])
                if j < 6:
                    nxt = PP[1 - cb]
                    for pr in range(4):
                        bankt, bname = sbk[rot % 3], sbn[rot % 3]
                        rot += 1
                        for u in range(2):
                            hh = pr * 2 + u
                            self.mm(bankt[:, u * 256:u * 256 + 128], cur[:, hh, 128:256], cur[:, hh, 0:128], True, True,
                                    reads=[("PP", cb, pr)], writes=[bname])
                            self.mm(bankt[:, u * 256 + 128:u * 256 + 256], cur[:, hh, 0:128], cur[:, hh, 128:256], True, True,
                                    reads=[("PP", cb, pr)], writes=[bname])
                        self.cp("dve" if pr == 0 else "act",
                                nxt[:, pr * 2:pr * 2 + 2, :].rearrange("p a b -> p (a b)"), bankt[:],
                                reads=[bname], writes=[("PP", 1 - cb, pr)])
                yield
            for hh in range(8):
                hs = slice(hh * 64, (hh + 1) * 64)
                self.mm(s4[:, hs], art[:, hh, 128:256], STb[:, hh, :], True, False, reads=[an, "STb"], writes=["sq4"])
                self.mm(s4[:, hs], ATall[:, hh, 128:256], Wb[:, hs], False, False, reads=[("AT", hh), "Wb"], writes=["sq4"])
                self.mm(s4[:, hs], ATall[:, hh, 384:512], vb[:, hs], False, True, reads=[("AT", hh), vn], writes=["sq4"])
            self.cp("act", osb[:], s4[:], reads=["sq4"], writes=["osb"])
            for hh in range(8):
                hs = slice(hh * 64, (hh + 1) * 64)
                self.mm(s3[0:64, hs], Btb[pb][:, hs], Wb[:, hs], True, False, reads=["Btb%d" % pb, "Wb"], writes=["sq3"])
                self.mm(s3[0:64, hs], Ktb[pb][:, hs], vb[:, hs], False, True, reads=["Ktb%d" % pb, vn], writes=["sq3"])
            self.tt("dve", ST32[:], ST32[:], gC[pb][:].unsqueeze(2).broadcast_to([64, 8, 64]), ALU.mult,
                    reads=["ST32", "gC%d" % pb], writes=["ST32"])
            self.tt("dve", ST32[:], ST32[:], s3[0:64, :].rearrange("p (h e) -> p h e", h=8), ALU.add,
                    reads=["ST32", "sq3"], writes=["ST32"])
            self.cp("act", STb[0:64, :, :], ST32[:], reads=["ST32"], writes=["STb"])
            yield
            if d == 0:
                self.store(self.of[t0:t0 + 128, :], osb[:], reads=["osb"], writes=[("of", i)])
                return
            r_ = pc[:, 0:512]
            k_ = pc[:, 512:1024]
            lg_ = pc[:, 1792:1952]
            self.load(oft[:], self.of[t0:t0 + 128, :], writes=["oft"])
            self.tt("dve", oft[:], oft[:], osb[:], ALU.add, reads=["oft", "osb"], writes=["oft"])
            self.red(st8[:], v3(oft[:]), reads=["oft"], writes=["st8"])
            self.ts(st8[:], st8[:], 1.0 / 64, None, ALU.mult, ALU.bypass, reads=["st8"], writes=["st8"])
            self.tt("dve", v3(cen[:]), v3(oft[:]), st8[:].unsqueeze(2).broadcast_to([128, 8, 64]), ALU.subtract,
                    reads=["oft", "st8"], writes=["cen"])
            self.tt("pool", sq2[:], cen[:], cen[:], ALU.mult, reads=["cen"], writes=["sq2"])
            self.red(sv8[:], v3(sq2[:]), reads=["sq2"], writes=["sv8"])
            self.ts(sv8[:], sv8[:], 1.0 / 64, GN_EPS, ALU.mult, ALU.add, reads=["sv8"], writes=["sv8"])
            self.act(sv8[:], sv8[:], AF.Sqrt, reads=["sv8"], writes=["sv8"])
            self.recip(sv8[:], sv8[:], reads=["sv8"], writes=["sv8"])
            yield
            self.tt("dve", v3(cen[:]), v3(cen[:]), sv8[:].unsqueeze(2).broadcast_to([128, 8, 64]), ALU.mult,
                    reads=["cen", "sv8"], writes=["cen"])
            self.tt("pool", cen[:], cen[:], gngbc[:], ALU.mult, reads=["cen", "gngbc"], writes=["cen"])
            self.tt("pool", cen[:], cen[:], gnbbc[:], ALU.add, reads=["cen", "gnbbc"], writes=["cen"])
            self.tt("dve", sq2[:], r_, k_, ALU.mult, reads=[pn_], writes=["sq2"])
            self.tt("pool", sq2[:], sq2[:], rkbc[:], ALU.mult, reads=["sq2", "rkbc"], writes=["sq2"])
            self.red(sb8[:], v3(sq2[:]), reads=["sq2"], writes=["sb8"])
            self.tt("dve", v3(bon[:]), v3(pc[:, 1024:1536]),
                    sb8[:].unsqueeze(2).broadcast_to([128, 8, 64]), ALU.mult, reads=[pn_, "sb8"], writes=["bon"])
            self.tt("dve", cen[:], cen[:], bon[:], ALU.add, reads=["cen", "bon"], writes=["cen"])
            self.act(gs[:], lg_, AF.Sigmoid, reads=[pn_], writes=["gs"])
            self.tr(s3b[:, 0:128], gs[:, 0:128], idb[:], reads=["gs", "idb"], writes=["sq3"])
            self.tr(s3b[0:32, 128:256], gs[:, 128:160], idb[:], reads=["gs", "idb"], writes=["sq3"])
            self.cp("act", gT[:, 0, :], s3b[:, 0:128], reads=["sq3"], writes=["gT"])
            self.cp("act", gT[0:32, 1, :], s3b[0:32, 128:256], reads=["sq3"], writes=["gT"])
            self.mm(s4[:], gT[:, 0, :], g2b[:, 0, :], True, False, reads=["gT", "g2b"], writes=["sq4"])
            self.mm(s4[:], gT[:, 1, :], g2b[:, 1, :], False, True, reads=["gT", "g2b"], writes=["sq4"])
            self.tt("dve", yat[:], cen[:], s4[:], ALU.mult, reads=["cen", "sq4"], writes=["yat"])
            self.store(self.ya[t0:t0 + 128, :], yat[:], reads=["yat"], writes=[("ya", i)])
            yield

        order = list(range(NT)) if d == 0 else list(range(NT - 1, -1, -1))
        for _ in prep(order[0], 0):
            pass
        for n, i in enumerate(order):
            gs_ = solve(i, n % 2)
            gp_ = prep(order[n + 1], (n + 1) % 2) if n + 1 < NT else None
            while gs_ is not None or gp_ is not None:
                if gs_ is not None:
                    try:
                        next(gs_)
                    except StopIteration:
                        gs_ = None
                if gp_ is not None:
                    try:
                        next(gp_)
                    except StopIteration:
                        gp_ = None
        self.end_phase()

    def phase_MP(self, l):
        NT = self.NT
        W = self.w
        self.begin_phase()
        sb, ps = self.sb, self.ps
        qg = sb("qg", [128, 256], F32)
        kvg = sb("kvg", [128, 128], F32)
        wuq = sb("wuq", [128, 2, 768], BF16)
        wukv = sb("wukv", [128, 1, 1024], BF16)
        idb = sb("idb", [128, 128], BF16)
        pm = sb("pm", [128, 416], F32)
        cs = sb("cs", [128, 32], F32)
        sn = sb("sn", [128, 32], F32)
        junk = sb("junk", [128, 256], F32)
        ss = sb("ss", [128, 1], F32)
        rstd = sb("rstd", [128, 1], F32)
        nb = sb("nb", [128, 384], BF16)
        nT = sb("nT", [128, 3, 128], BF16)
        qf = sb("qf", [128, 768], F32)
        kvf = sb("kvf", [128, 1024], F32)
        t1 = sb("t1", [128, 8, 32], F32)
        t2 = sb("t2", [128, 8, 32], F32)
        kro = sb("kro", [128, 32], F32)
        kr2 = sb("kr2", [128, 32], F32)
        Qa = sb("Qa", [128, 8, 96], BF16)
        Ka = sb("Ka", [128, 8, 96], BF16)
        Va = sb("Va", [128, 8, 65], BF16)
        QTt = sb("QTt", [96, 8, 128], BF16)
        KTt = sb("KTt", [96, 8, 128], BF16)
        q = [ps("q%d" % i, [128, 512], F32) for i in range(7)]
        qb = [t[:].bitcast(BF16) for t in q]
        self.load(idb[:], self.c_ident, writes=["idb"], cast=True)
        self.bcast_load(qg, W["q_norm_g"][l:l + 1, :], 256, "qg")
        self.bcast_load(kvg, W["kv_norm_g"][l:l + 1, :], 128, "kvg")
        self.load_w_bf16(wuq, W["w_uq"][l], 256, "wuq")
        self.load_w_bf16(wukv, W["w_ukv"][l], 128, "wukv")
        self.memset("dve", Va[:], 1.0, writes=["Va"])
        qf3 = qf[:].rearrange("p (h e) -> p h e", h=8)
        kvf3 = kvf[:].rearrange("p (h e) -> p h e", h=8)
        for i in range(NT):
            t0 = i * 128
            self.load(pm[:], self.P[t0:t0 + 128, 4000:4416], writes=["pm"])
            self.load(cs[:], self.c_cos[t0:t0 + 128, :], writes=["cs"])
            self.load(sn[:], self.c_sin[t0:t0 + 128, :], writes=["sn"])
            self.rmsnorm(pm[:, 0:256], "pm", 256, qg[:], "qg", nb[:, 0:256], "nbq", junk[:, 0:256], ss[:], rstd[:], "M")
            self.rmsnorm(pm[:, 256:384], "pm", 128, kvg[:], "kvg", nb[:, 256:384], "nbk", junk[:, 0:128], ss[:], rstd[:], "M")
            for c in range(3):
                self.tr(qb[0][:, c * 128:(c + 1) * 128], nb[:, c * 128:(c + 1) * 128], idb[:],
                        reads=["nbq", "nbk", "idb"], writes=["q0"])
            self.cp("act", nT[:].rearrange("p a b -> p (a b)"), qb[0][:, 0:384], reads=["q0"], writes=["nT"])
            for c in range(2):
                self.mm(q[1][:], nT[:, c, :], wuq[:, c, 0:512], c == 0, c == 1, reads=["nT", "wuq"], writes=["q1"])
            for c in range(2):
                self.mm(q[2][:, 0:256], nT[:, c, :], wuq[:, c, 512:768], c == 0, c == 1, reads=["nT", "wuq"], writes=["q2"])
            self.mm(q[3][:], nT[:, 2, :], wukv[:, 0, 0:512], True, True, reads=["nT", "wukv"], writes=["q3"])
            self.mm(q[4][:], nT[:, 2, :], wukv[:, 0, 512:1024], True, True, reads=["nT", "wukv"], writes=["q4"])
            self.cp("act", qf[:, 0:512], q[1][:], reads=["q1"], writes=["qf"])
            self.cp("dve", qf[:, 512:768], q[2][:, 0:256], reads=["q2"], writes=["qf"])
            self.cp("act", kvf[:, 0:512], q[3][:], reads=["q3"], writes=["kvf"])
            self.cp("dve", kvf[:, 512:1024], q[4][:], reads=["q4"], writes=["kvf"])
            self.cp("pool", Qa[:, :, 0:64], qf3[:, :, 0:64], reads=["qf"], writes=["Qa"])
            csb = cs[:].unsqueeze(1).broadcast_to([128, 8, 32])
            self.tt("dve", t1[:], qf3[:, :, 64:96], csb, ALU.mult, reads=["qf", "cs"], writes=["t1"])
            self.tt("dve", t2[:, :, 0:16], qf3[:, :, 80:96], sn[:, 0:16].unsqueeze(1).broadcast_to([128, 8, 16]), ALU.mult,
                    reads=["qf", "sn"], writes=["t2"])
            self.tt("dve", t2[:, :, 16:32], qf3[:, :, 64:80], sn[:, 16:32].unsqueeze(1).broadcast_to([128, 8, 16]), ALU.mult,
                    reads=["qf", "sn"], writes=["t2"])
            self.tt("dve", Qa[:, :, 64:96], t1[:], t2[:], ALU.add, reads=["t1", "t2"], writes=["Qa"])
            self.tt("dve", kro[:], pm[:, 384:416], cs[:], ALU.mult, reads=["pm", "cs"], writes=["kro"])
            self.tt("dve", kr2[:, 0:16], pm[:, 400:416], sn[:, 0:16], ALU.mult, reads=["pm", "sn"], writes=["kr2"])
            self.tt("dve", kr2[:, 16:32], pm[:, 384:400], sn[:, 16:32], ALU.mult, reads=["pm", "sn"], writes=["kr2"])
            self.tt("dve", kro[:], kro[:], kr2[:], ALU.add, reads=["kro", "kr2"], writes=["kro"])
            self.cp("dve", Ka[:, :, 64:96], kro[:].unsqueeze(1).broadcast_to([128, 8, 32]), reads=["kro"], writes=["Ka"])
            self.cp("pool", Ka[:, :, 0:64], kvf3[:, :, 0:64], reads=["kvf"], writes=["Ka"])
            self.cp("pool", Va[:, :, 0:64], kvf3[:, :, 64:128], reads=["kvf"], writes=["Va"])
            for hh in range(8):
                self.tr(qb[5][0:96, hh * 128:(hh + 1) * 128], Qa[:, hh, :], idb[:], reads=["Qa", "idb"], writes=["q5"])
                self.tr(qb[6][0:96, hh * 128:(hh + 1) * 128], Ka[:, hh, :], idb[:], reads=["Ka", "idb"], writes=["q6"])
            self.cp("act", QTt[:].rearrange("p a b -> p (a b)"), qb[5][0:96, :], reads=["q5"], writes=["QTt"])
            self.cp("dve", KTt[:].rearrange("p a b -> p (a b)"), qb[6][0:96, :], reads=["q6"], writes=["KTt"])
            self.store(self.QT[:, :, t0:t0 + 128].rearrange("h p t -> p h t"), QTt[:], reads=["QTt"], writes=[("QT", i)])
            self.store(self.KT[:, :, t0:t0 + 128].rearrange("h p t -> p h t"), KTt[:], reads=["KTt"], writes=[("KT", i)])
            self.store(self.Vd[t0:t0 + 128, :], Va[:].rearrange("p a b -> p (a b)"), reads=["Va"], writes=[("Vd", i)])
        self.end_phase()

    def phase_MM(self):
        NT = self.NT
        S_LEN = self.S_LEN
        QB = min(512, S_LEN)
        nqb = S_LEN // QB
        nj = QB // 128
        LOOK = 2
        self.begin_phase()
        sb, ps = self.sb, self.ps
        Vall = sb("Vall", [128, NT, 520], BF16)
        KTh = [sb("KTh", [96, S_LEN], BF16) for _ in range(2)]
        QTb = [sb("QTb", [96, QB], BF16) for _ in range(2)]
        PT = [sb("PT", [128, QB], BF16) for _ in range(4)]
        OT = sb("OT", [65, QB], F32)
        id32 = sb("id32", [128, 128], F32)
        osm = sb("osm", [128, nj, 64], F32)
        rec = sb("rec", [128, nj], F32)
        q = [ps("q%d" % i, [128, 512], F32) for i in range(7)]
        self.load(id32[:], self.c_ident, writes=["id32"])
        self.load(Vall[:], self.Vd.rearrange("(c p) f -> p c f", p=128), writes=["Vall"])
        blocks = [(hh, qi) for hh in range(8) for qi in range(nqb)]
        stream = [(bi, kc) for bi in range(len(blocks)) for kc in range(NT)]

        def load_k(hh):
            self.load(KTh[hh % 2][:], self.KT[hh], writes=["KTh%d" % (hh % 2)])

        def load_q(bi):
            hh, qi = blocks[bi]
            self.load(QTb[bi % 2][:], self.QT[hh, :, qi * QB:(qi + 1) * QB], writes=["QTb%d" % (bi % 2)])

        def emit_S(idx):
            bi, kc = stream[idx]
            hh, qi = blocks[bi]
            pb = idx % 4
            self.mm(q[pb][:, 0:QB], KTh[hh % 2][:, kc * 128:(kc + 1) * 128], QTb[bi % 2][:], True, True,
                    reads=["KTh%d" % (hh % 2), "QTb%d" % (bi % 2)], writes=["q%d" % pb])

        def epilogue_a(bi):
            ob = 4 + bi % 2
            self.cp("dve", OT[:], q[ob][0:65, 0:QB], reads=["q%d" % ob], writes=["OT"])

        def epilogue_b(bi):
            hh, qi = blocks[bi]
            for j in range(nj):
                self.tr(q[6][:, j * 65:(j + 1) * 65], OT[:, j * 128:(j + 1) * 128], id32[0:65, 0:65],
                        reads=["OT", "id32"], writes=["q6"])
            o3 = q[6][:, 0:nj * 65].rearrange("p (j e) -> p j e", j=nj)
            self.recip(rec[:], o3[:, :, 64], reads=["q6"], writes=["rec"])
            self.tt("dve", osm[:], o3[:, :, 0:64], rec[:].unsqueeze(2).broadcast_to([128, nj, 64]), ALU.mult,
                    reads=["q6", "rec"], writes=["osm"])
            self.store(self.yb[qi * QB:(qi + 1) * QB, hh * 64:(hh + 1) * 64].rearrange("(j p) e -> p j e", p=128),
                       osm[:], reads=["osm"], writes=[("yb", hh, qi)])

        load_k(0)
        load_q(0)
        if len(blocks) > 1:
            load_q(1)
        for idx in range(min(LOOK, len(stream))):
            emit_S(idx)
        pending = None
        for idx, (bi, kc) in enumerate(stream):
            hh, qi = blocks[bi]
            if kc == 0:
                if qi == 0 and hh + 1 < 8:
                    load_k(hh + 1)
            if idx + LOOK < len(stream):
                emit_S(idx + LOOK)
            pb = idx % 4
            ob = 4 + bi % 2
            self.act(PT[pb][:], q[pb][:, 0:QB], AF.Exp, reads=["q%d" % pb], writes=["PT%d" % pb], scale=SCALE)
            self.mm(q[ob][0:65, 0:QB], Vall[:, kc, hh * 65:(hh + 1) * 65], PT[pb][:], kc == 0, kc == NT - 1,
                    reads=["Vall", "PT%d" % pb], writes=["q%d" % ob])
            if pending is not None and kc == min(3, NT - 1):
                epilogue_b(pending)
                pending = None
            if kc == NT - 1:
                epilogue_a(bi)
                pending = bi
                if bi + 2 < len(blocks):
                    load_q(bi + 2)
        if pending is not None:
            epilogue_b(pending)
        self.end_phase()

    def phase_C1(self, l, xin):
        NT = self.NT
        W = self.w
        self.begin_phase()
        sb, ps = self.sb, self.ps
        woa = sb("woa", [128, 4, D], BF16)
        wob = sb("wob", [128, 4, D], BF16)
        wout = sb("wout", [128, 8, D], BF16)
        idb = sb("idb", [128, 128], BF16)
        xt = [sb("xt", [128, D], F32) for _ in range(4)]
        gt = [sb("gt", [128, 2048], F32) for _ in range(2)]
        yat = [sb("yat", [128, 512], F32) for _ in range(2)]
        ybt = [sb("ybt", [128, 512], F32) for _ in range(2)]
        yab = [sb("yab", [128, D], BF16) for _ in range(2)]
        yT = [sb("yT", [128, 8, 128], BF16) for _ in range(2)]
        m1 = sb("m1", [128, D], F32)
        m2 = sb("m2", [128, D], F32)
        mixb = [sb("mixb", [128, D], BF16) for _ in range(2)]
        mixT = sb("mixT", [128, 8, 128], BF16)
        x1t = sb("x1t", [128, D], F32)
        q = [ps("q%d" % i, [128, 512], F32) for i in range(8)]
        q6b = q[6][:].bitcast(BF16)
        q7b = q[7][:].bitcast(BF16)
        self.load(idb[:], self.c_ident, writes=["idb"], cast=True)
        self.load_w_bf16(woa, W["w_oa"][l], 512, "woa")
        self.load_w_bf16(wob, W["w_ob"][l], 512, "wob")
        self.load_w_bf16(wout, W["w_out"][l], D, "wout")

        def loads(i):
            b = i % 2
            t0 = i * 128
            self.load(xt[i % 4][:], xin[t0:t0 + 128, :], writes=["xt%d" % (i % 4)])
            self.load(gt[b][:], self.P[t0:t0 + 128, 0:2048], writes=["gt%d" % b])
            self.load(yat[b][:], self.ya[t0:t0 + 128, :], writes=["yat%d" % b])
            self.load(ybt[b][:], self.yb[t0:t0 + 128, :], writes=["ybt%d" % b])

        def s1(i):
            b = i % 2
            self.cp("dve", yab[b][:, 0:512], yat[b][:], reads=["yat%d" % b], writes=[("yab", b, 0)])
            self.cp("pool", yab[b][:, 512:1024], ybt[b][:], reads=["ybt%d" % b], writes=[("yab", b, 1)])
            for k in range(8):
                self.tr(q6b[:, k * 128:(k + 1) * 128], yab[b][:, k * 128:(k + 1) * 128], idb[:],
                        reads=[("yab", b, 0), ("yab", b, 1), "idb"], writes=["q6"])
            self.cp("act", yT[b][:].rearrange("p a b -> p (a b)"), q6b[:], reads=["q6"], writes=["yT%d" % b])
            self.act(gt[b][:], gt[b][:], AF.Sigmoid, reads=["gt%d" % b], writes=["gt%d" % b])

        def s2(i):
            b = i % 2
            for hf in range(2):
                hs = slice(hf * 512, (hf + 1) * 512)
                for k in range(4):
                    self.mm(q[hf][:], yT[b][:, k, :], woa[:, k, hs], k == 0, k == 3, reads=["yT%d" % b, "woa"], writes=["q%d" % hf])
                for k in range(4):
                    self.mm(q[2 + hf][:], yT[b][:, 4 + k, :], wob[:, k, hs], k == 0, k == 3, reads=["yT%d" % b, "wob"],
                            writes=["q%d" % (2 + hf)])
            for hf in range(2):
                hs = slice(hf * 512, (hf + 1) * 512)
                hs2 = slice(1024 + hf * 512, 1024 + (hf + 1) * 512)
                self.tt("dve", m1[:, hs], gt[b][:, hs], q[hf][:], ALU.mult, reads=["gt%d" % b, "q%d" % hf], writes=[("m1", hf)])
                self.tt("dve", m2[:, hs], gt[b][:, hs2], q[2 + hf][:], ALU.mult, reads=["gt%d" % b, "q%d" % (2 + hf)],
                        writes=[("m2", hf)])
                self.tt("pool", mixb[b][:, hs], m1[:, hs], m2[:, hs], ALU.add, reads=[("m1", hf), ("m2", hf)],
                        writes=[("mixb", b, hf)])

        def s3(i):
            b = i % 2
            t0 = i * 128
            xb = i % 4
            for k in range(8):
                self.tr(q7b[:, k * 128:(k + 1) * 128], mixb[b][:, k * 128:(k + 1) * 128], idb[:],
                        reads=[("mixb", b, 0), ("mixb", b, 1), "idb"], writes=["q7"])
            self.cp("act", mixT[:].rearrange("p a b -> p (a b)"), q7b[:], reads=["q7"], writes=["mixT"])
            for hf in range(2):
                hs = slice(hf * 512, (hf + 1) * 512)
                for k in range(8):
                    self.mm(q[4 + hf][:], mixT[:, k, :], wout[:, k, hs], k == 0, k == 7, reads=["mixT", "wout"],
                            writes=["q%d" % (4 + hf)])
                self.tt("dve", x1t[:, hs], xt[xb][:, hs], q[4 + hf][:], ALU.add, reads=["xt%d" % xb, "q%d" % (4 + hf)],
                        writes=[("x1t", hf)])
            self.store(self.x1[t0:t0 + 128, :], x1t[:], reads=[("x1t", 0), ("x1t", 1)], writes=[("x1", i)])

        loads(0)
        if NT > 1:
            loads(1)
        s1(0)
        for i in range(NT + 1):
            if i + 1 < NT:
                s1(i + 1)
            if i < NT:
                s2(i)
            if i + 2 < NT:
                loads(i + 2)
            if i >= 1:
                s3(i - 1)
        self.end_phase()

    def phase_C2(self, l, last, yout):
        NT = self.NT
        W = self.w
        self.begin_phase()
        sb, ps = self.sb, self.ps
        wgu = sb("wgu", [128, 8, 2 * DFF], BF16)
        wdn = sb("wdn", [128, 22, D], BF16)
        gbc = sb("gbc", [128, D], F32)
        idb = sb("idb", [128, 128], BF16)
        xt = [sb("xt", [128, D], F32) for _ in range(2)]
        junk = sb("junk", [128, D], F32)
        ss = sb("ss", [128, 1], F32)
        rstd = sb("rstd", [128, 1], F32)
        h = sb("h", [128, D], BF16)
        hT = [sb("hT", [128, 8, 128], BF16) for _ in range(2)]
        sl = [sb("sl", [128, 256], F32) for _ in range(2)]
        actb = sb("actb", [128, DFF], BF16)
        actT = sb("actT", [128, 22, 128], BF16)
        x2t = sb("x2t", [128, D], F32)
        if last:
            fbc = sb("fbc", [128, D], F32)
        q = [ps("q%d" % i, [128, 512], F32) for i in range(6)]
        qTb = q[4][:].bitcast(BF16)
        qT2 = q[5][:].bitcast(BF16)
        self.load(idb[:], self.c_ident, writes=["idb"], cast=True)
        self.bcast_load(gbc, W["norm_ffn_g"][l:l + 1, :], D, "gbc")
        if last:
            self.bcast_load(fbc, W["final_norm_g"][0:1, :], D, "fbc")
        self.load_w_bf16(wgu, W["w_gu"][l], D, "wgu")
        self.load_w_bf16(wdn, W["w_down"][l], DFF, "wdn")

        def norm(i):
            b = i % 2
            self.load(xt[b][:], self.x1[i * 128:(i + 1) * 128, :], writes=["xt%d" % b])
            self.rmsnorm(xt[b][:], "xt%d" % b, D, gbc[:], "gbc", h[:], "h", junk[:], ss[:], rstd[:], "F")

        def trans(i):
            b = i % 2
            for k in range(8):
                self.tr(qTb[:, k * 128:(k + 1) * 128], h[:, k * 128:(k + 1) * 128], idb[:], reads=["h", "idb"], writes=["q4"])
            self.cp("act", hT[b][:].rearrange("p a b -> p (a b)"), qTb[:], reads=["q4"], writes=["hT%d" % b])

        def tpose(j):
            o = (j % 4) * 256
            for u in range(2):
                self.tr(qT2[:, o + u * 128:o + (u + 1) * 128], actb[:, j * 256 + u * 128:j * 256 + (u + 1) * 128], idb[:],
                        reads=[("actb", j), "idb"], writes=["q5"])
            self.cp("dve", actT[:, 2 * j:2 * j + 2, :].rearrange("p a b -> p (a b)"),
                    qT2[:, o:o + 256], reads=["q5"], writes=[("actT", j)])

        norm(0)
        trans(0)
        for i in range(NT):
            b = i % 2
            t0 = i * 128
            xn = "xt%d" % b
            hn = "hT%d" % b
            if i + 1 < NT:
                norm(i + 1)
            for j in range(11):
                bk = q[j % 2]
                bn = "q%d" % (j % 2)
                for k in range(8):
                    self.mm(bk[:, 0:256], hT[b][:, k, :], wgu[:, k, j * 256:(j + 1) * 256], k == 0, k == 7,
                            reads=[hn, "wgu"], writes=[bn])
                for k in range(8):
                    self.mm(bk[:, 256:512], hT[b][:, k, :], wgu[:, k, DFF + j * 256:DFF + (j + 1) * 256], k == 0, k == 7,
                            reads=[hn, "wgu"], writes=[bn])
                self.act(sl[j % 2][:], bk[:, 0:256], AF.Silu, reads=[bn], writes=["sl%d" % (j % 2)])
                self.tt("dve", actb[:, j * 256:(j + 1) * 256], sl[j % 2][:], bk[:, 256:512], ALU.mult,
                        reads=["sl%d" % (j % 2), bn], writes=[("actb", j)])
                if j >= 1:
                    tpose(j - 1)
            tpose(10)
            if i + 1 < NT:
                trans(i + 1)
            for hf in range(2):
                hs = slice(hf * 512, (hf + 1) * 512)
                for c in range(22):
                    self.mm(q[2 + hf][:], actT[:, c, :], wdn[:, c, hs], c == 0, c == 21,
                            reads=[("actT", c // 2), "wdn"], writes=["q%d" % (2 + hf)])
                self.tt("dve", x2t[:, hs], xt[b][:, hs], q[2 + hf][:], ALU.add, reads=[xn, "q%d" % (2 + hf)],
                        writes=["x2t"])
            if last:
                self.rmsnorm(x2t[:], "x2t", D, fbc[:], "fbc", x2t[:], "x2t", junk[:], ss[:], rstd[:], "F")
                self.store(yout[t0:t0 + 128, :], x2t[:], reads=["x2t"], writes=[("y", i)])
            else:
                self.store(self.x2[t0:t0 + 128, :], x2t[:], reads=["x2t"], writes=[("x2", i)])
        self.end_phase()

    def build(self, phases=None):
        def on(p):
            return phases is None or p in phases
        for s in range(self.NSEQ):
            for l in range(self.depth):
                last = l == self.depth - 1
                xin = self.x[s] if l == 0 else self.x2
                if on("A"):
                    self.phase_A(l, xin)
                if on("R0"):
                    self.phase_R(l, 0)
                if on("R1"):
                    self.phase_R(l, 1)
                if on("MP"):
                    self.phase_MP(l)
                if on("MM"):
                    self.phase_MM()
                if on("C1"):
                    self.phase_C1(l, xin)
                if on("C2"):
                    self.phase_C2(l, last, self.y[s])
        self.S.emit()
        self.S.stack.close()
        return self.nc


def make_consts(S_LEN):
    s = np.arange(128)[:, None]
    t = np.arange(128)[None, :]
    tri = np.stack([(s <= t), (s >= t)]).astype(np.float32)
    strict = [(s < t).astype(np.float32), (s > t).astype(np.float32)]
    incl = [(s <= t).astype(np.float32), (s >= t).astype(np.float32)]
    m4 = np.stack([np.concatenate([strict[d], incl[d], strict[d], incl[d]], axis=1) for d in range(2)])
    mn = [(t < s).astype(np.float32), (t > s).astype(np.float32)]
    mn4 = np.stack([np.concatenate([mn[d]] * 4, axis=1) for d in range(2)])
    pos = np.arange(S_LEN, dtype=np.float32)
    inv_freq = (1.0 / (np.float32(10000.0) ** (np.arange(0, 32, 2, dtype=np.float32) / np.float32(32)))).astype(np.float32)
    ang = pos[:, None] * inv_freq[None, :]
    ang = np.concatenate([ang, ang], axis=-1).astype(np.float32)
    cos = np.cos(ang).astype(np.float32)
    sin = np.sin(ang).astype(np.float32)
    sin_s = sin.copy()
    sin_s[:, 0:16] = -sin_s[:, 0:16]
    return dict(c_ident=np.eye(128, dtype=np.float32), c_tri=tri, c_m4=m4.astype(np.float32),
                c_mn4=mn4.astype(np.float32), c_ones=np.ones((128, 128), np.float32),
                c_cos=cos, c_sin=sin_s)


_WNAMES = ["norm_mix_g", "w_in", "shift_mu", "decay_w2", "decay_w0", "iclr_a2", "iclr_a0", "gate_g2", "k_k", "k_a",
           "r_k", "gn_g", "gn_b", "w_oa", "q_norm_g", "w_uq", "kv_norm_g", "w_ukv", "w_ob", "w_out", "norm_ffn_g",
           "w_gu", "w_down", "final_norm_g"]


def prep_weights(inputs, depth):
    out = {}
    for n in _WNAMES:
        a = np.ascontiguousarray(np.asarray(inputs[n], dtype=np.float32))
        if n == "r_k":
            a = a.reshape(a.shape[0], 512)
        if n == "final_norm_g":
            a = a.reshape(1, D)
        else:
            a = a[:depth]
        out[n] = np.ascontiguousarray(a)
    return out


def kernel(**inputs):
    xp = np.asarray(inputs["x_prompt"], dtype=np.float32)
    xs = np.asarray(inputs["x_sample"], dtype=np.float32)
    S_LEN = xp.shape[1]
    x_all = np.concatenate([xp, xs], axis=0)
    nseq = x_all.shape[0] // NCORES
    wts = prep_weights(inputs, DEPTH)
    consts = make_consts(S_LEN)
    nc = Builder(S_LEN, nseq, DEPTH).build()
    in_maps = []
    for c in range(NCORES):
        m = dict(x=np.ascontiguousarray(x_all[c * nseq:(c + 1) * nseq]))
        m.update(wts)
        m.update(consts)
        in_maps.append(m)
    res = run_bass_kernel_spmd(nc, in_maps, core_ids=list(range(NCORES)))
    y = np.concatenate([r["y"] for r in res.results], axis=0)
    return (np.ascontiguousarray(y[:xp.shape[0]]), np.ascontiguousarray(y[xp.shape[0]:]))
```

```python
import contextlib
import os
import numpy as np
import concourse.bass as bass
import concourse.mybir as mybir
from concourse.alu_op_type import AluOpType as ALU
from concourse.bass_utils import run_bass_kernel_spmd

F32 = mybir.dt.float32
BF16 = mybir.dt.bfloat16
AF = mybir.ActivationFunctionType
AX = mybir.AxisListType

D = 1024
NIN = 4416
DFF = 2816
DEPTH = 2
NCORES = 8
SEQ_FULL = 4096
RMS_EPS = 1e-6
GN_EPS = 64e-5
CDEC = 0.6065306597126334
SCALE = 96.0 ** -0.5

ENGS = ("pe", "act", "dve", "pool", "sp")
N_DMA_SEMS = 8
SAME_ENGINE_SYNC = True


def _is_psum(r):
    n = r[0] if isinstance(r, tuple) else r
    return isinstance(n, str) and len(n) >= 2 and n[0] in "qp" and (n[1].isdigit() or n[1] in "TP")


class Sched:
    def __init__(self, nc):
        self.nc = nc
        self.q = {e: [] for e in ENGS}
        self.cnt = {e: 0 for e in ENGS}
        self.seen = {e: {} for e in ENGS}
        self.last_w = {}
        self.readers = {}
        self.dma_val = {}
        self.dma_rr = {e: 0 for e in ENGS}
        self.stack = contextlib.ExitStack()
        self.sems = {}
        self.nops = 0
        self.limit = int(os.environ.get("OPLIMIT", "1000000000"))
        self.marks = []

    def mark(self, label):
        self.marks.append((label, self.nops))

    def _deps(self, eng, reads, writes):
        deps = []
        for r in reads:
            ev = self.last_w.get(r)
            if ev is not None:
                deps.append(ev)
            if eng != "pe" and _is_psum(r):
                deps.extend(e2 for e2 in self.readers.get(r, ()) if e2[0] != eng)
        for w in writes:
            ev = self.last_w.get(w)
            if ev is not None:
                deps.append(ev)
            deps.extend(self.readers.get(w, ()))
        waits = {}
        seen = self.seen[eng]
        for sk, v in deps:
            if sk == eng and (eng == "pe" or not SAME_ENGINE_SYNC):
                continue
            if seen.get(sk, 0) >= v:
                continue
            if waits.get(sk, 0) < v:
                waits[sk] = v
        for sk, v in waits.items():
            seen[sk] = v
        return waits

    def _record(self, ev, reads, writes):
        for r in reads:
            self.readers.setdefault(r, []).append(ev)
        for w in writes:
            self.last_w[w] = ev
            self.readers[w] = []

    def op(self, eng, fn, reads=(), writes=()):
        self.nops += 1
        if self.nops > self.limit:
            return
        waits = self._deps(eng, reads, writes)
        self.cnt[eng] += 1
        ev = (eng, self.cnt[eng])
        self.q[eng].append((list(waits.items()), fn, (eng, 1)))
        self._record(ev, reads, writes)

    def dma(self, eng, fn, reads=(), writes=()):
        self.nops += 1
        if self.nops > self.limit:
            return
        k = self.dma_rr[eng]
        self.dma_rr[eng] = (k + 1) % N_DMA_SEMS
        sk = ("dma", eng, k)
        prev = self.dma_val.get(sk, 0)
        waits = self._deps(eng, reads, writes)
        if prev > 0 and self.seen[eng].get(sk, 0) < prev:
            waits[sk] = prev
            self.seen[eng][sk] = prev
        self.dma_val[sk] = prev + 16
        ev = (sk, prev + 16)
        self.q[eng].append((list(waits.items()), fn, (sk, 16)))
        self._record(ev, reads, writes)

    def barrier(self):
        tgt = {e: self.cnt[e] for e in ENGS if self.cnt[e] > 0}
        tgt.update(self.dma_val)
        for e in ENGS:
            waits = []
            for sk, v in tgt.items():
                if sk == e:
                    continue
                if self.seen[e].get(sk, 0) < v:
                    waits.append((sk, v))
                    self.seen[e][sk] = v
            if waits:
                self.q[e].append((waits, None, None))
        self.last_w = {}
        self.readers = {}

    def emit(self):
        nc = self.nc
        st = self.stack
        keys = list(ENGS) + list(self.dma_val)
        for sk in keys:
            nm = sk if isinstance(sk, str) else "d_%s_%d" % (sk[1], sk[2])
            self.sems[sk] = st.enter_context(nc.semaphore("s_" + nm))
        final = list(self.dma_val.items())
        block = st.enter_context(nc.Block())
        sems = self.sems

        def run(engname, final_waits=()):
            def body(e):
                for waits, fn, inc in self.q[engname]:
                    for sk, v in waits:
                        e.wait_ge(sems[sk], v)
                    if fn is not None:
                        fn(e).then_inc(sems[inc[0]], inc[1])
                for sk, v in final_waits:
                    e.wait_ge(sems[sk], v)
            return body

        block.tensor(run("pe"))
        block.scalar(run("act"))
        block.vector(run("dve"))
        block.gpsimd(run("pool"))
        block.sync(run("sp", final))


class Builder:
    def __init__(self, S_LEN, NSEQ, depth=DEPTH):
        self.S_LEN = S_LEN
        self.NSEQ = NSEQ
        self.depth = depth
        self.NT = S_LEN // 128
        nc = bass.Bass("TRN2", target_bir_lowering=False)
        self.nc = nc
        self.S = Sched(nc)
        self.ph = None
        self._uid = 0

        def inp(name, shape):
            return nc.dram_tensor(name, list(shape), F32, kind="ExternalInput").ap()

        L = depth
        self.x = inp("x", [NSEQ, S_LEN, D])
        self.w = dict(
            norm_mix_g=inp("norm_mix_g", [L, D]), w_in=inp("w_in", [L, D, NIN]),
            shift_mu=inp("shift_mu", [L, 2, 1952]), decay_w2=inp("decay_w2", [L, 2, 64, 512]),
            decay_w0=inp("decay_w0", [L, 2, 512]), iclr_a2=inp("iclr_a2", [L, 2, 64, 512]),
            iclr_a0=inp("iclr_a0", [L, 2, 512]), gate_g2=inp("gate_g2", [L, 160, 512]),
            k_k=inp("k_k", [L, 512]), k_a=inp("k_a", [L, 512]), r_k=inp("r_k", [L, 512]),
            gn_g=inp("gn_g", [L, 512]), gn_b=inp("gn_b", [L, 512]), w_oa=inp("w_oa", [L, 512, D]),
            q_norm_g=inp("q_norm_g", [L, 256]), w_uq=inp("w_uq", [L, 256, 768]),
            kv_norm_g=inp("kv_norm_g", [L, 128]), w_ukv=inp("w_ukv", [L, 128, 1024]),
            w_ob=inp("w_ob", [L, 512, D]), w_out=inp("w_out", [L, D, D]),
            norm_ffn_g=inp("norm_ffn_g", [L, D]), w_gu=inp("w_gu", [L, D, 2 * DFF]),
            w_down=inp("w_down", [L, DFF, D]), final_norm_g=inp("final_norm_g", [1, D]),
        )
        self.c_ident = inp("c_ident", [128, 128])
        self.c_tri = inp("c_tri", [2, 128, 128])
        self.c_m4 = inp("c_m4", [2, 128, 512])
        self.c_mn4 = inp("c_mn4", [2, 128, 512])
        self.c_ones = inp("c_ones", [128, 128])
        self.c_cos = inp("c_cos", [S_LEN, 32])
        self.c_sin = inp("c_sin", [S_LEN, 32])
        self.y = nc.dram_tensor("y", [NSEQ, S_LEN, D], F32, kind="ExternalOutput").ap()
        def scr(name, shape, dt=F32):
            return nc.dram_tensor(name, list(shape), dt).ap()
        self.P = scr("scr_P", [S_LEN, NIN])
        self.of = scr("scr_of", [S_LEN, 512])
        self.PSd = scr("scr_PS", [S_LEN, 1952])
        self.KKd = scr("scr_KK", [S_LEN, 512])
        self.ya = scr("scr_ya", [S_LEN, 512])
        self.yb = scr("scr_yb", [S_LEN, 512])
        self.x1 = scr("scr_x1", [S_LEN, D])
        self.x2 = scr("scr_x2", [S_LEN, D])
        self.QT = scr("scr_QT", [8, 96, S_LEN], BF16)
        self.KT = scr("scr_KT", [8, 96, S_LEN], BF16)
        self.Vd = scr("scr_V", [S_LEN, 8 * 65], BF16)

    def begin_phase(self):
        self.ph = contextlib.ExitStack()

    def end_phase(self):
        self.S.barrier()
        self.ph.close()
        self.ph = None

    def sb(self, name, shape, dt):
        self._uid += 1
        return self.ph.enter_context(self.nc.sbuf_tensor("%s_%d" % (name, self._uid), list(shape), dt))

    def ps(self, name, shape, dt):
        self._uid += 1
        return self.ph.enter_context(self.nc.psum_tensor("%s_%d" % (name, self._uid), list(shape), dt))

    def load(self, out_ap, in_ap, writes, reads=(), cast=False):
        eng = "pool" if cast else "sp"
        self.S.dma(eng, lambda e: e.dma_start(out=out_ap, in_=in_ap), reads=reads, writes=writes)

    def store(self, out_ap, in_ap, reads, writes):
        self.S.dma("sp", lambda e: e.dma_start(out=out_ap, in_=in_ap), reads=reads, writes=writes)

    def bcast_load(self, tile, row_ap, width, name):
        self.load(tile[:], row_ap.broadcast_to([128, width]), writes=[name])

    def load_w_bf16(self, tile, w_ap, K, name):
        for k in range(K // 128):
            self.load(tile[:, k, :], w_ap[k * 128:(k + 1) * 128, :], writes=[name], cast=True)

    def mm(self, out, lhsT, rhs, start, stop, reads, writes):
        self.S.op("pe", lambda e: e.matmul(out=out, lhsT=lhsT, rhs=rhs, start=start, stop=stop),
                  reads=reads, writes=writes)

    def tr(self, out, in_, ident, reads, writes):
        self.S.op("pe", lambda e: e.transpose(out=out, in_=in_, identity=ident), reads=reads, writes=writes)

    def act(self, out, in_, func, reads, writes, scale=None, bias=None, accum_out=None):
        kw = {}
        if scale is not None:
            kw["scale"] = scale
        if bias is not None:
            kw["bias"] = bias
        if accum_out is not None:
            kw["accum_out"] = accum_out
        self.S.op("act", lambda e: e.activation(out=out, in_=in_, func=func, **kw), reads=reads, writes=writes)

    def tt(self, eng, out, in0, in1, op, reads, writes):
        self.S.op(eng, lambda e: e.tensor_tensor(out=out, in0=in0, in1=in1, op=op), reads=reads, writes=writes)

    def ts(self, out, in0, s1, s2, op0, op1, reads, writes, eng="dve"):
        self.S.op(eng, lambda e: e.tensor_scalar(out=out, in0=in0, scalar1=s1, scalar2=s2, op0=op0, op1=op1),
                  reads=reads, writes=writes)

    def stt(self, out, in0, scalar, in1, op0, op1, reads, writes):
        self.S.op("dve", lambda e: e.scalar_tensor_tensor(out=out, in0=in0, scalar=scalar, in1=in1, op0=op0, op1=op1),
                  reads=reads, writes=writes)

    def cp(self, eng, out, in_, reads, writes):
        if eng == "act":
            self.S.op("act", lambda e: e.activation(out=out, in_=in_, func=AF.Copy), reads=reads, writes=writes)
        else:
            self.S.op(eng, lambda e: e.tensor_copy(out=out, in_=in_), reads=reads, writes=writes)

    def red(self, out, in_, reads, writes):
        self.S.op("dve", lambda e: e.tensor_reduce(out=out, in_=in_, axis=AX.X, op=ALU.add), reads=reads, writes=writes)

    def recip(self, out, in_, reads, writes):
        self.S.op("dve", lambda e: e.reciprocal(out=out, in_=in_), reads=reads, writes=writes)

    def memset(self, eng, ap, val, writes):
        self.S.op(eng, lambda e: e.memset(ap, val), writes=writes)

    def rmsnorm(self, x_ap, xn, width, gbc_ap, gn, out_ap, outn, junk, ss, rstd, tag):
        jn, sn, rn = "junk" + tag, "ss" + tag, "rstd" + tag
        self.act(junk, x_ap, AF.Square, reads=[xn], writes=[jn, sn], accum_out=ss)
        self.ts(rstd, ss, 1.0 / width, RMS_EPS, ALU.mult, ALU.add, reads=[sn], writes=[rn])
        self.act(rstd, rstd, AF.Sqrt, reads=[rn], writes=[rn])
        self.recip(rstd, rstd, reads=[rn], writes=[rn])
        self.stt(out_ap, x_ap, rstd, gbc_ap, ALU.mult, ALU.mult, reads=[xn, rn, gn], writes=[outn])

    def phase_A(self, l, xin):
        NT = self.NT
        self.begin_phase()
        wA = self.sb("wA", [128, 8, NIN], BF16)
        gbc = self.sb("gA", [128, D], F32)
        idb = self.sb("idb", [128, 128], BF16)
        junk = self.sb("junk", [128, D], F32)
        ss = self.sb("ss", [128, 1], F32)
        rstd = self.sb("rstd", [128, 1], F32)
        xt = [self.sb("xt", [128, D], F32) for _ in range(2)]
        h = [self.sb("h", [128, D], BF16) for _ in range(2)]
        hT = [self.sb("hT", [128, 8, 128], BF16) for _ in range(2)]
        Pt = [self.sb("Pt", [128, NIN], F32) for _ in range(2)]
        pT = [self.ps("pT", [128, 8, 128], BF16) for _ in range(2)]
        pP = [self.ps("pP", [128, 512], F32) for _ in range(4)]
        self.load(idb[:], self.c_ident, writes=["idb"], cast=True)
        self.bcast_load(gbc, self.w["norm_mix_g"][l:l + 1, :], D, "gA")
        self.load_w_bf16(wA, self.w["w_in"][l], D, "wA")
        npieces = (NIN + 511) // 512

        def norm(i):
            b = i % 2
            self.load(xt[b][:], xin[i * 128:(i + 1) * 128, :], writes=["xt%d" % b])
            self.rmsnorm(xt[b][:], "xt%d" % b, D, gbc[:], "gA", h[b][:], "h%d" % b, junk[:], ss[:], rstd[:], "A")

        def trans(i):
            b = i % 2
            for k in range(8):
                self.tr(pT[b][:, k, :], h[b][:, k * 128:(k + 1) * 128], idb[:], reads=["h%d" % b, "idb"], writes=["pT%d" % b])
            self.cp("act", hT[b][:], pT[b][:], reads=["pT%d" % b], writes=["hT%d" % b])

        norm(0)
        trans(0)
        for i in range(NT):
            b = i % 2
            if i + 1 < NT:
                norm(i + 1)
            for j in range(npieces):
                n0 = j * 512
                n = min(512, NIN - n0)
                pp = pP[j % 4]
                pn = "pP%d" % (j % 4)
                for k in range(8):
                    self.mm(pp[:, 0:n], hT[b][:, k, :], wA[:, k, n0:n0 + n], k == 0, k == 7,
                            reads=["hT%d" % b, "wA"], writes=[pn])
                self.cp("dve" if j % 2 == 0 else "act", Pt[b][:, n0:n0 + n], pp[:, 0:n],
                        reads=[pn], writes=[("Pt", b, j)])
                if j == 5 and i + 1 < NT:
                    trans(i + 1)
            self.store(self.P[i * 128:(i + 1) * 128, :], Pt[b][:], reads=[("Pt", b, j) for j in range(npieces)],
                       writes=[("P", i)])
        self.end_phase()

    def phase_R(self, l, d):
        NT = self.NT
        S_LEN = self.S_LEN
        W = self.w
        self.begin_phase()
        sb, ps = self.sb, self.ps
        if d == 0:
            mu0 = sb("mu0", [128, 1952], F32)
            mu1 = sb("mu1", [128, 1952], F32)
            c0 = sb("c0", [128, 1952], F32)
            pap = sb("pap", [128, 1952], F32)
            pan = sb("pan", [128, 1952], F32)
            kkbc = sb("kkbc", [128, 512], F32)
        w0bc = sb("w0bc", [128, 512], F32)
        a0bc = sb("a0bc", [128, 512], F32)
        kabc = sb("kabc", [128, 512], F32)
        w2b = sb("w2b", [64, 512], BF16)
        a2b = sb("a2b", [64, 512], BF16)
        tri = sb("tri", [128, 128], F32)
        ones = sb("ones", [128, 128], F32)
        m4 = sb("m4", [128, 512], F32)
        mn4 = sb("mn4", [128, 512], F32)
        idb = sb("idb", [128, 128], BF16)
        pac = [sb("pac", [128, 1952], F32) for _ in range(3)]
        lo = sb("lo", [128, 128], BF16)
        loT = sb("loT", [64, 2, 128], BF16)
        sgm = sb("sgm", [128, 512], F32)
        av = sb("av", [128, 512], F32)
        kk = sb("kk", [128, 512], F32)
        tmp = sb("tmp", [128, 512], F32)
        kd = sb("kd", [128, 512], F32)
        ka = sb("ka", [128, 512], F32)
        Ls = sb("Ls", [128, 512], F32)
        Ld = sb("Ld", [128, 512], F32)
        E1 = sb("E1", [128, 512], F32)
        E2 = sb("E2", [128, 512], F32)
        E3 = sb("E3", [128, 512], F32)
        E4 = sb("E4", [128, 512], F32)
        ssq = sb("ssq", [128, 8], F32)
        gC = [sb("gC", [64, 8], F32) for _ in range(3)]
        Ab = [sb("Ab", [128, 512], BF16) for _ in range(3)]
        Rb = [sb("Rb", [128, 512], BF16) for _ in range(3)]
        Bb = [sb("Bb", [128, 512], BF16) for _ in range(3)]
        Kb = [sb("Kb", [128, 512], BF16) for _ in range(3)]
        Btb = [sb("Btb", [128, 512], BF16) for _ in range(3)]
        Ktb = [sb("Ktb", [128, 512], BF16) for _ in range(3)]
        Vb = [sb("Vb", [128, 512], BF16) for _ in range(3)]
        ART = [sb("ART", [128, 8, 256], BF16) for _ in range(3)]
        BT = [sb("BT", [64, 8, 128], BF16) for _ in range(3)]
        KTt = [sb("KTt", [64, 8, 128], BF16) for _ in range(3)]
        ATall = sb("ATall", [128, 8, 512], BF16)
        PP = [sb("PP", [128, 8, 256], BF16) for _ in range(2)]
        Wb = sb("Wb", [128, 512], BF16)
        osb = sb("osb", [128, 512], F32)
        ST32 = sb("ST32", [64, 8, 64], F32)
        STb = sb("STb", [128, 8, 64], BF16)
        if d == 1:
            rkbc = sb("rkbc", [128, 512], F32)
            gngbc = sb("gngbc", [128, 512], F32)
            gnbbc = sb("gnbbc", [128, 512], F32)
            g2b = sb("g2b", [128, 2, 512], BF16)
            oft = sb("oft", [128, 512], F32)
            cen = sb("cen", [128, 512], F32)
            sq2 = sb("sq2", [128, 512], F32)
            bon = sb("bon", [128, 512], F32)
            st8 = sb("st8", [128, 8], F32)
            sv8 = sb("sv8", [128, 8], F32)
            sb8 = sb("sb8", [128, 8], F32)
            gs = sb("gs", [128, 160], BF16)
            gT = sb("gT", [128, 2, 128], BF16)
            yat = sb("yat", [128, 512], F32)
        pbk = [ps("pq%d" % i, [128, 512], F32) for i in range(3)]
        pbn = ["pq0", "pq1", "pq2"]
        pbb = [t[:].bitcast(BF16) for t in pbk]
        sbk = [ps("sq%d" % i, [128, 512], F32) for i in range(5)]
        sbn = ["sq0", "sq1", "sq2", "sq3", "sq4"]
        s3, s4 = sbk[3], sbk[4]
        s3b = s3[:].bitcast(BF16)

        self.load(idb[:], self.c_ident, writes=["idb"], cast=True)
        self.load(tri[:], self.c_tri[d], writes=["tri"])
        self.load(ones[:], self.c_ones, writes=["ones"])
        self.load(m4[:], self.c_m4[d], writes=["m4"])
        self.load(mn4[:], self.c_mn4[d], writes=["mn4"])
        if d == 0:
            self.bcast_load(mu0, W["shift_mu"][l, 0:1, :], 1952, "mu0")
            self.bcast_load(mu1, W["shift_mu"][l, 1:2, :], 1952, "mu1")
            self.bcast_load(kkbc, W["k_k"][l:l + 1, :], 512, "kkbc")
            self.tt("dve", c0[:], mu0[:], mu1[:], ALU.add, reads=["mu0", "mu1"], writes=["c0"])
            self.ts(c0[:], c0[:], -1.0, 1.0, ALU.mult, ALU.add, reads=["c0"], writes=["c0"])
        self.bcast_load(w0bc, W["decay_w0"][l, d:d + 1, :], 512, "w0bc")
        self.bcast_load(a0bc, W["iclr_a0"][l, d:d + 1, :], 512, "a0bc")
        self.bcast_load(kabc, W["k_a"][l:l + 1, :], 512, "kabc")
        self.load(w2b[:], W["decay_w2"][l, d], writes=["w2b"], cast=True)
        self.load(a2b[:], W["iclr_a2"][l, d], writes=["a2b"], cast=True)
        self.memset("dve", ST32[:], 0.0, writes=["ST32"])
        self.memset("dve", STb[:], 0.0, writes=["STb"])
        for b in range(3):
            self.memset("dve", ART[b][:], 0.0, writes=["ART%d" % b])
        if d == 1:
            self.bcast_load(rkbc, W["r_k"][l:l + 1, :], 512, "rkbc")
            self.bcast_load(gngbc, W["gn_g"][l:l + 1, :], 512, "gngbc")
            self.bcast_load(gnbbc, W["gn_b"][l:l + 1, :], 512, "gnbbc")
            self.memset("dve", gT[:], 0.0, writes=["gT"])
            self.memset("dve", g2b[:], 0.0, writes=["g2b"])
            self.load(g2b[:, 0, :], W["gate_g2"][l, 0:128, :], writes=["g2b"], cast=True)
            self.load(g2b[0:32, 1, :], W["gate_g2"][l, 128:160, :], writes=["g2b"], cast=True)

        def v3(ap):
            return ap.rearrange("p (h e) -> p h e", h=8)

        def prep(i, pb):
            t0 = i * 128
            pc = pac[pb]
            pn_ = "pac%d" % pb
            if d == 0:
                self.load(pc[:], self.P[t0:t0 + 128, 2048:4000], writes=[pn_])
                if i == 0:
                    self.memset("pool", pap[:], 0.0, writes=["pap"])
                    self.load(pap[1:128, :], self.P[0:127, 2048:4000], writes=["pap"])
                else:
                    self.load(pap[:], self.P[t0 - 1:t0 + 127, 2048:4000], writes=["pap"])
                if i == NT - 1:
                    self.memset("pool", pan[:], 0.0, writes=["pan"])
                    self.load(pan[0:127, :], self.P[t0 + 1:S_LEN, 2048:4000], writes=["pan"])
                else:
                    self.load(pan[:], self.P[t0 + 1:t0 + 129, 2048:4000], writes=["pan"])
                yield
                yield
                yield
                self.tt("pool", pap[:], pap[:], mu0[:], ALU.mult, reads=["pap", "mu0"], writes=["pap"])
                self.tt("dve", pan[:], pan[:], mu1[:], ALU.mult, reads=["pan", "mu1"], writes=["pan"])
                self.tt("pool", pc[:], pc[:], c0[:], ALU.mult, reads=[pn_, "c0"], writes=[pn_])
                yield
                self.tt("dve", pc[:], pc[:], pan[:], ALU.add, reads=[pn_, "pan"], writes=[pn_])
                self.tt("dve", pc[:], pc[:], pap[:], ALU.add, reads=[pn_, "pap"], writes=[pn_])
                self.store(self.PSd[t0:t0 + 128, :], pc[:], reads=[pn_], writes=[("PS", i)])
            else:
                self.load(pc[:], self.PSd[t0:t0 + 128, :], writes=[pn_])
                self.load(kk[:], self.KKd[t0:t0 + 128, :], writes=["kk"])
                yield
                yield
            yield
            r_ = pc[:, 0:512]
            k_ = pc[:, 512:1024]
            v_ = pc[:, 1024:1536]
            lw_ = pc[:, 1536 + 64 * d:1600 + 64 * d]
            la_ = pc[:, 1664 + 64 * d:1728 + 64 * d]
            self.act(lo[:, 0:64], lw_, AF.Tanh, reads=[pn_], writes=["lo"])
            self.cp("dve", lo[:, 64:128], la_, reads=[pn_], writes=["lo"])
            self.tr(pbb[0][0:64, 0:128], lo[:, 0:64], idb[:], reads=["lo", "idb"], writes=[pbn[0]])
            self.tr(pbb[0][0:64, 128:256], lo[:, 64:128], idb[:], reads=["lo", "idb"], writes=[pbn[0]])
            self.cp("act", loT[:].rearrange("p a b -> p (a b)"), pbb[0][0:64, 0:256], reads=[pbn[0]], writes=["loT"])
            self.mm(pbk[1][:], loT[:, 0, :], w2b[:], True, True, reads=["loT", "w2b"], writes=[pbn[1]])
            self.mm(pbk[2][:], loT[:, 1, :], a2b[:], True, True, reads=["loT", "a2b"], writes=[pbn[2]])
            self.tt("dve", sgm[:], pbk[1][:], w0bc[:], ALU.add, reads=[pbn[1], "w0bc"], writes=["sgm"])
            self.act(sgm[:], sgm[:], AF.Sigmoid, reads=["sgm"], writes=["sgm"])
            self.tt("dve", av[:], pbk[2][:], a0bc[:], ALU.add, reads=[pbn[2], "a0bc"], writes=["av"])
            self.act(av[:], av[:], AF.Sigmoid, reads=["av"], writes=["av"])
            yield
            if d == 0:
                self.tt("dve", kk[:], k_, kkbc[:], ALU.mult, reads=[pn_, "kkbc"], writes=["kk"])
                self.tt("pool", tmp[:], kk[:], kk[:], ALU.mult, reads=["kk"], writes=["tmp"])
                self.red(ssq[:], v3(tmp[:]), reads=["tmp"], writes=["ssq"])
                self.act(ssq[:], ssq[:], AF.Sqrt, reads=["ssq"], writes=["ssq"])
                self.ts(ssq[:], ssq[:], 1e-12, None, ALU.max, ALU.bypass, reads=["ssq"], writes=["ssq"])
                self.recip(ssq[:], ssq[:], reads=["ssq"], writes=["ssq"])
                self.tt("dve", v3(kk[:]), v3(kk[:]), ssq[:].unsqueeze(2).broadcast_to([128, 8, 64]), ALU.mult,
                        reads=["kk", "ssq"], writes=["kk"])
                self.store(self.KKd[t0:t0 + 128, :], kk[:], reads=["kk"], writes=[("KK", i)])
            self.stt(tmp[:], av[:], -1.0, kabc[:], ALU.add, ALU.mult, reads=["av", "kabc"], writes=["tmp"])
            self.stt(kd[:], tmp[:], 1.0, k_, ALU.add, ALU.mult, reads=["tmp", pn_], writes=["kd"])
            self.tt("pool", ka[:], kk[:], av[:], ALU.mult, reads=["kk", "av"], writes=["ka"])
            yield
            self.mm(pbk[0][:], tri[:], sgm[:], True, True, reads=["tri", "sgm"], writes=[pbn[0]])
            self.mm(pbk[1][:], ones[:], sgm[:], True, True, reads=["ones", "sgm"], writes=[pbn[1]])
            for hh in range(8):
                self.mm(pbk[2][0:64, hh * 2:hh * 2 + 2], sgm[:, hh * 64:(hh + 1) * 64], ones[:, 0:2], True, True,
                        reads=["sgm", "ones"], writes=[pbn[2]])
            self.act(gC[pb][:], pbk[2][0:64, 0:16].rearrange("p (h t) -> p h t", t=2)[:, :, 0], AF.Exp,
                     reads=[pbn[2]], writes=["gC%d" % pb], scale=-CDEC)
            self.cp("act", Ls[:], pbk[0][:], reads=[pbn[0]], writes=["Ls"])
            self.tt("dve", Ld[:], pbk[1][:], Ls[:], ALU.subtract, reads=[pbn[1], "Ls"], writes=["Ld"])
            self.act(E2[:], pbk[0][:], AF.Exp, reads=[pbn[0]], writes=["E2"], scale=-CDEC)
            self.act(E3[:], pbk[0][:], AF.Exp, reads=[pbn[0]], writes=["E3"], scale=CDEC)
            yield
            self.tt("dve", Ls[:], Ls[:], sgm[:], ALU.subtract, reads=["Ls", "sgm"], writes=["Ls"])
            self.act(E1[:], Ls[:], AF.Exp, reads=["Ls"], writes=["E1"], scale=-CDEC)
            self.act(E4[:], Ld[:], AF.Exp, reads=["Ld"], writes=["E4"], scale=-CDEC)
            self.tt("dve", Rb[pb][:], r_, E2[:], ALU.mult, reads=[pn_, "E2"], writes=["Rb%d" % pb])
            self.tt("pool", Bb[pb][:], ka[:], E3[:], ALU.mult, reads=["ka", "E3"], writes=["Bb%d" % pb])
            self.tt("pool", Kb[pb][:], kd[:], E3[:], ALU.mult, reads=["kd", "E3"], writes=["Kb%d" % pb])
            self.stt(Ab[pb][:], kk[:], -1.0, E1[:], ALU.mult, ALU.mult, reads=["kk", "E1"], writes=["Ab%d" % pb])
            yield
            self.tt("pool", Btb[pb][:], ka[:], E4[:], ALU.mult, reads=["ka", "E4"], writes=["Btb%d" % pb])
            self.tt("dve", Ktb[pb][:], kd[:], E4[:], ALU.mult, reads=["kd", "E4"], writes=["Ktb%d" % pb])
            self.cp("pool", Vb[pb][:], v_, reads=[pn_], writes=["Vb%d" % pb])
            for hh in range(8):
                hs = slice(hh * 64, (hh + 1) * 64)
                ts_ = slice(hh * 128, (hh + 1) * 128)
                self.tr(pbb[0][0:64, ts_], Ab[pb][:, hs], idb[:], reads=["Ab%d" % pb, "idb"], writes=[pbn[0]])
                self.tr(pbb[1][0:64, ts_], Rb[pb][:, hs], idb[:], reads=["Rb%d" % pb, "idb"], writes=[pbn[1]])
                self.tr(pbb[2][0:64, ts_], Bb[pb][:, hs], idb[:], reads=["Bb%d" % pb, "idb"], writes=[pbn[2]])
            self.cp("act", ART[pb][0:64, :, 0:128], pbb[0][0:64, :].rearrange("p (h t) -> p h t", h=8),
                    reads=[pbn[0]], writes=["ART%d" % pb])
            self.cp("dve", ART[pb][0:64, :, 128:256], pbb[1][0:64, :].rearrange("p (h t) -> p h t", h=8),
                    reads=[pbn[1]], writes=["ART%d" % pb])
            self.cp("act", BT[pb][:], pbb[2][0:64, :].rearrange("p (h t) -> p h t", h=8), reads=[pbn[2]], writes=["BT%d" % pb])
            yield
            for hh in range(8):
                hs = slice(hh * 64, (hh + 1) * 64)
                ts_ = slice(hh * 128, (hh + 1) * 128)
                self.tr(pbb[0][0:64, ts_], Kb[pb][:, hs], idb[:], reads=["Kb%d" % pb, "idb"], writes=[pbn[0]])
            self.cp("dve", KTt[pb][:], pbb[0][0:64, :].rearrange("p (h t) -> p h t", h=8), reads=[pbn[0]], writes=["KTt%d" % pb])
            yield

        def solve(i, pb):
            t0 = i * 128
            pc = pac[pb]
            pn_ = "pac%d" % pb
            art, bt, ktt = ART[pb], BT[pb], KTt[pb]
            an, bn_, kn = "ART%d" % pb, "BT%d" % pb, "KTt%d" % pb
            vb, vn = Vb[pb], "Vb%d" % pb
            for hh in range(8):
                qq, qn = sbk[hh % 3], sbn[hh % 3]
                self.mm(qq[:, 0:256], bt[:, hh, :], art[0:64, hh, :], True, True, reads=[bn_, an], writes=[qn])
                self.mm(qq[:, 256:512], ktt[:, hh, :], art[0:64, hh, :], True, True, reads=[kn, an], writes=[qn])
                self.tt("dve", ATall[:, hh, :], qq[:], m4[:], ALU.mult, reads=[qn, "m4"], writes=[("AT", hh)])
                if hh == 3:
                    yield
            yield
            for g in range(2):
                for j in range(4):
                    hh = g * 4 + j
                    self.mm(s3[:, j * 128:(j + 1) * 128], art[0:64, hh, 0:128], bt[:, hh, :], True, True,
                            reads=[an, bn_], writes=["sq3"])
                self.tt("dve", PP[0][:, g * 4:(g + 1) * 4, 0:128], s3[:].rearrange("p (j s) -> p j s", j=4),
                        mn4[:].rearrange("p (j s) -> p j s", j=4), ALU.mult, reads=["sq3", "mn4"],
                        writes=[("PP", 0, 2 * g), ("PP", 0, 2 * g + 1)])
            self.cp("pool", PP[0][:, :, 128:256], ATall[:, :, 0:128], reads=[("AT", hh) for hh in range(8)],
                    writes=[("PP", 0, pr) for pr in range(4)])
            for hh in range(8):
                hs = slice(hh * 64, (hh + 1) * 64)
                self.mm(s4[:, hs], art[:, hh, 0:128], STb[:, hh, :], True, False, reads=[an, "STb"], writes=["sq4"])
                self.mm(s4[:, hs], ATall[:, hh, 256:384], vb[:, hs], False, True, reads=[("AT", hh), vn], writes=["sq4"])
            self.cp("act", Wb[:], s4[:], reads=["sq4"], writes=["Wb"])
            yield
            rot = 0
            for j in range(7):
                cb = j % 2
                cur = PP[cb]
                for hh in range(8):
                    hs = slice(hh * 64, (hh + 1) * 64)
                    self.mm(s3[:, hs], cur[:, hh, 128:256], Wb[:, hs], True, False,
                            reads=[("PP", cb, hh // 2), "Wb"], writes=["sq3"])
                    self.mm(s3[:, hs], idb[:], Wb[:, hs], False, True, reads=["idb", "Wb"], writes=["sq3"])
                self.cp("act", Wb[:], s3[:], reads=["sq3"], writes=["Wb"])
                if j < 6:
                    nxt = PP[1 - cb]
                    for pr in range(4):
                        bankt, bname = sbk[rot % 3], sbn[rot % 3]
                        rot += 1
                        for u in range(2):
                            hh = pr * 2 + u
                            self.mm(bankt[:, u * 256:u * 256 + 128], cur[:, hh, 128:256], cur[:, hh, 0:128], True, True,
                                    reads=[("PP", cb, pr)], writes=[bname])
                            self.mm(bankt[:, u * 256 + 128:u * 256 + 256], cur[:, hh, 0:128], cur[:, hh, 128:256], True, True,
                                    reads=[("PP", cb, pr)], writes=[bname])
                        self.cp("dve" if pr == 0 else "act",
                                nxt[:, pr * 2:pr * 2 + 2, :].rearrange("p a b -> p (a b)"), bankt[:],
                                reads=[bname], writes=[("PP", 1 - cb, pr)])
                yield
            for hh in range(8):
                hs = slice(hh * 64, (hh + 1) * 64)
                self.mm(s4[:, hs], art[:, hh, 128:256], STb[:, hh, :], True, False, reads=[an, "STb"], writes=["sq4"])
                self.mm(s4[:, hs], ATall[:, hh, 128:256], Wb[:, hs], False, False, reads=[("AT", hh), "Wb"], writes=["sq4"])
                self.mm(s4[:, hs], ATall[:, hh, 384:512], vb[:, hs], False, True, reads=[("AT", hh), vn], writes=["sq4"])
            self.cp("act", osb[:], s4[:], reads=["sq4"], writes=["osb"])
            for hh in range(8):
                hs = slice(hh * 64, (hh + 1) * 64)
                self.mm(s3[0:64, hs], Btb[pb][:, hs], Wb[:, hs], True, False, reads=["Btb%d" % pb, "Wb"], writes=["sq3"])
                self.mm(s3[0:64, hs], Ktb[pb][:, hs], vb[:, hs], False, True, reads=["Ktb%d" % pb, vn], writes=["sq3"])
            self.tt("dve", ST32[:], ST32[:], gC[pb][:].unsqueeze(2).broadcast_to([64, 8, 64]), ALU.mult,
                    reads=["ST32", "gC%d" % pb], writes=["ST32"])
            self.tt("dve", ST32[:], ST32[:], s3[0:64, :].rearrange("p (h e) -> p h e", h=8), ALU.add,
                    reads=["ST32", "sq3"], writes=["ST32"])
            self.cp("act", STb[0:64, :, :], ST32[:], reads=["ST32"], writes=["STb"])
            yield
            if d == 0:
                self.store(self.of[t0:t0 + 128, :], osb[:], reads=["osb"], writes=[("of", i)])
                return
            r_ = pc[:, 0:512]
            k_ = pc[:, 512:1024]
            lg_ = pc[:, 1792:1952]
            self.load(oft[:], self.of[t0:t0 + 128, :], writes=["oft"])
            self.tt("dve", oft[:], oft[:], osb[:], ALU.add, reads=["oft", "osb"], writes=["oft"])
            self.red(st8[:], v3(oft[:]), reads=["oft"], writes=["st8"])
            self.ts(st8[:], st8[:], 1.0 / 64, None, ALU.mult, ALU.bypass, reads=["st8"], writes=["st8"])
            self.tt("dve", v3(cen[:]), v3(oft[:]), st8[:].unsqueeze(2).broadcast_to([128, 8, 64]), ALU.subtract,
                    reads=["oft", "st8"], writes=["cen"])
            self.tt("pool", sq2[:], cen[:], cen[:], ALU.mult, reads=["cen"], writes=["sq2"])
            self.red(sv8[:], v3(sq2[:]), reads=["sq2"], writes=["sv8"])
            self.ts(sv8[:], sv8[:], 1.0 / 64, GN_EPS, ALU.mult, ALU.add, reads=["sv8"], writes=["sv8"])
            self.act(sv8[:], sv8[:], AF.Sqrt, reads=["sv8"], writes=["sv8"])
            self.recip(sv8[:], sv8[:], reads=["sv8"], writes=["sv8"])
            yield
            self.tt("dve", v3(cen[:]), v3(cen[:]), sv8[:].unsqueeze(2).broadcast_to([128, 8, 64]), ALU.mult,
                    reads=["cen", "sv8"], writes=["cen"])
            self.tt("pool", cen[:], cen[:], gngbc[:], ALU.mult, reads=["cen", "gngbc"], writes=["cen"])
            self.tt("pool", cen[:], cen[:], gnbbc[:], ALU.add, reads=["cen", "gnbbc"], writes=["cen"])
            self.tt("dve", sq2[:], r_, k_, ALU.mult, reads=[pn_], writes=["sq2"])
            self.tt("pool", sq2[:], sq2[:], rkbc[:], ALU.mult, reads=["sq2", "rkbc"], writes=["sq2"])
            self.red(sb8[:], v3(sq2[:]), reads=["sq2"], writes=["sb8"])
            self.tt("dve", v3(bon[:]), v3(pc[:, 1024:1536]),
                    sb8[:].unsqueeze(2).broadcast_to([128, 8, 64]), ALU.mult, reads=[pn_, "sb8"], writes=["bon"])
            self.tt("dve", cen[:], cen[:], bon[:], ALU.add, reads=["cen", "bon"], writes=["cen"])
            self.act(gs[:], lg_, AF.Sigmoid, reads=[pn_], writes=["gs"])
            self.tr(s3b[:, 0:128], gs[:, 0:128], idb[:], reads=["gs", "idb"], writes=["sq3"])
            self.tr(s3b[0:32, 128:256], gs[:, 128:160], idb[:], reads=["gs", "idb"], writes=["sq3"])
            self.cp("act", gT[:, 0, :], s3b[:, 0:128], reads=["sq3"], writes=["gT"])
            self.cp("act", gT[0:32, 1, :], s3b[0:32, 128:256], reads=["sq3"], writes=["gT"])
            self.mm(s4[:], gT[:, 0, :], g2b[:, 0, :], True, False, reads=["gT", "g2b"], writes=["sq4"])
            self.mm(s4[:], gT[:, 1, :], g2b[:, 1, :], False, True, reads=["gT", "g2b"], writes=["sq4"])
            self.tt("dve", yat[:], cen[:], s4[:], ALU.mult, reads=["cen", "sq4"], writes=["yat"])
            self.store(self.ya[t0:t0 + 128, :], yat[:], reads=["yat"], writes=[("ya", i)])
            yield

        order = list(range(NT)) if d == 0 else list(range(NT - 1, -1, -1))
        for n0 in range(min(2, NT)):
            for _ in prep(order[n0], n0 % 3):
                pass
        for n, i in enumerate(order):
            gs_ = solve(i, n % 3)
            gp_ = prep(order[n + 2], (n + 2) % 3) if n + 2 < NT else None
            while gs_ is not None or gp_ is not None:
                if gs_ is not None:
                    try:
                        next(gs_)
                    except StopIteration:
                        gs_ = None
                if gp_ is not None:
                    try:
                        next(gp_)
                    except StopIteration:
                        gp_ = None
        self.end_phase()

    def phase_MP(self, l):
        NT = self.NT
        W = self.w
        self.begin_phase()
        sb, ps = self.sb, self.ps
        qg = sb("qg", [128, 256], F32)
        kvg = sb("kvg", [128, 128], F32)
        wuq = sb("wuq", [128, 2, 768], BF16)
        wukv = sb("wukv", [128, 1, 1024], BF16)
        idb = sb("idb", [128, 128], BF16)
        pm = sb("pm", [128, 416], F32)
        cs = sb("cs", [128, 32], F32)
        sn = sb("sn", [128, 32], F32)
        junk = sb("junk", [128, 256], F32)
        ss = sb("ss", [128, 1], F32)
        rstd = sb("rstd", [128, 1], F32)
        nb = sb("nb", [128, 384], BF16)
        nT = sb("nT", [128, 3, 128], BF16)
        qf = sb("qf", [128, 768], F32)
        kvf = sb("kvf", [128, 1024], F32)
        t1 = sb("t1", [128, 8, 32], F32)
        t2 = sb("t2", [128, 8, 32], F32)
        kro = sb("kro", [128, 32], F32)
        kr2 = sb("kr2", [128, 32], F32)
        Qa = sb("Qa", [128, 8, 96], BF16)
        Ka = sb("Ka", [128, 8, 96], BF16)
        Va = sb("Va", [128, 8, 65], BF16)
        QTt = sb("QTt", [96, 8, 128], BF16)
        KTt = sb("KTt", [96, 8, 128], BF16)
        q = [ps("q%d" % i, [128, 512], F32) for i in range(7)]
        qb = [t[:].bitcast(BF16) for t in q]
        self.load(idb[:], self.c_ident, writes=["idb"], cast=True)
        self.bcast_load(qg, W["q_norm_g"][l:l + 1, :], 256, "qg")
        self.bcast_load(kvg, W["kv_norm_g"][l:l + 1, :], 128, "kvg")
        self.load_w_bf16(wuq, W["w_uq"][l], 256, "wuq")
        self.load_w_bf16(wukv, W["w_ukv"][l], 128, "wukv")
        self.memset("dve", Va[:], 1.0, writes=["Va"])
        qf3 = qf[:].rearrange("p (h e) -> p h e", h=8)
        kvf3 = kvf[:].rearrange("p (h e) -> p h e", h=8)
        for i in range(NT):
            t0 = i * 128
            self.load(pm[:], self.P[t0:t0 + 128, 4000:4416], writes=["pm"])
            self.load(cs[:], self.c_cos[t0:t0 + 128, :], writes=["cs"])
            self.load(sn[:], self.c_sin[t0:t0 + 128, :], writes=["sn"])
            self.rmsnorm(pm[:, 0:256], "pm", 256, qg[:], "qg", nb[:, 0:256], "nbq", junk[:, 0:256], ss[:], rstd[:], "M")
            self.rmsnorm(pm[:, 256:384], "pm", 128, kvg[:], "kvg", nb[:, 256:384], "nbk", junk[:, 0:128], ss[:], rstd[:], "M")
            for c in range(3):
                self.tr(qb[0][:, c * 128:(c + 1) * 128], nb[:, c * 128:(c + 1) * 128], idb[:],
                        reads=["nbq", "nbk", "idb"], writes=["q0"])
            self.cp("act", nT[:].rearrange("p a b -> p (a b)"), qb[0][:, 0:384], reads=["q0"], writes=["nT"])
            for c in range(2):
                self.mm(q[1][:], nT[:, c, :], wuq[:, c, 0:512], c == 0, c == 1, reads=["nT", "wuq"], writes=["q1"])
            for c in range(2):
                self.mm(q[2][:, 0:256], nT[:, c, :], wuq[:, c, 512:768], c == 0, c == 1, reads=["nT", "wuq"], writes=["q2"])
            self.mm(q[3][:], nT[:, 2, :], wukv[:, 0, 0:512], True, True, reads=["nT", "wukv"], writes=["q3"])
            self.mm(q[4][:], nT[:, 2, :], wukv[:, 0, 512:1024], True, True, reads=["nT", "wukv"], writes=["q4"])
            self.cp("act", qf[:, 0:512], q[1][:], reads=["q1"], writes=["qf"])
            self.cp("dve", qf[:, 512:768], q[2][:, 0:256], reads=["q2"], writes=["qf"])
            self.cp("act", kvf[:, 0:512], q[3][:], reads=["q3"], writes=["kvf"])
            self.cp("dve", kvf[:, 512:1024], q[4][:], reads=["q4"], writes=["kvf"])
            self.cp("pool", Qa[:, :, 0:64], qf3[:, :, 0:64], reads=["qf"], writes=["Qa"])
            csb = cs[:].unsqueeze(1).broadcast_to([128, 8, 32])
            self.tt("dve", t1[:], qf3[:, :, 64:96], csb, ALU.mult, reads=["qf", "cs"], writes=["t1"])
            self.tt("dve", t2[:, :, 0:16], qf3[:, :, 80:96], sn[:, 0:16].unsqueeze(1).broadcast_to([128, 8, 16]), ALU.mult,
                    reads=["qf", "sn"], writes=["t2"])
            self.tt("dve", t2[:, :, 16:32], qf3[:, :, 64:80], sn[:, 16:32].unsqueeze(1).broadcast_to([128, 8, 16]), ALU.mult,
                    reads=["qf", "sn"], writes=["t2"])
            self.tt("dve", Qa[:, :, 64:96], t1[:], t2[:], ALU.add, reads=["t1", "t2"], writes=["Qa"])
            self.tt("dve", kro[:], pm[:, 384:416], cs[:], ALU.mult, reads=["pm", "cs"], writes=["kro"])
            self.tt("dve", kr2[:, 0:16], pm[:, 400:416], sn[:, 0:16], ALU.mult, reads=["pm", "sn"], writes=["kr2"])
            self.tt("dve", kr2[:, 16:32], pm[:, 384:400], sn[:, 16:32], ALU.mult, reads=["pm", "sn"], writes=["kr2"])
            self.tt("dve", kro[:], kro[:], kr2[:], ALU.add, reads=["kro", "kr2"], writes=["kro"])
            self.cp("dve", Ka[:, :, 64:96], kro[:].unsqueeze(1).broadcast_to([128, 8, 32]), reads=["kro"], writes=["Ka"])
            self.cp("pool", Ka[:, :, 0:64], kvf3[:, :, 0:64], reads=["kvf"], writes=["Ka"])
            self.cp("pool", Va[:, :, 0:64], kvf3[:, :, 64:128], reads=["kvf"], writes=["Va"])
            for hh in range(8):
                self.tr(qb[5][0:96, hh * 128:(hh + 1) * 128], Qa[:, hh, :], idb[:], reads=["Qa", "idb"], writes=["q5"])
                self.tr(qb[6][0:96, hh * 128:(hh + 1) * 128], Ka[:, hh, :], idb[:], reads=["Ka", "idb"], writes=["q6"])
            self.cp("act", QTt[:].rearrange("p a b -> p (a b)"), qb[5][0:96, :], reads=["q5"], writes=["QTt"])
            self.cp("dve", KTt[:].rearrange("p a b -> p (a b)"), qb[6][0:96, :], reads=["q6"], writes=["KTt"])
            self.store(self.QT[:, :, t0:t0 + 128].rearrange("h p t -> p h t"), QTt[:], reads=["QTt"], writes=[("QT", i)])
            self.store(self.KT[:, :, t0:t0 + 128].rearrange("h p t -> p h t"), KTt[:], reads=["KTt"], writes=[("KT", i)])
            self.store(self.Vd[t0:t0 + 128, :], Va[:].rearrange("p a b -> p (a b)"), reads=["Va"], writes=[("Vd", i)])
        self.end_phase()

    def phase_MM(self):
        NT = self.NT
        S_LEN = self.S_LEN
        QB = min(512, S_LEN)
        nqb = S_LEN // QB
        nj = QB // 128
        LOOK = 2
        self.begin_phase()
        sb, ps = self.sb, self.ps
        Vall = sb("Vall", [128, NT, 520], BF16)
        KTh = [sb("KTh", [96, S_LEN], BF16) for _ in range(2)]
        QTb = [sb("QTb", [96, QB], BF16) for _ in range(2)]
        PT = [sb("PT", [128, QB], BF16) for _ in range(4)]
        OT = sb("OT", [65, QB], F32)
        id32 = sb("id32", [128, 128], F32)
        osm = sb("osm", [128, nj, 64], F32)
        rec = sb("rec", [128, nj], F32)
        q = [ps("q%d" % i, [128, 512], F32) for i in range(7)]
        self.load(id32[:], self.c_ident, writes=["id32"])
        self.load(Vall[:], self.Vd.rearrange("(c p) f -> p c f", p=128), writes=["Vall"])
        blocks = [(hh, qi) for hh in range(8) for qi in range(nqb)]
        stream = [(bi, kc) for bi in range(len(blocks)) for kc in range(NT)]

        def load_k(hh):
            self.load(KTh[hh % 2][:], self.KT[hh], writes=["KTh%d" % (hh % 2)])

        def load_q(bi):
            hh, qi = blocks[bi]
            self.load(QTb[bi % 2][:], self.QT[hh, :, qi * QB:(qi + 1) * QB], writes=["QTb%d" % (bi % 2)])

        def emit_S(idx):
            bi, kc = stream[idx]
            hh, qi = blocks[bi]
            pb = idx % 4
            self.mm(q[pb][:, 0:QB], KTh[hh % 2][:, kc * 128:(kc + 1) * 128], QTb[bi % 2][:], True, True,
                    reads=["KTh%d" % (hh % 2), "QTb%d" % (bi % 2)], writes=["q%d" % pb])

        def epilogue_a(bi):
            ob = 4 + bi % 2
            self.cp("dve", OT[:], q[ob][0:65, 0:QB], reads=["q%d" % ob], writes=["OT"])

        def epilogue_b(bi):
            hh, qi = blocks[bi]
            for j in range(nj):
                self.tr(q[6][:, j * 65:(j + 1) * 65], OT[:, j * 128:(j + 1) * 128], id32[0:65, 0:65],
                        reads=["OT", "id32"], writes=["q6"])
            o3 = q[6][:, 0:nj * 65].rearrange("p (j e) -> p j e", j=nj)
            self.recip(rec[:], o3[:, :, 64], reads=["q6"], writes=["rec"])
            self.tt("dve", osm[:], o3[:, :, 0:64], rec[:].unsqueeze(2).broadcast_to([128, nj, 64]), ALU.mult,
                    reads=["q6", "rec"], writes=["osm"])
            self.store(self.yb[qi * QB:(qi + 1) * QB, hh * 64:(hh + 1) * 64].rearrange("(j p) e -> p j e", p=128),
                       osm[:], reads=["osm"], writes=[("yb", hh, qi)])

        load_k(0)
        load_q(0)
        if len(blocks) > 1:
            load_q(1)
        for idx in range(min(LOOK, len(stream))):
            emit_S(idx)
        pending = None
        for idx, (bi, kc) in enumerate(stream):
            hh, qi = blocks[bi]
            if kc == 0:
                if qi == 0 and hh + 1 < 8:
                    load_k(hh + 1)
            if idx + LOOK < len(stream):
                emit_S(idx + LOOK)
            pb = idx % 4
            ob = 4 + bi % 2
            self.act(PT[pb][:], q[pb][:, 0:QB], AF.Exp, reads=["q%d" % pb], writes=["PT%d" % pb], scale=SCALE)
            self.mm(q[ob][0:65, 0:QB], Vall[:, kc, hh * 65:(hh + 1) * 65], PT[pb][:], kc == 0, kc == NT - 1,
                    reads=["Vall", "PT%d" % pb], writes=["q%d" % ob])
            if pending is not None and kc == min(3, NT - 1):
                epilogue_b(pending)
                pending = None
            if kc == NT - 1:
                epilogue_a(bi)
                pending = bi
                if bi + 2 < len(blocks):
                    load_q(bi + 2)
        if pending is not None:
            epilogue_b(pending)
        self.end_phase()

    def phase_C1(self, l, xin):
        NT = self.NT
        W = self.w
        self.begin_phase()
        sb, ps = self.sb, self.ps
        woa = sb("woa", [128, 4, D], BF16)
        wob = sb("wob", [128, 4, D], BF16)
        wout = sb("wout", [128, 8, D], BF16)
        idb = sb("idb", [128, 128], BF16)
        xt = [sb("xt", [128, D], F32) for _ in range(4)]
        gt = [sb("gt", [128, 2048], F32) for _ in range(2)]
        yat = [sb("yat", [128, 512], F32) for _ in range(2)]
        ybt = [sb("ybt", [128, 512], F32) for _ in range(2)]
        yab = [sb("yab", [128, D], BF16) for _ in range(2)]
        yT = [sb("yT", [128, 8, 128], BF16) for _ in range(2)]
        m1 = sb("m1", [128, D], F32)
        m2 = sb("m2", [128, D], F32)
        mixb = [sb("mixb", [128, D], BF16) for _ in range(2)]
        mixT = sb("mixT", [128, 8, 128], BF16)
        x1t = sb("x1t", [128, D], F32)
        q = [ps("q%d" % i, [128, 512], F32) for i in range(8)]
        q6b = q[6][:].bitcast(BF16)
        q7b = q[7][:].bitcast(BF16)
        self.load(idb[:], self.c_ident, writes=["idb"], cast=True)
        self.load_w_bf16(woa, W["w_oa"][l], 512, "woa")
        self.load_w_bf16(wob, W["w_ob"][l], 512, "wob")
        self.load_w_bf16(wout, W["w_out"][l], D, "wout")

        def loads(i):
            b = i % 2
            t0 = i * 128
            self.load(xt[i % 4][:], xin[t0:t0 + 128, :], writes=["xt%d" % (i % 4)])
            self.load(gt[b][:], self.P[t0:t0 + 128, 0:2048], writes=["gt%d" % b])
            self.load(yat[b][:], self.ya[t0:t0 + 128, :], writes=["yat%d" % b])
            self.load(ybt[b][:], self.yb[t0:t0 + 128, :], writes=["ybt%d" % b])

        def s1(i):
            b = i % 2
            self.cp("dve", yab[b][:, 0:512], yat[b][:], reads=["yat%d" % b], writes=[("yab", b, 0)])
            self.cp("pool", yab[b][:, 512:1024], ybt[b][:], reads=["ybt%d" % b], writes=[("yab", b, 1)])
            for k in range(8):
                self.tr(q6b[:, k * 128:(k + 1) * 128], yab[b][:, k * 128:(k + 1) * 128], idb[:],
                        reads=[("yab", b, 0), ("yab", b, 1), "idb"], writes=["q6"])
            self.cp("act", yT[b][:].rearrange("p a b -> p (a b)"), q6b[:], reads=["q6"], writes=["yT%d" % b])
            self.act(gt[b][:], gt[b][:], AF.Sigmoid, reads=["gt%d" % b], writes=["gt%d" % b])

        def s2(i):
            b = i % 2
            for hf in range(2):
                hs = slice(hf * 512, (hf + 1) * 512)
                for k in range(4):
                    self.mm(q[hf][:], yT[b][:, k, :], woa[:, k, hs], k == 0, k == 3, reads=["yT%d" % b, "woa"], writes=["q%d" % hf])
                for k in range(4):
                    self.mm(q[2 + hf][:], yT[b][:, 4 + k, :], wob[:, k, hs], k == 0, k == 3, reads=["yT%d" % b, "wob"],
                            writes=["q%d" % (2 + hf)])
            for hf in range(2):
                hs = slice(hf * 512, (hf + 1) * 512)
                hs2 = slice(1024 + hf * 512, 1024 + (hf + 1) * 512)
                self.tt("dve", m1[:, hs], gt[b][:, hs], q[hf][:], ALU.mult, reads=["gt%d" % b, "q%d" % hf], writes=[("m1", hf)])
                self.tt("dve", m2[:, hs], gt[b][:, hs2], q[2 + hf][:], ALU.mult, reads=["gt%d" % b, "q%d" % (2 + hf)],
                        writes=[("m2", hf)])
                self.tt("pool", mixb[b][:, hs], m1[:, hs], m2[:, hs], ALU.add, reads=[("m1", hf), ("m2", hf)],
                        writes=[("mixb", b, hf)])

        def s3(i):
            b = i % 2
            t0 = i * 128
            xb = i % 4
            for k in range(8):
                self.tr(q7b[:, k * 128:(k + 1) * 128], mixb[b][:, k * 128:(k + 1) * 128], idb[:],
                        reads=[("mixb", b, 0), ("mixb", b, 1), "idb"], writes=["q7"])
            self.cp("act", mixT[:].rearrange("p a b -> p (a b)"), q7b[:], reads=["q7"], writes=["mixT"])
            for hf in range(2):
                hs = slice(hf * 512, (hf + 1) * 512)
                for k in range(8):
                    self.mm(q[4 + hf][:], mixT[:, k, :], wout[:, k, hs], k == 0, k == 7, reads=["mixT", "wout"],
                            writes=["q%d" % (4 + hf)])
                self.tt("dve", x1t[:, hs], xt[xb][:, hs], q[4 + hf][:], ALU.add, reads=["xt%d" % xb, "q%d" % (4 + hf)],
                        writes=[("x1t", hf)])
            self.store(self.x1[t0:t0 + 128, :], x1t[:], reads=[("x1t", 0), ("x1t", 1)], writes=[("x1", i)])

        loads(0)
        if NT > 1:
            loads(1)
        s1(0)
        for i in range(NT + 1):
            if i + 1 < NT:
                s1(i + 1)
            if i < NT:
                s2(i)
            if i + 2 < NT:
                loads(i + 2)
            if i >= 1:
                s3(i - 1)
        self.end_phase()

    def phase_C2(self, l, last, yout):
        NT = self.NT
        W = self.w
        self.begin_phase()
        sb, ps = self.sb, self.ps
        wgu = sb("wgu", [128, 8, 2 * DFF], BF16)
        wdn = sb("wdn", [128, 22, D], BF16)
        gbc = sb("gbc", [128, D], F32)
        idb = sb("idb", [128, 128], BF16)
        xt = [sb("xt", [128, D], F32) for _ in range(2)]
        junk = sb("junk", [128, D], F32)
        ss = sb("ss", [128, 1], F32)
        rstd = sb("rstd", [128, 1], F32)
        h = sb("h", [128, D], BF16)
        hT = [sb("hT", [128, 8, 128], BF16) for _ in range(2)]
        sl = [sb("sl", [128, 256], F32) for _ in range(2)]
        actb = sb("actb", [128, DFF], BF16)
        actT = sb("actT", [128, 22, 128], BF16)
        x2t = sb("x2t", [128, D], F32)
        if last:
            fbc = sb("fbc", [128, D], F32)
        q = [ps("q%d" % i, [128, 512], F32) for i in range(6)]
        qTb = q[4][:].bitcast(BF16)
        qT2 = q[5][:].bitcast(BF16)
        self.load(idb[:], self.c_ident, writes=["idb"], cast=True)
        self.bcast_load(gbc, W["norm_ffn_g"][l:l + 1, :], D, "gbc")
        if last:
            self.bcast_load(fbc, W["final_norm_g"][0:1, :], D, "fbc")
        self.load_w_bf16(wgu, W["w_gu"][l], D, "wgu")
        self.load_w_bf16(wdn, W["w_down"][l], DFF, "wdn")

        def norm(i):
            b = i % 2
            self.load(xt[b][:], self.x1[i * 128:(i + 1) * 128, :], writes=["xt%d" % b])
            self.rmsnorm(xt[b][:], "xt%d" % b, D, gbc[:], "gbc", h[:], "h", junk[:], ss[:], rstd[:], "F")

        def trans(i):
            b = i % 2
            for k in range(8):
                self.tr(qTb[:, k * 128:(k + 1) * 128], h[:, k * 128:(k + 1) * 128], idb[:], reads=["h", "idb"], writes=["q4"])
            self.cp("act", hT[b][:].rearrange("p a b -> p (a b)"), qTb[:], reads=["q4"], writes=["hT%d" % b])

        def tpose(j):
            o = (j % 4) * 256
            for u in range(2):
                self.tr(qT2[:, o + u * 128:o + (u + 1) * 128], actb[:, j * 256 + u * 128:j * 256 + (u + 1) * 128], idb[:],
                        reads=[("actb", j), "idb"], writes=["q5"])
            self.cp("dve", actT[:, 2 * j:2 * j + 2, :].rearrange("p a b -> p (a b)"),
                    qT2[:, o:o + 256], reads=["q5"], writes=[("actT", j)])

        norm(0)
        trans(0)
        for i in range(NT):
            b = i % 2
            t0 = i * 128
            xn = "xt%d" % b
            hn = "hT%d" % b
            if i + 1 < NT:
                norm(i + 1)
            for j in range(11):
                bk = q[j % 2]
                bn = "q%d" % (j % 2)
                for k in range(8):
                    self.mm(bk[:, 0:256], hT[b][:, k, :], wgu[:, k, j * 256:(j + 1) * 256], k == 0, k == 7,
                            reads=[hn, "wgu"], writes=[bn])
                for k in range(8):
                    self.mm(bk[:, 256:512], hT[b][:, k, :], wgu[:, k, DFF + j * 256:DFF + (j + 1) * 256], k == 0, k == 7,
                            reads=[hn, "wgu"], writes=[bn])
                self.act(sl[j % 2][:], bk[:, 0:256], AF.Silu, reads=[bn], writes=["sl%d" % (j % 2)])
                self.tt("dve", actb[:, j * 256:(j + 1) * 256], sl[j % 2][:], bk[:, 256:512], ALU.mult,
                        reads=["sl%d" % (j % 2), bn], writes=[("actb", j)])
                if j >= 1:
                    tpose(j - 1)
            tpose(10)
            if i + 1 < NT:
                trans(i + 1)
            for hf in range(2):
                hs = slice(hf * 512, (hf + 1) * 512)
                for c in range(22):
                    self.mm(q[2 + hf][:], actT[:, c, :], wdn[:, c, hs], c == 0, c == 21,
                            reads=[("actT", c // 2), "wdn"], writes=["q%d" % (2 + hf)])
                self.tt("dve", x2t[:, hs], xt[b][:, hs], q[2 + hf][:], ALU.add, reads=[xn, "q%d" % (2 + hf)],
                        writes=["x2t"])
            if last:
                self.rmsnorm(x2t[:], "x2t", D, fbc[:], "fbc", x2t[:], "x2t", junk[:], ss[:], rstd[:], "F")
                self.store(yout[t0:t0 + 128, :], x2t[:], reads=["x2t"], writes=[("y", i)])
            else:
                self.store(self.x2[t0:t0 + 128, :], x2t[:], reads=["x2t"], writes=[("x2", i)])
        self.end_phase()

    def build(self, phases=None):
        def on(p):
            return phases is None or p in phases
        for s in range(self.NSEQ):
            for l in range(self.depth):
                last = l == self.depth - 1
                xin = self.x[s] if l == 0 else self.x2
                if on("A"):
                    self.phase_A(l, xin)
                if on("R0"):
                    self.phase_R(l, 0)
                if on("R1"):
                    self.phase_R(l, 1)
                if on("MP"):
                    self.phase_MP(l)
                if on("MM"):
                    self.phase_MM()
                if on("C1"):
                    self.phase_C1(l, xin)
                if on("C2"):
                    self.phase_C2(l, last, self.y[s])
        self.S.emit()
        self.S.stack.close()
        return self.nc


def make_consts(S_LEN):
    s = np.arange(128)[:, None]
    t = np.arange(128)[None, :]
    tri = np.stack([(s <= t), (s >= t)]).astype(np.float32)
    strict = [(s < t).astype(np.float32), (s > t).astype(np.float32)]
    incl = [(s <= t).astype(np.float32), (s >= t).astype(np.float32)]
    m4 = np.stack([np.concatenate([strict[d], incl[d], strict[d], incl[d]], axis=1) for d in range(2)])
    mn = [(t < s).astype(np.float32), (t > s).astype(np.float32)]
    mn4 = np.stack([np.concatenate([mn[d]] * 4, axis=1) for d in range(2)])
    pos = np.arange(S_LEN, dtype=np.float32)
    inv_freq = (1.0 / (np.float32(10000.0) ** (np.arange(0, 32, 2, dtype=np.float32) / np.float32(32)))).astype(np.float32)
    ang = pos[:, None] * inv_freq[None, :]
    ang = np.concatenate([ang, ang], axis=-1).astype(np.float32)
    cos = np.cos(ang).astype(np.float32)
    sin = np.sin(ang).astype(np.float32)
    sin_s = sin.copy()
    sin_s[:, 0:16] = -sin_s[:, 0:16]
    return dict(c_ident=np.eye(128, dtype=np.float32), c_tri=tri, c_m4=m4.astype(np.float32),
                c_mn4=mn4.astype(np.float32), c_ones=np.ones((128, 128), np.float32),
                c_cos=cos, c_sin=sin_s)


_WNAMES = ["norm_mix_g", "w_in", "shift_mu", "decay_w2", "decay_w0", "iclr_a2", "iclr_a0", "gate_g2", "k_k", "k_a",
           "r_k", "gn_g", "gn_b", "w_oa", "q_norm_g", "w_uq", "kv_norm_g", "w_ukv", "w_ob", "w_out", "norm_ffn_g",
           "w_gu", "w_down", "final_norm_g"]


def prep_weights(inputs, depth):
    out = {}
    for n in _WNAMES:
        a = np.ascontiguousarray(np.asarray(inputs[n], dtype=np.float32))
        if n == "r_k":
            a = a.reshape(a.shape[0], 512)
        if n == "final_norm_g":
            a = a.reshape(1, D)
        else:
            a = a[:depth]
        out[n] = np.ascontiguousarray(a)
    return out


def kernel(**inputs):
    xp = np.asarray(inputs["x_prompt"], dtype=np.float32)
    xs = np.asarray(inputs["x_sample"], dtype=np.float32)
    S_LEN = xp.shape[1]
    x_all = np.concatenate([xp, xs], axis=0)
    nseq = x_all.shape[0] // NCORES
    wts = prep_weights(inputs, DEPTH)
    consts = make_consts(S_LEN)
    nc = Builder(S_LEN, nseq, DEPTH).build()
    in_maps = []
    for c in range(NCORES):
        m = dict(x=np.ascontiguousarray(x_all[c * nseq:(c + 1) * nseq]))
        m.update(wts)
        m.update(consts)
        in_maps.append(m)
    res = run_bass_kernel_spmd(nc, in_maps, core_ids=list(range(NCORES)))
    y = np.concatenate([r["y"] for r in res.results], axis=0)
    return (np.ascontiguousarray(y[:xp.shape[0]]), np.ascontiguousarray(y[xp.shape[0]:]))
```

```python
import contextlib
import os
import numpy as np
import concourse.bass as bass
import concourse.mybir as mybir
from concourse.alu_op_type import AluOpType as ALU
from concourse.bass_utils import run_bass_kernel_spmd

F32 = mybir.dt.float32
BF16 = mybir.dt.bfloat16
AF = mybir.ActivationFunctionType
AX = mybir.AxisListType

D = 1024
NIN = 4416
DFF = 2816
DEPTH = 2
NCORES = 8
SEQ_FULL = 4096
RMS_EPS = 1e-6
GN_EPS = 64e-5
CDEC = 0.6065306597126334
SCALE = 96.0 ** -0.5

ENGS = ("pe", "act", "dve", "pool", "sp")
N_DMA_SEMS = 8
SAME_ENGINE_SYNC = True


def _is_psum(r):
    n = r[0] if isinstance(r, tuple) else r
    return isinstance(n, str) and len(n) >= 2 and n[0] in "qp" and (n[1].isdigit() or n[1] in "TP")


class Sched:
    def __init__(self, nc):
        self.nc = nc
        self.q = {e: [] for e in ENGS}
        self.cnt = {e: 0 for e in ENGS}
        self.seen = {e: {} for e in ENGS}
        self.last_w = {}
        self.readers = {}
        self.dma_val = {}
        self.dma_rr = {e: 0 for e in ENGS}
        self.stack = contextlib.ExitStack()
        self.sems = {}
        self.nops = 0
        self.limit = int(os.environ.get("OPLIMIT", "1000000000"))
        self.marks = []

    def mark(self, label):
        self.marks.append((label, self.nops))

    def _deps(self, eng, reads, writes):
        deps = []
        for r in reads:
            ev = self.last_w.get(r)
            if ev is not None:
                deps.append(ev)
            if eng != "pe" and _is_psum(r):
                deps.extend(e2 for e2 in self.readers.get(r, ()) if e2[0] != eng)
        for w in writes:
            ev = self.last_w.get(w)
            if ev is not None:
                deps.append(ev)
            deps.extend(self.readers.get(w, ()))
        waits = {}
        seen = self.seen[eng]
        for sk, v in deps:
            if sk == eng and (eng == "pe" or not SAME_ENGINE_SYNC):
                continue
            if seen.get(sk, 0) >= v:
                continue
            if waits.get(sk, 0) < v:
                waits[sk] = v
        for sk, v in waits.items():
            seen[sk] = v
        return waits

    def _record(self, ev, reads, writes):
        for r in reads:
            self.readers.setdefault(r, []).append(ev)
        for w in writes:
            self.last_w[w] = ev
            self.readers[w] = []

    def op(self, eng, fn, reads=(), writes=()):
        self.nops += 1
        if self.nops > self.limit:
            return
        waits = self._deps(eng, reads, writes)
        self.cnt[eng] += 1
        ev = (eng, self.cnt[eng])
        self.q[eng].append((list(waits.items()), fn, (eng, 1)))
        self._record(ev, reads, writes)

    def dma(self, eng, fn, reads=(), writes=()):
        self.nops += 1
        if self.nops > self.limit:
            return
        k = self.dma_rr[eng]
        self.dma_rr[eng] = (k + 1) % N_DMA_SEMS
        sk = ("dma", eng, k)
        prev = self.dma_val.get(sk, 0)
        waits = self._deps(eng, reads, writes)
        if prev > 0 and self.seen[eng].get(sk, 0) < prev:
            waits[sk] = prev
            self.seen[eng][sk] = prev
        self.dma_val[sk] = prev + 16
        ev = (sk, prev + 16)
        self.q[eng].append((list(waits.items()), fn, (sk, 16)))
        self._record(ev, reads, writes)

    def barrier(self):
        tgt = {e: self.cnt[e] for e in ENGS if self.cnt[e] > 0}
        tgt.update(self.dma_val)
        for e in ENGS:
            waits = []
            for sk, v in tgt.items():
                if sk == e:
                    continue
                if self.seen[e].get(sk, 0) < v:
                    waits.append((sk, v))
                    self.seen[e][sk] = v
            if waits:
                self.q[e].append((waits, None, None))
        self.last_w = {}
        self.readers = {}

    def emit(self):
        nc = self.nc
        st = self.stack
        keys = list(ENGS) + list(self.dma_val)
        for sk in keys:
            nm = sk if isinstance(sk, str) else "d_%s_%d" % (sk[1], sk[2])
            self.sems[sk] = st.enter_context(nc.semaphore("s_" + nm))
        final = list(self.dma_val.items())
        block = st.enter_context(nc.Block())
        sems = self.sems

        def run(engname, final_waits=()):
            def body(e):
                for waits, fn, inc in self.q[engname]:
                    for sk, v in waits:
                        e.wait_ge(sems[sk], v)
                    if fn is not None:
                        fn(e).then_inc(sems[inc[0]], inc[1])
                for sk, v in final_waits:
                    e.wait_ge(sems[sk], v)
            return body

        block.tensor(run("pe"))
        block.scalar(run("act"))
        block.vector(run("dve"))
        block.gpsimd(run("pool"))
        block.sync(run("sp", final))


class Builder:
    def __init__(self, S_LEN, NSEQ, depth=DEPTH):
        self.S_LEN = S_LEN
        self.NSEQ = NSEQ
        self.depth = depth
        self.NT = S_LEN // 128
        nc = bass.Bass("TRN2", target_bir_lowering=False)
        self.nc = nc
        self.S = Sched(nc)
        self.ph = None
        self._uid = 0

        def inp(name, shape):
            return nc.dram_tensor(name, list(shape), F32, kind="ExternalInput").ap()

        L = depth
        self.x = inp("x", [NSEQ, S_LEN, D])
        self.w = dict(
            norm_mix_g=inp("norm_mix_g", [L, D]), w_in=inp("w_in", [L, D, NIN]),
            shift_mu=inp("shift_mu", [L, 2, 1952]), decay_w2=inp("decay_w2", [L, 2, 64, 512]),
            decay_w0=inp("decay_w0", [L, 2, 512]), iclr_a2=inp("iclr_a2", [L, 2, 64, 512]),
            iclr_a0=inp("iclr_a0", [L, 2, 512]), gate_g2=inp("gate_g2", [L, 160, 512]),
            k_k=inp("k_k", [L, 512]), k_a=inp("k_a", [L, 512]), r_k=inp("r_k", [L, 512]),
            gn_g=inp("gn_g", [L, 512]), gn_b=inp("gn_b", [L, 512]), w_oa=inp("w_oa", [L, 512, D]),
            q_norm_g=inp("q_norm_g", [L, 256]), w_uq=inp("w_uq", [L, 256, 768]),
            kv_norm_g=inp("kv_norm_g", [L, 128]), w_ukv=inp("w_ukv", [L, 128, 1024]),
            w_ob=inp("w_ob", [L, 512, D]), w_out=inp("w_out", [L, D, D]),
            norm_ffn_g=inp("norm_ffn_g", [L, D]), w_gu=inp("w_gu", [L, D, 2 * DFF]),
            w_down=inp("w_down", [L, DFF, D]), final_norm_g=inp("final_norm_g", [1, D]),
        )
        self.c_ident = inp("c_ident", [128, 128])
        self.c_tri = inp("c_tri", [2, 128, 128])
        self.c_m4 = inp("c_m4", [2, 128, 512])
        self.c_mn4 = inp("c_mn4", [2, 128, 512])
        self.c_ones = inp("c_ones", [128, 128])
        self.c_cos = inp("c_cos", [S_LEN, 32])
        self.c_sin = inp("c_sin", [S_LEN, 32])
        self.y = nc.dram_tensor("y", [NSEQ, S_LEN, D], F32, kind="ExternalOutput").ap()
        def scr(name, shape, dt=F32):
            return nc.dram_tensor(name, list(shape), dt).ap()
        self.P = scr("scr_P", [S_LEN, NIN])
        self.of = scr("scr_of", [S_LEN, 512])
        self.PSd = scr("scr_PS", [S_LEN, 1952])
        self.KKd = scr("scr_KK", [S_LEN, 512])
        self.ya = scr("scr_ya", [S_LEN, 512])
        self.yb = scr("scr_yb", [S_LEN, 512])
        self.x1 = scr("scr_x1", [S_LEN, D])
        self.x2 = scr("scr_x2", [S_LEN, D])
        self.QT = scr("scr_QT", [8, 96, S_LEN], BF16)
        self.KT = scr("scr_KT", [8, 96, S_LEN], BF16)
        self.Vd = scr("scr_V", [S_LEN, 8 * 65], BF16)

    def begin_phase(self):
        self.ph = contextlib.ExitStack()

    def end_phase(self):
        self.S.barrier()
        self.ph.close()
        self.ph = None

    def sb(self, name, shape, dt):
        self._uid += 1
        return self.ph.enter_context(self.nc.sbuf_tensor("%s_%d" % (name, self._uid), list(shape), dt))

    def ps(self, name, shape, dt):
        self._uid += 1
        return self.ph.enter_context(self.nc.psum_tensor("%s_%d" % (name, self._uid), list(shape), dt))

    def load(self, out_ap, in_ap, writes, reads=(), cast=False):
        eng = "pool" if cast else "sp"
        self.S.dma(eng, lambda e: e.dma_start(out=out_ap, in_=in_ap), reads=reads, writes=writes)

    def store(self, out_ap, in_ap, reads, writes):
        self.S.dma("sp", lambda e: e.dma_start(out=out_ap, in_=in_ap), reads=reads, writes=writes)

    def bcast_load(self, tile, row_ap, width, name):
        self.load(tile[:], row_ap.broadcast_to([128, width]), writes=[name])

    def load_w_bf16(self, tile, w_ap, K, name):
        for k in range(K // 128):
            self.load(tile[:, k, :], w_ap[k * 128:(k + 1) * 128, :], writes=[name], cast=True)

    def mm(self, out, lhsT, rhs, start, stop, reads, writes):
        self.S.op("pe", lambda e: e.matmul(out=out, lhsT=lhsT, rhs=rhs, start=start, stop=stop),
                  reads=reads, writes=writes)

    def tr(self, out, in_, ident, reads, writes):
        self.S.op("pe", lambda e: e.transpose(out=out, in_=in_, identity=ident), reads=reads, writes=writes)

    def act(self, out, in_, func, reads, writes, scale=None, bias=None, accum_out=None):
        kw = {}
        if scale is not None:
            kw["scale"] = scale
        if bias is not None:
            kw["bias"] = bias
        if accum_out is not None:
            kw["accum_out"] = accum_out
        self.S.op("act", lambda e: e.activation(out=out, in_=in_, func=func, **kw), reads=reads, writes=writes)

    def tt(self, eng, out, in0, in1, op, reads, writes):
        self.S.op(eng, lambda e: e.tensor_tensor(out=out, in0=in0, in1=in1, op=op), reads=reads, writes=writes)

    def ts(self, out, in0, s1, s2, op0, op1, reads, writes, eng="dve"):
        self.S.op(eng, lambda e: e.tensor_scalar(out=out, in0=in0, scalar1=s1, scalar2=s2, op0=op0, op1=op1),
                  reads=reads, writes=writes)

    def stt(self, out, in0, scalar, in1, op0, op1, reads, writes):
        self.S.op("dve", lambda e: e.scalar_tensor_tensor(out=out, in0=in0, scalar=scalar, in1=in1, op0=op0, op1=op1),
                  reads=reads, writes=writes)

    def cp(self, eng, out, in_, reads, writes):
        if eng == "act":
            self.S.op("act", lambda e: e.activation(out=out, in_=in_, func=AF.Copy), reads=reads, writes=writes)
        else:
            self.S.op(eng, lambda e: e.tensor_copy(out=out, in_=in_), reads=reads, writes=writes)

    def red(self, out, in_, reads, writes):
        self.S.op("dve", lambda e: e.tensor_reduce(out=out, in_=in_, axis=AX.X, op=ALU.add), reads=reads, writes=writes)

    def recip(self, out, in_, reads, writes):
        self.S.op("dve", lambda e: e.reciprocal(out=out, in_=in_), reads=reads, writes=writes)

    def memset(self, eng, ap, val, writes):
        self.S.op(eng, lambda e: e.memset(ap, val), writes=writes)

    def rmsnorm(self, x_ap, xn, width, gbc_ap, gn, out_ap, outn, junk, ss, rstd, tag):
        jn, sn, rn = "junk" + tag, "ss" + tag, "rstd" + tag
        self.act(junk, x_ap, AF.Square, reads=[xn], writes=[jn, sn], accum_out=ss)
        self.ts(rstd, ss, 1.0 / width, RMS_EPS, ALU.mult, ALU.add, reads=[sn], writes=[rn])
        self.act(rstd, rstd, AF.Sqrt, reads=[rn], writes=[rn])
        self.recip(rstd, rstd, reads=[rn], writes=[rn])
        self.stt(out_ap, x_ap, rstd, gbc_ap, ALU.mult, ALU.mult, reads=[xn, rn, gn], writes=[outn])

    def phase_A(self, l, xin):
        NT = self.NT
        self.begin_phase()
        wA = self.sb("wA", [128, 8, NIN], BF16)
        gbc = self.sb("gA", [128, D], F32)
        idb = self.sb("idb", [128, 128], BF16)
        junk = self.sb("junk", [128, D], F32)
        ss = self.sb("ss", [128, 1], F32)
        rstd = self.sb("rstd", [128, 1], F32)
        xt = [self.sb("xt", [128, D], F32) for _ in range(2)]
        h = [self.sb("h", [128, D], BF16) for _ in range(2)]
        hT = [self.sb("hT", [128, 8, 128], BF16) for _ in range(2)]
        Pt = [self.sb("Pt", [128, NIN], F32) for _ in range(2)]
        pT = [self.ps("pT", [128, 8, 128], BF16) for _ in range(2)]
        pP = [self.ps("pP", [128, 512], F32) for _ in range(4)]
        self.load(idb[:], self.c_ident, writes=["idb"], cast=True)
        self.bcast_load(gbc, self.w["norm_mix_g"][l:l + 1, :], D, "gA")
        self.load_w_bf16(wA, self.w["w_in"][l], D, "wA")
        npieces = (NIN + 511) // 512

        def norm(i):
            b = i % 2
            self.load(xt[b][:], xin[i * 128:(i + 1) * 128, :], writes=["xt%d" % b])
            self.rmsnorm(xt[b][:], "xt%d" % b, D, gbc[:], "gA", h[b][:], "h%d" % b, junk[:], ss[:], rstd[:], "A")

        def trans(i):
            b = i % 2
            for k in range(8):
                self.tr(pT[b][:, k, :], h[b][:, k * 128:(k + 1) * 128], idb[:], reads=["h%d" % b, "idb"], writes=["pT%d" % b])
            self.cp("act", hT[b][:], pT[b][:], reads=["pT%d" % b], writes=["hT%d" % b])

        norm(0)
        trans(0)
        for i in range(NT):
            b = i % 2
            if i + 1 < NT:
                norm(i + 1)
            for j in range(npieces):
                n0 = j * 512
                n = min(512, NIN - n0)
                pp = pP[j % 4]
                pn = "pP%d" % (j % 4)
                for k in range(8):
                    self.mm(pp[:, 0:n], hT[b][:, k, :], wA[:, k, n0:n0 + n], k == 0, k == 7,
                            reads=["hT%d" % b, "wA"], writes=[pn])
                self.cp("dve" if j % 2 == 0 else "act", Pt[b][:, n0:n0 + n], pp[:, 0:n],
                        reads=[pn], writes=[("Pt", b, j)])
                if j == 5 and i + 1 < NT:
                    trans(i + 1)
            self.store(self.P[i * 128:(i + 1) * 128, :], Pt[b][:], reads=[("Pt", b, j) for j in range(npieces)],
                       writes=[("P", i)])
        self.end_phase()

    def phase_R(self, l, d):
        NT = self.NT
        S_LEN = self.S_LEN
        W = self.w
        self.begin_phase()
        sb, ps = self.sb, self.ps
        if d == 0:
            mu0 = sb("mu0", [128, 1952], F32)
            mu1 = sb("mu1", [128, 1952], F32)
            c0 = sb("c0", [128, 1952], F32)
            pap = sb("pap", [128, 1952], F32)
            pan = sb("pan", [128, 1952], F32)
            kkbc = sb("kkbc", [128, 512], F32)
        w0bc = sb("w0bc", [128, 512], F32)
        a0bc = sb("a0bc", [128, 512], F32)
        kabc = sb("kabc", [128, 512], F32)
        w2b = sb("w2b", [64, 512], BF16)
        a2b = sb("a2b", [64, 512], BF16)
        tri = sb("tri", [128, 128], F32)
        ones = sb("ones", [128, 128], F32)
        m4 = sb("m4", [128, 512], F32)
        mn4 = sb("mn4", [128, 512], F32)
        idb = sb("idb", [128, 128], BF16)
        pac = [sb("pac", [128, 1952], F32) for _ in range(3)]
        lo = sb("lo", [128, 128], BF16)
        loT = sb("loT", [64, 2, 128], BF16)
        sgm = sb("sgm", [128, 512], F32)
        av = sb("av", [128, 512], F32)
        kk = sb("kk", [128, 512], F32)
        tmp = sb("tmp", [128, 512], F32)
        kd = sb("kd", [128, 512], F32)
        ka = sb("ka", [128, 512], F32)
        Ls = sb("Ls", [128, 512], F32)
        Ld = sb("Ld", [128, 512], F32)
        E1 = sb("E1", [128, 512], F32)
        E2 = sb("E2", [128, 512], F32)
        E3 = sb("E3", [128, 512], F32)
        E4 = sb("E4", [128, 512], F32)
        ssq = sb("ssq", [128, 8], F32)
        gC = [sb("gC", [64, 8], F32) for _ in range(3)]
        Ab = [sb("Ab", [128, 512], BF16) for _ in range(3)]
        Rb = [sb("Rb", [128, 512], BF16) for _ in range(3)]
        Bb = [sb("Bb", [128, 512], BF16) for _ in range(3)]
        Kb = [sb("Kb", [128, 512], BF16) for _ in range(3)]
        Btb = [sb("Btb", [128, 512], BF16) for _ in range(3)]
        Ktb = [sb("Ktb", [128, 512], BF16) for _ in range(3)]
        Vb = [sb("Vb", [128, 512], BF16) for _ in range(3)]
        ART = [sb("ART", [128, 8, 256], BF16) for _ in range(3)]
        BT = [sb("BT", [64, 8, 128], BF16) for _ in range(3)]
        KTt = [sb("KTt", [64, 8, 128], BF16) for _ in range(3)]
        ATall = sb("ATall", [128, 8, 512], BF16)
        PP = [sb("PP", [128, 8, 256], BF16) for _ in range(2)]
        Wb = sb("Wb", [128, 512], BF16)
        osb = sb("osb", [128, 512], F32)
        ST32 = sb("ST32", [64, 8, 64], F32)
        STb = sb("STb", [128, 8, 64], BF16)
        if d == 1:
            rkbc = sb("rkbc", [128, 512], F32)
            gngbc = sb("gngbc", [128, 512], F32)
            gnbbc = sb("gnbbc", [128, 512], F32)
            g2b = sb("g2b", [128, 2, 512], BF16)
            oft = sb("oft", [128, 512], F32)
            cen = sb("cen", [128, 512], F32)
            sq2 = sb("sq2", [128, 512], F32)
            bon = sb("bon", [128, 512], F32)
            st8 = sb("st8", [128, 8], F32)
            sv8 = sb("sv8", [128, 8], F32)
            sb8 = sb("sb8", [128, 8], F32)
            gs = sb("gs", [128, 160], BF16)
            gT = sb("gT", [128, 2, 128], BF16)
            yat = sb("yat", [128, 512], F32)
        pbk = [ps("pq%d" % i, [128, 512], F32) for i in range(3)]
        pbn = ["pq0", "pq1", "pq2"]
        pbb = [t[:].bitcast(BF16) for t in pbk]
        sbk = [ps("sq%d" % i, [128, 512], F32) for i in range(5)]
        sbn = ["sq0", "sq1", "sq2", "sq3", "sq4"]
        s3, s4 = sbk[3], sbk[4]
        s3b = s3[:].bitcast(BF16)

        self.load(idb[:], self.c_ident, writes=["idb"], cast=True)
        self.load(tri[:], self.c_tri[d], writes=["tri"])
        self.load(ones[:], self.c_ones, writes=["ones"])
        self.load(m4[:], self.c_m4[d], writes=["m4"])
        self.load(mn4[:], self.c_mn4[d], writes=["mn4"])
        if d == 0:
            self.bcast_load(mu0, W["shift_mu"][l, 0:1, :], 1952, "mu0")
            self.bcast_load(mu1, W["shift_mu"][l, 1:2, :], 1952, "mu1")
            self.bcast_load(kkbc, W["k_k"][l:l + 1, :], 512, "kkbc")
            self.tt("dve", c0[:], mu0[:], mu1[:], ALU.add, reads=["mu0", "mu1"], writes=["c0"])
            self.ts(c0[:], c0[:], -1.0, 1.0, ALU.mult, ALU.add, reads=["c0"], writes=["c0"])
        self.bcast_load(w0bc, W["decay_w0"][l, d:d + 1, :], 512, "w0bc")
        self.bcast_load(a0bc, W["iclr_a0"][l, d:d + 1, :], 512, "a0bc")
        self.bcast_load(kabc, W["k_a"][l:l + 1, :], 512, "kabc")
        self.load(w2b[:], W["decay_w2"][l, d], writes=["w2b"], cast=True)
        self.load(a2b[:], W["iclr_a2"][l, d], writes=["a2b"], cast=True)
        self.memset("dve", ST32[:], 0.0, writes=["ST32"])
        self.memset("dve", STb[:], 0.0, writes=["STb"])
        for b in range(3):
            self.memset("dve", ART[b][:], 0.0, writes=["ART%d" % b])
        if d == 1:
            self.bcast_load(rkbc, W["r_k"][l:l + 1, :], 512, "rkbc")
            self.bcast_load(gngbc, W["gn_g"][l:l + 1, :], 512, "gngbc")
            self.bcast_load(gnbbc, W["gn_b"][l:l + 1, :], 512, "gnbbc")
            self.memset("dve", gT[:], 0.0, writes=["gT"])
            self.memset("dve", g2b[:], 0.0, writes=["g2b"])
            self.load(g2b[:, 0, :], W["gate_g2"][l, 0:128, :], writes=["g2b"], cast=True)
            self.load(g2b[0:32, 1, :], W["gate_g2"][l, 128:160, :], writes=["g2b"], cast=True)

        def v3(ap):
            return ap.rearrange("p (h e) -> p h e", h=8)

        def prep(i, pb):
            t0 = i * 128
            pc = pac[pb]
            pn_ = "pac%d" % pb
            if d == 0:
                self.load(pc[:], self.P[t0:t0 + 128, 2048:4000], writes=[pn_])
                if i == 0:
                    self.memset("pool", pap[:], 0.0, writes=["pap"])
                    self.load(pap[1:128, :], self.P[0:127, 2048:4000], writes=["pap"])
                else:
                    self.load(pap[:], self.P[t0 - 1:t0 + 127, 2048:4000], writes=["pap"])
                if i == NT - 1:
                    self.memset("pool", pan[:], 0.0, writes=["pan"])
                    self.load(pan[0:127, :], self.P[t0 + 1:S_LEN, 2048:4000], writes=["pan"])
                else:
                    self.load(pan[:], self.P[t0 + 1:t0 + 129, 2048:4000], writes=["pan"])
                yield
                yield
                yield
                self.tt("pool", pap[:], pap[:], mu0[:], ALU.mult, reads=["pap", "mu0"], writes=["pap"])
                self.tt("dve", pan[:], pan[:], mu1[:], ALU.mult, reads=["pan", "mu1"], writes=["pan"])
                self.tt("pool", pc[:], pc[:], c0[:], ALU.mult, reads=[pn_, "c0"], writes=[pn_])
                yield
                self.tt("dve", pc[:], pc[:], pan[:], ALU.add, reads=[pn_, "pan"], writes=[pn_])
                self.tt("dve", pc[:], pc[:], pap[:], ALU.add, reads=[pn_, "pap"], writes=[pn_])
                self.store(self.PSd[t0:t0 + 128, :], pc[:], reads=[pn_], writes=[("PS", i)])
            else:
                self.load(pc[:], self.PSd[t0:t0 + 128, :], writes=[pn_])
                self.load(kk[:], self.KKd[t0:t0 + 128, :], writes=["kk"])
                yield
                yield
            yield
            r_ = pc[:, 0:512]
            k_ = pc[:, 512:1024]
            v_ = pc[:, 1024:1536]
            lw_ = pc[:, 1536 + 64 * d:1600 + 64 * d]
            la_ = pc[:, 1664 + 64 * d:1728 + 64 * d]
            self.act(lo[:, 0:64], lw_, AF.Tanh, reads=[pn_], writes=["lo"])
            self.cp("dve", lo[:, 64:128], la_, reads=[pn_], writes=["lo"])
            self.tr(pbb[0][0:64, 0:128], lo[:, 0:64], idb[:], reads=["lo", "idb"], writes=[pbn[0]])
            self.tr(pbb[0][0:64, 128:256], lo[:, 64:128], idb[:], reads=["lo", "idb"], writes=[pbn[0]])
            self.cp("act", loT[:].rearrange("p a b -> p (a b)"), pbb[0][0:64, 0:256], reads=[pbn[0]], writes=["loT"])
            self.mm(pbk[1][:], loT[:, 0, :], w2b[:], True, True, reads=["loT", "w2b"], writes=[pbn[1]])
            self.mm(pbk[2][:], loT[:, 1, :], a2b[:], True, True, reads=["loT", "a2b"], writes=[pbn[2]])
            self.tt("dve", sgm[:], pbk[1][:], w0bc[:], ALU.add, reads=[pbn[1], "w0bc"], writes=["sgm"])
            self.act(sgm[:], sgm[:], AF.Sigmoid, reads=["sgm"], writes=["sgm"])
            self.tt("dve", av[:], pbk[2][:], a0bc[:], ALU.add, reads=[pbn[2], "a0bc"], writes=["av"])
            self.act(av[:], av[:], AF.Sigmoid, reads=["av"], writes=["av"])
            yield
            if d == 0:
                self.tt("dve", kk[:], k_, kkbc[:], ALU.mult, reads=[pn_, "kkbc"], writes=["kk"])
                self.tt("pool", tmp[:], kk[:], kk[:], ALU.mult, reads=["kk"], writes=["tmp"])
                self.red(ssq[:], v3(tmp[:]), reads=["tmp"], writes=["ssq"])
                self.act(ssq[:], ssq[:], AF.Sqrt, reads=["ssq"], writes=["ssq"])
                self.ts(ssq[:], ssq[:], 1e-12, None, ALU.max, ALU.bypass, reads=["ssq"], writes=["ssq"])
                self.recip(ssq[:], ssq[:], reads=["ssq"], writes=["ssq"])
                self.tt("dve", v3(kk[:]), v3(kk[:]), ssq[:].unsqueeze(2).broadcast_to([128, 8, 64]), ALU.mult,
                        reads=["kk", "ssq"], writes=["kk"])
                self.store(self.KKd[t0:t0 + 128, :], kk[:], reads=["kk"], writes=[("KK", i)])
            self.stt(tmp[:], av[:], -1.0, kabc[:], ALU.add, ALU.mult, reads=["av", "kabc"], writes=["tmp"])
            self.stt(kd[:], tmp[:], 1.0, k_, ALU.add, ALU.mult, reads=["tmp", pn_], writes=["kd"])
            self.tt("pool", ka[:], kk[:], av[:], ALU.mult, reads=["kk", "av"], writes=["ka"])
            yield
            self.mm(pbk[0][:], tri[:], sgm[:], True, True, reads=["tri", "sgm"], writes=[pbn[0]])
            self.mm(pbk[1][:], ones[:], sgm[:], True, True, reads=["ones", "sgm"], writes=[pbn[1]])
            for hh in range(8):
                self.mm(pbk[2][0:64, hh * 2:hh * 2 + 2], sgm[:, hh * 64:(hh + 1) * 64], ones[:, 0:2], True, True,
                        reads=["sgm", "ones"], writes=[pbn[2]])
            self.act(gC[pb][:], pbk[2][0:64, 0:16].rearrange("p (h t) -> p h t", t=2)[:, :, 0], AF.Exp,
                     reads=[pbn[2]], writes=["gC%d" % pb], scale=-CDEC)
            self.cp("act", Ls[:], pbk[0][:], reads=[pbn[0]], writes=["Ls"])
            self.tt("dve", Ld[:], pbk[1][:], Ls[:], ALU.subtract, reads=[pbn[1], "Ls"], writes=["Ld"])
            self.act(E2[:], pbk[0][:], AF.Exp, reads=[pbn[0]], writes=["E2"], scale=-CDEC)
            self.act(E3[:], pbk[0][:], AF.Exp, reads=[pbn[0]], writes=["E3"], scale=CDEC)
            yield
            self.tt("dve", Ls[:], Ls[:], sgm[:], ALU.subtract, reads=["Ls", "sgm"], writes=["Ls"])
            self.act(E1[:], Ls[:], AF.Exp, reads=["Ls"], writes=["E1"], scale=-CDEC)
            self.act(E4[:], Ld[:], AF.Exp, reads=["Ld"], writes=["E4"], scale=-CDEC)
            self.tt("dve", Rb[pb][:], r_, E2[:], ALU.mult, reads=[pn_, "E2"], writes=["Rb%d" % pb])
            self.tt("pool", Bb[pb][:], ka[:], E3[:], ALU.mult, reads=["ka", "E3"], writes=["Bb%d" % pb])
            self.tt("pool", Kb[pb][:], kd[:], E3[:], ALU.mult, reads=["kd", "E3"], writes=["Kb%d" % pb])
            self.stt(Ab[pb][:], kk[:], -1.0, E1[:], ALU.mult, ALU.mult, reads=["kk", "E1"], writes=["Ab%d" % pb])
            yield
            self.tt("pool", Btb[pb][:], ka[:], E4[:], ALU.mult, reads=["ka", "E4"], writes=["Btb%d" % pb])
            self.tt("dve", Ktb[pb][:], kd[:], E4[:], ALU.mult, reads=["kd", "E4"], writes=["Ktb%d" % pb])
            self.cp("pool", Vb[pb][:], v_, reads=[pn_], writes=["Vb%d" % pb])
            for hh in range(8):
                hs = slice(hh * 64, (hh + 1) * 64)
                ts_ = slice(hh * 128, (hh + 1) * 128)
                self.tr(pbb[0][0:64, ts_], Ab[pb][:, hs], idb[:], reads=["Ab%d" % pb, "idb"], writes=[pbn[0]])
                self.tr(pbb[1][0:64, ts_], Rb[pb][:, hs], idb[:], reads=["Rb%d" % pb, "idb"], writes=[pbn[1]])
                self.tr(pbb[2][0:64, ts_], Bb[pb][:, hs], idb[:], reads=["Bb%d" % pb, "idb"], writes=[pbn[2]])
            self.cp("act", ART[pb][0:64, :, 0:128], pbb[0][0:64, :].rearrange("p (h t) -> p h t", h=8),
                    reads=[pbn[0]], writes=["ART%d" % pb])
            self.cp("dve", ART[pb][0:64, :, 128:256], pbb[1][0:64, :].rearrange("p (h t) -> p h t", h=8),
                    reads=[pbn[1]], writes=["ART%d" % pb])
            self.cp("act", BT[pb][:], pbb[2][0:64, :].rearrange("p (h t) -> p h t", h=8), reads=[pbn[2]], writes=["BT%d" % pb])
            yield
            for hh in range(8):
                hs = slice(hh * 64, (hh + 1) * 64)
                ts_ = slice(hh * 128, (hh + 1) * 128)
                self.tr(pbb[0][0:64, ts_], Kb[pb][:, hs], idb[:], reads=["Kb%d" % pb, "idb"], writes=[pbn[0]])
            self.cp("dve", KTt[pb][:], pbb[0][0:64, :].rearrange("p (h t) -> p h t", h=8), reads=[pbn[0]], writes=["KTt%d" % pb])
            yield

        def solve(i, pb):
            t0 = i * 128
            pc = pac[pb]
            pn_ = "pac%d" % pb
            art, bt, ktt = ART[pb], BT[pb], KTt[pb]
            an, bn_, kn = "ART%d" % pb, "BT%d" % pb, "KTt%d" % pb
            vb, vn = Vb[pb], "Vb%d" % pb
            if d == 1:
                self.load(oft[:], self.of[t0:t0 + 128, :], writes=["oft"])
            for hh in range(8):
                qq, qn = sbk[hh % 3], sbn[hh % 3]
                self.mm(qq[:, 0:256], bt[:, hh, :], art[0:64, hh, :], True, True, reads=[bn_, an], writes=[qn])
                self.mm(qq[:, 256:512], ktt[:, hh, :], art[0:64, hh, :], True, True, reads=[kn, an], writes=[qn])
                self.tt("dve", ATall[:, hh, :], qq[:], m4[:], ALU.mult, reads=[qn, "m4"], writes=[("AT", hh)])
                if hh == 3:
                    yield
            yield
            for g in range(2):
                for j in range(4):
                    hh = g * 4 + j
                    self.mm(s3[:, j * 128:(j + 1) * 128], art[0:64, hh, 0:128], bt[:, hh, :], True, True,
                            reads=[an, bn_], writes=["sq3"])
                self.tt("dve", PP[0][:, g * 4:(g + 1) * 4, 0:128], s3[:].rearrange("p (j s) -> p j s", j=4),
                        mn4[:].rearrange("p (j s) -> p j s", j=4), ALU.mult, reads=["sq3", "mn4"],
                        writes=[("PP", 0, 2 * g), ("PP", 0, 2 * g + 1)])
            self.cp("pool", PP[0][:, :, 128:256], ATall[:, :, 0:128], reads=[("AT", hh) for hh in range(8)],
                    writes=[("PP", 0, pr) for pr in range(4)])
            for hh in range(8):
                hs = slice(hh * 64, (hh + 1) * 64)
                self.mm(s4[:, hs], art[:, hh, 0:128], STb[:, hh, :], True, False, reads=[an, "STb"], writes=["sq4"])
                self.mm(s4[:, hs], ATall[:, hh, 256:384], vb[:, hs], False, True, reads=[("AT", hh), vn], writes=["sq4"])
            self.cp("act", Wb[:], s4[:], reads=["sq4"], writes=["Wb"])
            yield
            rot = 0
            for j in range(7):
                cb = j % 2
                cur = PP[cb]
                for hh in range(8):
                    hs = slice(hh * 64, (hh + 1) * 64)
                    self.mm(s3[:, hs], cur[:, hh, 128:256], Wb[:, hs], True, False,
                            reads=[("PP", cb, hh // 2), "Wb"], writes=["sq3"])
                    self.mm(s3[:, hs], idb[:], Wb[:, hs], False, True, reads=["idb", "Wb"], writes=["sq3"])
                self.cp("act", Wb[:], s3[:], reads=["sq3"], writes=["Wb"])
                if j < 6:
                    nxt = PP[1 - cb]
                    for pr in range(4):
                        bankt, bname = sbk[rot % 3], sbn[rot % 3]
                        rot += 1
                        for u in range(2):
                            hh = pr * 2 + u
                            self.mm(bankt[:, u * 256:u * 256 + 128], cur[:, hh, 128:256], cur[:, hh, 0:128], True, True,
                                    reads=[("PP", cb, pr)], writes=[bname])
                            self.mm(bankt[:, u * 256 + 128:u * 256 + 256], cur[:, hh, 0:128], cur[:, hh, 128:256], True, True,
                                    reads=[("PP", cb, pr)], writes=[bname])
                        self.cp("dve" if pr == 0 else "act",
                                nxt[:, pr * 2:pr * 2 + 2, :].rearrange("p a b -> p (a b)"), bankt[:],
                                reads=[bname], writes=[("PP", 1 - cb, pr)])
                yield
            for hh in range(8):
                hs = slice(hh * 64, (hh + 1) * 64)
                self.mm(s4[:, hs], art[:, hh, 128:256], STb[:, hh, :], True, False, reads=[an, "STb"], writes=["sq4"])
                self.mm(s4[:, hs], ATall[:, hh, 128:256], Wb[:, hs], False, False, reads=[("AT", hh), "Wb"], writes=["sq4"])
                self.mm(s4[:, hs], ATall[:, hh, 384:512], vb[:, hs], False, True, reads=[("AT", hh), vn], writes=["sq4"])
            self.cp("act", osb[:], s4[:], reads=["sq4"], writes=["osb"])
            for hh in range(8):
                hs = slice(hh * 64, (hh + 1) * 64)
                self.mm(s3[0:64, hs], Btb[pb][:, hs], Wb[:, hs], True, False, reads=["Btb%d" % pb, "Wb"], writes=["sq3"])
                self.mm(s3[0:64, hs], Ktb[pb][:, hs], vb[:, hs], False, True, reads=["Ktb%d" % pb, vn], writes=["sq3"])
            self.tt("dve", ST32[:], ST32[:], gC[pb][:].unsqueeze(2).broadcast_to([64, 8, 64]), ALU.mult,
                    reads=["ST32", "gC%d" % pb], writes=["ST32"])
            self.tt("dve", ST32[:], ST32[:], s3[0:64, :].rearrange("p (h e) -> p h e", h=8), ALU.add,
                    reads=["ST32", "sq3"], writes=["ST32"])
            self.cp("act", STb[0:64, :, :], ST32[:], reads=["ST32"], writes=["STb"])
            yield
            if d == 0:
                self.store(self.of[t0:t0 + 128, :], osb[:], reads=["osb"], writes=[("of", i)])
                return
            r_ = pc[:, 0:512]
            k_ = pc[:, 512:1024]
            lg_ = pc[:, 1792:1952]
            self.tt("dve", oft[:], oft[:], osb[:], ALU.add, reads=["oft", "osb"], writes=["oft"])
            self.red(st8[:], v3(oft[:]), reads=["oft"], writes=["st8"])
            self.ts(st8[:], st8[:], 1.0 / 64, None, ALU.mult, ALU.bypass, reads=["st8"], writes=["st8"])
            self.tt("dve", v3(cen[:]), v3(oft[:]), st8[:].unsqueeze(2).broadcast_to([128, 8, 64]), ALU.subtract,
                    reads=["oft", "st8"], writes=["cen"])
            self.tt("pool", sq2[:], cen[:], cen[:], ALU.mult, reads=["cen"], writes=["sq2"])
            self.red(sv8[:], v3(sq2[:]), reads=["sq2"], writes=["sv8"])
            self.ts(sv8[:], sv8[:], 1.0 / 64, GN_EPS, ALU.mult, ALU.add, reads=["sv8"], writes=["sv8"])
            self.act(sv8[:], sv8[:], AF.Sqrt, reads=["sv8"], writes=["sv8"])
            self.recip(sv8[:], sv8[:], reads=["sv8"], writes=["sv8"])
            yield
            self.tt("dve", v3(cen[:]), v3(cen[:]), sv8[:].unsqueeze(2).broadcast_to([128, 8, 64]), ALU.mult,
                    reads=["cen", "sv8"], writes=["cen"])
            self.tt("pool", cen[:], cen[:], gngbc[:], ALU.mult, reads=["cen", "gngbc"], writes=["cen"])
            self.tt("pool", cen[:], cen[:], gnbbc[:], ALU.add, reads=["cen", "gnbbc"], writes=["cen"])
            self.tt("dve", sq2[:], r_, k_, ALU.mult, reads=[pn_], writes=["sq2"])
            self.tt("pool", sq2[:], sq2[:], rkbc[:], ALU.mult, reads=["sq2", "rkbc"], writes=["sq2"])
            self.red(sb8[:], v3(sq2[:]), reads=["sq2"], writes=["sb8"])
            self.tt("dve", v3(bon[:]), v3(pc[:, 1024:1536]),
                    sb8[:].unsqueeze(2).broadcast_to([128, 8, 64]), ALU.mult, reads=[pn_, "sb8"], writes=["bon"])
            self.tt("dve", cen[:], cen[:], bon[:], ALU.add, reads=["cen", "bon"], writes=["cen"])
            self.act(gs[:], lg_, AF.Sigmoid, reads=[pn_], writes=["gs"])
            self.tr(s3b[:, 0:128], gs[:, 0:128], idb[:], reads=["gs", "idb"], writes=["sq3"])
            self.tr(s3b[0:32, 128:256], gs[:, 128:160], idb[:], reads=["gs", "idb"], writes=["sq3"])
            self.cp("act", gT[:, 0, :], s3b[:, 0:128], reads=["sq3"], writes=["gT"])
            self.cp("act", gT[0:32, 1, :], s3b[0:32, 128:256], reads=["sq3"], writes=["gT"])
            self.mm(s4[:], gT[:, 0, :], g2b[:, 0, :], True, False, reads=["gT", "g2b"], writes=["sq4"])
            self.mm(s4[:], gT[:, 1, :], g2b[:, 1, :], False, True, reads=["gT", "g2b"], writes=["sq4"])
            self.tt("dve", yat[:], cen[:], s4[:], ALU.mult, reads=["cen", "sq4"], writes=["yat"])
            self.store(self.ya[t0:t0 + 128, :], yat[:], reads=["yat"], writes=[("ya", i)])
            yield

        order = list(range(NT)) if d == 0 else list(range(NT - 1, -1, -1))
        for n0 in range(min(2, NT)):
            for _ in prep(order[n0], n0 % 3):
                pass
        for n, i in enumerate(order):
            gs_ = solve(i, n % 3)
            gp_ = prep(order[n + 2], (n + 2) % 3) if n + 2 < NT else None
            while gs_ is not None or gp_ is not None:
                if gs_ is not None:
                    try:
                        next(gs_)
                    except StopIteration:
                        gs_ = None
                if gp_ is not None:
                    try:
                        next(gp_)
                    except StopIteration:
                        gp_ = None
        self.end_phase()

    def phase_MP(self, l):
        NT = self.NT
        W = self.w
        self.begin_phase()
        sb, ps = self.sb, self.ps
        qg = sb("qg", [128, 256], F32)
        kvg = sb("kvg", [128, 128], F32)
        wuq = sb("wuq", [128, 2, 768], BF16)
        wukv = sb("wukv", [128, 1, 1024], BF16)
        idb = sb("idb", [128, 128], BF16)
        pm2 = [sb("pm", [128, 416], F32) for _ in range(2)]
        cs2 = [sb("cs", [128, 32], F32) for _ in range(2)]
        sn2 = [sb("sn", [128, 32], F32) for _ in range(2)]
        junk = sb("junk", [128, 256], F32)
        ss = sb("ss", [128, 1], F32)
        rstd = sb("rstd", [128, 1], F32)
        nb = sb("nb", [128, 384], BF16)
        nT = sb("nT", [128, 3, 128], BF16)
        qf = sb("qf", [128, 768], F32)
        kvf = sb("kvf", [128, 1024], F32)
        t1 = sb("t1", [128, 8, 32], F32)
        t2 = sb("t2", [128, 8, 32], F32)
        kro = sb("kro", [128, 32], F32)
        kr2 = sb("kr2", [128, 32], F32)
        Qa = sb("Qa", [128, 8, 96], BF16)
        Ka = sb("Ka", [128, 8, 96], BF16)
        Va = sb("Va", [128, 8, 65], BF16)
        QTt = sb("QTt", [96, 8, 128], BF16)
        KTt = sb("KTt", [96, 8, 128], BF16)
        q = [ps("q%d" % i, [128, 512], F32) for i in range(7)]
        qb = [t[:].bitcast(BF16) for t in q]
        self.load(idb[:], self.c_ident, writes=["idb"], cast=True)
        self.bcast_load(qg, W["q_norm_g"][l:l + 1, :], 256, "qg")
        self.bcast_load(kvg, W["kv_norm_g"][l:l + 1, :], 128, "kvg")
        self.load_w_bf16(wuq, W["w_uq"][l], 256, "wuq")
        self.load_w_bf16(wukv, W["w_ukv"][l], 128, "wukv")
        self.memset("dve", Va[:], 1.0, writes=["Va"])
        qf3 = qf[:].rearrange("p (h e) -> p h e", h=8)
        kvf3 = kvf[:].rearrange("p (h e) -> p h e", h=8)
        def loads(i):
            b = i % 2
            t0 = i * 128
            self.load(pm2[b][:], self.P[t0:t0 + 128, 4000:4416], writes=["pm%d" % b])
            self.load(cs2[b][:], self.c_cos[t0:t0 + 128, :], writes=["cs%d" % b])
            self.load(sn2[b][:], self.c_sin[t0:t0 + 128, :], writes=["sn%d" % b])

        loads(0)
        for i in range(NT):
            t0 = i * 128
            if i + 1 < NT:
                loads(i + 1)
            pm, cs, sn = pm2[i % 2], cs2[i % 2], sn2[i % 2]
            pmn, csn, snn = "pm%d" % (i % 2), "cs%d" % (i % 2), "sn%d" % (i % 2)
            self.rmsnorm(pm[:, 0:256], pmn, 256, qg[:], "qg", nb[:, 0:256], "nbq", junk[:, 0:256], ss[:], rstd[:], "M")
            self.rmsnorm(pm[:, 256:384], pmn, 128, kvg[:], "kvg", nb[:, 256:384], "nbk", junk[:, 0:128], ss[:], rstd[:], "M")
            for c in range(3):
                self.tr(qb[0][:, c * 128:(c + 1) * 128], nb[:, c * 128:(c + 1) * 128], idb[:],
                        reads=["nbq", "nbk", "idb"], writes=["q0"])
            self.cp("act", nT[:].rearrange("p a b -> p (a b)"), qb[0][:, 0:384], reads=["q0"], writes=["nT"])
            for c in range(2):
                self.mm(q[1][:], nT[:, c, :], wuq[:, c, 0:512], c == 0, c == 1, reads=["nT", "wuq"], writes=["q1"])
            for c in range(2):
                self.mm(q[2][:, 0:256], nT[:, c, :], wuq[:, c, 512:768], c == 0, c == 1, reads=["nT", "wuq"], writes=["q2"])
            self.mm(q[3][:], nT[:, 2, :], wukv[:, 0, 0:512], True, True, reads=["nT", "wukv"], writes=["q3"])
            self.mm(q[4][:], nT[:, 2, :], wukv[:, 0, 512:1024], True, True, reads=["nT", "wukv"], writes=["q4"])
            self.cp("act", qf[:, 0:512], q[1][:], reads=["q1"], writes=["qf"])
            self.cp("dve", qf[:, 512:768], q[2][:, 0:256], reads=["q2"], writes=["qf"])
            self.cp("act", kvf[:, 0:512], q[3][:], reads=["q3"], writes=["kvf"])
            self.cp("dve", kvf[:, 512:1024], q[4][:], reads=["q4"], writes=["kvf"])
            self.cp("pool", Qa[:, :, 0:64], qf3[:, :, 0:64], reads=["qf"], writes=["Qa"])
            csb = cs[:].unsqueeze(1).broadcast_to([128, 8, 32])
            self.tt("dve", t1[:], qf3[:, :, 64:96], csb, ALU.mult, reads=["qf", csn], writes=["t1"])
            self.tt("dve", t2[:, :, 0:16], qf3[:, :, 80:96], sn[:, 0:16].unsqueeze(1).broadcast_to([128, 8, 16]), ALU.mult,
                    reads=["qf", snn], writes=["t2"])
            self.tt("dve", t2[:, :, 16:32], qf3[:, :, 64:80], sn[:, 16:32].unsqueeze(1).broadcast_to([128, 8, 16]), ALU.mult,
                    reads=["qf", snn], writes=["t2"])
            self.tt("dve", Qa[:, :, 64:96], t1[:], t2[:], ALU.add, reads=["t1", "t2"], writes=["Qa"])
            self.tt("dve", kro[:], pm[:, 384:416], cs[:], ALU.mult, reads=[pmn, csn], writes=["kro"])
            self.tt("dve", kr2[:, 0:16], pm[:, 400:416], sn[:, 0:16], ALU.mult, reads=[pmn, snn], writes=["kr2"])
            self.tt("dve", kr2[:, 16:32], pm[:, 384:400], sn[:, 16:32], ALU.mult, reads=[pmn, snn], writes=["kr2"])
            self.tt("dve", kro[:], kro[:], kr2[:], ALU.add, reads=["kro", "kr2"], writes=["kro"])
            self.cp("dve", Ka[:, :, 64:96], kro[:].unsqueeze(1).broadcast_to([128, 8, 32]), reads=["kro"], writes=["Ka"])
            self.cp("pool", Ka[:, :, 0:64], kvf3[:, :, 0:64], reads=["kvf"], writes=["Ka"])
            self.cp("pool", Va[:, :, 0:64], kvf3[:, :, 64:128], reads=["kvf"], writes=["Va"])
            for hh in range(8):
                self.tr(qb[5][0:96, hh * 128:(hh + 1) * 128], Qa[:, hh, :], idb[:], reads=["Qa", "idb"], writes=["q5"])
                self.tr(qb[6][0:96, hh * 128:(hh + 1) * 128], Ka[:, hh, :], idb[:], reads=["Ka", "idb"], writes=["q6"])
            self.cp("act", QTt[:].rearrange("p a b -> p (a b)"), qb[5][0:96, :], reads=["q5"], writes=["QTt"])
            self.cp("dve", KTt[:].rearrange("p a b -> p (a b)"), qb[6][0:96, :], reads=["q6"], writes=["KTt"])
            self.store(self.QT[:, :, t0:t0 + 128].rearrange("h p t -> p h t"), QTt[:], reads=["QTt"], writes=[("QT", i)])
            self.store(self.KT[:, :, t0:t0 + 128].rearrange("h p t -> p h t"), KTt[:], reads=["KTt"], writes=[("KT", i)])
            self.store(self.Vd[t0:t0 + 128, :], Va[:].rearrange("p a b -> p (a b)"), reads=["Va"], writes=[("Vd", i)])
        self.end_phase()

    def phase_MM(self):
        NT = self.NT
        S_LEN = self.S_LEN
        QB = min(512, S_LEN)
        nqb = S_LEN // QB
        nj = QB // 128
        LOOK = 2
        self.begin_phase()
        sb, ps = self.sb, self.ps
        Vall = sb("Vall", [128, NT, 520], BF16)
        KTh = [sb("KTh", [96, S_LEN], BF16) for _ in range(2)]
        QTb = [sb("QTb", [96, QB], BF16) for _ in range(2)]
        PT = [sb("PT", [128, QB], BF16) for _ in range(4)]
        OT = sb("OT", [65, QB], F32)
        id32 = sb("id32", [128, 128], F32)
        osm = sb("osm", [128, nj, 64], F32)
        rec = sb("rec", [128, nj], F32)
        q = [ps("q%d" % i, [128, 512], F32) for i in range(7)]
        self.load(id32[:], self.c_ident, writes=["id32"])
        self.load(Vall[:], self.Vd.rearrange("(c p) f -> p c f", p=128), writes=["Vall"])
        blocks = [(hh, qi) for hh in range(8) for qi in range(nqb)]
        stream = [(bi, kc) for bi in range(len(blocks)) for kc in range(NT)]

        def load_k(hh):
            self.load(KTh[hh % 2][:], self.KT[hh], writes=["KTh%d" % (hh % 2)])

        def load_q(bi):
            hh, qi = blocks[bi]
            self.load(QTb[bi % 2][:], self.QT[hh, :, qi * QB:(qi + 1) * QB], writes=["QTb%d" % (bi % 2)])

        def emit_S(idx):
            bi, kc = stream[idx]
            hh, qi = blocks[bi]
            pb = idx % 4
            self.mm(q[pb][:, 0:QB], KTh[hh % 2][:, kc * 128:(kc + 1) * 128], QTb[bi % 2][:], True, True,
                    reads=["KTh%d" % (hh % 2), "QTb%d" % (bi % 2)], writes=["q%d" % pb])

        def epilogue_a(bi):
            ob = 4 + bi % 2
            self.cp("dve", OT[:], q[ob][0:65, 0:QB], reads=["q%d" % ob], writes=["OT"])

        def epilogue_b(bi):
            hh, qi = blocks[bi]
            for j in range(nj):
                self.tr(q[6][:, j * 65:(j + 1) * 65], OT[:, j * 128:(j + 1) * 128], id32[0:65, 0:65],
                        reads=["OT", "id32"], writes=["q6"])
            o3 = q[6][:, 0:nj * 65].rearrange("p (j e) -> p j e", j=nj)
            self.recip(rec[:], o3[:, :, 64], reads=["q6"], writes=["rec"])
            self.tt("dve", osm[:], o3[:, :, 0:64], rec[:].unsqueeze(2).broadcast_to([128, nj, 64]), ALU.mult,
                    reads=["q6", "rec"], writes=["osm"])
            self.store(self.yb[qi * QB:(qi + 1) * QB, hh * 64:(hh + 1) * 64].rearrange("(j p) e -> p j e", p=128),
                       osm[:], reads=["osm"], writes=[("yb", hh, qi)])

        load_k(0)
        load_q(0)
        if len(blocks) > 1:
            load_q(1)
        for idx in range(min(LOOK, len(stream))):
            emit_S(idx)
        pending = None
        for idx, (bi, kc) in enumerate(stream):
            hh, qi = blocks[bi]
            if kc == 0:
                if qi == 0 and hh + 1 < 8:
                    load_k(hh + 1)
            if idx + LOOK < len(stream):
                emit_S(idx + LOOK)
            pb = idx % 4
            ob = 4 + bi % 2
            self.act(PT[pb][:], q[pb][:, 0:QB], AF.Exp, reads=["q%d" % pb], writes=["PT%d" % pb], scale=SCALE)
            self.mm(q[ob][0:65, 0:QB], Vall[:, kc, hh * 65:(hh + 1) * 65], PT[pb][:], kc == 0, kc == NT - 1,
                    reads=["Vall", "PT%d" % pb], writes=["q%d" % ob])
            if pending is not None and kc == min(3, NT - 1):
                epilogue_b(pending)
                pending = None
            if kc == NT - 1:
                epilogue_a(bi)
                pending = bi
                if bi + 2 < len(blocks):
                    load_q(bi + 2)
        if pending is not None:
            epilogue_b(pending)
        self.end_phase()

    def phase_C1(self, l, xin):
        NT = self.NT
        W = self.w
        self.begin_phase()
        sb, ps = self.sb, self.ps
        woa = sb("woa", [128, 4, D], BF16)
        wob = sb("wob", [128, 4, D], BF16)
        wout = sb("wout", [128, 8, D], BF16)
        idb = sb("idb", [128, 128], BF16)
        xt = [sb("xt", [128, D], F32) for _ in range(4)]
        gt = [sb("gt", [128, 2048], F32) for _ in range(2)]
        yat = [sb("yat", [128, 512], F32) for _ in range(2)]
        ybt = [sb("ybt", [128, 512], F32) for _ in range(2)]
        yab = [sb("yab", [128, D], BF16) for _ in range(2)]
        yT = [sb("yT", [128, 8, 128], BF16) for _ in range(2)]
        m1 = sb("m1", [128, D], F32)
        m2 = sb("m2", [128, D], F32)
        mixb = [sb("mixb", [128, D], BF16) for _ in range(2)]
        mixT = sb("mixT", [128, 8, 128], BF16)
        x1t = sb("x1t", [128, D], F32)
        q = [ps("q%d" % i, [128, 512], F32) for i in range(8)]
        q6b = q[6][:].bitcast(BF16)
        q7b = q[7][:].bitcast(BF16)
        self.load(idb[:], self.c_ident, writes=["idb"], cast=True)
        self.load_w_bf16(woa, W["w_oa"][l], 512, "woa")
        self.load_w_bf16(wob, W["w_ob"][l], 512, "wob")
        self.load_w_bf16(wout, W["w_out"][l], D, "wout")

        def loads(i):
            b = i % 2
            t0 = i * 128
            self.load(xt[i % 4][:], xin[t0:t0 + 128, :], writes=["xt%d" % (i % 4)])
            self.load(gt[b][:], self.P[t0:t0 + 128, 0:2048], writes=["gt%d" % b])
            self.load(yat[b][:], self.ya[t0:t0 + 128, :], writes=["yat%d" % b])
            self.load(ybt[b][:], self.yb[t0:t0 + 128, :], writes=["ybt%d" % b])

        def s1(i):
            b = i % 2
            self.cp("dve", yab[b][:, 0:512], yat[b][:], reads=["yat%d" % b], writes=[("yab", b, 0)])
            self.cp("pool", yab[b][:, 512:1024], ybt[b][:], reads=["ybt%d" % b], writes=[("yab", b, 1)])
            for k in range(8):
                self.tr(q6b[:, k * 128:(k + 1) * 128], yab[b][:, k * 128:(k + 1) * 128], idb[:],
                        reads=[("yab", b, 0), ("yab", b, 1), "idb"], writes=["q6"])
            self.cp("act", yT[b][:].rearrange("p a b -> p (a b)"), q6b[:], reads=["q6"], writes=["yT%d" % b])
            self.act(gt[b][:], gt[b][:], AF.Sigmoid, reads=["gt%d" % b], writes=["gt%d" % b])

        def s2(i):
            b = i % 2
            for hf in range(2):
                hs = slice(hf * 512, (hf + 1) * 512)
                for k in range(4):
                    self.mm(q[hf][:], yT[b][:, k, :], woa[:, k, hs], k == 0, k == 3, reads=["yT%d" % b, "woa"], writes=["q%d" % hf])
                for k in range(4):
                    self.mm(q[2 + hf][:], yT[b][:, 4 + k, :], wob[:, k, hs], k == 0, k == 3, reads=["yT%d" % b, "wob"],
                            writes=["q%d" % (2 + hf)])
            for hf in range(2):
                hs = slice(hf * 512, (hf + 1) * 512)
                hs2 = slice(1024 + hf * 512, 1024 + (hf + 1) * 512)
                self.tt("dve", m1[:, hs], gt[b][:, hs], q[hf][:], ALU.mult, reads=["gt%d" % b, "q%d" % hf], writes=[("m1", hf)])
                self.tt("dve", m2[:, hs], gt[b][:, hs2], q[2 + hf][:], ALU.mult, reads=["gt%d" % b, "q%d" % (2 + hf)],
                        writes=[("m2", hf)])
                self.tt("pool", mixb[b][:, hs], m1[:, hs], m2[:, hs], ALU.add, reads=[("m1", hf), ("m2", hf)],
                        writes=[("mixb", b, hf)])

        def s3(i):
            b = i % 2
            t0 = i * 128
            xb = i % 4
            for k in range(8):
                self.tr(q7b[:, k * 128:(k + 1) * 128], mixb[b][:, k * 128:(k + 1) * 128], idb[:],
                        reads=[("mixb", b, 0), ("mixb", b, 1), "idb"], writes=["q7"])
            self.cp("act", mixT[:].rearrange("p a b -> p (a b)"), q7b[:], reads=["q7"], writes=["mixT"])
            for hf in range(2):
                hs = slice(hf * 512, (hf + 1) * 512)
                for k in range(8):
                    self.mm(q[4 + hf][:], mixT[:, k, :], wout[:, k, hs], k == 0, k == 7, reads=["mixT", "wout"],
                            writes=["q%d" % (4 + hf)])
                self.tt("dve", x1t[:, hs], xt[xb][:, hs], q[4 + hf][:], ALU.add, reads=["xt%d" % xb, "q%d" % (4 + hf)],
                        writes=[("x1t", hf)])
            self.store(self.x1[t0:t0 + 128, :], x1t[:], reads=[("x1t", 0), ("x1t", 1)], writes=[("x1", i)])

        loads(0)
        if NT > 1:
            loads(1)
        s1(0)
        for i in range(NT + 1):
            if i + 1 < NT:
                s1(i + 1)
            if i < NT:
                s2(i)
            if i + 2 < NT:
                loads(i + 2)
            if i >= 1:
                s3(i - 1)
        self.end_phase()

    def phase_C2(self, l, last, yout):
        NT = self.NT
        W = self.w
        self.begin_phase()
        sb, ps = self.sb, self.ps
        wgu = sb("wgu", [128, 8, 2 * DFF], BF16)
        wdn = sb("wdn", [128, 22, D], BF16)
        gbc = sb("gbc", [128, D], F32)
        idb = sb("idb", [128, 128], BF16)
        xt = [sb("xt", [128, D], F32) for _ in range(2)]
        junk = sb("junk", [128, D], F32)
        ss = sb("ss", [128, 1], F32)
        rstd = sb("rstd", [128, 1], F32)
        h = sb("h", [128, D], BF16)
        hT = [sb("hT", [128, 8, 128], BF16) for _ in range(2)]
        sl = [sb("sl", [128, 256], F32) for _ in range(2)]
        actb = sb("actb", [128, DFF], BF16)
        actT = sb("actT", [128, 22, 128], BF16)
        x2t = sb("x2t", [128, D], F32)
        if last:
            fbc = sb("fbc", [128, D], F32)
        q = [ps("q%d" % i, [128, 512], F32) for i in range(6)]
        qTb = q[4][:].bitcast(BF16)
        qT2 = q[5][:].bitcast(BF16)
        self.load(idb[:], self.c_ident, writes=["idb"], cast=True)
        self.bcast_load(gbc, W["norm_ffn_g"][l:l + 1, :], D, "gbc")
        if last:
            self.bcast_load(fbc, W["final_norm_g"][0:1, :], D, "fbc")
        self.load_w_bf16(wgu, W["w_gu"][l], D, "wgu")
        self.load_w_bf16(wdn, W["w_down"][l], DFF, "wdn")

        def norm(i):
            b = i % 2
            self.load(xt[b][:], self.x1[i * 128:(i + 1) * 128, :], writes=["xt%d" % b])
            self.rmsnorm(xt[b][:], "xt%d" % b, D, gbc[:], "gbc", h[:], "h", junk[:], ss[:], rstd[:], "F")

        def trans(i):
            b = i % 2
            for k in range(8):
                self.tr(qTb[:, k * 128:(k + 1) * 128], h[:, k * 128:(k + 1) * 128], idb[:], reads=["h", "idb"], writes=["q4"])
            self.cp("act", hT[b][:].rearrange("p a b -> p (a b)"), qTb[:], reads=["q4"], writes=["hT%d" % b])

        def tpose(j):
            o = (j % 4) * 256
            for u in range(2):
                self.tr(qT2[:, o + u * 128:o + (u + 1) * 128], actb[:, j * 256 + u * 128:j * 256 + (u + 1) * 128], idb[:],
                        reads=[("actb", j), "idb"], writes=["q5"])
            self.cp("dve", actT[:, 2 * j:2 * j + 2, :].rearrange("p a b -> p (a b)"),
                    qT2[:, o:o + 256], reads=["q5"], writes=[("actT", j)])

        norm(0)
        trans(0)
        for i in range(NT):
            b = i % 2
            t0 = i * 128
            xn = "xt%d" % b
            hn = "hT%d" % b
            if i + 1 < NT:
                norm(i + 1)
            for j in range(11):
                bk = q[j % 2]
                bn = "q%d" % (j % 2)
                for k in range(8):
                    self.mm(bk[:, 0:256], hT[b][:, k, :], wgu[:, k, j * 256:(j + 1) * 256], k == 0, k == 7,
                            reads=[hn, "wgu"], writes=[bn])
                for k in range(8):
                    self.mm(bk[:, 256:512], hT[b][:, k, :], wgu[:, k, DFF + j * 256:DFF + (j + 1) * 256], k == 0, k == 7,
                            reads=[hn, "wgu"], writes=[bn])
                self.act(sl[j % 2][:], bk[:, 0:256], AF.Silu, reads=[bn], writes=["sl%d" % (j % 2)])
                self.tt("dve", actb[:, j * 256:(j + 1) * 256], sl[j % 2][:], bk[:, 256:512], ALU.mult,
                        reads=["sl%d" % (j % 2), bn], writes=[("actb", j)])
                if j >= 1:
                    tpose(j - 1)
            tpose(10)
            if i + 1 < NT:
                trans(i + 1)
            for hf in range(2):
                hs = slice(hf * 512, (hf + 1) * 512)
                for c in range(22):
                    self.mm(q[2 + hf][:], actT[:, c, :], wdn[:, c, hs], c == 0, c == 21,
                            reads=[("actT", c // 2), "wdn"], writes=["q%d" % (2 + hf)])
                self.tt("dve", x2t[:, hs], xt[b][:, hs], q[2 + hf][:], ALU.add, reads=[xn, "q%d" % (2 + hf)],
                        writes=["x2t"])
            if last:
                self.rmsnorm(x2t[:], "x2t", D, fbc[:], "fbc", x2t[:], "x2t", junk[:], ss[:], rstd[:], "F")
                self.store(yout[t0:t0 + 128, :], x2t[:], reads=["x2t"], writes=[("y", i)])
            else:
                self.store(self.x2[t0:t0 + 128, :], x2t[:], reads=["x2t"], writes=[("x2", i)])
        self.end_phase()

    def build(self, phases=None):
        def on(p):
            return phases is None or p in phases
        for s in range(self.NSEQ):
            for l in range(self.depth):
                last = l == self.depth - 1
                xin = self.x[s] if l == 0 else self.x2
                if on("A"):
                    self.phase_A(l, xin)
                if on("R0"):
                    self.phase_R(l, 0)
                if on("R1"):
                    self.phase_R(l, 1)
                if on("MP"):
                    self.phase_MP(l)
                if on("MM"):
                    self.phase_MM()
                if on("C1"):
                    self.phase_C1(l, xin)
                if on("C2"):
                    self.phase_C2(l, last, self.y[s])
        self.S.emit()
        self.S.stack.close()
        return self.nc


def make_consts(S_LEN):
    s = np.arange(128)[:, None]
    t = np.arange(128)[None, :]
    tri = np.stack([(s <= t), (s >= t)]).astype(np.float32)
    strict = [(s < t).astype(np.float32), (s > t).astype(np.float32)]
    incl = [(s <= t).astype(np.float32), (s >= t).astype(np.float32)]
    m4 = np.stack([np.concatenate([strict[d], incl[d], strict[d], incl[d]], axis=1) for d in range(2)])
    mn = [(t < s).astype(np.float32), (t > s).astype(np.float32)]
    mn4 = np.stack([np.concatenate([mn[d]] * 4, axis=1) for d in range(2)])
    pos = np.arange(S_LEN, dtype=np.float32)
    inv_freq = (1.0 / (np.float32(10000.0) ** (np.arange(0, 32, 2, dtype=np.float32) / np.float32(32)))).astype(np.float32)
    ang = pos[:, None] * inv_freq[None, :]
    ang = np.concatenate([ang, ang], axis=-1).astype(np.float32)
    cos = np.cos(ang).astype(np.float32)
    sin = np.sin(ang).astype(np.float32)
    sin_s = sin.copy()
    sin_s[:, 0:16] = -sin_s[:, 0:16]
    return dict(c_ident=np.eye(128, dtype=np.float32), c_tri=tri, c_m4=m4.astype(np.float32),
                c_mn4=mn4.astype(np.float32), c_ones=np.ones((128, 128), np.float32),
                c_cos=cos, c_sin=sin_s)


_WNAMES = ["norm_mix_g", "w_in", "shift_mu", "decay_w2", "decay_w0", "iclr_a2", "iclr_a0", "gate_g2", "k_k", "k_a",
           "r_k", "gn_g", "gn_b", "w_oa", "q_norm_g", "w_uq", "kv_norm_g", "w_ukv", "w_ob", "w_out", "norm_ffn_g",
           "w_gu", "w_down", "final_norm_g"]


def prep_weights(inputs, depth):
    out = {}
    for n in _WNAMES:
        a = np.ascontiguousarray(np.asarray(inputs[n], dtype=np.float32))
        if n == "r_k":
            a = a.reshape(a.shape[0], 512)
        if n == "final_norm_g":
            a = a.reshape(1, D)
        else:
            a = a[:depth]
        out[n] = np.ascontiguousarray(a)
    return out


def kernel(**inputs):
    xp = np.asarray(inputs["x_prompt"], dtype=np.float32)
    xs = np.asarray(inputs["x_sample"], dtype=np.float32)
    S_LEN = xp.shape[1]
    x_all = np.concatenate([xp, xs], axis=0)
    nseq = x_all.shape[0] // NCORES
    wts = prep_weights(inputs, DEPTH)
    consts = make_consts(S_LEN)
    nc = Builder(S_LEN, nseq, DEPTH).build()
    in_maps = []
    for c in range(NCORES):
        m = dict(x=np.ascontiguousarray(x_all[c * nseq:(c + 1) * nseq]))
        m.update(wts)
        m.update(consts)
        in_maps.append(m)
    res = run_bass_kernel_spmd(nc, in_maps, core_ids=list(range(NCORES)))
    y = np.concatenate([r["y"] for r in res.results], axis=0)
    return (np.ascontiguousarray(y[:xp.shape[0]]), np.ascontiguousarray(y[xp.shape[0]:]))
```

```python
import contextlib
import os
import numpy as np
import concourse.bass as bass
import concourse.mybir as mybir
from concourse.alu_op_type import AluOpType as ALU
from concourse.bass_utils import run_bass_kernel_spmd

F32 = mybir.dt.float32
BF16 = mybir.dt.bfloat16
AF = mybir.ActivationFunctionType
AX = mybir.AxisListType

D = 1024
NIN = 4416
DFF = 2816
DEPTH = 2
NCORES = 8
SEQ_FULL = 4096
RMS_EPS = 1e-6
GN_EPS = 64e-5
CDEC = 0.6065306597126334
SCALE = 96.0 ** -0.5

ENGS = ("pe", "act", "dve", "pool", "sp")
N_DMA_SEMS = 8
SAME_ENGINE_SYNC = True


def _is_psum(r):
    n = r[0] if isinstance(r, tuple) else r
    return isinstance(n, str) and len(n) >= 2 and n[0] in "qp" and (n[1].isdigit() or n[1] in "TP")


class Sched:
    def __init__(self, nc):
        self.nc = nc
        self.q = {e: [] for e in ENGS}
        self.cnt = {e: 0 for e in ENGS}
        self.seen = {e: {} for e in ENGS}
        self.last_w = {}
        self.readers = {}
        self.dma_val = {}
        self.dma_rr = {e: 0 for e in ENGS}
        self.stack = contextlib.ExitStack()
        self.sems = {}
        self.nops = 0
        self.limit = int(os.environ.get("OPLIMIT", "1000000000"))
        self.marks = []

    def mark(self, label):
        self.marks.append((label, self.nops))

    def _deps(self, eng, reads, writes):
        deps = []
        for r in reads:
            ev = self.last_w.get(r)
            if ev is not None:
                deps.append(ev)
            if eng != "pe" and _is_psum(r):
                deps.extend(e2 for e2 in self.readers.get(r, ()) if e2[0] != eng)
        for w in writes:
            ev = self.last_w.get(w)
            if ev is not None:
                deps.append(ev)
            deps.extend(self.readers.get(w, ()))
        waits = {}
        seen = self.seen[eng]
        for sk, v in deps:
            if sk == eng and (eng == "pe" or not SAME_ENGINE_SYNC):
                continue
            if seen.get(sk, 0) >= v:
                continue
            if waits.get(sk, 0) < v:
                waits[sk] = v
        for sk, v in waits.items():
            seen[sk] = v
        return waits

    def _record(self, ev, reads, writes):
        for r in reads:
            self.readers.setdefault(r, []).append(ev)
        for w in writes:
            self.last_w[w] = ev
            self.readers[w] = []

    def op(self, eng, fn, reads=(), writes=()):
        self.nops += 1
        if self.nops > self.limit:
            return
        waits = self._deps(eng, reads, writes)
        self.cnt[eng] += 1
        ev = (eng, self.cnt[eng])
        self.q[eng].append((list(waits.items()), fn, (eng, 1)))
        self._record(ev, reads, writes)

    def dma(self, eng, fn, reads=(), writes=()):
        self.nops += 1
        if self.nops > self.limit:
            return
        k = self.dma_rr[eng]
        self.dma_rr[eng] = (k + 1) % N_DMA_SEMS
        sk = ("dma", eng, k)
        prev = self.dma_val.get(sk, 0)
        waits = self._deps(eng, reads, writes)
        if prev > 0 and self.seen[eng].get(sk, 0) < prev:
            waits[sk] = prev
            self.seen[eng][sk] = prev
        self.dma_val[sk] = prev + 16
        ev = (sk, prev + 16)
        self.q[eng].append((list(waits.items()), fn, (sk, 16)))
        self._record(ev, reads, writes)

    def barrier(self):
        tgt = {e: self.cnt[e] for e in ENGS if self.cnt[e] > 0}
        tgt.update(self.dma_val)
        for e in ENGS:
            waits = []
            for sk, v in tgt.items():
                if sk == e:
                    continue
                if self.seen[e].get(sk, 0) < v:
                    waits.append((sk, v))
                    self.seen[e][sk] = v
            if waits:
                self.q[e].append((waits, None, None))
        self.last_w = {}
        self.readers = {}

    def emit(self):
        nc = self.nc
        st = self.stack
        keys = list(ENGS) + list(self.dma_val)
        for sk in keys:
            nm = sk if isinstance(sk, str) else "d_%s_%d" % (sk[1], sk[2])
            self.sems[sk] = st.enter_context(nc.semaphore("s_" + nm))
        final = list(self.dma_val.items())
        block = st.enter_context(nc.Block())
        sems = self.sems

        def run(engname, final_waits=()):
            def body(e):
                for waits, fn, inc in self.q[engname]:
                    for sk, v in waits:
                        e.wait_ge(sems[sk], v)
                    if fn is not None:
                        fn(e).then_inc(sems[inc[0]], inc[1])
                for sk, v in final_waits:
                    e.wait_ge(sems[sk], v)
            return body

        block.tensor(run("pe"))
        block.scalar(run("act"))
        block.vector(run("dve"))
        block.gpsimd(run("pool"))
        block.sync(run("sp", final))


class Builder:
    def __init__(self, S_LEN, NSEQ, depth=DEPTH):
        self.S_LEN = S_LEN
        self.NSEQ = NSEQ
        self.depth = depth
        self.NT = S_LEN // 128
        nc = bass.Bass("TRN2", target_bir_lowering=False)
        self.nc = nc
        self.S = Sched(nc)
        self.ph = None
        self._uid = 0

        def inp(name, shape):
            return nc.dram_tensor(name, list(shape), F32, kind="ExternalInput").ap()

        L = depth
        self.x = inp("x", [NSEQ, S_LEN, D])
        self.w = dict(
            norm_mix_g=inp("norm_mix_g", [L, D]), w_in=inp("w_in", [L, D, NIN]),
            shift_mu=inp("shift_mu", [L, 2, 1952]), decay_w2=inp("decay_w2", [L, 2, 64, 512]),
            decay_w0=inp("decay_w0", [L, 2, 512]), iclr_a2=inp("iclr_a2", [L, 2, 64, 512]),
            iclr_a0=inp("iclr_a0", [L, 2, 512]), gate_g2=inp("gate_g2", [L, 160, 512]),
            k_k=inp("k_k", [L, 512]), k_a=inp("k_a", [L, 512]), r_k=inp("r_k", [L, 512]),
            gn_g=inp("gn_g", [L, 512]), gn_b=inp("gn_b", [L, 512]), w_oa=inp("w_oa", [L, 512, D]),
            q_norm_g=inp("q_norm_g", [L, 256]), w_uq=inp("w_uq", [L, 256, 768]),
            kv_norm_g=inp("kv_norm_g", [L, 128]), w_ukv=inp("w_ukv", [L, 128, 1024]),
            w_ob=inp("w_ob", [L, 512, D]), w_out=inp("w_out", [L, D, D]),
            norm_ffn_g=inp("norm_ffn_g", [L, D]), w_gu=inp("w_gu", [L, D, 2 * DFF]),
            w_down=inp("w_down", [L, DFF, D]), final_norm_g=inp("final_norm_g", [1, D]),
        )
        self.c_ident = inp("c_ident", [128, 128])
        self.c_tri = inp("c_tri", [2, 128, 128])
        self.c_m4 = inp("c_m4", [2, 128, 512])
        self.c_mn4 = inp("c_mn4", [2, 128, 512])
        self.c_ones = inp("c_ones", [128, 128])
        self.c_cos = inp("c_cos", [S_LEN, 32])
        self.c_sin = inp("c_sin", [S_LEN, 32])
        self.y = nc.dram_tensor("y", [NSEQ, S_LEN, D], F32, kind="ExternalOutput").ap()
        def scr(name, shape, dt=F32):
            return nc.dram_tensor(name, list(shape), dt).ap()
        self.P = scr("scr_P", [S_LEN, NIN])
        self.of = scr("scr_of", [S_LEN, 512])
        self.PSd = scr("scr_PS", [S_LEN, 1952])
        self.KKd = scr("scr_KK", [S_LEN, 512])
        self.ya = scr("scr_ya", [S_LEN, 512])
        self.yb = scr("scr_yb", [S_LEN, 512])
        self.x1 = scr("scr_x1", [S_LEN, D])
        self.x2 = scr("scr_x2", [S_LEN, D])
        self.QT = scr("scr_QT", [8, 96, S_LEN], BF16)
        self.KT = scr("scr_KT", [8, 96, S_LEN], BF16)
        self.Vd = scr("scr_V", [S_LEN, 8 * 65], BF16)

    def begin_phase(self):
        self.ph = contextlib.ExitStack()

    def end_phase(self):
        self.S.barrier()
        self.ph.close()
        self.ph = None

    def sb(self, name, shape, dt):
        self._uid += 1
        return self.ph.enter_context(self.nc.sbuf_tensor("%s_%d" % (name, self._uid), list(shape), dt))

    def ps(self, name, shape, dt):
        self._uid += 1
        return self.ph.enter_context(self.nc.psum_tensor("%s_%d" % (name, self._uid), list(shape), dt))

    def load(self, out_ap, in_ap, writes, reads=(), cast=False):
        eng = "pool" if cast else "sp"
        self.S.dma(eng, lambda e: e.dma_start(out=out_ap, in_=in_ap), reads=reads, writes=writes)

    def store(self, out_ap, in_ap, reads, writes):
        self.S.dma("sp", lambda e: e.dma_start(out=out_ap, in_=in_ap), reads=reads, writes=writes)

    def bcast_load(self, tile, row_ap, width, name):
        self.load(tile[:], row_ap.broadcast_to([128, width]), writes=[name])

    def load_w_bf16(self, tile, w_ap, K, name):
        for k in range(K // 128):
            self.load(tile[:, k, :], w_ap[k * 128:(k + 1) * 128, :], writes=[name], cast=True)

    def mm(self, out, lhsT, rhs, start, stop, reads, writes):
        self.S.op("pe", lambda e: e.matmul(out=out, lhsT=lhsT, rhs=rhs, start=start, stop=stop),
                  reads=reads, writes=writes)

    def tr(self, out, in_, ident, reads, writes):
        self.S.op("pe", lambda e: e.transpose(out=out, in_=in_, identity=ident), reads=reads, writes=writes)

    def act(self, out, in_, func, reads, writes, scale=None, bias=None, accum_out=None):
        kw = {}
        if scale is not None:
            kw["scale"] = scale
        if bias is not None:
            kw["bias"] = bias
        if accum_out is not None:
            kw["accum_out"] = accum_out
        self.S.op("act", lambda e: e.activation(out=out, in_=in_, func=func, **kw), reads=reads, writes=writes)

    def tt(self, eng, out, in0, in1, op, reads, writes):
        self.S.op(eng, lambda e: e.tensor_tensor(out=out, in0=in0, in1=in1, op=op), reads=reads, writes=writes)

    def ts(self, out, in0, s1, s2, op0, op1, reads, writes, eng="dve"):
        self.S.op(eng, lambda e: e.tensor_scalar(out=out, in0=in0, scalar1=s1, scalar2=s2, op0=op0, op1=op1),
                  reads=reads, writes=writes)

    def stt(self, out, in0, scalar, in1, op0, op1, reads, writes):
        self.S.op("dve", lambda e: e.scalar_tensor_tensor(out=out, in0=in0, scalar=scalar, in1=in1, op0=op0, op1=op1),
                  reads=reads, writes=writes)

    def cp(self, eng, out, in_, reads, writes):
        if eng == "act":
            self.S.op("act", lambda e: e.activation(out=out, in_=in_, func=AF.Copy), reads=reads, writes=writes)
        else:
            self.S.op(eng, lambda e: e.tensor_copy(out=out, in_=in_), reads=reads, writes=writes)

    def red(self, out, in_, reads, writes):
        self.S.op("dve", lambda e: e.tensor_reduce(out=out, in_=in_, axis=AX.X, op=ALU.add), reads=reads, writes=writes)

    def recip(self, out, in_, reads, writes):
        self.S.op("dve", lambda e: e.reciprocal(out=out, in_=in_), reads=reads, writes=writes)

    def memset(self, eng, ap, val, writes):
        self.S.op(eng, lambda e: e.memset(ap, val), writes=writes)

    def rmsnorm(self, x_ap, xn, width, gbc_ap, gn, out_ap, outn, junk, ss, rstd, tag):
        jn, sn, rn = "junk" + tag, "ss" + tag, "rstd" + tag
        self.act(junk, x_ap, AF.Square, reads=[xn], writes=[jn, sn], accum_out=ss)
        self.ts(rstd, ss, 1.0 / width, RMS_EPS, ALU.mult, ALU.add, reads=[sn], writes=[rn])
        self.act(rstd, rstd, AF.Sqrt, reads=[rn], writes=[rn])
        self.recip(rstd, rstd, reads=[rn], writes=[rn])
        self.stt(out_ap, x_ap, rstd, gbc_ap, ALU.mult, ALU.mult, reads=[xn, rn, gn], writes=[outn])

    def phase_A(self, l, xin):
        NT = self.NT
        self.begin_phase()
        wA = self.sb("wA", [128, 8, NIN], BF16)
        gbc = self.sb("gA", [128, D], F32)
        idb = self.sb("idb", [128, 128], BF16)
        junk = self.sb("junk", [128, D], F32)
        ss = self.sb("ss", [128, 1], F32)
        rstd = self.sb("rstd", [128, 1], F32)
        xt = [self.sb("xt", [128, D], F32) for _ in range(2)]
        h = [self.sb("h", [128, D], BF16) for _ in range(2)]
        hT = [self.sb("hT", [128, 8, 128], BF16) for _ in range(2)]
        Pt = [self.sb("Pt", [128, NIN], F32) for _ in range(2)]
        pT = [self.ps("pT", [128, 8, 128], BF16) for _ in range(2)]
        pP = [self.ps("pP", [128, 512], F32) for _ in range(4)]
        self.load(idb[:], self.c_ident, writes=["idb"], cast=True)
        self.bcast_load(gbc, self.w["norm_mix_g"][l:l + 1, :], D, "gA")
        self.load_w_bf16(wA, self.w["w_in"][l], D, "wA")
        npieces = (NIN + 511) // 512

        def norm(i):
            b = i % 2
            self.load(xt[b][:], xin[i * 128:(i + 1) * 128, :], writes=["xt%d" % b])
            self.rmsnorm(xt[b][:], "xt%d" % b, D, gbc[:], "gA", h[b][:], "h%d" % b, junk[:], ss[:], rstd[:], "A")

        def trans(i):
            b = i % 2
            for k in range(8):
                self.tr(pT[b][:, k, :], h[b][:, k * 128:(k + 1) * 128], idb[:], reads=["h%d" % b, "idb"], writes=["pT%d" % b])
            self.cp("act", hT[b][:], pT[b][:], reads=["pT%d" % b], writes=["hT%d" % b])

        norm(0)
        trans(0)
        for i in range(NT):
            b = i % 2
            if i + 1 < NT:
                norm(i + 1)
            for j in range(npieces):
                n0 = j * 512
                n = min(512, NIN - n0)
                pp = pP[j % 4]
                pn = "pP%d" % (j % 4)
                for k in range(8):
                    self.mm(pp[:, 0:n], hT[b][:, k, :], wA[:, k, n0:n0 + n], k == 0, k == 7,
                            reads=["hT%d" % b, "wA"], writes=[pn])
                self.cp("dve" if j % 2 == 0 else "act", Pt[b][:, n0:n0 + n], pp[:, 0:n],
                        reads=[pn], writes=[("Pt", b, j)])
                if j == 5 and i + 1 < NT:
                    trans(i + 1)
            self.store(self.P[i * 128:(i + 1) * 128, :], Pt[b][:], reads=[("Pt", b, j) for j in range(npieces)],
                       writes=[("P", i)])
        self.end_phase()

    def phase_R(self, l, d):
        NT = self.NT
        S_LEN = self.S_LEN
        W = self.w
        self.begin_phase()
        sb, ps = self.sb, self.ps
        if d == 0:
            mu0 = sb("mu0", [128, 1952], F32)
            mu1 = sb("mu1", [128, 1952], F32)
            c0 = sb("c0", [128, 1952], F32)
            pap = sb("pap", [128, 1952], F32)
            pan = sb("pan", [128, 1952], F32)
            kkbc = sb("kkbc", [128, 512], F32)
        w0bc = sb("w0bc", [128, 512], F32)
        a0bc = sb("a0bc", [128, 512], F32)
        kabc = sb("kabc", [128, 512], F32)
        w2b = sb("w2b", [64, 512], BF16)
        a2b = sb("a2b", [64, 512], BF16)
        tri = sb("tri", [128, 128], F32)
        ones = sb("ones", [128, 128], F32)
        m4 = sb("m4", [128, 512], F32)
        mn4 = sb("mn4", [128, 512], F32)
        idb = sb("idb", [128, 128], BF16)
        pac = [sb("pac", [128, 1952], F32) for _ in range(3)]
        lo = sb("lo", [128, 128], BF16)
        loT = sb("loT", [64, 2, 128], BF16)
        sgm = sb("sgm", [128, 512], F32)
        av = sb("av", [128, 512], F32)
        kk = sb("kk", [128, 512], F32)
        tmp = sb("tmp", [128, 512], F32)
        kd = sb("kd", [128, 512], F32)
        ka = sb("ka", [128, 512], F32)
        Ls = sb("Ls", [128, 512], F32)
        Ld = sb("Ld", [128, 512], F32)
        E1 = sb("E1", [128, 512], F32)
        E2 = sb("E2", [128, 512], F32)
        E3 = sb("E3", [128, 512], F32)
        E4 = sb("E4", [128, 512], F32)
        ssq = sb("ssq", [128, 8], F32)
        gC = [sb("gC", [64, 8], F32) for _ in range(3)]
        Ab = [sb("Ab", [128, 512], BF16) for _ in range(3)]
        Rb = [sb("Rb", [128, 512], BF16) for _ in range(3)]
        Bb = [sb("Bb", [128, 512], BF16) for _ in range(3)]
        Kb = [sb("Kb", [128, 512], BF16) for _ in range(3)]
        Btb = [sb("Btb", [128, 512], BF16) for _ in range(3)]
        Ktb = [sb("Ktb", [128, 512], BF16) for _ in range(3)]
        Vb = [sb("Vb", [128, 512], BF16) for _ in range(3)]
        ART = [sb("ART", [128, 8, 256], BF16) for _ in range(3)]
        BT = [sb("BT", [64, 8, 128], BF16) for _ in range(3)]
        KTt = [sb("KTt", [64, 8, 128], BF16) for _ in range(3)]
        ATall = sb("ATall", [128, 8, 512], BF16)
        PP = [sb("PP", [128, 8, 256], BF16) for _ in range(2)]
        Wb = sb("Wb", [128, 512], BF16)
        osb = sb("osb", [128, 512], F32)
        ST32 = sb("ST32", [64, 8, 64], F32)
        STb = sb("STb", [128, 8, 64], BF16)
        if d == 1:
            rkbc = sb("rkbc", [128, 512], F32)
            gngbc = sb("gngbc", [128, 512], F32)
            gnbbc = sb("gnbbc", [128, 512], F32)
            g2b = sb("g2b", [128, 2, 512], BF16)
            oft = sb("oft", [128, 512], F32)
            cen = sb("cen", [128, 512], F32)
            sq2 = sb("sq2", [128, 512], F32)
            bon = sb("bon", [128, 512], F32)
            st8 = sb("st8", [128, 8], F32)
            sv8 = sb("sv8", [128, 8], F32)
            sb8 = sb("sb8", [128, 8], F32)
            gs = sb("gs", [128, 160], BF16)
            gT = sb("gT", [128, 2, 128], BF16)
            yat = sb("yat", [128, 512], F32)
        pbk = [ps("pq%d" % i, [128, 512], F32) for i in range(3)]
        pbn = ["pq0", "pq1", "pq2"]
        pbb = [t[:].bitcast(BF16) for t in pbk]
        sbk = [ps("sq%d" % i, [128, 512], F32) for i in range(5)]
        sbn = ["sq0", "sq1", "sq2", "sq3", "sq4"]
        s3, s4 = sbk[3], sbk[4]
        s3b = s3[:].bitcast(BF16)

        self.load(idb[:], self.c_ident, writes=["idb"], cast=True)
        self.load(tri[:], self.c_tri[d], writes=["tri"])
        self.load(ones[:], self.c_ones, writes=["ones"])
        self.load(m4[:], self.c_m4[d], writes=["m4"])
        self.load(mn4[:], self.c_mn4[d], writes=["mn4"])
        if d == 0:
            self.bcast_load(mu0, W["shift_mu"][l, 0:1, :], 1952, "mu0")
            self.bcast_load(mu1, W["shift_mu"][l, 1:2, :], 1952, "mu1")
            self.bcast_load(kkbc, W["k_k"][l:l + 1, :], 512, "kkbc")
            self.tt("dve", c0[:], mu0[:], mu1[:], ALU.add, reads=["mu0", "mu1"], writes=["c0"])
            self.ts(c0[:], c0[:], -1.0, 1.0, ALU.mult, ALU.add, reads=["c0"], writes=["c0"])
        self.bcast_load(w0bc, W["decay_w0"][l, d:d + 1, :], 512, "w0bc")
        self.bcast_load(a0bc, W["iclr_a0"][l, d:d + 1, :], 512, "a0bc")
        self.bcast_load(kabc, W["k_a"][l:l + 1, :], 512, "kabc")
        self.load(w2b[:], W["decay_w2"][l, d], writes=["w2b"], cast=True)
        self.load(a2b[:], W["iclr_a2"][l, d], writes=["a2b"], cast=True)
        self.memset("dve", ST32[:], 0.0, writes=["ST32"])
        self.memset("dve", STb[:], 0.0, writes=["STb"])
        for b in range(3):
            self.memset("dve", ART[b][:], 0.0, writes=["ART%d" % b])
        if d == 1:
            self.bcast_load(rkbc, W["r_k"][l:l + 1, :], 512, "rkbc")
            self.bcast_load(gngbc, W["gn_g"][l:l + 1, :], 512, "gngbc")
            self.bcast_load(gnbbc, W["gn_b"][l:l + 1, :], 512, "gnbbc")
            self.memset("dve", gT[:], 0.0, writes=["gT"])
            self.memset("dve", g2b[:], 0.0, writes=["g2b"])
            self.load(g2b[:, 0, :], W["gate_g2"][l, 0:128, :], writes=["g2b"], cast=True)
            self.load(g2b[0:32, 1, :], W["gate_g2"][l, 128:160, :], writes=["g2b"], cast=True)

        def v3(ap):
            return ap.rearrange("p (h e) -> p h e", h=8)

        def prep(i, pb):
            t0 = i * 128
            pc = pac[pb]
            pn_ = "pac%d" % pb
            if d == 0:
                self.load(pc[:], self.P[t0:t0 + 128, 2048:4000], writes=[pn_])
                if i == 0:
                    self.memset("pool", pap[:], 0.0, writes=["pap"])
                    self.load(pap[1:128, :], self.P[0:127, 2048:4000], writes=["pap"])
                else:
                    self.load(pap[:], self.P[t0 - 1:t0 + 127, 2048:4000], writes=["pap"])
                if i == NT - 1:
                    self.memset("pool", pan[:], 0.0, writes=["pan"])
                    self.load(pan[0:127, :], self.P[t0 + 1:S_LEN, 2048:4000], writes=["pan"])
                else:
                    self.load(pan[:], self.P[t0 + 1:t0 + 129, 2048:4000], writes=["pan"])
                yield
                yield
                yield
                self.tt("pool", pap[:], pap[:], mu0[:], ALU.mult, reads=["pap", "mu0"], writes=["pap"])
                self.tt("dve", pan[:], pan[:], mu1[:], ALU.mult, reads=["pan", "mu1"], writes=["pan"])
                self.tt("pool", pc[:], pc[:], c0[:], ALU.mult, reads=[pn_, "c0"], writes=[pn_])
                yield
                self.tt("dve", pc[:], pc[:], pan[:], ALU.add, reads=[pn_, "pan"], writes=[pn_])
                self.tt("dve", pc[:], pc[:], pap[:], ALU.add, reads=[pn_, "pap"], writes=[pn_])
                self.store(self.PSd[t0:t0 + 128, :], pc[:], reads=[pn_], writes=[("PS", i)])
            else:
                self.load(pc[:], self.PSd[t0:t0 + 128, :], writes=[pn_])
                self.load(kk[:], self.KKd[t0:t0 + 128, :], writes=["kk"])
                yield
                yield
            yield
            r_ = pc[:, 0:512]
            k_ = pc[:, 512:1024]
            v_ = pc[:, 1024:1536]
            lw_ = pc[:, 1536 + 64 * d:1600 + 64 * d]
            la_ = pc[:, 1664 + 64 * d:1728 + 64 * d]
            self.act(lo[:, 0:64], lw_, AF.Tanh, reads=[pn_], writes=["lo"])
            self.cp("dve", lo[:, 64:128], la_, reads=[pn_], writes=["lo"])
            self.tr(pbb[0][0:64, 0:128], lo[:, 0:64], idb[:], reads=["lo", "idb"], writes=[pbn[0]])
            self.tr(pbb[0][0:64, 128:256], lo[:, 64:128], idb[:], reads=["lo", "idb"], writes=[pbn[0]])
            self.cp("act", loT[:].rearrange("p a b -> p (a b)"), pbb[0][0:64, 0:256], reads=[pbn[0]], writes=["loT"])
            self.mm(pbk[1][:], loT[:, 0, :], w2b[:], True, True, reads=["loT", "w2b"], writes=[pbn[1]])
            self.mm(pbk[2][:], loT[:, 1, :], a2b[:], True, True, reads=["loT", "a2b"], writes=[pbn[2]])
            self.tt("dve", sgm[:], pbk[1][:], w0bc[:], ALU.add, reads=[pbn[1], "w0bc"], writes=["sgm"])
            self.act(sgm[:], sgm[:], AF.Sigmoid, reads=["sgm"], writes=["sgm"])
            self.tt("dve", av[:], pbk[2][:], a0bc[:], ALU.add, reads=[pbn[2], "a0bc"], writes=["av"])
            self.act(av[:], av[:], AF.Sigmoid, reads=["av"], writes=["av"])
            yield
            if d == 0:
                self.tt("dve", kk[:], k_, kkbc[:], ALU.mult, reads=[pn_, "kkbc"], writes=["kk"])
                self.tt("pool", tmp[:], kk[:], kk[:], ALU.mult, reads=["kk"], writes=["tmp"])
                self.red(ssq[:], v3(tmp[:]), reads=["tmp"], writes=["ssq"])
                self.act(ssq[:], ssq[:], AF.Sqrt, reads=["ssq"], writes=["ssq"])
                self.ts(ssq[:], ssq[:], 1e-12, None, ALU.max, ALU.bypass, reads=["ssq"], writes=["ssq"])
                self.recip(ssq[:], ssq[:], reads=["ssq"], writes=["ssq"])
                self.tt("dve", v3(kk[:]), v3(kk[:]), ssq[:].unsqueeze(2).broadcast_to([128, 8, 64]), ALU.mult,
                        reads=["kk", "ssq"], writes=["kk"])
                self.store(self.KKd[t0:t0 + 128, :], kk[:], reads=["kk"], writes=[("KK", i)])
            self.stt(tmp[:], av[:], -1.0, kabc[:], ALU.add, ALU.mult, reads=["av", "kabc"], writes=["tmp"])
            self.stt(kd[:], tmp[:], 1.0, k_, ALU.add, ALU.mult, reads=["tmp", pn_], writes=["kd"])
            self.tt("pool", ka[:], kk[:], av[:], ALU.mult, reads=["kk", "av"], writes=["ka"])
            yield
            self.mm(pbk[0][:], tri[:], sgm[:], True, True, reads=["tri", "sgm"], writes=[pbn[0]])
            self.mm(pbk[1][:], ones[:], sgm[:], True, True, reads=["ones", "sgm"], writes=[pbn[1]])
            for hh in range(8):
                self.mm(pbk[2][0:64, hh * 2:hh * 2 + 2], sgm[:, hh * 64:(hh + 1) * 64], ones[:, 0:2], True, True,
                        reads=["sgm", "ones"], writes=[pbn[2]])
            self.act(gC[pb][:], pbk[2][0:64, 0:16].rearrange("p (h t) -> p h t", t=2)[:, :, 0], AF.Exp,
                     reads=[pbn[2]], writes=["gC%d" % pb], scale=-CDEC)
            self.cp("act", Ls[:], pbk[0][:], reads=[pbn[0]], writes=["Ls"])
            self.tt("dve", Ld[:], pbk[1][:], Ls[:], ALU.subtract, reads=[pbn[1], "Ls"], writes=["Ld"])
            self.act(E2[:], pbk[0][:], AF.Exp, reads=[pbn[0]], writes=["E2"], scale=-CDEC)
            self.act(E3[:], pbk[0][:], AF.Exp, reads=[pbn[0]], writes=["E3"], scale=CDEC)
            yield
            self.tt("dve", Ls[:], Ls[:], sgm[:], ALU.subtract, reads=["Ls", "sgm"], writes=["Ls"])
            self.act(E1[:], Ls[:], AF.Exp, reads=["Ls"], writes=["E1"], scale=-CDEC)
            self.act(E4[:], Ld[:], AF.Exp, reads=["Ld"], writes=["E4"], scale=-CDEC)
            self.tt("dve", Rb[pb][:], r_, E2[:], ALU.mult, reads=[pn_, "E2"], writes=["Rb%d" % pb])
            self.tt("pool", Bb[pb][:], ka[:], E3[:], ALU.mult, reads=["ka", "E3"], writes=["Bb%d" % pb])
            self.tt("pool", Kb[pb][:], kd[:], E3[:], ALU.mult, reads=["kd", "E3"], writes=["Kb%d" % pb])
            self.stt(Ab[pb][:], kk[:], -1.0, E1[:], ALU.mult, ALU.mult, reads=["kk", "E1"], writes=["Ab%d" % pb])
            yield
            self.tt("pool", Btb[pb][:], ka[:], E4[:], ALU.mult, reads=["ka", "E4"], writes=["Btb%d" % pb])
            self.tt("dve", Ktb[pb][:], kd[:], E4[:], ALU.mult, reads=["kd", "E4"], writes=["Ktb%d" % pb])
            self.cp("pool", Vb[pb][:], v_, reads=[pn_], writes=["Vb%d" % pb])
            for hh in range(8):
                hs = slice(hh * 64, (hh + 1) * 64)
                ts_ = slice(hh * 128, (hh + 1) * 128)
                self.tr(pbb[0][0:64, ts_], Ab[pb][:, hs], idb[:], reads=["Ab%d" % pb, "idb"], writes=[pbn[0]])
                self.tr(pbb[1][0:64, ts_], Rb[pb][:, hs], idb[:], reads=["Rb%d" % pb, "idb"], writes=[pbn[1]])
                self.tr(pbb[2][0:64, ts_], Bb[pb][:, hs], idb[:], reads=["Bb%d" % pb, "idb"], writes=[pbn[2]])
            self.cp("act", ART[pb][0:64, :, 0:128], pbb[0][0:64, :].rearrange("p (h t) -> p h t", h=8),
                    reads=[pbn[0]], writes=["ART%d" % pb])
            self.cp("dve", ART[pb][0:64, :, 128:256], pbb[1][0:64, :].rearrange("p (h t) -> p h t", h=8),
                    reads=[pbn[1]], writes=["ART%d" % pb])
            self.cp("act", BT[pb][:], pbb[2][0:64, :].rearrange("p (h t) -> p h t", h=8), reads=[pbn[2]], writes=["BT%d" % pb])
            yield
            for hh in range(8):
                hs = slice(hh * 64, (hh + 1) * 64)
                ts_ = slice(hh * 128, (hh + 1) * 128)
                self.tr(pbb[0][0:64, ts_], Kb[pb][:, hs], idb[:], reads=["Kb%d" % pb, "idb"], writes=[pbn[0]])
            self.cp("dve", KTt[pb][:], pbb[0][0:64, :].rearrange("p (h t) -> p h t", h=8), reads=[pbn[0]], writes=["KTt%d" % pb])
            yield

        def solve(i, pb):
            t0 = i * 128
            pc = pac[pb]
            pn_ = "pac%d" % pb
            art, bt, ktt = ART[pb], BT[pb], KTt[pb]
            an, bn_, kn = "ART%d" % pb, "BT%d" % pb, "KTt%d" % pb
            vb, vn = Vb[pb], "Vb%d" % pb
            if d == 1:
                self.load(oft[:], self.of[t0:t0 + 128, :], writes=["oft"])
            for hh in range(8):
                qq, qn = sbk[hh % 3], sbn[hh % 3]
                self.mm(qq[:, 0:256], bt[:, hh, :], art[0:64, hh, :], True, True, reads=[bn_, an], writes=[qn])
                self.mm(qq[:, 256:512], ktt[:, hh, :], art[0:64, hh, :], True, True, reads=[kn, an], writes=[qn])
                self.tt("dve", ATall[:, hh, :], qq[:], m4[:], ALU.mult, reads=[qn, "m4"], writes=[("AT", hh)])
                if hh == 3:
                    yield
            yield
            for g in range(2):
                for j in range(4):
                    hh = g * 4 + j
                    self.mm(s3[:, j * 128:(j + 1) * 128], art[0:64, hh, 0:128], bt[:, hh, :], True, True,
                            reads=[an, bn_], writes=["sq3"])
                self.tt("dve", PP[0][:, g * 4:(g + 1) * 4, 0:128], s3[:].rearrange("p (j s) -> p j s", j=4),
                        mn4[:].rearrange("p (j s) -> p j s", j=4), ALU.mult, reads=["sq3", "mn4"],
                        writes=[("PP", 0, 2 * g), ("PP", 0, 2 * g + 1)])
            self.cp("pool", PP[0][:, :, 128:256], ATall[:, :, 0:128], reads=[("AT", hh) for hh in range(8)],
                    writes=[("PP", 0, pr) for pr in range(4)])
            for hh in range(8):
                hs = slice(hh * 64, (hh + 1) * 64)
                self.mm(s4[:, hs], art[:, hh, 0:128], STb[:, hh, :], True, False, reads=[an, "STb"], writes=["sq4"])
                self.mm(s4[:, hs], ATall[:, hh, 256:384], vb[:, hs], False, True, reads=[("AT", hh), vn], writes=["sq4"])
            self.cp("act", Wb[:], s4[:], reads=["sq4"], writes=["Wb"])
            yield
            rot = 0
            for j in range(7):
                cb = j % 2
                cur = PP[cb]
                for hh in range(8):
                    hs = slice(hh * 64, (hh + 1) * 64)
                    self.mm(s3[:, hs], cur[:, hh, 128:256], Wb[:, hs], True, False,
                            reads=[("PP", cb, hh // 2), "Wb"], writes=["sq3"])
                    self.mm(s3[:, hs], idb[:], Wb[:, hs], False, True, reads=["idb", "Wb"], writes=["sq3"])
                self.cp("act", Wb[:], s3[:], reads=["sq3"], writes=["Wb"])
                if j < 6:
                    nxt = PP[1 - cb]
                    for pr in range(4):
                        bankt, bname = sbk[rot % 3], sbn[rot % 3]
                        rot += 1
                        for u in range(2):
                            hh = pr * 2 + u
                            self.mm(bankt[:, u * 256:u * 256 + 128], cur[:, hh, 128:256], cur[:, hh, 0:128], True, True,
                                    reads=[("PP", cb, pr)], writes=[bname])
                            self.mm(bankt[:, u * 256 + 128:u * 256 + 256], cur[:, hh, 0:128], cur[:, hh, 128:256], True, True,
                                    reads=[("PP", cb, pr)], writes=[bname])
                        self.cp("dve" if pr == 0 else "act",
                                nxt[:, pr * 2:pr * 2 + 2, :].rearrange("p a b -> p (a b)"), bankt[:],
                                reads=[bname], writes=[("PP", 1 - cb, pr)])
                yield
            for hh in range(8):
                hs = slice(hh * 64, (hh + 1) * 64)
                self.mm(s4[:, hs], art[:, hh, 128:256], STb[:, hh, :], True, False, reads=[an, "STb"], writes=["sq4"])
                self.mm(s4[:, hs], ATall[:, hh, 128:256], Wb[:, hs], False, False, reads=[("AT", hh), "Wb"], writes=["sq4"])
                self.mm(s4[:, hs], ATall[:, hh, 384:512], vb[:, hs], False, True, reads=[("AT", hh), vn], writes=["sq4"])
            self.cp("act", osb[:], s4[:], reads=["sq4"], writes=["osb"])
            for hh in range(8):
                hs = slice(hh * 64, (hh + 1) * 64)
                self.mm(s3[0:64, hs], Btb[pb][:, hs], Wb[:, hs], True, False, reads=["Btb%d" % pb, "Wb"], writes=["sq3"])
                self.mm(s3[0:64, hs], Ktb[pb][:, hs], vb[:, hs], False, True, reads=["Ktb%d" % pb, vn], writes=["sq3"])
            self.tt("dve", ST32[:], ST32[:], gC[pb][:].unsqueeze(2).broadcast_to([64, 8, 64]), ALU.mult,
                    reads=["ST32", "gC%d" % pb], writes=["ST32"])
            self.tt("dve", ST32[:], ST32[:], s3[0:64, :].rearrange("p (h e) -> p h e", h=8), ALU.add,
                    reads=["ST32", "sq3"], writes=["ST32"])
            self.cp("act", STb[0:64, :, :], ST32[:], reads=["ST32"], writes=["STb"])
            yield
            if d == 0:
                self.store(self.of[t0:t0 + 128, :], osb[:], reads=["osb"], writes=[("of", i)])
                return
            r_ = pc[:, 0:512]
            k_ = pc[:, 512:1024]
            lg_ = pc[:, 1792:1952]
            self.tt("dve", oft[:], oft[:], osb[:], ALU.add, reads=["oft", "osb"], writes=["oft"])
            self.red(st8[:], v3(oft[:]), reads=["oft"], writes=["st8"])
            self.ts(st8[:], st8[:], 1.0 / 64, None, ALU.mult, ALU.bypass, reads=["st8"], writes=["st8"])
            self.tt("dve", v3(cen[:]), v3(oft[:]), st8[:].unsqueeze(2).broadcast_to([128, 8, 64]), ALU.subtract,
                    reads=["oft", "st8"], writes=["cen"])
            self.tt("pool", sq2[:], cen[:], cen[:], ALU.mult, reads=["cen"], writes=["sq2"])
            self.red(sv8[:], v3(sq2[:]), reads=["sq2"], writes=["sv8"])
            self.ts(sv8[:], sv8[:], 1.0 / 64, GN_EPS, ALU.mult, ALU.add, reads=["sv8"], writes=["sv8"])
            self.act(sv8[:], sv8[:], AF.Sqrt, reads=["sv8"], writes=["sv8"])
            self.recip(sv8[:], sv8[:], reads=["sv8"], writes=["sv8"])
            yield
            self.tt("dve", v3(cen[:]), v3(cen[:]), sv8[:].unsqueeze(2).broadcast_to([128, 8, 64]), ALU.mult,
                    reads=["cen", "sv8"], writes=["cen"])
            self.tt("pool", cen[:], cen[:], gngbc[:], ALU.mult, reads=["cen", "gngbc"], writes=["cen"])
            self.tt("pool", cen[:], cen[:], gnbbc[:], ALU.add, reads=["cen", "gnbbc"], writes=["cen"])
            self.tt("dve", sq2[:], r_, k_, ALU.mult, reads=[pn_], writes=["sq2"])
            self.tt("pool", sq2[:], sq2[:], rkbc[:], ALU.mult, reads=["sq2", "rkbc"], writes=["sq2"])
            self.red(sb8[:], v3(sq2[:]), reads=["sq2"], writes=["sb8"])
            self.tt("dve", v3(bon[:]), v3(pc[:, 1024:1536]),
                    sb8[:].unsqueeze(2).broadcast_to([128, 8, 64]), ALU.mult, reads=[pn_, "sb8"], writes=["bon"])
            self.tt("dve", cen[:], cen[:], bon[:], ALU.add, reads=["cen", "bon"], writes=["cen"])
            self.act(gs[:], lg_, AF.Sigmoid, reads=[pn_], writes=["gs"])
            self.tr(s3b[:, 0:128], gs[:, 0:128], idb[:], reads=["gs", "idb"], writes=["sq3"])
            self.tr(s3b[0:32, 128:256], gs[:, 128:160], idb[:], reads=["gs", "idb"], writes=["sq3"])
            self.cp("act", gT[:, 0, :], s3b[:, 0:128], reads=["sq3"], writes=["gT"])
            self.cp("act", gT[0:32, 1, :], s3b[0:32, 128:256], reads=["sq3"], writes=["gT"])
            self.mm(s4[:], gT[:, 0, :], g2b[:, 0, :], True, False, reads=["gT", "g2b"], writes=["sq4"])
            self.mm(s4[:], gT[:, 1, :], g2b[:, 1, :], False, True, reads=["gT", "g2b"], writes=["sq4"])
            self.tt("dve", yat[:], cen[:], s4[:], ALU.mult, reads=["cen", "sq4"], writes=["yat"])
            self.store(self.ya[t0:t0 + 128, :], yat[:], reads=["yat"], writes=[("ya", i)])
            yield

        order = list(range(NT)) if d == 0 else list(range(NT - 1, -1, -1))
        for n0 in range(min(2, NT)):
            for _ in prep(order[n0], n0 % 3):
                pass
        for n, i in enumerate(order):
            gs_ = solve(i, n % 3)
            gp_ = prep(order[n + 2], (n + 2) % 3) if n + 2 < NT else None
            while gs_ is not None or gp_ is not None:
                if gs_ is not None:
                    try:
                        next(gs_)
                    except StopIteration:
                        gs_ = None
                if gp_ is not None:
                    try:
                        next(gp_)
                    except StopIteration:
                        gp_ = None
        self.end_phase()

    def phase_MP(self, l):
        NT = self.NT
        W = self.w
        self.begin_phase()
        sb, ps = self.sb, self.ps
        qg = sb("qg", [128, 256], F32)
        kvg = sb("kvg", [128, 128], F32)
        wuq = sb("wuq", [128, 2, 768], BF16)
        wukv = sb("wukv", [128, 1, 1024], BF16)
        idb = sb("idb", [128, 128], BF16)
        pm2 = [sb("pm", [128, 416], F32) for _ in range(2)]
        cs2 = [sb("cs", [128, 32], F32) for _ in range(2)]
        sn2 = [sb("sn", [128, 32], F32) for _ in range(2)]
        junk = sb("junk", [128, 256], F32)
        ss = sb("ss", [128, 1], F32)
        rstd = sb("rstd", [128, 1], F32)
        nb = sb("nb", [128, 384], BF16)
        nT = sb("nT", [128, 3, 128], BF16)
        qf = sb("qf", [128, 768], F32)
        kvf = sb("kvf", [128, 1024], F32)
        t1 = sb("t1", [128, 8, 32], F32)
        t2 = sb("t2", [128, 8, 32], F32)
        kro = sb("kro", [128, 32], F32)
        kr2 = sb("kr2", [128, 32], F32)
        Qa = sb("Qa", [128, 8, 96], BF16)
        Ka = sb("Ka", [128, 8, 96], BF16)
        Va = sb("Va", [128, 8, 65], BF16)
        QTt = sb("QTt", [96, 8, 128], BF16)
        KTt = sb("KTt", [96, 8, 128], BF16)
        q = [ps("q%d" % i, [128, 512], F32) for i in range(7)]
        qb = [t[:].bitcast(BF16) for t in q]
        self.load(idb[:], self.c_ident, writes=["idb"], cast=True)
        self.bcast_load(qg, W["q_norm_g"][l:l + 1, :], 256, "qg")
        self.bcast_load(kvg, W["kv_norm_g"][l:l + 1, :], 128, "kvg")
        self.load_w_bf16(wuq, W["w_uq"][l], 256, "wuq")
        self.load_w_bf16(wukv, W["w_ukv"][l], 128, "wukv")
        self.memset("dve", Va[:], 1.0, writes=["Va"])
        qf3 = qf[:].rearrange("p (h e) -> p h e", h=8)
        kvf3 = kvf[:].rearrange("p (h e) -> p h e", h=8)
        def loads(i):
            b = i % 2
            t0 = i * 128
            self.load(pm2[b][:], self.P[t0:t0 + 128, 4000:4416], writes=["pm%d" % b])
            self.load(cs2[b][:], self.c_cos[t0:t0 + 128, :], writes=["cs%d" % b])
            self.load(sn2[b][:], self.c_sin[t0:t0 + 128, :], writes=["sn%d" % b])

        loads(0)
        for i in range(NT):
            t0 = i * 128
            if i + 1 < NT:
                loads(i + 1)
            pm, cs, sn = pm2[i % 2], cs2[i % 2], sn2[i % 2]
            pmn, csn, snn = "pm%d" % (i % 2), "cs%d" % (i % 2), "sn%d" % (i % 2)
            self.rmsnorm(pm[:, 0:256], pmn, 256, qg[:], "qg", nb[:, 0:256], "nbq", junk[:, 0:256], ss[:], rstd[:], "M")
            self.rmsnorm(pm[:, 256:384], pmn, 128, kvg[:], "kvg", nb[:, 256:384], "nbk", junk[:, 0:128], ss[:], rstd[:], "M")
            for c in range(3):
                self.tr(qb[0][:, c * 128:(c + 1) * 128], nb[:, c * 128:(c + 1) * 128], idb[:],
                        reads=["nbq", "nbk", "idb"], writes=["q0"])
            self.cp("act", nT[:].rearrange("p a b -> p (a b)"), qb[0][:, 0:384], reads=["q0"], writes=["nT"])
            for c in range(2):
                self.mm(q[1][:], nT[:, c, :], wuq[:, c, 0:512], c == 0, c == 1, reads=["nT", "wuq"], writes=["q1"])
            for c in range(2):
                self.mm(q[2][:, 0:256], nT[:, c, :], wuq[:, c, 512:768], c == 0, c == 1, reads=["nT", "wuq"], writes=["q2"])
            self.mm(q[3][:], nT[:, 2, :], wukv[:, 0, 0:512], True, True, reads=["nT", "wukv"], writes=["q3"])
            self.mm(q[4][:], nT[:, 2, :], wukv[:, 0, 512:1024], True, True, reads=["nT", "wukv"], writes=["q4"])
            self.cp("act", qf[:, 0:512], q[1][:], reads=["q1"], writes=["qf"])
            self.cp("dve", qf[:, 512:768], q[2][:, 0:256], reads=["q2"], writes=["qf"])
            self.cp("act", kvf[:, 0:512], q[3][:], reads=["q3"], writes=["kvf"])
            self.cp("dve", kvf[:, 512:1024], q[4][:], reads=["q4"], writes=["kvf"])
            self.cp("pool", Qa[:, :, 0:64], qf3[:, :, 0:64], reads=["qf"], writes=["Qa"])
            csb = cs[:].unsqueeze(1).broadcast_to([128, 8, 32])
            self.tt("dve", t1[:], qf3[:, :, 64:96], csb, ALU.mult, reads=["qf", csn], writes=["t1"])
            self.tt("dve", t2[:, :, 0:16], qf3[:, :, 80:96], sn[:, 0:16].unsqueeze(1).broadcast_to([128, 8, 16]), ALU.mult,
                    reads=["qf", snn], writes=["t2"])
            self.tt("dve", t2[:, :, 16:32], qf3[:, :, 64:80], sn[:, 16:32].unsqueeze(1).broadcast_to([128, 8, 16]), ALU.mult,
                    reads=["qf", snn], writes=["t2"])
            self.tt("dve", Qa[:, :, 64:96], t1[:], t2[:], ALU.add, reads=["t1", "t2"], writes=["Qa"])
            self.tt("dve", kro[:], pm[:, 384:416], cs[:], ALU.mult, reads=[pmn, csn], writes=["kro"])
            self.tt("dve", kr2[:, 0:16], pm[:, 400:416], sn[:, 0:16], ALU.mult, reads=[pmn, snn], writes=["kr2"])
            self.tt("dve", kr2[:, 16:32], pm[:, 384:400], sn[:, 16:32], ALU.mult, reads=[pmn, snn], writes=["kr2"])
            self.tt("dve", kro[:], kro[:], kr2[:], ALU.add, reads=["kro", "kr2"], writes=["kro"])
            self.cp("dve", Ka[:, :, 64:96], kro[:].unsqueeze(1).broadcast_to([128, 8, 32]), reads=["kro"], writes=["Ka"])
            self.cp("pool", Ka[:, :, 0:64], kvf3[:, :, 0:64], reads=["kvf"], writes=["Ka"])
            self.cp("pool", Va[:, :, 0:64], kvf3[:, :, 64:128], reads=["kvf"], writes=["Va"])
            for hh in range(8):
                self.tr(qb[5][0:96, hh * 128:(hh + 1) * 128], Qa[:, hh, :], idb[:], reads=["Qa", "idb"], writes=["q5"])
                self.tr(qb[6][0:96, hh * 128:(hh + 1) * 128], Ka[:, hh, :], idb[:], reads=["Ka", "idb"], writes=["q6"])
            self.cp("act", QTt[:].rearrange("p a b -> p (a b)"), qb[5][0:96, :], reads=["q5"], writes=["QTt"])
            self.cp("dve", KTt[:].rearrange("p a b -> p (a b)"), qb[6][0:96, :], reads=["q6"], writes=["KTt"])
            self.store(self.QT[:, :, t0:t0 + 128].rearrange("h p t -> p h t"), QTt[:], reads=["QTt"], writes=[("QT", i)])
            self.store(self.KT[:, :, t0:t0 + 128].rearrange("h p t -> p h t"), KTt[:], reads=["KTt"], writes=[("KT", i)])
            self.store(self.Vd[t0:t0 + 128, :], Va[:].rearrange("p a b -> p (a b)"), reads=["Va"], writes=[("Vd", i)])
        self.end_phase()

    def phase_MM(self):
        NT = self.NT
        S_LEN = self.S_LEN
        QB = min(512, S_LEN)
        nqb = S_LEN // QB
        nj = QB // 128
        LOOK = 2
        self.begin_phase()
        sb, ps = self.sb, self.ps
        Vall = sb("Vall", [128, NT, 520], BF16)
        KTh = [sb("KTh", [96, S_LEN], BF16) for _ in range(2)]
        QTb = [sb("QTb", [96, QB], BF16) for _ in range(2)]
        PT = [sb("PT", [128, QB], BF16) for _ in range(4)]
        OT = sb("OT", [65, QB], F32)
        id32 = sb("id32", [128, 128], F32)
        osm = sb("osm", [128, nj, 64], F32)
        rec = sb("rec", [128, nj], F32)
        q = [ps("q%d" % i, [128, 512], F32) for i in range(7)]
        self.load(id32[:], self.c_ident, writes=["id32"])
        self.load(Vall[:], self.Vd.rearrange("(c p) f -> p c f", p=128), writes=["Vall"])
        blocks = [(hh, qi) for hh in range(8) for qi in range(nqb)]
        stream = [(bi, kc) for bi in range(len(blocks)) for kc in range(NT)]

        def load_k(hh):
            self.load(KTh[hh % 2][:], self.KT[hh], writes=["KTh%d" % (hh % 2)])

        def load_q(bi):
            hh, qi = blocks[bi]
            self.load(QTb[bi % 2][:], self.QT[hh, :, qi * QB:(qi + 1) * QB], writes=["QTb%d" % (bi % 2)])

        def emit_S(idx):
            bi, kc = stream[idx]
            hh, qi = blocks[bi]
            pb = idx % 4
            self.mm(q[pb][:, 0:QB], KTh[hh % 2][:, kc * 128:(kc + 1) * 128], QTb[bi % 2][:], True, True,
                    reads=["KTh%d" % (hh % 2), "QTb%d" % (bi % 2)], writes=["q%d" % pb])

        def epilogue_a(bi):
            ob = 4 + bi % 2
            self.cp("dve", OT[:], q[ob][0:65, 0:QB], reads=["q%d" % ob], writes=["OT"])

        def epilogue_b(bi):
            hh, qi = blocks[bi]
            for j in range(nj):
                self.tr(q[6][:, j * 65:(j + 1) * 65], OT[:, j * 128:(j + 1) * 128], id32[0:65, 0:65],
                        reads=["OT", "id32"], writes=["q6"])
            o3 = q[6][:, 0:nj * 65].rearrange("p (j e) -> p j e", j=nj)
            self.recip(rec[:], o3[:, :, 64], reads=["q6"], writes=["rec"])
            self.tt("dve", osm[:], o3[:, :, 0:64], rec[:].unsqueeze(2).broadcast_to([128, nj, 64]), ALU.mult,
                    reads=["q6", "rec"], writes=["osm"])
            self.store(self.yb[qi * QB:(qi + 1) * QB, hh * 64:(hh + 1) * 64].rearrange("(j p) e -> p j e", p=128),
                       osm[:], reads=["osm"], writes=[("yb", hh, qi)])

        load_k(0)
        load_q(0)
        if len(blocks) > 1:
            load_q(1)
        for idx in range(min(LOOK, len(stream))):
            emit_S(idx)
        pending = None
        for idx, (bi, kc) in enumerate(stream):
            hh, qi = blocks[bi]
            if kc == 0:
                if qi == 0 and hh + 1 < 8:
                    load_k(hh + 1)
            if idx + LOOK < len(stream):
                emit_S(idx + LOOK)
            pb = idx % 4
            ob = 4 + bi % 2
            self.act(PT[pb][:], q[pb][:, 0:QB], AF.Exp, reads=["q%d" % pb], writes=["PT%d" % pb], scale=SCALE)
            self.mm(q[ob][0:65, 0:QB], Vall[:, kc, hh * 65:(hh + 1) * 65], PT[pb][:], kc == 0, kc == NT - 1,
                    reads=["Vall", "PT%d" % pb], writes=["q%d" % ob])
            if pending is not None and kc == min(3, NT - 1):
                epilogue_b(pending)
                pending = None
            if kc == NT - 1:
                epilogue_a(bi)
                pending = bi
                if bi + 2 < len(blocks):
                    load_q(bi + 2)
        if pending is not None:
            epilogue_b(pending)
        self.end_phase()

    def phase_C1(self, l, xin):
        NT = self.NT
        W = self.w
        self.begin_phase()
        sb, ps = self.sb, self.ps
        woa = sb("woa", [128, 4, D], BF16)
        wob = sb("wob", [128, 4, D], BF16)
        wout = sb("wout", [128, 8, D], BF16)
        idb = sb("idb", [128, 128], BF16)
        xt = [sb("xt", [128, D], F32) for _ in range(4)]
        gt = [sb("gt", [128, 2048], F32) for _ in range(2)]
        yat = [sb("yat", [128, 512], F32) for _ in range(2)]
        ybt = [sb("ybt", [128, 512], F32) for _ in range(2)]
        yab = [sb("yab", [128, D], BF16) for _ in range(2)]
        yT = [sb("yT", [128, 8, 128], BF16) for _ in range(2)]
        m1 = sb("m1", [128, D], F32)
        m2 = sb("m2", [128, D], F32)
        mixb = [sb("mixb", [128, D], BF16) for _ in range(2)]
        mixT = sb("mixT", [128, 8, 128], BF16)
        x1t = sb("x1t", [128, D], F32)
        q = [ps("q%d" % i, [128, 512], F32) for i in range(8)]
        q6b = q[6][:].bitcast(BF16)
        q7b = q[7][:].bitcast(BF16)
        self.load(idb[:], self.c_ident, writes=["idb"], cast=True)
        self.load_w_bf16(woa, W["w_oa"][l], 512, "woa")
        self.load_w_bf16(wob, W["w_ob"][l], 512, "wob")
        self.load_w_bf16(wout, W["w_out"][l], D, "wout")

        def loads(i):
            b = i % 2
            t0 = i * 128
            self.load(xt[i % 4][:], xin[t0:t0 + 128, :], writes=["xt%d" % (i % 4)])
            self.load(gt[b][:], self.P[t0:t0 + 128, 0:2048], writes=["gt%d" % b])
            self.load(yat[b][:], self.ya[t0:t0 + 128, :], writes=["yat%d" % b])
            self.load(ybt[b][:], self.yb[t0:t0 + 128, :], writes=["ybt%d" % b])

        def s1(i):
            b = i % 2
            self.cp("dve", yab[b][:, 0:512], yat[b][:], reads=["yat%d" % b], writes=[("yab", b, 0)])
            self.cp("pool", yab[b][:, 512:1024], ybt[b][:], reads=["ybt%d" % b], writes=[("yab", b, 1)])
            for k in range(8):
                self.tr(q6b[:, k * 128:(k + 1) * 128], yab[b][:, k * 128:(k + 1) * 128], idb[:],
                        reads=[("yab", b, 0), ("yab", b, 1), "idb"], writes=["q6"])
            self.cp("act", yT[b][:].rearrange("p a b -> p (a b)"), q6b[:], reads=["q6"], writes=["yT%d" % b])
            self.act(gt[b][:], gt[b][:], AF.Sigmoid, reads=["gt%d" % b], writes=["gt%d" % b])

        def s2(i):
            b = i % 2
            for hf in range(2):
                hs = slice(hf * 512, (hf + 1) * 512)
                for k in range(4):
                    self.mm(q[hf][:], yT[b][:, k, :], woa[:, k, hs], k == 0, k == 3, reads=["yT%d" % b, "woa"], writes=["q%d" % hf])
                for k in range(4):
                    self.mm(q[2 + hf][:], yT[b][:, 4 + k, :], wob[:, k, hs], k == 0, k == 3, reads=["yT%d" % b, "wob"],
                            writes=["q%d" % (2 + hf)])
            for hf in range(2):
                hs = slice(hf * 512, (hf + 1) * 512)
                hs2 = slice(1024 + hf * 512, 1024 + (hf + 1) * 512)
                self.tt("dve", m1[:, hs], gt[b][:, hs], q[hf][:], ALU.mult, reads=["gt%d" % b, "q%d" % hf], writes=[("m1", hf)])
                self.tt("dve", m2[:, hs], gt[b][:, hs2], q[2 + hf][:], ALU.mult, reads=["gt%d" % b, "q%d" % (2 + hf)],
                        writes=[("m2", hf)])
                self.tt("pool", mixb[b][:, hs], m1[:, hs], m2[:, hs], ALU.add, reads=[("m1", hf), ("m2", hf)],
                        writes=[("mixb", b, hf)])

        def s3(i):
            b = i % 2
            t0 = i * 128
            xb = i % 4
            for k in range(8):
                self.tr(q7b[:, k * 128:(k + 1) * 128], mixb[b][:, k * 128:(k + 1) * 128], idb[:],
                        reads=[("mixb", b, 0), ("mixb", b, 1), "idb"], writes=["q7"])
            self.cp("act", mixT[:].rearrange("p a b -> p (a b)"), q7b[:], reads=["q7"], writes=["mixT"])
            for hf in range(2):
                hs = slice(hf * 512, (hf + 1) * 512)
                for k in range(8):
                    self.mm(q[4 + hf][:], mixT[:, k, :], wout[:, k, hs], k == 0, k == 7, reads=["mixT", "wout"],
                            writes=["q%d" % (4 + hf)])
                self.tt("dve", x1t[:, hs], xt[xb][:, hs], q[4 + hf][:], ALU.add, reads=["xt%d" % xb, "q%d" % (4 + hf)],
                        writes=[("x1t", hf)])
            self.store(self.x1[t0:t0 + 128, :], x1t[:], reads=[("x1t", 0), ("x1t", 1)], writes=[("x1", i)])

        loads(0)
        if NT > 1:
            loads(1)
        s1(0)
        for i in range(NT + 1):
            if i + 1 < NT:
                s1(i + 1)
            if i < NT:
                s2(i)
            if i + 2 < NT:
                loads(i + 2)
            if i >= 1:
                s3(i - 1)
        self.end_phase()

    def phase_C2(self, l, last, yout):
        NT = self.NT
        W = self.w
        self.begin_phase()
        sb, ps = self.sb, self.ps
        wgu = sb("wgu", [128, 8, 2 * DFF], BF16)
        wdn = sb("wdn", [128, 22, D], BF16)
        gbc = sb("gbc", [128, D], F32)
        idb = sb("idb", [128, 128], BF16)
        xt = [sb("xt", [128, D], F32) for _ in range(2)]
        junk = sb("junk", [128, D], F32)
        ss = sb("ss", [128, 1], F32)
        rstd = sb("rstd", [128, 1], F32)
        h = sb("h", [128, D], BF16)
        hT = [sb("hT", [128, 8, 128], BF16) for _ in range(2)]
        sl = [sb("sl", [128, 256], F32) for _ in range(2)]
        actb = sb("actb", [128, DFF], BF16)
        actT = sb("actT", [128, 22, 128], BF16)
        x2t = sb("x2t", [128, D], F32)
        if last:
            fbc = sb("fbc", [128, D], F32)
        q = [ps("q%d" % i, [128, 512], F32) for i in range(6)]
        qTb = q[4][:].bitcast(BF16)
        qT2 = q[5][:].bitcast(BF16)
        self.load(idb[:], self.c_ident, writes=["idb"], cast=True)
        self.bcast_load(gbc, W["norm_ffn_g"][l:l + 1, :], D, "gbc")
        if last:
            self.bcast_load(fbc, W["final_norm_g"][0:1, :], D, "fbc")
        self.load_w_bf16(wgu, W["w_gu"][l], D, "wgu")
        self.load_w_bf16(wdn, W["w_down"][l], DFF, "wdn")

        def norm(i):
            b = i % 2
            self.load(xt[b][:], self.x1[i * 128:(i + 1) * 128, :], writes=["xt%d" % b])
            self.rmsnorm(xt[b][:], "xt%d" % b, D, gbc[:], "gbc", h[:], "h", junk[:], ss[:], rstd[:], "F")

        def trans(i):
            b = i % 2
            for k in range(8):
                self.tr(qTb[:, k * 128:(k + 1) * 128], h[:, k * 128:(k + 1) * 128], idb[:], reads=["h", "idb"], writes=["q4"])
            self.cp("act", hT[b][:].rearrange("p a b -> p (a b)"), qTb[:], reads=["q4"], writes=["hT%d" % b])

        def tpose(j):
            o = (j % 4) * 256
            for u in range(2):
                self.tr(qT2[:, o + u * 128:o + (u + 1) * 128], actb[:, j * 256 + u * 128:j * 256 + (u + 1) * 128], idb[:],
                        reads=[("actb", j), "idb"], writes=["q5"])
            self.cp("dve", actT[:, 2 * j:2 * j + 2, :].rearrange("p a b -> p (a b)"),
                    qT2[:, o:o + 256], reads=["q5"], writes=[("actT", j)])

        norm(0)
        trans(0)
        for i in range(NT):
            b = i % 2
            t0 = i * 128
            xn = "xt%d" % b
            hn = "hT%d" % b
            if i + 1 < NT:
                norm(i + 1)
            for j in range(11):
                bk = q[j % 2]
                bn = "q%d" % (j % 2)
                for k in range(8):
                    self.mm(bk[:, 0:256], hT[b][:, k, :], wgu[:, k, j * 256:(j + 1) * 256], k == 0, k == 7,
                            reads=[hn, "wgu"], writes=[bn])
                for k in range(8):
                    self.mm(bk[:, 256:512], hT[b][:, k, :], wgu[:, k, DFF + j * 256:DFF + (j + 1) * 256], k == 0, k == 7,
                            reads=[hn, "wgu"], writes=[bn])
                self.act(sl[j % 2][:], bk[:, 0:256], AF.Silu, reads=[bn], writes=["sl%d" % (j % 2)])
                self.tt("dve", actb[:, j * 256:(j + 1) * 256], sl[j % 2][:], bk[:, 256:512], ALU.mult,
                        reads=["sl%d" % (j % 2), bn], writes=[("actb", j)])
                if j >= 1:
                    tpose(j - 1)
            if i + 1 < NT:
                trans(i + 1)
            tpose(10)
            for hf in range(2):
                hs = slice(hf * 512, (hf + 1) * 512)
                for c in range(22):
                    self.mm(q[2 + hf][:], actT[:, c, :], wdn[:, c, hs], c == 0, c == 21,
                            reads=[("actT", c // 2), "wdn"], writes=["q%d" % (2 + hf)])
                self.tt("dve", x2t[:, hs], xt[b][:, hs], q[2 + hf][:], ALU.add, reads=[xn, "q%d" % (2 + hf)],
                        writes=["x2t"])
            if last:
                self.rmsnorm(x2t[:], "x2t", D, fbc[:], "fbc", x2t[:], "x2t", junk[:], ss[:], rstd[:], "F")
                self.store(yout[t0:t0 + 128, :], x2t[:], reads=["x2t"], writes=[("y", i)])
            else:
                self.store(self.x2[t0:t0 + 128, :], x2t[:], reads=["x2t"], writes=[("x2", i)])
        self.end_phase()

    def build(self, phases=None):
        def on(p):
            return phases is None or p in phases
        for s in range(self.NSEQ):
            for l in range(self.depth):
                last = l == self.depth - 1
                xin = self.x[s] if l == 0 else self.x2
                if on("A"):
                    self.phase_A(l, xin)
                if on("R0"):
                    self.phase_R(l, 0)
                if on("R1"):
                    self.phase_R(l, 1)
                if on("MP"):
                    self.phase_MP(l)
                if on("MM"):
                    self.phase_MM()
                if on("C1"):
                    self.phase_C1(l, xin)
                if on("C2"):
                    self.phase_C2(l, last, self.y[s])
        self.S.emit()
        self.S.stack.close()
        return self.nc


def make_consts(S_LEN):
    s = np.arange(128)[:, None]
    t = np.arange(128)[None, :]
    tri = np.stack([(s <= t), (s >= t)]).astype(np.float32)
    strict = [(s < t).astype(np.float32), (s > t).astype(np.float32)]
    incl = [(s <= t).astype(np.float32), (s >= t).astype(np.float32)]
    m4 = np.stack([np.concatenate([strict[d], incl[d], strict[d], incl[d]], axis=1) for d in range(2)])
    mn = [(t < s).astype(np.float32), (t > s).astype(np.float32)]
    mn4 = np.stack([np.concatenate([mn[d]] * 4, axis=1) for d in range(2)])
    pos = np.arange(S_LEN, dtype=np.float32)
    inv_freq = (1.0 / (np.float32(10000.0) ** (np.arange(0, 32, 2, dtype=np.float32) / np.float32(32)))).astype(np.float32)
    ang = pos[:, None] * inv_freq[None, :]
    ang = np.concatenate([ang, ang], axis=-1).astype(np.float32)
    cos = np.cos(ang).astype(np.float32)
    sin = np.sin(ang).astype(np.float32)
    sin_s = sin.copy()
    sin_s[:, 0:16] = -sin_s[:, 0:16]
    return dict(c_ident=np.eye(128, dtype=np.float32), c_tri=tri, c_m4=m4.astype(np.float32),
                c_mn4=mn4.astype(np.float32), c_ones=np.ones((128, 128), np.float32),
                c_cos=cos, c_sin=sin_s)


_WNAMES = ["norm_mix_g", "w_in", "shift_mu", "decay_w2", "decay_w0", "iclr_a2", "iclr_a0", "gate_g2", "k_k", "k_a",
           "r_k", "gn_g", "gn_b", "w_oa", "q_norm_g", "w_uq", "kv_norm_g", "w_ukv", "w_ob", "w_out", "norm_ffn_g",
           "w_gu", "w_down", "final_norm_g"]


def prep_weights(inputs, depth):
    out = {}
    for n in _WNAMES:
        a = np.ascontiguousarray(np.asarray(inputs[n], dtype=np.float32))
        if n == "r_k":
            a = a.reshape(a.shape[0], 512)
        if n == "final_norm_g":
            a = a.reshape(1, D)
        else:
            a = a[:depth]
        out[n] = np.ascontiguousarray(a)
    return out


def kernel(**inputs):
    xp = np.asarray(inputs["x_prompt"], dtype=np.float32)
    xs = np.asarray(inputs["x_sample"], dtype=np.float32)
    S_LEN = xp.shape[1]
    x_all = np.concatenate([xp, xs], axis=0)
    nseq = x_all.shape[0] // NCORES
    wts = prep_weights(inputs, DEPTH)
    consts = make_consts(S_LEN)
    nc = Builder(S_LEN, nseq, DEPTH).build()
    in_maps = []
    for c in range(NCORES):
        m = dict(x=np.ascontiguousarray(x_all[c * nseq:(c + 1) * nseq]))
        m.update(wts)
        m.update(consts)
        in_maps.append(m)
    res = run_bass_kernel_spmd(nc, in_maps, core_ids=list(range(NCORES)))
    y = np.concatenate([r["y"] for r in res.results], axis=0)
    return (np.ascontiguousarray(y[:xp.shape[0]]), np.ascontiguousarray(y[xp.shape[0]:]))
```
